# Optimizing a Trainium2 kernel written in Bass

```python
import math, functools
import jax, jax.numpy as jnp
from jax import lax
import numpy as np

D_MODEL = 1024
BATCH = 8
SEQ = 8192
DEPTH = 1

CTX_LEN = 256
GRID_W = 64
HG_HEADS = 4
HG_DK = 128
HG_DV = 128
HG_WIDTH = HG_HEADS * HG_DV
HG_COLS = 5 * HG_WIDTH
HG_CHUNK = 64
RW_HEADS = 8
RW_HEAD = 64
RW_WIDTH = RW_HEADS * RW_HEAD
RW_W_LORA = 32
RW_A_LORA = 32
RW_G_LORA = 96
RW_COLS = 3 * RW_WIDTH + 2 * RW_W_LORA + 2 * RW_A_LORA + RW_G_LORA
IN_COLS = HG_COLS + RW_COLS
MIX_WIDTH = HG_WIDTH + RW_WIDTH
D_FF = int(math.ceil(8 * D_MODEL / 3 / 256)) * 256
ADA_EPS = 1e-6
LN_EPS = 1e-5
HG_NORM_EPS = 1e-5
RW_GN_EPS = 64e-5

kernel_name = 'hybrid_hgrn2_rwkv7_flow_block'


def _ln(x, eps=ADA_EPS):
    xf = x.astype(jnp.float32)
    m = jnp.mean(xf, -1, keepdims=True)
    v = jnp.mean(jnp.square(xf - m), -1, keepdims=True)
    return (xf - m) * lax.rsqrt(v + eps)


def _post_ln(x, g, b):
    return (_ln(x, LN_EPS) * g + b).astype(x.dtype)


def _modulate(x, shift, scale):
    return (_ln(x) * (1.0 + scale) + shift).astype(x.dtype)


def _rev(a):
    return jnp.flip(a, axis=1)


def _shift_seq(z):
    h = z.shape[-1] // 2
    p = jnp.pad(z, ((0, 0), (1, 1), (0, 0)))
    return jnp.concatenate([p[:, :-2, :h], p[:, 2:, h:]], axis=-1)


def _qshift_grid(z, rows):
    B, T, C = z.shape
    q = C // 4
    g = jnp.pad(z.reshape(B, rows, GRID_W, C), ((0, 0), (1, 1), (1, 1), (0, 0)))
    left = g[:, 1:-1, :-2, :q]
    right = g[:, 1:-1, 2:, q:2 * q]
    up = g[:, :-2, 1:-1, 2 * q:3 * q]
    down = g[:, 2:, 1:-1, 3 * q:]
    return jnp.concatenate([left, right, up, down], axis=-1).reshape(B, T, C)


def _hgrn2_chunk_scan(q, k, log_f, v, s0):
    B, T, H, DK = q.shape
    DV = v.shape[-1]
    n = T // HG_CHUNK

    def chunks(a):
        return a.reshape(B, n, HG_CHUNK, H, a.shape[-1]).transpose(1, 0, 3, 2, 4)

    mask = jnp.tril(jnp.ones((HG_CHUNK, HG_CHUNK), bool))[:, :, None]

    def step(s, inp):
        qc, kc, gc, vc = inp
        b = jnp.cumsum(gc, axis=2)
        diff = jnp.where(mask, b[:, :, :, None, :] - b[:, :, None, :, :], -jnp.inf)
        scores = jnp.einsum('bhtk,bhtsk->bhts', qc, jnp.exp(diff) * kc[:, :, None, :, :])
        o = (jnp.einsum('bhts,bhsv->bhtv', scores, vc)
             + jnp.einsum('bhtk,bhkv->bhtv', qc * jnp.exp(b), s))
        b_end = b[:, :, -1:, :]
        s = (jnp.exp(b_end[:, :, 0, :])[..., None] * s
             + jnp.einsum('bhsk,bhsv->bhkv', kc * jnp.exp(b_end - b), vc))
        return s, o

    s, o = lax.scan(step, s0, (chunks(q), chunks(k), chunks(log_f), chunks(v)))
    return o.transpose(1, 0, 3, 2, 4).reshape(B, T, H, DV), s


def _rwkv7_scan(r, w, k, v, a, b, s0):
    def step(s, inp):
        rt, wt, kt, vt, at, bt = inp
        sa = jnp.einsum('bhvk,bhk->bhv', s, at)
        s = s * wt[:, :, None, :] + sa[..., None] * bt[:, :, None, :] + vt[..., None] * kt[:, :, None, :]
        return s, jnp.einsum('bhvk,bhk->bhv', s, rt)

    xs = tuple(jnp.swapaxes(z, 0, 1) for z in (r, w, k, v, a, b))
    s, y = lax.scan(step, s0, xs)
    return jnp.swapaxes(y, 0, 1), s


def _token_mix(u, shift_fn, w_in_l, mu_l, lb_l, hg_norm_w_l, w0_l, w2_l, a0_l, a2_l, g2_l,
               kk_l, ka_l, rk_l, lnw_l, lnb_l, states):
    B, T, _ = u.shape
    f32 = jnp.float32
    z = jnp.einsum('btd,dc->btc', u, w_in_l).astype(f32)
    zh = z[..., :HG_COLS]
    zr = z[..., HG_COLS:]
    zr = zr + mu_l * (shift_fn(zr) - zr)
    hs_f, hs_b, rs_f, rs_b = states

    q, f_fwd, f_bwd, i, og = jnp.split(zh, 5, axis=-1)
    hg = lambda a: a.reshape(B, T, HG_HEADS, -1)
    q = hg(jax.nn.silu(q))
    i = hg(i)
    ff = lb_l[0] + (1.0 - lb_l[0]) * jax.nn.sigmoid(f_fwd)
    fb = lb_l[1] + (1.0 - lb_l[1]) * jax.nn.sigmoid(f_bwd)
    o_f, hs_f = _hgrn2_chunk_scan(q, hg(1.0 - ff), hg(jnp.log(ff)), i, hs_f)
    o_b, hs_b = _hgrn2_chunk_scan(_rev(q), _rev(hg(1.0 - fb)), _rev(hg(jnp.log(fb))), _rev(i), hs_b)
    o = o_f + _rev(o_b)
    o = o * lax.rsqrt(jnp.mean(jnp.square(o), -1, keepdims=True) + HG_NORM_EPS) * hg_norm_w_l
    y_hg = (o * jax.nn.silu(hg(og))).reshape(B, T, HG_WIDTH)

    W = RW_WIDTH
    r, k, v = zr[..., :W], zr[..., W:2 * W], zr[..., 2 * W:3 * W]
    o1 = 3 * W
    o2 = o1 + 2 * RW_W_LORA
    o3 = o2 + 2 * RW_A_LORA
    wd = zr[..., o1:o2].reshape(B, T, 2, RW_W_LORA)
    ad = zr[..., o2:o3].reshape(B, T, 2, RW_A_LORA)
    gd = zr[..., o3:]
    w_pre = w0_l + jnp.einsum('btel,elc->btec', jnp.tanh(wd), w2_l)
    decay = jnp.exp(-jnp.exp(-jax.nn.softplus(-w_pre) - 0.5))
    a = jax.nn.sigmoid(a0_l + jnp.einsum('btel,elc->btec', ad, a2_l))
    g = jnp.einsum('btl,lc->btc', jax.nn.sigmoid(gd), g2_l)
    rh = lambda t_: t_.reshape(B, T, RW_HEADS, RW_HEAD)
    kk = rh(k * kk_l)
    kk = kk / jnp.maximum(jnp.sqrt(jnp.sum(jnp.square(kk), -1, keepdims=True)), 1e-12)
    kd = k[:, :, None, :] * (1.0 + (a - 1.0) * ka_l)
    r_h, v_h = rh(r), rh(v)
    k_f, k_b = rh(kd[:, :, 0]), rh(kd[:, :, 1])
    a_f, a_b = rh(a[:, :, 0]), rh(a[:, :, 1])
    y_f, rs_f = _rwkv7_scan(r_h, rh(decay[:, :, 0]), k_f, v_h, -kk, kk * a_f, rs_f)
    y_b, rs_b = _rwkv7_scan(_rev(r_h), _rev(rh(decay[:, :, 1])), _rev(k_b), _rev(v_h),
                            _rev(-kk), _rev(kk * a_b), rs_b)
    yr = y_f + _rev(y_b)
    m = jnp.mean(yr, -1, keepdims=True)
    var = jnp.mean(jnp.square(yr - m), -1, keepdims=True)
    yr = (yr - m) * lax.rsqrt(var + RW_GN_EPS) * lnw_l.reshape(RW_HEADS, RW_HEAD) + lnb_l.reshape(RW_HEADS, RW_HEAD)
    bonus = (jnp.sum(r_h * k_f * rk_l, -1, keepdims=True) + jnp.sum(r_h * k_b * rk_l, -1, keepdims=True)) * v_h
    y_rw = (yr + bonus).reshape(B, T, W) * g

    y = jnp.concatenate([y_hg, y_rw], axis=-1).astype(u.dtype)
    return y, (hs_f, hs_b, rs_f, rs_b)


def _swiglu(u, wg, wu, wd):
    h = jax.nn.silu(jnp.einsum('btd,df->btf', u, wg)) * jnp.einsum('btd,df->btf', u, wu)
    return jnp.einsum('btf,fd->btd', h, wd)


def setup_inputs(seed: int = 0) -> dict:
    key = jax.random.key(seed)
    ks = jax.random.split(key, 32)
    f32 = jnp.float32
    nrm = lambda k, s, sc: jax.random.normal(k, s, f32) * sc
    beta = (8.0 * DEPTH) ** -0.25
    L = DEPTH
    return {
        'x': nrm(ks[0], (BATCH, SEQ, D_MODEL), 1.0),
        'c': nrm(ks[1], (BATCH, D_MODEL), 1.0),
        'ctx': nrm(ks[2], (BATCH, CTX_LEN, D_MODEL), 1.0),
        'c_ctx': nrm(ks[3], (D_MODEL,), 1.0),
        'w_ada': nrm(ks[4], (L, D_MODEL, 6 * D_MODEL), D_MODEL ** -0.5),
        'b_ada': nrm(ks[5], (L, 6 * D_MODEL), 0.02),
        'w_in': nrm(ks[6], (L, D_MODEL, IN_COLS), D_MODEL ** -0.5),
        'hgrn_lb_logits': nrm(ks[7], (L + 1, 2, HG_WIDTH), 0.5),
        'hgrn_norm_w': 1.0 + nrm(ks[8], (L, HG_DV), 0.02),
        'rwkv_mu': jax.random.uniform(ks[9], (L, RW_COLS), f32, 0.0, 1.0),
        'rwkv_w0': jax.random.uniform(ks[10], (L, 2, RW_WIDTH), f32, -4.0, 0.0),
        'rwkv_w2': nrm(ks[11], (L, 2, RW_W_LORA, RW_WIDTH), 0.1),
        'rwkv_a0': nrm(ks[12], (L, 2, RW_WIDTH), 0.3),
        'rwkv_a2': nrm(ks[13], (L, 2, RW_A_LORA, RW_WIDTH), 0.3 * RW_A_LORA ** -0.5),
        'rwkv_g2': nrm(ks[14], (L, RW_G_LORA, RW_WIDTH), RW_G_LORA ** -0.5),
        'rwkv_k_k': 0.85 + nrm(ks[15], (L, RW_WIDTH), 0.05),
        'rwkv_k_a': 1.0 + nrm(ks[16], (L, RW_WIDTH), 0.05),
        'rwkv_r_k': nrm(ks[17], (L, RW_HEADS, RW_HEAD), 0.1),
        'rwkv_lnx_w': 1.0 + nrm(ks[18], (L, RW_WIDTH), 0.02),
        'rwkv_lnx_b': nrm(ks[19], (L, RW_WIDTH), 0.02),
        'w_out': nrm(ks[20], (L, MIX_WIDTH, D_MODEL), beta * MIX_WIDTH ** -0.5),
        'ln1_g': 1.0 + nrm(ks[21], (L, D_MODEL), 0.02),
        'ln1_b': nrm(ks[22], (L, D_MODEL), 0.02),
        'w_ffn_gate': nrm(ks[23], (L, D_MODEL, D_FF), D_MODEL ** -0.5),
        'w_ffn_up': nrm(ks[24], (L, D_MODEL, D_FF), D_MODEL ** -0.5),
        'w_ffn_down': nrm(ks[25], (L, D_FF, D_MODEL), beta * D_FF ** -0.5),
        'ln2_g': 1.0 + nrm(ks[26], (L, D_MODEL), 0.02),
        'ln2_b': nrm(ks[27], (L, D_MODEL), 0.02),
    }


def reference(x, c, ctx, c_ctx, w_ada, b_ada, w_in, hgrn_lb_logits, hgrn_norm_w, rwkv_mu, rwkv_w0,
              rwkv_w2, rwkv_a0, rwkv_a2, rwkv_g2, rwkv_k_k, rwkv_k_a, rwkv_r_k, rwkv_lnx_w, rwkv_lnx_b,
              w_out, ln1_g, ln1_b, w_ffn_gate, w_ffn_up, w_ffn_down, ln2_g, ln2_b):
    B = x.shape[0]
    rows = x.shape[1] // GRID_W
    latent_shift = functools.partial(_qshift_grid, rows=rows)
    alpha = (2.0 * DEPTH) ** 0.25
    lb_all = jnp.cumsum(jax.nn.softmax(hgrn_lb_logits.astype(jnp.float32), axis=0), axis=0)
    zero_states = (jnp.zeros((B, HG_HEADS, HG_DK, HG_DV), jnp.float32),
                   jnp.zeros((B, HG_HEADS, HG_DK, HG_DV), jnp.float32),
                   jnp.zeros((B, RW_HEADS, RW_HEAD, RW_HEAD), jnp.float32),
                   jnp.zeros((B, RW_HEADS, RW_HEAD, RW_HEAD), jnp.float32))
    for l in range(DEPTH):
        mod = jnp.einsum('bd,de->be', jax.nn.silu(c), w_ada[l]) + b_ada[l]
        sh1, sc1, g1, sh2, sc2, g2 = [m[:, None, :] for m in jnp.split(mod, 6, axis=-1)]
        mod_c = jnp.einsum('d,de->e', jax.nn.silu(c_ctx), w_ada[l]) + b_ada[l]
        ch1, cs1, cg1, ch2, cs2, cg2 = jnp.split(mod_c, 6)
        mix_w = (w_in[l], rwkv_mu[l], lb_all[l], hgrn_norm_w[l], rwkv_w0[l], rwkv_w2[l], rwkv_a0[l],
                 rwkv_a2[l], rwkv_g2[l], rwkv_k_k[l], rwkv_k_a[l], rwkv_r_k[l], rwkv_lnx_w[l], rwkv_lnx_b[l])
        y_ctx, ctx_states = _token_mix(_modulate(ctx, ch1, cs1), _shift_seq, *mix_w, zero_states)
        y, _ = _token_mix(_modulate(x, sh1, sc1), latent_shift, *mix_w, ctx_states)
        x = _post_ln(alpha * x + g1 * jnp.einsum('btm,md->btd', y, w_out[l]), ln1_g[l], ln1_b[l])
        ffn = _swiglu(_modulate(x, sh2, sc2), w_ffn_gate[l], w_ffn_up[l], w_ffn_down[l])
        x = _post_ln(alpha * x + g2 * ffn, ln2_g[l], ln2_b[l])
        if l < DEPTH - 1:
            ctx = _post_ln(alpha * ctx + cg1 * jnp.einsum('btm,md->btd', y_ctx, w_out[l]), ln1_g[l], ln1_b[l])
            ffn_c = _swiglu(_modulate(ctx, ch2, cs2), w_ffn_gate[l], w_ffn_up[l], w_ffn_down[l])
            ctx = _post_ln(alpha * ctx + cg2 * ffn_c, ln2_g[l], ln2_b[l])
    return x
```

```python
from contextlib import ExitStack
import numpy as np
import concourse.bass as bass
import concourse.mybir as mybir
from concourse.bass_utils import run_bass_kernel_spmd

F32 = mybir.dt.float32
BF16 = mybir.dt.bfloat16
ALU = mybir.AluOpType
AF = mybir.ActivationFunctionType

D = 1024
CTX = 256
NCT = 34
ZC = NCT * 128
IN_COLS = 4320
DFF = 2816
NFT = DFF // 128
LWS = 0.6065306597126334
ALPHA = 2.0 ** 0.25

C_ID, C_M32F, C_M32B, C_MSF, C_MIF, C_MSB, C_MIB, C_BLK64, C_ONES, C_RST32, C_RST64, C_ROWM = range(12)
NCONST = 12
PT_L0 = 0
PT_L1 = 8
PT_NW = 16
PT_MU = 17
PT_ML = 31
PT_W0 = 115
PT_A0 = 123
PT_KK = 131
PT_KA = 135
PT_RK = 139
PT_LNW = 143
PT_LNB = 147
PT_BADA = 151
NPT = 199


class Buf:
    __slots__ = ("ap", "w", "r", "name")

    def __init__(self, ap, name=""):
        self.ap = ap
        self.w = {}
        self.r = {}
        self.name = name

    def __getitem__(self, k):
        return self.ap[k]


class KB:
    NR = 8

    def __init__(self, nc):
        self.nc = nc
        self.E = {"pe": nc.tensor, "dve": nc.vector, "act": nc.scalar, "pool": nc.gpsimd, "sp": nc.sync}
        self.sems = []
        self.semval = []
        self.esem = {e: self._newsem("c_" + e) for e in self.E}
        self.dsem = {"sp": [self._newsem(f"d_sp{i}") for i in range(self.NR)]}
        self.didx = {"sp": 0}
        self.seen = {e: {} for e in self.E}
        self.nbuf = 0
        self.nwait = 0
        self.ninst = 0
        self.stack = None
        self._st = None

    def _newsem(self, name):
        self.sems.append(self.nc.alloc_semaphore(name))
        self.semval.append(0)
        return len(self.sems) - 1

    def sb(self, shape, dtype=F32, name=None, perm=False):
        self.nbuf += 1
        name = f"{name or 't'}_{self.nbuf}"
        if perm or self.stack is None:
            h = self.nc.alloc_sbuf_tensor(name, list(shape), dtype)
        else:
            h = self.stack.enter_context(self.nc.sbuf_tensor(name, list(shape), dtype))
        return Buf(h.ap(), name)

    def ps(self, shape, dtype=F32, name=None):
        self.nbuf += 1
        return Buf(self.nc.alloc_psum_tensor(f"{name or 'p'}_{self.nbuf}", list(shape), dtype).ap(), name)

    def dram(self, name, shape, dtype=F32, kind="Internal"):
        return Buf(self.nc.dram_tensor(name, list(shape), dtype, kind=kind).ap(), name)

    def _need(self, reads, writes):
        need = {}
        for b in reads:
            for k, v in b.w.items():
                if need.get(k, 0) < v:
                    need[k] = v
        for b in writes:
            for k, v in b.w.items():
                if need.get(k, 0) < v:
                    need[k] = v
            for k, v in b.r.items():
                if need.get(k, 0) < v:
                    need[k] = v
        return need

    def _wait(self, e, need):
        own = self.esem[e]
        seen = self.seen[e]
        eng = self.E[e]
        for k, v in need.items():
            if k == own and e == "pe":
                continue
            if seen.get(k, 0) < v:
                eng.wait_ge(self.sems[k], v)
                seen[k] = v
                self.nwait += 1

    def _commit(self, k, v, reads, writes):
        for b in writes:
            b.w = {k: v}
            b.r = {}
        for b in reads:
            if b.r.get(k, 0) < v:
                b.r[k] = v

    def op(self, e, fn, reads=(), writes=()):
        self._yield()
        self._wait(e, self._need(reads, writes))
        ins = fn()
        k = self.esem[e]
        self.semval[k] += 1
        ins.then_inc(self.sems[k], 1)
        self._commit(k, self.semval[k], reads, writes)
        self.ninst += 1
        return ins

    def dma(self, out, in_, reads=(), writes=(), q="sp"):
        self._yield()
        i = self.didx[q]
        self.didx[q] += 1
        k = self.dsem[q][i % self.NR]
        need = self._need(reads, writes)
        if self.semval[k] > 0 and need.get(k, 0) < self.semval[k]:
            need[k] = self.semval[k]
        self._wait(q, need)
        self.semval[k] += 16
        self.E[q].dma_start(out=out, in_=in_).then_inc(self.sems[k], 16)
        self._commit(k, self.semval[k], reads, writes)
        self.ninst += 1


    def run_streams(self, fns, quota=4):
        import threading
        n = len(fns)
        if n == 1:
            fns[0]()
            return
        st = {"turn": 0, "alive": [True] * n, "err": [], "cnt": 0}
        cv = threading.Condition()
        self._st, self._cv, self._quota = st, cv, quota
        self._tls = threading.local()

        def nxt_(i):
            for d in range(1, n + 1):
                k = (i + d) % n
                if st["alive"][k]:
                    return k
            return -1

        def runner(i):
            self._tls.sid = i
            self._tls.atomic = 0
            with cv:
                while st["turn"] != i:
                    cv.wait()
            try:
                fns[i]()
            except BaseException as e:
                st["err"].append(e)
            finally:
                with cv:
                    st["alive"][i] = False
                    st["turn"] = nxt_(i)
                    st["cnt"] = 0
                    cv.notify_all()
        self._nxt = nxt_
        ths = [threading.Thread(target=runner, args=(i,)) for i in range(n)]
        for t in ths:
            t.start()
        for t in ths:
            t.join()
        self._st = None
        if st["err"]:
            raise st["err"][0]

    def _yield(self):
        st = getattr(self, "_st", None)
        if st is None:
            return
        tls = self._tls
        if getattr(tls, "sid", None) is None or tls.atomic:
            return
        st["cnt"] += 1
        if st["cnt"] < self._quota:
            return
        i = tls.sid
        cv = self._cv
        with cv:
            k = self._nxt(i)
            st["cnt"] = 0
            if k == i or k < 0:
                return
            st["turn"] = k
            cv.notify_all()
            while st["turn"] != i:
                cv.wait()

    def atomic(self):
        kb = self

        class _A:
            def __enter__(self_):
                if getattr(kb, "_st", None) is not None and getattr(kb._tls, "sid", None) is not None:
                    kb._tls.atomic += 1

            def __exit__(self_, *a):
                if getattr(kb, "_st", None) is not None and getattr(kb._tls, "sid", None) is not None:
                    kb._tls.atomic -= 1
        return _A()

    def barrier(self):
        need = {k: v for k, v in enumerate(self.semval) if v > 0}
        for e in self.E:
            self._wait(e, need)

    def mm(self, out, lhsT, rhs, reads, writes, start=True, stop=True):
        nc = self.nc
        return self.op("pe", lambda: nc.tensor.matmul(out, lhsT, rhs, start=start, stop=stop), reads, writes)

    def tr(self, out, in_, ident, reads, writes):
        nc = self.nc
        return self.op("pe", lambda: nc.tensor.transpose(out, in_, ident), reads, writes)

    def act(self, out, in_, func, reads, writes, bias=None, scale=None):
        nc = self.nc
        kw = {}
        if bias is not None:
            kw["bias"] = bias
        if scale is not None:
            kw["scale"] = scale
        return self.op("act", lambda: nc.scalar.activation(out, in_, func, **kw), reads, writes)

    def tt(self, e, out, in0, in1, op, reads, writes):
        eng = self.E[e]
        return self.op(e, lambda: eng.tensor_tensor(out, in0, in1, op), reads, writes)

    def ts(self, e, out, in0, s1, s2, op0, op1, reads, writes):
        eng = self.E[e]
        if s2 is None:
            return self.op(e, lambda: eng.tensor_scalar(out, in0, s1, None, op0), reads, writes)
        return self.op(e, lambda: eng.tensor_scalar(out, in0, s1, s2, op0, op1), reads, writes)

    def stt(self, out, in0, scalar, in1, op0, op1, reads, writes):
        nc = self.nc
        return self.op("dve", lambda: nc.vector.scalar_tensor_tensor(out, in0, scalar, in1, op0, op1), reads, writes)

    def cp(self, e, out, in_, reads, writes):
        if e == "act":
            nc = self.nc
            return self.op("act", lambda: nc.scalar.copy(out, in_), reads, writes)
        eng = self.E[e]
        return self.op(e, lambda: eng.tensor_copy(out, in_), reads, writes)

    def memset(self, e, ap, val, writes):
        eng = self.E[e]
        return self.op(e, lambda: eng.memset(ap, val), [], writes)


def _bc(ap, shape):
    return ap.to_broadcast(list(shape))


def build(T, debug=False, upto=9):
    NTX = T // 128
    NE = CTX + T
    nc = bass.Bass("TRN2", target_bir_lowering=False)
    K = KB(nc)
    x_d = K.dram("x", [T, D], kind="ExternalInput")
    ctx_d = K.dram("ctx", [CTX, D], kind="ExternalInput")
    cv_d = K.dram("cv", [128, 16], kind="ExternalInput")
    wada_d = K.dram("w_ada", [D, 6 * D], kind="ExternalInput")
    brow_d = K.dram("b_ada_row", [1, 6 * D], kind="ExternalInput")
    win_d = K.dram("w_in", [D, IN_COLS], kind="ExternalInput")
    ptab_d = K.dram("ptab", [128, NPT], kind="ExternalInput")
    const_d = K.dram("consts", [128, NCONST, 128], kind="ExternalInput")
    wl_d = K.dram("wl4", [128, 4, 512], kind="ExternalInput")
    g2_d = K.dram("g2", [96, 512], kind="ExternalInput")
    wout_d = K.dram("w_out", [D, D], kind="ExternalInput")
    lnrow_d = K.dram("lnrows", [4, D], kind="ExternalInput")
    wg_d = K.dram("w_gate", [D, DFF], kind="ExternalInput")
    wu_d = K.dram("w_up", [D, DFF], kind="ExternalInput")
    wd_d = K.dram("w_down", [DFF, D], kind="ExternalInput")
    out_d = K.dram("out", [T, D], kind="ExternalOutput")
    zT_d = K.dram("zT", [ZC, NE])
    of_d = K.dram("ofwd", [NTX, 128, 8, 128])
    x1_d = K.dram("x1s", [T, D])
    ob_d = K.dram("obwd", [NTX, 128, 8, 128])
    ex_d = K.dram("extra", [NTX, 128, 8, 128])
    dbg = {}
    if debug:
        dbg["zT"] = K.dram("dbg_zT", [ZC, NE], kind="ExternalOutput")
        dbg["yT"] = K.dram("dbg_yT", [NTX, 128, 8, 128], kind="ExternalOutput")
        dbg["x1"] = K.dram("dbg_x1", [T, D], kind="ExternalOutput")

    cst = K.sb([128, NCONST, 128], F32, "cst", perm=True)
    cstb = K.sb([128, NCONST, 128], BF16, "cstb", perm=True)
    ptab = K.sb([128, NPT], F32, "ptab", perm=True)
    modT = K.sb([128, 48, 2], F32, "modT", perm=True)
    opsc = K.sb([128, 3, 8], F32, "opsc", perm=True)
    epsc = K.sb([128, 4], F32, "epsc", perm=True)
    gb = K.sb([128, 2, D], F32, "gb", perm=True)
    drv = K.sb([128, 128], F32, "drv", perm=True)
    DV_LB, DV_OML, DV_NOML = 0, 8, 16
    DV_C0 = 24
    DV_CS = 38
    DV_OMKA = 122
    PS = [K.ps([128, 512], F32, f"bank{i}") for i in range(8)]

    def psb(i):
        return PS[i].ap.bitcast(BF16)

    ident = cst[:, C_ID, :]
    identb = cstb[:, C_ID, :]

    K.dma(cst[:], const_d[:, :, :], [const_d], [cst])
    K.dma(ptab[:], ptab_d[:, :], [ptab_d], [ptab])
    K.cp("dve", cstb[:], cst[:], [cst], [cstb])
    K.memset("pool", epsc[:, 0:1], 1e-6, [epsc])
    K.memset("pool", epsc[:, 1:2], 1e-5, [epsc])
    K.memset("pool", epsc[:, 2:3], 64e-5, [epsc])
    K.memset("pool", epsc[:, 3:4], 1e-24, [epsc])
    K.tt("dve", drv[:, 0:8], ptab[:, PT_L0:PT_L0 + 8], ptab[:, PT_L1:PT_L1 + 8], ALU.subtract, [ptab], [drv])
    K.act(drv[:, DV_LB:DV_LB + 8], drv[:, 0:8], AF.Sigmoid, [drv], [drv])
    K.ts("dve", drv[:, DV_OML:DV_OML + 8], drv[:, DV_LB:DV_LB + 8], -1.0, 1.0, ALU.mult, ALU.add, [drv], [drv])
    K.ts("dve", drv[:, DV_NOML:DV_NOML + 8], drv[:, DV_OML:DV_OML + 8], -1.0, None, ALU.mult, None, [drv], [drv])
    K.ts("dve", drv[:, DV_C0:DV_C0 + 14], ptab[:, PT_MU:PT_MU + 14], -1.0, 1.0, ALU.mult, ALU.add, [ptab], [drv])
    for i in range(6):
        K.tt("dve", drv[:, DV_CS + 14 * i:DV_CS + 14 * (i + 1)], ptab[:, PT_MU:PT_MU + 14],
             ptab[:, PT_ML + 14 * i:PT_ML + 14 * (i + 1)], ALU.mult, [ptab], [drv])
    K.ts("dve", drv[:, DV_OMKA:DV_OMKA + 4], ptab[:, PT_KA:PT_KA + 4], -1.0, 1.0, ALU.mult, ALU.add, [ptab], [drv])

    if upto == 0.1:
        K.barrier()
        return nc, K
    K.stack = ExitStack()
    winb = K.sb([128, 8, ZC], BF16, "winb")
    cv = K.sb([128, 16], F32, "cv")
    cvs = K.sb([128, 16], F32, "cvs")
    K.dma(cv[:], cv_d[:, :], [cv_d], [cv])
    outer0 = K.stack
    K.stack = ExitStack()
    brow = K.sb([1, 4, 512], F32, "brow")
    for i_, eg_ in enumerate((4, 5, 10, 11)):
        K.dma(brow[0:1, i_, :], brow_d[0:1, eg_ * 512:(eg_ + 1) * 512], [brow_d], [brow])
    K.act(cvs[:], cv[:], AF.Sigmoid, [cv], [cvs])
    K.tt("dve", cvs[:], cvs[:], cv[:], ALU.mult, [cvs, cv], [cvs])
    wa = [K.sb([128, 8, 512], F32, f"wa{i}") for i in range(2)]
    wada_v = wada_d.ap.rearrange("(k p) e -> p k e", p=128)
    grow = K.sb([1, 512], F32, "grow")
    for eg in range(12):
        w = wa[eg % 2]
        K.dma(w[:], wada_v[:, :, eg * 512:(eg + 1) * 512], [wada_d], [w])
        bank = PS[eg % 2]
        for j in range(4):
            for dk in range(8):
                K.mm(bank[:, 2 * j:2 * j + 2], w[:, dk, j * 128:(j + 1) * 128], cvs[:, 2 * dk:2 * dk + 2], [w, cvs], [bank],
                     start=(dk == 0), stop=(dk == 7))
        for j in range(4):
            et = eg * 4 + j
            K.ts("dve", modT[:, et, :], bank[:, 2 * j:2 * j + 2], ptab[:, PT_BADA + et:PT_BADA + et + 1], None, ALU.add, None,
                 [bank, ptab], [modT])
        if eg in (4, 5, 10, 11):
            gi = 0 if eg < 6 else 1
            half = eg % 2 if eg < 6 else (eg - 10)
            rb_ = PS[2]
            for dk in range(8):
                K.mm(rb_[0:1, 0:512], cvs[:, 2 * dk:2 * dk + 1], w[:, dk, :], [w, cvs], [rb_], start=(dk == 0), stop=(dk == 7))
            K.tt("dve", grow[:], rb_[0:1, 0:512], brow[0:1, (4, 5, 10, 11).index(eg), :], ALU.add, [rb_, brow], [grow])
            bb = PS[3]
            K.mm(bb[:, 0:512], cst[0:1, C_ONES, :], grow[:], [cst, grow], [bb])
            K.cp("act", gb[:, gi, half * 512:(half + 1) * 512], bb[:, 0:512], [bb], [gb])
    K.ts("dve", opsc[:, 0, :], modT[:, 8:16, 0], 1.0, None, ALU.add, None, [modT], [opsc])
    K.ts("dve", opsc[:, 1, :], modT[:, 8:16, 1], 1.0, None, ALU.add, None, [modT], [opsc])
    K.ts("dve", opsc[:, 2, :], modT[:, 32:40, 0], 1.0, None, ALU.add, None, [modT], [opsc])
    K.barrier()
    if upto == 0.2:
        return nc, K
    K.stack.close()
    K.stack = ExitStack()
    wst = [K.sb([128, IN_COLS], F32, f"wst{i}") for i in range(2)]
    win_v = win_d.ap.rearrange("(k p) c -> p k c", p=128)
    K.memset("pool", winb[:, :, IN_COLS:ZC], 0.0, [winb])
    for dk in range(8):
        s = wst[dk % 2]
        K.dma(s[:], win_v[:, dk, :], [win_d], [s])
        K.cp("act" if dk % 2 else "dve", winb[:, dk, 0:IN_COLS], s[:], [s], [winb])

    K.barrier()
    if upto == 0.3:
        return nc, K
    K.stack.close()
    K.stack = outer0
    xts = [K.sb([128, D], F32, f"xt{i}") for i in range(2)]
    xnb = [K.sb([128, D], BF16, f"xnb{i}") for i in range(2)]
    stt_ = [K.sb([128, 16], F32, f"st{i}") for i in range(2)]

    def ln_stats(src_ap, src_bufs, st, eps_col):
        K.op("dve", lambda: nc.vector.bn_stats(st[:, 0:6], src_ap[:, 0:512]), src_bufs, [st])
        K.op("dve", lambda: nc.vector.bn_stats(st[:, 6:12], src_ap[:, 512:1024]), src_bufs, [st])
        K.op("dve", lambda: nc.vector.bn_aggr(st[:, 12:14], st[:, 0:12]), [st], [st])
        K.act(st[:, 14:15], st[:, 13:14], AF.Ln, [st, epsc], [st], bias=epsc[:, eps_col:eps_col + 1])
        K.act(st[:, 15:16], st[:, 14:15], AF.Exp, [st], [st], scale=-0.5)

    def modulate_T(src, i, uT, col0, sc_ap, sh_ap, tbank):
        st = stt_[i % 2]
        xb = xnb[i % 2]
        ln_stats(src.ap, [src], st, 0)
        K.ts("dve", xb[:], src[:], st[:, 12:13], st[:, 15:16], ALU.subtract, ALU.mult, [src, st], [xb])
        tb = psb(tbank)
        for dk in range(8):
            K.tr(tb[:, dk * 128:(dk + 1) * 128], xb[:, dk * 128:(dk + 1) * 128], identb, [xb, cstb], [PS[tbank]])
        for dk in range(8):
            K.act(uT[:, dk, col0:col0 + 128], tb[:, dk * 128:(dk + 1) * 128], AF.Identity, [PS[tbank], opsc, modT], [uT],
                  bias=sh_ap(dk), scale=sc_ap(dk))

    uTs = [K.sb([128, 8, 512], BF16, f"uT{i}") for i in range(2)]
    zsb = [K.sb([128, 512], F32, f"zsb{i}") for i in range(4)]
    groups = [("ctx", 0, 2)] + [("x", g * 4, min(4, NTX - g * 4)) for g in range((NTX + 3) // 4)]
    ti = 0
    zi = 0
    for gi_, (kind, t0, nt) in enumerate(groups):
        uT = uTs[gi_ % 2]
        src_d = ctx_d if kind == "ctx" else x_d
        mj = 1 if kind == "ctx" else 0
        for i in range(nt):
            xt = xts[ti % 2]
            K.dma(xt[:], src_d[(t0 + i) * 128:(t0 + i + 1) * 128, :], [src_d], [xt])
            modulate_T(xt, ti, uT, i * 128, lambda dk: opsc[:, mj, dk:dk + 1], lambda dk: modT[:, dk, mj:mj + 1], 7)
            ti += 1
        ncol = nt * 128
        ecol0 = (0 if kind == "ctx" else CTX) + t0 * 128
        import os
        ZD = int(os.environ.get("ZDBG", "0"))
        if ZD == 1:
            continue
        for ct in range(NCT):
            bank = PS[ct % 4]
            for dk in range(8):
                K.mm(bank[:, 0:ncol], winb[:, dk, ct * 128:(ct + 1) * 128], uT[:, dk, 0:ncol], [winb, uT], [bank],
                     start=(dk == 0), stop=(dk == 7))
            z = zsb[zi % 4]
            zi += 1
            K.cp("act" if ct % 2 else "dve", z[:, 0:ncol], bank[:, 0:ncol], [bank], [z])
            if ZD == 2:
                continue
            if ZD != 4:
                K.dma(zT_d[ct * 128:(ct + 1) * 128, ecol0:ecol0 + ncol], z[:, 0:ncol], [z], [zT_d])
            if debug and ZD != 3:
                K.dma(dbg["zT"][ct * 128:(ct + 1) * 128, ecol0:ecol0 + ncol], z[:, 0:ncol], [z], [dbg["zT"]])
    K.barrier()
    K.stack.close()

    def run_pass(dirn):
        final = dirn == 1
        K.stack = ExitStack()
        wlb = K.sb([128, 4, 512], BF16, "wlb")
        g2b = K.sb([96, 512], BF16, "g2b")
        outer = K.stack
        K.stack = ExitStack()
        tmpw = K.sb([128, 4, 512], F32, "tmpw")
        K.dma(tmpw[:], wl_d[:, :, :], [wl_d], [tmpw])
        K.cp("dve", wlb[:], tmpw[:], [tmpw], [wlb])
        tmpg = K.sb([96, 512], F32, "tmpg")
        K.dma(tmpg[:], g2_d[:, :], [g2_d], [tmpg])
        K.cp("dve", g2b[:], tmpg[:], [tmpg], [g2b])
        K.barrier()
        K.stack.close()
        K.stack = outer
        S = K.sb([128, 4, 128], F32, "S")
        Sbf = [K.sb([128, 4, 128], BF16, f"Sbf{i}") for i in range(4)]
        H = [K.sb([128, 64], F32, f"H{j}") for j in range(4)]
        Hbd = [[K.sb([128, 128], BF16, f"Hbd{j}_{c}") for c in range(2)] for j in range(4)]
        MTbd = [[K.sb([128, 128], F32, f"MT{j}_{c}") for c in range(2)] for j in range(4)]
        K.memset("pool", S[:], 0.0, [S])
        for j in range(4):
            K.memset("pool", H[j][:], 0.0, [H[j]])
            for c in range(2):
                K.memset("pool", Hbd[j][c][:], 0.0, [Hbd[j][c]])
                K.memset("pool", MTbd[j][c][:], 0.0, [MTbd[j][c]])
        for i in range(4):
            K.memset("pool", Sbf[i][:], 0.0, [Sbf[i]])
        ld_q = K.sb([128, 4, 128], F32, "ldq")
        ld_f = K.sb([128, 4, 128], F32, "ldf")
        ld_i = K.sb([128, 4, 128], F32, "ldi")
        ld_rw = [K.sb([128, 4, 256], F32, f"ldrw{g}") for g in range(4)]
        zp = K.sb([128, 14, 4, 66], F32, "zp")
        zc = K.sb([128, 14, 130], F32, "zc")
        K.memset("pool", zp[:], 0.0, [zp])
        K.memset("pool", zc[:], 0.0, [zc])
        zrl = K.sb([128, 14, 128], F32, "zrl")
        mshg = cstb[:, C_M32B if dirn else C_M32F, :]
        msi = cstb[:, C_MSB:C_MSB + 2, :] if dirn else cstb[:, C_MSF:C_MSF + 2, :]
        ms32 = cst[:, C_MSB, :] if dirn else cst[:, C_MSF, :]
        mnt32 = cst[:, C_MSF, :] if dirn else cst[:, C_MSB, :]
        id32r = K.sb([128, 2, 128], F32, "id32r")
        msir = K.sb([128, 2, 2, 128], BF16, "msir")
        mshgr = K.sb([128, 4, 128], BF16, "mshgr")
        for e_ in range(2):
            K.cp("pool", id32r[:, e_, :], ident, [cst], [id32r])
            K.cp("pool", msir[:, e_, :, :], msi, [cstb], [msir])
        for h_ in range(4):
            K.cp("pool", mshgr[:, h_, :], mshg, [cstb], [mshgr])
        rst32 = cst[:, C_RST32, :]
        rst64 = cst[:, C_RST64, :]
        blk64 = cst[:, C_BLK64, :]

        if dirn == 0:
            order = [("ctx", 0), ("ctx", 1)] + [("x", j) for j in range(NTX)]
        else:
            order = [("ctx", 1), ("ctx", 0)] + [("x", j) for j in range(NTX - 1, -1, -1)]

        def issue_loads(n):
            kind, j = order[n]
            ec0 = (0 if kind == "ctx" else CTX) + j * 128

            def hv(ct0):
                return zT_d.ap[ct0 * 128:(ct0 + 4) * 128, ec0:ec0 + 128].rearrange("(h p) t -> p h t", p=128)
            K.dma(ld_q[:], hv(0), [zT_d], [ld_q])
            K.dma(ld_f[:], hv(4 + 4 * dirn), [zT_d], [ld_f])
            K.dma(ld_i[:], hv(12), [zT_d], [ld_i])
            if kind == "x":
                lo = 64 if j > 0 else 0
                hi = 64 if j < NTX - 1 else 0
            else:
                lo = 1 if j > 0 else 0
                hi = 1 if j < 1 else 0
            for g in range(4):
                nt_ = 4 if g < 3 else 2
                src = zT_d.ap[(20 + 4 * g) * 128:(20 + 4 * g + nt_) * 128, ec0 - lo:ec0 + 128 + hi].rearrange("(h p) t -> p h t", p=128)
                K.dma(ld_rw[g][:, 0:nt_, 64 - lo:192 + hi], src, [zT_d], [ld_rw[g]])

        def T32(name, shape=(128, 4, 128)):
            return K.sb(list(shape), F32, name)

        def T16(name, shape=(128, 4, 128)):
            return K.sb(list(shape), BF16, name)
        P32 = [T32(f"w32_{i}") for i in range(11)]
        sgq = qh = P32[0]
        sgf = P32[1]
        ff = lg = P32[2]
        kdh = P32[3]
        bcum = P32[4]
        tmp1 = P32[5]
        e1 = P32[6]
        e2 = P32[7]
        sw = sq = P32[0]
        asg = P32[1]
        cs = kdr = P32[2]
        csm = rn = P32[3]
        E1 = P32[4]
        E2 = P32[5]
        tmpa = P32[6]
        bvec = P32[7]
        E3 = P32[8]
        kk = P32[9]
        kkn = P32[10]
        qb, kb, ib16, ATm = [T16(n) for n in ("qb", "kb", "ib16", "ATm")]
        kbTm = [T16(f"kbTm{c}") for c in range(4)]
        iT = T16("iT")
        stmp = T32("stmp")
        Lb = K.sb([128, 128], BF16, "Lb")
        vb16 = T16("vb16")
        if final:
            sgd = K.sb([96, 128], BF16, "sgd")
            asg2, tmpa2, bonp = T32("asg2"), T32("tmpa2"), T32("bonp")
        osb = [K.sb([128, 8, 128], F32, f"osb{i}") for i in range(2)]
        exb = [K.sb([128, 8, 128], F32, f"exb{i}") for i in range(2)] if final else None
        AR = [K.sb([128, 4, 2, 128], BF16, f"AR{i}") for i in range(2)]
        bt = [T16(f"bt{i}") for i in range(2)]
        kt = [T16(f"kt{i}") for i in range(2)]
        aT = [T16(f"aT{i}") for i in range(2)]
        vT = [T16(f"vT{i}") for i in range(2)]
        btTm = [[T16(f"btTm{i}_{c}") for c in range(2)] for i in range(2)]
        ktTm = [[T16(f"ktTm{i}_{c}") for c in range(2)] for i in range(2)]
        gam = [K.sb([128, 4, 2], F32, f"gam{i}") for i in range(2)]
        Amat = [K.sb([128, 2, 2, 2, 128], BF16, f"Amat{i}") for i in range(2)]
        XP = [[K.sb([128, 2, 2, 128], F32, f"XP{i}_{p}") for p in range(2)] for i in range(2)]
        XT = [[K.sb([128, 2, 128], F32, f"XT{i}_{p}") for p in range(2)] for i in range(2)]
        Pbf = [K.sb([128, 2, 128], BF16, f"Pbf{i}") for i in range(2)]
        AW = [K.sb([128, 2, 128], BF16, f"AW{i}") for i in range(2)]
        AU = [K.sb([128, 2, 128], BF16, f"AU{i}") for i in range(2)]
        QT = [K.sb([128, 128], BF16, f"QT{i}") for i in range(2)]
        yb = PS[7]

        def front(n):
            kind, j = order[n]
            isx = kind == "x"
            fb = n % 2
            issue_loads(n)
            lrw = ld_rw
            if isx:
                if j == 0:
                    for g in range(4):
                        K.memset("pool", lrw[g][:, :, 0:64], 0.0, [lrw[g]])
                if j == NTX - 1:
                    for g in range(4):
                        K.memset("pool", lrw[g][:, :, 192:256], 0.0, [lrw[g]])
                for g in range(4):
                    nt_ = 4 if g < 3 else 2
                    for q_ in range(nt_):
                        K.cp("pool", zp[:, 4 * g + q_, :, 1:65], lrw[g][:, q_, :].rearrange("p (r c) -> p r c", c=64), [lrw[g]], [zp])
            else:
                if j == 0:
                    for g in range(4):
                        K.memset("pool", lrw[g][:, :, 63:64], 0.0, [lrw[g]])
                if j == 1:
                    for g in range(4):
                        K.memset("pool", lrw[g][:, :, 192:193], 0.0, [lrw[g]])
                for g in range(4):
                    nt_ = 4 if g < 3 else 2
                    K.cp("pool", zc[:, 4 * g:4 * g + nt_, :], lrw[g][:, 0:nt_, 63:193], [lrw[g]], [zc])
            zq, zf, zi_ = ld_q, ld_f, ld_i
            K.act(sgq[:], zq[:], AF.Sigmoid, [zq], [sgq])
            K.act(sgf[:], zf[:], AF.Sigmoid, [zf], [sgf])
            K.tt("pool", qh[:], zq[:], sgq[:], ALU.mult, [zq, sgq], [qh])
            for h in range(4):
                c_ = dirn * 4 + h
                K.ts("dve", ff[:, h, :], sgf[:, h, :], drv[:, DV_OML + c_:DV_OML + c_ + 1], drv[:, DV_LB + c_:DV_LB + c_ + 1],
                     ALU.mult, ALU.add, [sgf, drv], [ff])
                K.ts("pool", kdh[:, h, :], sgf[:, h, :], drv[:, DV_NOML + c_:DV_NOML + c_ + 1], drv[:, DV_OML + c_:DV_OML + c_ + 1],
                     ALU.mult, ALU.add, [sgf, drv], [kdh])
            K.act(lg[:], ff[:], AF.Ln, [ff], [lg])
            for h in range(4):
                K.op("dve", lambda h=h: nc.vector.tensor_tensor_scan(bcum[:, h, :], rst32, lg[:, h, :], 0.0, ALU.mult, ALU.add),
                     [lg, cst], [bcum])
            if dirn:
                bv4 = bcum[:].rearrange("p h (c t) -> p (h c) t", t=32)
                K.tt("pool", tmp1[:], lg[:], bcum[:], ALU.subtract, [lg, bcum], [tmp1])
                K.tt("dve", e2[:].rearrange("p h (c t) -> p (h c) t", t=32), tmp1[:].rearrange("p h (c t) -> p (h c) t", t=32),
                     _bc(bv4[:, :, 31:32], [128, 16, 32]), ALU.add, [tmp1, bcum], [e2])
                K.cp("pool", bcum[:], e2[:], [e2], [bcum])
            K.act(e1[:], bcum[:], AF.Exp, [bcum], [e1])
            K.act(e2[:], bcum[:], AF.Exp, [bcum], [e2], scale=-1.0)
            K.tt("dve", qb[:], qh[:], e1[:], ALU.mult, [qh, e1], [qb])
            K.tt("pool", kb[:], kdh[:], e2[:], ALU.mult, [kdh, e2], [kb])
            K.cp("pool", ib16[:], zi_[:], [zi_], [ib16])
            tb = psb(0)
            for h in range(4):
                K.tr(tb[:, h * 128:(h + 1) * 128], kb[:, h, :], identb, [kb, cstb], [PS[0]])
            for h in range(4):
                K.tr(tb[:, 512 + h * 128:512 + (h + 1) * 128], ib16[:, h, :], identb, [ib16, cstb], [PS[0]])
            for c in range(4):
                K.act(kbTm[c][:].rearrange("p h t -> p (h t)"), tb[:, 0:512], AF.Identity, [PS[0], cst], [kbTm[c]],
                      scale=cst[:, C_ROWM, c:c + 1])
            K.cp("act", iT[:].rearrange("p h t -> p (h t)"), tb[:, 512:1024], [PS[0]], [iT])
            if isx:
                for h in range(4):
                    K.mm(PS[1][:, h * 128:(h + 1) * 128], kb[:, h, :], qb[:, h, :], [kb, qb], [PS[1]])
                K.tt("dve", ATm[:], PS[1][:, :].rearrange("p (h t) -> p h t", h=4), mshgr[:], ALU.mult, [PS[1], mshgr], [ATm])
            corder = [0, 1, 2, 3] if dirn == 0 else [3, 2, 1, 0]
            for ci, c in enumerate(corder):
                K.cp("act", Sbf[c][:], S[:], [S], [Sbf[c]])
                kvb = PS[2]
                for h in range(4):
                    K.mm(kvb[:, h * 128:(h + 1) * 128], kbTm[c][:, h, :], iT[:, h, :], [kbTm[c], iT], [kvb])
                dcol = c * 32 + (0 if dirn else 31)
                K.tt("dve", stmp[:], kvb[:, :].rearrange("p (h v) -> p h v", h=4), S[:], ALU.add, [kvb, S], [stmp])
                K.tt("pool", S[:], stmp[:], _bc(e1[:, :, dcol:dcol + 1], [128, 4, 128]), ALU.mult, [stmp, e1], [S])
            if isx:
                ob = PS[1]
                for h in range(4):
                    with K.atomic():
                        K.mm(ob[:, h * 128:(h + 1) * 128], iT[:, h, :], ATm[:, h, :], [iT, ATm], [ob], start=True, stop=False)
                        for c in range(4):
                            K.mm(ob[:, h * 128 + c * 32:h * 128 + (c + 1) * 32], Sbf[c][:, h, :], qb[:, h, c * 32:(c + 1) * 32],
                                 [Sbf[c], qb], [ob], start=False, stop=(c == 3))
                K.cp("act", osb[fb][:, 0:4, :], ob[:, :].rearrange("p (h t) -> p h t", h=4), [ob], [osb[fb]])
            if isx:
                for ct in range(14):
                    views = {"L": zp[:, ct, 1:3, 0:64], "R": zp[:, ct, 1:3, 2:66], "U": zp[:, ct, 0:2, 1:65], "D": zp[:, ct, 2:4, 1:65]}
                    cen = zp[:, ct, 1:3, 1:65]
                    lo_, hi_ = ct * 128, ct * 128 + 128
                    kinds = []
                    if lo_ < 440: kinds.append(("L", 0))
                    if hi_ > 440 and lo_ < 880: kinds.append(("R", 1))
                    if hi_ > 880 and lo_ < 1320: kinds.append(("U", 2))
                    if hi_ > 1320: kinds.append(("D", 3))
                    o3 = zrl[:, ct, :].rearrange("p (r c) -> p r c", c=64)
                    K.act(o3, cen, AF.Identity, [zp, drv], [zrl], scale=drv[:, DV_C0 + ct:DV_C0 + ct + 1])
                    for (vn, ki) in kinds:
                        K.stt(o3, views[vn], drv[:, DV_CS + 14 * ki + ct:DV_CS + 14 * ki + ct + 1], o3, ALU.mult, ALU.add, [zp, drv, zrl], [zrl])
            else:
                for ct in range(14):
                    lo_, hi_ = ct * 128, ct * 128 + 128
                    kinds = []
                    if lo_ < 880: kinds.append((zc[:, ct, 0:128], 4))
                    if hi_ > 880: kinds.append((zc[:, ct, 2:130], 5))
                    K.act(zrl[:, ct, :], zc[:, ct, 1:129], AF.Identity, [zc, drv], [zrl], scale=drv[:, DV_C0 + ct:DV_C0 + ct + 1])
                    for (vw, ki) in kinds:
                        K.stt(zrl[:, ct, :], vw, drv[:, DV_CS + 14 * ki + ct:DV_CS + 14 * ki + ct + 1], zrl[:, ct, :], ALU.mult, ALU.add,
                              [zc, drv, zrl], [zrl])
            r_ = zrl[:, 0:4, :]
            k_ = zrl[:, 4:8, :]
            v_ = zrl[:, 8:12, :]
            K.act(Lb[0:64, :], zrl[0:64, 12, :], AF.Tanh, [zrl], [Lb])
            K.cp("pool", Lb[64:128, :], zrl[64:128, 12, :], [zrl], [Lb])
            pw, pa = PS[0], PS[1]
            for jj in range(4):
                K.mm(pw[:, jj * 128:(jj + 1) * 128], wlb[:, dirn, jj * 128:(jj + 1) * 128], Lb[:], [wlb, Lb], [pw])
            for jj in range(4):
                K.mm(pa[:, jj * 128:(jj + 1) * 128], wlb[:, 2 + dirn, jj * 128:(jj + 1) * 128], Lb[:], [wlb, Lb], [pa])
            for jj in range(4):
                K.act(sw[:, jj, :], pw[:, jj * 128:(jj + 1) * 128], AF.Sigmoid, [pw, ptab], [sw],
                      bias=ptab[:, PT_W0 + dirn * 4 + jj:PT_W0 + dirn * 4 + jj + 1])
                K.act(asg[:, jj, :], pa[:, jj * 128:(jj + 1) * 128], AF.Sigmoid, [pa, ptab], [asg],
                      bias=ptab[:, PT_A0 + dirn * 4 + jj:PT_A0 + dirn * 4 + jj + 1])
            if final and isx:
                for jj in range(4):
                    K.mm(pa[:, jj * 128:(jj + 1) * 128], wlb[:, 2, jj * 128:(jj + 1) * 128], Lb[:], [wlb, Lb], [pa])
                for jj in range(4):
                    K.act(asg2[:, jj, :], pa[:, jj * 128:(jj + 1) * 128], AF.Sigmoid, [pa, ptab], [asg2],
                          bias=ptab[:, PT_A0 + jj:PT_A0 + jj + 1])
            for jj in range(4):
                K.op("dve", lambda jj=jj: nc.vector.tensor_tensor_scan(cs[:, jj, :], rst64, sw[:, jj, :], 0.0, ALU.mult, ALU.add),
                     [sw, cst], [cs])
            if dirn == 0:
                K.tt("pool", csm[:], cs[:], sw[:], ALU.subtract, [cs, sw], [csm])
            else:
                c8 = cs[:].rearrange("p j (c t) -> p (j c) t", t=64)
                K.tt("dve", csm[:].rearrange("p j (c t) -> p (j c) t", t=64), _bc(c8[:, :, 63:64], [128, 8, 64]), c8, ALU.subtract,
                     [cs], [csm])
                K.tt("pool", cs[:], csm[:], sw[:], ALU.add, [csm, sw], [cs])
            K.act(E1[:], csm[:], AF.Exp, [csm], [E1], scale=-LWS)
            K.act(E2[:], cs[:], AF.Exp, [cs], [E2], scale=-LWS)
            K.act(E3[:], cs[:], AF.Exp, [cs], [E3], scale=LWS)
            goff = 0 if dirn else 63
            K.cp("pool", gam[fb][:], E2[:].rearrange("p j (c t) -> p j c t", t=64)[:, :, :, goff], [E2], [gam[fb]])
            for jj in range(4):
                K.ts("pool", kk[:, jj, :], k_[:, jj, :], ptab[:, PT_KK + jj:PT_KK + jj + 1], None, ALU.mult, None, [zrl, ptab], [kk])
            K.tt("pool", sq[:], kk[:], kk[:], ALU.mult, [kk], [sq])
            K.mm(PS[2][:, :], blk64, sq[:].rearrange("p j t -> p (j t)"), [cst, sq], [PS[2]])
            K.ts("dve", rn[:].rearrange("p j t -> p (j t)"), PS[2][:, :], epsc[:, 3:4], None, ALU.max, None, [PS[2], epsc], [rn])
            K.act(rn[:], rn[:], AF.Ln, [rn], [rn])
            K.act(rn[:], rn[:], AF.Exp, [rn], [rn], scale=-0.5)
            K.tt("dve", kkn[:], kk[:], rn[:], ALU.mult, [kk, rn], [kkn])
            for jj in range(4):
                K.ts("pool", tmpa[:, jj, :], asg[:, jj, :], ptab[:, PT_KA + jj:PT_KA + jj + 1], drv[:, DV_OMKA + jj:DV_OMKA + jj + 1],
                     ALU.mult, ALU.add, [asg, ptab, drv], [tmpa])
            K.tt("pool", kdr[:], k_, tmpa[:], ALU.mult, [zrl, tmpa], [kdr])
            K.tt("pool", bvec[:], kkn[:], asg[:], ALU.mult, [kkn, asg], [bvec])
            K.stt(AR[fb][:, :, 0, :], kkn[:], -1.0, E1[:], ALU.mult, ALU.mult, [kkn, E1], [AR[fb]])
            K.tt("dve", AR[fb][:, :, 1, :], r_, E2[:], ALU.mult, [zrl, E2], [AR[fb]])
            K.tt("dve", bt[fb][:], bvec[:], E3[:], ALU.mult, [bvec, E3], [bt[fb]])
            K.tt("pool", kt[fb][:], kdr[:], E3[:], ALU.mult, [kdr, E3], [kt[fb]])
            K.cp("pool", vb16[:], v_, [zrl], [vb16])
            tb0, tb1 = psb(0), psb(1)
            for jj in range(4):
                K.tr(tb0[:, jj * 128:(jj + 1) * 128], AR[fb][:, jj, 0, :], identb, [AR[fb], cstb], [PS[0]])
                K.tr(tb0[:, 512 + jj * 128:512 + (jj + 1) * 128], vb16[:, jj, :], identb, [vb16, cstb], [PS[0]])
                K.tr(tb1[:, jj * 128:(jj + 1) * 128], bt[fb][:, jj, :], identb, [bt[fb], cstb], [PS[1]])
                K.tr(tb1[:, 512 + jj * 128:512 + (jj + 1) * 128], kt[fb][:, jj, :], identb, [kt[fb], cstb], [PS[1]])
            K.cp("act", aT[fb][:].rearrange("p j t -> p (j t)"), tb0[:, 0:512], [PS[0]], [aT[fb]])
            K.cp("act", vT[fb][:].rearrange("p j t -> p (j t)"), tb0[:, 512:1024], [PS[0]], [vT[fb]])
            for c in range(2):
                K.act(btTm[fb][c][:].rearrange("p j t -> p (j t)"), tb1[:, 0:512], AF.Identity, [PS[1], cst], [btTm[fb][c]],
                      scale=cst[:, C_ROWM, 4 + c:5 + c])
                K.act(ktTm[fb][c][:].rearrange("p j t -> p (j t)"), tb1[:, 512:1024], AF.Identity, [PS[1], cst], [ktTm[fb][c]],
                      scale=cst[:, C_ROWM, 4 + c:5 + c])
            if final and isx:
                K.act(sgd[:], zrl[0:96, 13, :], AF.Sigmoid, [zrl], [sgd])
                for jj in range(4):
                    K.ts("pool", tmpa2[:, jj, :], asg2[:, jj, :], ptab[:, PT_KA + jj:PT_KA + jj + 1], drv[:, DV_OMKA + jj:DV_OMKA + jj + 1],
                         ALU.mult, ALU.add, [asg2, ptab, drv], [tmpa2])
                K.tt("pool", tmpa2[:], tmpa2[:], tmpa[:], ALU.add, [tmpa2, tmpa], [tmpa2])
                K.tt("pool", tmpa2[:], tmpa2[:], k_, ALU.mult, [tmpa2, zrl], [tmpa2])
                for jj in range(4):
                    K.stt(bonp[:, jj, :], r_[:, jj, :], ptab[:, PT_RK + jj:PT_RK + jj + 1], tmpa2[:, jj, :], ALU.mult, ALU.mult,
                          [zrl, ptab, tmpa2], [bonp])
                K.mm(PS[2][:, :], blk64, bonp[:].rearrange("p j t -> p (j t)"), [cst, bonp], [PS[2]])
                K.tt("dve", exb[fb][:, 0:4, :], PS[2][:, :].rearrange("p (j t) -> p j t", j=4), v_, ALU.mult, [PS[2], zrl], [exb[fb]])
                pg = PS[0]
                for jj in range(4):
                    K.mm(pg[:, jj * 128:(jj + 1) * 128], g2b[:, jj * 128:(jj + 1) * 128], sgd[:], [g2b, sgd], [pg])
                K.cp("act", exb[fb][:, 4:8, :], pg[:, :].rearrange("p (j t) -> p j t", j=4), [pg], [exb[fb]])
                K.dma(ex_d.ap[j], exb[fb][:], [exb[fb]], [ex_d])

        def pairs(n, s):
            kind, j = order[n]
            isx = kind == "x"
            fb = n % 2
            ba, bb = PS[3 + 2 * s], PS[4 + 2 * s]
            bke = (ba, bb)
            am, aw, au, qt, pbf = Amat[s], AW[s], AU[s], QT[s], Pbf[s]
            xp0, xp1 = XP[s]
            xt0, xt1 = XT[s]
            ar, bt_, kt_, aT_, vT_ = AR[fb], bt[fb], kt[fb], aT[fb], vT[fb]
            for jj in (s, s + 2):
                for e in range(2):
                    ep = slice(e * 64, (e + 1) * 64)
                    bk = bke[e]
                    arf = ar[ep, jj, :, :].rearrange("p a t -> p (a t)")
                    K.mm(bk[:, 0:256], bt_[ep, jj, :], arf, [bt_, ar], [bk])
                    K.mm(bk[:, 256:512], kt_[ep, jj, :], arf, [kt_, ar], [bk])
                for e in range(2):
                    bk = bke[e]
                    bv_ = bk[:, :].rearrange("p (w a t) -> p w a t", w=2, a=2)
                    K.tt("dve", xp0[:, e, 0, :], bk[:, 0:128], ms32, ALU.mult, [bk, cst], [xp0])
                    K.tt("dve", am[:, e, :, :, :], bv_, msir[:], ALU.mult, [bk, msir], [am])
                for e in range(2):
                    ep = slice(e * 64, (e + 1) * 64)
                    bk = bke[e]
                    K.mm(bk[:, 0:128], ar[ep, jj, 0, :], bt_[ep, jj, :], [ar, bt_], [bk])
                    K.tt("dve", xt0[:, e, :], bk[:, 0:128], mnt32, ALU.mult, [bk, cst], [xt0])
                K.cp("pool", xp0[:, :, 1, :], id32r[:], [id32r], [xp0])
                cur, nxt = (xp0, xt0), (xp1, xt1)
                for lvl in range(6):
                    cxp, cxt = cur
                    nxp, nxt_ = nxt
                    last = lvl == 5
                    for e in range(2):
                        bk = bke[e]
                        if not last:
                            K.mm(bk[:, 0:256], cxt[:, e, :], cxp[:, e, :, :].rearrange("p a t -> p (a t)"), [cxt, cxp], [bk])
                            K.mm(bk[:, 256:384], cxp[:, e, 0, :], cxt[:, e, :], [cxp, cxt], [bk])
                        else:
                            K.mm(bk[:, 128:256], cxt[:, e, :], cxp[:, e, 1, :], [cxt, cxp], [bk])
                    for e in range(2):
                        bk = bke[e]
                        K.tt("dve", nxp[:, e, 1, :], bk[:, 128:256], cxp[:, e, 1, :], ALU.add, [bk, cxp], [nxp])
                        if not last:
                            K.cp("act", nxp[:, e, 0, :], bk[:, 0:128], [bk], [nxp])
                            K.cp("act", nxt_[:, e, :], bk[:, 256:384], [bk], [nxt_])
                    cur, nxt = nxt, cur
                K.cp("pool", pbf[:], cur[0][:, :, 1, :], [cur[0]], [pbf])
                for e in range(2):
                    K.mm(ba[:, e * 64:(e + 1) * 64], am[:, e, 1, 0, :], vT_[:, jj, e * 64:(e + 1) * 64], [am, vT_], [ba])
                K.cp("pool", aw[:, :, 0:64], aT_[:, jj, :].rearrange("p (e k) -> p e k", e=2), [aT_], [aw])
                K.cp("act", aw[:, :, 64:128], ba[:, 0:128].rearrange("p (e v) -> p e v", e=2), [ba], [aw])
                for e in range(2):
                    K.mm(bb[:, e * 128:(e + 1) * 128], pbf[:, e, :], aw[:, e, :], [pbf, aw], [bb])
                K.cp("dve", au[:], bb[:, 0:256].rearrange("p (e c) -> p e c", e=2), [bb], [au])
                if isx:
                    for e in range(2):
                        K.mm(ba[e * 64:(e + 1) * 64, 128:256], au[:, e, 0:64], am[:, e, 0, 1, :], [au, am], [ba])
                    K.tt("dve", qt[:], ba[:, 128:256], ar[:, jj, 1, :], ALU.add, [ba, ar], [qt])
                corder2 = [0, 1] if dirn == 0 else [1, 0]
                Hj = H[jj]
                for ci, c in enumerate(corder2):
                    hb = Hbd[jj][c]
                    mt = MTbd[jj][c]
                    for e in range(2):
                        K.cp("act", hb[e * 64:(e + 1) * 64, e * 64:(e + 1) * 64], Hj[e * 64:(e + 1) * 64, :], [Hj], [hb])
                    for e in range(2):
                        K.mm(bb[e * 64:(e + 1) * 64, 256:320], au[:, e, 0:64], btTm[fb][c][:, jj, e * 64:(e + 1) * 64], [au, btTm[fb][c]], [bb])
                    for e in range(2):
                        K.tt("dve", mt[e * 64:(e + 1) * 64, e * 64:(e + 1) * 64], bb[e * 64:(e + 1) * 64, 256:320],
                             ident[e * 64:(e + 1) * 64, e * 64:(e + 1) * 64], ALU.add, [bb, cst], [mt])
                    with K.atomic():
                        K.mm(ba[:, 384:448], mt[:], Hj[:], [mt, Hj], [ba], start=True, stop=False)
                        for e in range(2):
                            ep = slice(e * 64, (e + 1) * 64)
                            K.mm(ba[ep, 384:448], btTm[fb][c][:, jj, ep], au[:, e, 64:128], [btTm[fb][c], au], [ba], start=False, stop=False)
                            K.mm(ba[ep, 384:448], ktTm[fb][c][:, jj, ep], vT_[:, jj, ep], [ktTm[fb][c], vT_], [ba], start=False, stop=True)
                    K.ts("dve", Hj[:], ba[:, 384:448], gam[fb][:, jj, c:c + 1], None, ALU.mult, None, [ba, gam[fb]], [Hj])
                if isx:
                    with K.atomic():
                        for e in range(2):
                            ep = slice(e * 64, (e + 1) * 64)
                            K.mm(yb[ep, jj * 128:(jj + 1) * 128], au[:, e, 64:128], am[:, e, 0, 1, :], [au, am], [yb], start=True, stop=False)
                            K.mm(yb[ep, jj * 128:(jj + 1) * 128], vT_[:, jj, ep], am[:, e, 1, 1, :], [vT_, am], [yb], start=False, stop=False)
                        for c in range(2):
                            K.mm(yb[:, jj * 128 + c * 64:jj * 128 + (c + 1) * 64], Hbd[jj][c][:], qt[:, c * 64:(c + 1) * 64], [Hbd[jj][c], qt], [yb],
                                 start=False, stop=(c == 1))

        def tail(n):
            kind, j = order[n]
            if kind != "x":
                return
            fb = n % 2
            K.cp("act", osb[fb][:, 4:8, :], yb[:, :].rearrange("p (j t) -> p j t", j=4), [yb], [osb[fb]])
            K.dma((ob_d if final else of_d).ap[j], osb[fb][:], [osb[fb]], [ob_d if final else of_d])

        import os
        NOIL = os.environ.get("NOIL", "0") == "1"
        NT_ = len(order)
        if NOIL:
            for n in range(NT_):
                front(n); pairs(n, 0); pairs(n, 1); tail(n)
        else:
            front(0)
            for n in range(NT_):
                fns = [lambda n=n: pairs(n, 0), lambda n=n: pairs(n, 1)]
                if n + 1 < NT_:
                    fns.append(lambda n=n: front(n + 1))
                K.run_streams(fns)
                tail(n)
        K.barrier()
        K.stack.close()

    def run_pc():
        K.stack = ExitStack()
        woutb = K.sb([128, 8, D], BF16, "woutb")
        lnb_ = K.sb([128, 2, D], F32, "ln1bc")
        K.dma(lnb_[:, 0, :], lnrow_d.ap[0:1, :].partition_broadcast(128), [lnrow_d], [lnb_])
        K.dma(lnb_[:, 1, :], lnrow_d.ap[1:2, :].partition_broadcast(128), [lnrow_d], [lnb_])
        wo_v = wout_d.ap.rearrange("(k p) c -> p k c", p=128)
        wos = [K.sb([128, D], F32, f"wos{i}") for i in range(2)]
        for dk in range(8):
            K.dma(wos[dk % 2][:], wo_v[:, dk, :], [wout_d], [wos[dk % 2]])
            K.cp("act" if dk % 2 else "dve", woutb[:, dk, :], wos[dk % 2][:], [wos[dk % 2]], [woutb])
        blk64 = cst[:, C_BLK64, :]
        onesf = cst[:, C_ONES, :]
        NS = 2
        bufs = []
        for s in range(NS):
            d_ = {}
            d_["of"] = K.sb([128, 8, 128], F32, f"pc_of{s}")
            d_["ob"] = K.sb([128, 8, 128], F32, f"pc_ob{s}")
            d_["ex"] = K.sb([128, 8, 128], F32, f"pc_ex{s}")
            d_["og"] = K.sb([128, 4, 128], F32, f"pc_og{s}")
            d_["x"] = K.sb([128, D], F32, f"pc_x{s}")
            for nm in ("ohg", "sqh", "rsth", "sog", "ysb", "ycen", "sq2", "rstd2", "yn2"):
                d_[nm] = K.sb([128, 4, 128], F32, f"pc_{nm}{s}")
            d_["yT"] = K.sb([128, 8, 128], BF16, f"pc_yT{s}")
            d_["h1"] = K.sb([128, D], F32, f"pc_h1{s}")
            d_["x1t"] = K.sb([128, D], F32, f"pc_x1t{s}")
            d_["st1"] = K.sb([128, 16], F32, f"pc_st{s}")
            d_["dbg"] = K.sb([128, 8, 128], F32, f"pc_dbg{s}") if debug else None
            bufs.append(d_)

        def pc_stream(s):
            B_ = bufs[s]
            pA, pB, pC, pD = PS[4 * s], PS[4 * s + 1], PS[4 * s + 2], PS[4 * s + 3]
            for j in range(s, NTX, NS):
                ec0 = CTX + j * 128
                K.dma(B_["of"][:], of_d.ap[j], [of_d], [B_["of"]])
                K.dma(B_["ob"][:], ob_d.ap[j], [ob_d], [B_["ob"]])
                K.dma(B_["ex"][:], ex_d.ap[j], [ex_d], [B_["ex"]])
                K.dma(B_["og"][:], zT_d.ap[16 * 128:20 * 128, ec0:ec0 + 128].rearrange("(h p) t -> p h t", p=128), [zT_d], [B_["og"]])
                K.dma(B_["x"][:], x_d[j * 128:(j + 1) * 128, :], [x_d], [B_["x"]])
                ohg, sqh, rsth, sog, ysb, ycen, sq2, rstd2, yn2 = [B_[k_] for k_ in ("ohg", "sqh", "rsth", "sog", "ysb", "ycen", "sq2", "rstd2", "yn2")]
                yT, h1, x1t, st1, xt = B_["yT"], B_["h1"], B_["x1t"], B_["st1"], B_["x"]
                K.tt("pool", ohg[:], B_["of"][:, 0:4, :], B_["ob"][:, 0:4, :], ALU.add, [B_["of"], B_["ob"]], [ohg])
                K.tt("pool", sqh[:], ohg[:], ohg[:], ALU.mult, [ohg], [sqh])
                K.mm(pA[:, :], onesf, sqh[:].rearrange("p h t -> p (h t)"), [cst, sqh], [pA])
                K.act(rsth[:].rearrange("p h t -> p (h t)"), pA[:, :], AF.Ln, [pA, epsc], [rsth], bias=epsc[:, 1:2], scale=1.0 / 128.0)
                K.act(rsth[:], rsth[:], AF.Exp, [rsth], [rsth], scale=-0.5)
                K.act(sog[:], B_["og"][:], AF.Sigmoid, [B_["og"]], [sog])
                K.tt("pool", sog[:], sog[:], B_["og"][:], ALU.mult, [sog, B_["og"]], [sog])
                K.tt("dve", ohg[:], ohg[:], rsth[:], ALU.mult, [ohg, rsth], [ohg])
                K.stt(yT[:, 0:4, :], ohg[:], ptab[:, PT_NW:PT_NW + 1], sog[:], ALU.mult, ALU.mult, [ohg, ptab, sog], [yT])
                K.tt("pool", ysb[:], B_["of"][:, 4:8, :], B_["ob"][:, 4:8, :], ALU.add, [B_["of"], B_["ob"]], [ysb])
                K.mm(pB[:, :], blk64, ysb[:].rearrange("p j t -> p (j t)"), [cst, ysb], [pB])
                K.stt(ycen[:], pB[:, :].rearrange("p (j t) -> p j t", j=4), -1.0 / 64.0, ysb[:], ALU.mult, ALU.add, [pB, ysb], [ycen])
                K.tt("pool", sq2[:], ycen[:], ycen[:], ALU.mult, [ycen], [sq2])
                K.mm(pB[:, :], blk64, sq2[:].rearrange("p j t -> p (j t)"), [cst, sq2], [pB])
                K.act(rstd2[:].rearrange("p j t -> p (j t)"), pB[:, :], AF.Ln, [pB, epsc], [rstd2], bias=epsc[:, 2:3], scale=1.0 / 64.0)
                K.act(rstd2[:], rstd2[:], AF.Exp, [rstd2], [rstd2], scale=-0.5)
                K.tt("dve", ycen[:], ycen[:], rstd2[:], ALU.mult, [ycen, rstd2], [ycen])
                for jj in range(4):
                    K.ts("pool", yn2[:, jj, :], ycen[:, jj, :], ptab[:, PT_LNW + jj:PT_LNW + jj + 1], ptab[:, PT_LNB + jj:PT_LNB + jj + 1],
                         ALU.mult, ALU.add, [ycen, ptab], [yn2])
                K.tt("pool", yn2[:], yn2[:], B_["ex"][:, 0:4, :], ALU.add, [yn2, B_["ex"]], [yn2])
                K.tt("dve", yT[:, 4:8, :], yn2[:], B_["ex"][:, 4:8, :], ALU.mult, [yn2, B_["ex"]], [yT])
                if debug:
                    K.cp("pool", B_["dbg"][:], yT[:], [yT], [B_["dbg"]])
                    K.dma(dbg["yT"].ap[j], B_["dbg"][:], [B_["dbg"]], [dbg["yT"]])
                for dh in range(2):
                    bank = pC if dh == 0 else pD
                    with K.atomic():
                        for m in range(8):
                            K.mm(bank[:, :], yT[:, m, :], woutb[:, m, dh * 512:(dh + 1) * 512], [yT, woutb], [bank], start=(m == 0), stop=(m == 7))
                    K.tt("dve", h1[:, dh * 512:(dh + 1) * 512], bank[:, :], gb[:, 0, dh * 512:(dh + 1) * 512], ALU.mult, [bank, gb], [h1])
                K.stt(h1[:], xt[:], ALPHA, h1[:], ALU.mult, ALU.add, [xt, h1], [h1])
                ln_stats2(K, nc, h1, h1.ap, st1, epsc, 1)
                K.ts("dve", x1t[:], h1[:], st1[:, 12:13], st1[:, 15:16], ALU.subtract, ALU.mult, [h1, st1], [x1t])
                K.tt("pool", x1t[:], x1t[:], lnb_[:, 0, :], ALU.mult, [x1t, lnb_], [x1t])
                K.tt("pool", x1t[:], x1t[:], lnb_[:, 1, :], ALU.add, [x1t, lnb_], [x1t])
                K.dma(x1_d[j * 128:(j + 1) * 128, :], x1t[:], [x1t], [x1_d])
                if debug:
                    K.dma(dbg["x1"][j * 128:(j + 1) * 128, :], x1t[:], [x1t], [dbg["x1"]])
        K.run_streams([lambda s=s: pc_stream(s) for s in range(NS)])
        K.barrier()
        K.stack.close()

    if upto < 1:
        return nc, K
    run_pass(0)
    if upto < 2:
        return nc, K
    run_pass(1)
    if upto < 2.5:
        return nc, K
    run_pc()
    if upto < 3:
        return nc, K

    GT = 2
    GC = GT * 128
    K.stack = ExitStack()
    wgb = K.sb([128, 8, DFF], BF16, "wgb")
    wub = K.sb([128, 8, DFF], BF16, "wub")
    wdb = K.sb([128, NFT, D], BF16, "wdb")
    outer4 = K.stack
    K.stack = ExitStack()
    stg = [K.sb([128, DFF], F32, f"stg{i}") for i in range(2)]
    si = 0
    for (wd_, wb_, nk, ncol) in ((wg_d, wgb, 8, DFF), (wu_d, wub, 8, DFF), (wd_d, wdb, NFT, D)):
        v = wd_.ap.rearrange("(k p) c -> p k c", p=128)
        for kk_ in range(nk):
            s = stg[si % 2]
            K.dma(s[:, 0:ncol], v[:, kk_, :], [wd_], [s])
            K.cp(("act", "dve", "pool")[si % 3], wb_[:, kk_, :], s[:, 0:ncol], [s], [wb_])
            si += 1
    K.barrier()
    K.stack.close()
    K.stack = outer4
    ln2bc = K.sb([128, 2, D], F32, "ln2bc")
    K.dma(ln2bc[:, 0, :], lnrow_d.ap[2:3, :].partition_broadcast(128), [lnrow_d], [ln2bc])
    K.dma(ln2bc[:, 1, :], lnrow_d.ap[3:4, :].partition_broadcast(128), [lnrow_d], [ln2bc])
    x1g = [K.sb([128, GT, D], F32, f"x1g{i}") for i in range(2)]
    u2T = [K.sb([128, 8, GC], BF16, f"u2T{i}") for i in range(2)]
    hT = K.sb([128, NFT, GC], BF16, "hT")
    xnb2 = [K.sb([128, D], BF16, "xnb2_0")] * 2
    st2 = [K.sb([128, 16], F32, f"st2_{i}") for i in range(2)]
    sgl = [K.sb([128, GC], F32, f"sgl{i}") for i in range(2)]
    h2 = [K.sb([128, D], F32, f"h2_{i}") for i in range(2)]
    st3 = [K.sb([128, 16], F32, f"st3_{i}") for i in range(2)]
    ngrp = (NTX + GT - 1) // GT
    tcount = 0
    for g in range(ngrp):
        nt = min(GT, NTX - g * GT)
        ncol = nt * 128
        xg = x1g[g % 2]
        ut = u2T[g % 2]
        for i in range(nt):
            t = g * GT + i
            K.dma(xg[:, i, :], x1_d[t * 128:(t + 1) * 128, :], [x1_d], [xg])
        for i in range(nt):
            st = st2[tcount % 2]
            xb = xnb2[tcount % 2]
            tcount += 1
            ln_stats2(K, nc, xg, xg[:, i, :], st, epsc, 0)
            K.ts("dve", xb[:], xg[:, i, :], st[:, 12:13], st[:, 15:16], ALU.subtract, ALU.mult, [xg, st], [xb])
            tb = psb(7)
            for dk in range(8):
                K.tr(tb[:, dk * 128:(dk + 1) * 128], xb[:, dk * 128:(dk + 1) * 128], identb, [xb, cstb], [PS[7]])
            for dk in range(8):
                K.act(ut[:, dk, i * 128:(i + 1) * 128], tb[:, dk * 128:(dk + 1) * 128], AF.Identity, [PS[7], opsc, modT], [ut],
                      bias=modT[:, 24 + dk, 0:1], scale=opsc[:, 2, dk:dk + 1])
        for ft in range(NFT):
            bg, bu = PS[(2 * ft) % 4], PS[(2 * ft + 1) % 4]
            for dk in range(8):
                K.mm(bg[:, 0:ncol], wgb[:, dk, ft * 128:(ft + 1) * 128], ut[:, dk, 0:ncol], [wgb, ut], [bg], start=(dk == 0), stop=(dk == 7))
            for dk in range(8):
                K.mm(bu[:, 0:ncol], wub[:, dk, ft * 128:(ft + 1) * 128], ut[:, dk, 0:ncol], [wub, ut], [bu], start=(dk == 0), stop=(dk == 7))
            sg_ = sgl[ft % 2]
            K.act(sg_[:, 0:ncol], bg[:, 0:ncol], AF.Sigmoid, [bg], [sg_])
            K.tt("dve", sg_[:, 0:ncol], sg_[:, 0:ncol], bg[:, 0:ncol], ALU.mult, [sg_, bg], [sg_])
            K.tt("dve", hT[:, ft, 0:ncol], sg_[:, 0:ncol], bu[:, 0:ncol], ALU.mult, [sg_, bu], [hT])
        for i in range(nt):
            t = g * GT + i
            hh = h2[t % 2]
            st = st3[t % 2]
            for dh in range(2):
                bank = PS[4 + dh]
                for ft in range(NFT):
                    K.mm(bank[:, :], hT[:, ft, i * 128:(i + 1) * 128], wdb[:, ft, dh * 512:(dh + 1) * 512], [hT, wdb], [bank],
                         start=(ft == 0), stop=(ft == NFT - 1))
                K.tt("dve", hh[:, dh * 512:(dh + 1) * 512], bank[:, :], gb[:, 1, dh * 512:(dh + 1) * 512], ALU.mult, [bank, gb], [hh])
            K.stt(hh[:], xg[:, i, :], ALPHA, hh[:], ALU.mult, ALU.add, [xg, hh], [hh])
            ln_stats2(K, nc, hh, hh.ap, st, epsc, 1)
            K.ts("dve", hh[:], hh[:], st[:, 12:13], st[:, 15:16], ALU.subtract, ALU.mult, [hh, st], [hh])
            K.tt("pool", hh[:], hh[:], ln2bc[:, 0, :], ALU.mult, [hh, ln2bc], [hh])
            K.tt("pool", hh[:], hh[:], ln2bc[:, 1, :], ALU.add, [hh, ln2bc], [hh])
            K.dma(out_d[t * 128:(t + 1) * 128, :], hh[:], [hh], [out_d])
    K.barrier()
    K.stack.close()
    return nc, K


def ln_stats2(K, nc, src, ap, st, epsc, eps_col):
    K.op("dve", lambda: nc.vector.bn_stats(st[:, 0:6], ap[:, 0:512]), [src], [st])
    K.op("dve", lambda: nc.vector.bn_stats(st[:, 6:12], ap[:, 512:1024]), [src], [st])
    K.op("dve", lambda: nc.vector.bn_aggr(st[:, 12:14], st[:, 0:12]), [st], [st])
    K.act(st[:, 14:15], st[:, 13:14], AF.Ln, [st, epsc], [st], bias=epsc[:, eps_col:eps_col + 1])
    K.act(st[:, 15:16], st[:, 14:15], AF.Exp, [st], [st], scale=-0.5)


def _consts():
    c = np.zeros((128, NCONST, 128), np.float32)
    s = np.arange(128)[:, None]
    t = np.arange(128)[None, :]
    c[:, C_ID] = (s == t)
    c[:, C_M32F] = (s // 32 == t // 32) & (s <= t)
    c[:, C_M32B] = (s // 32 == t // 32) & (s >= t)
    c[:, C_MSF] = (s // 64 == t // 64) & (s < t)
    c[:, C_MIF] = (s // 64 == t // 64) & (s <= t)
    c[:, C_MSB] = (s // 64 == t // 64) & (s > t)
    c[:, C_MIB] = (s // 64 == t // 64) & (s >= t)
    c[:, C_BLK64] = (s // 64 == t // 64)
    c[:, C_ONES] = 1.0
    c[:, C_RST32] = np.broadcast_to((t % 32 != 0), (128, 128))
    c[:, C_RST64] = np.broadcast_to((t % 64 != 0), (128, 128))
    rm = np.zeros((128, 128), np.float32)
    for k in range(4):
        rm[:, k] = (np.arange(128) // 32 == k)
    for k in range(2):
        rm[:, 4 + k] = (np.arange(128) // 64 == k)
    c[:, C_ROWM] = rm
    return c


def _fm(v, nt):
    return np.ascontiguousarray(np.asarray(v, np.float32).reshape(nt, 128).T)


def _ptab(inp):
    pt = np.zeros((128, NPT), np.float32)
    lbl = np.asarray(inp["hgrn_lb_logits"], np.float32)
    for d in range(2):
        pt[:, PT_L0 + 4 * d:PT_L0 + 4 * d + 4] = _fm(lbl[0, d], 4)
        pt[:, PT_L1 + 4 * d:PT_L1 + 4 * d + 4] = _fm(lbl[1, d], 4)
    pt[:, PT_NW] = np.asarray(inp["hgrn_norm_w"], np.float32)[0]
    mu = np.zeros(14 * 128, np.float32)
    mu[:1760] = np.asarray(inp["rwkv_mu"], np.float32)[0]
    pt[:, PT_MU:PT_MU + 14] = _fm(mu, 14)
    ch = np.arange(14 * 128)
    valid = ch < 1760
    masks = [ch < 440, (ch >= 440) & (ch < 880), (ch >= 880) & (ch < 1320), (ch >= 1320) & valid, ch < 880, (ch >= 880) & valid]
    for i, m in enumerate(masks):
        pt[:, PT_ML + 14 * i:PT_ML + 14 * (i + 1)] = _fm(m.astype(np.float32), 14)
    for d in range(2):
        pt[:, PT_W0 + 4 * d:PT_W0 + 4 * d + 4] = _fm(inp["rwkv_w0"][0, d], 4)
        pt[:, PT_A0 + 4 * d:PT_A0 + 4 * d + 4] = _fm(inp["rwkv_a0"][0, d], 4)
    pt[:, PT_KK:PT_KK + 4] = _fm(inp["rwkv_k_k"][0], 4)
    pt[:, PT_KA:PT_KA + 4] = _fm(inp["rwkv_k_a"][0], 4)
    pt[:, PT_RK:PT_RK + 4] = _fm(np.asarray(inp["rwkv_r_k"])[0].reshape(512), 4)
    pt[:, PT_LNW:PT_LNW + 4] = _fm(inp["rwkv_lnx_w"][0], 4)
    pt[:, PT_LNB:PT_LNB + 4] = _fm(inp["rwkv_lnx_b"][0], 4)
    pt[:, PT_BADA:PT_BADA + 48] = _fm(inp["b_ada"][0], 48)
    return pt


def _shared_maps(inp):
    f = lambda a: np.ascontiguousarray(np.asarray(a, np.float32))
    wl4 = np.zeros((128, 4, 512), np.float32)
    wl4[0:32, 0] = inp["rwkv_w2"][0, 0]
    wl4[32:64, 1] = inp["rwkv_w2"][0, 1]
    wl4[64:96, 2] = inp["rwkv_a2"][0, 0]
    wl4[96:128, 3] = inp["rwkv_a2"][0, 1]
    lnrows = np.stack([f(inp["ln1_g"])[0], f(inp["ln1_b"])[0], f(inp["ln2_g"])[0], f(inp["ln2_b"])[0]], 0)
    return {
        "w_ada": f(inp["w_ada"])[0], "b_ada_row": f(inp["b_ada"]), "w_in": f(inp["w_in"])[0], "ptab": _ptab(inp),
        "consts": _consts(), "wl4": wl4, "g2": f(inp["rwkv_g2"])[0], "w_out": f(inp["w_out"])[0],
        "lnrows": np.ascontiguousarray(lnrows), "w_gate": f(inp["w_ffn_gate"])[0], "w_up": f(inp["w_ffn_up"])[0],
        "w_down": f(inp["w_ffn_down"])[0],
    }


def _core_map(inp, shared, b):
    m = dict(shared)
    m["x"] = np.ascontiguousarray(np.asarray(inp["x"][b], np.float32))
    m["ctx"] = np.ascontiguousarray(np.asarray(inp["ctx"][b], np.float32))
    cv = np.zeros((128, 16), np.float32)
    cv[:, 0::2] = np.asarray(inp["c"][b], np.float32).reshape(8, 128).T
    cv[:, 1::2] = np.asarray(inp["c_ctx"], np.float32).reshape(8, 128).T
    m["cv"] = cv
    return m


_NC_CACHE = {}


def kernel(**inputs):
    x = np.asarray(inputs["x"])
    B, T, _ = x.shape
    if T not in _NC_CACHE:
        _NC_CACHE[T] = build(T)[0]
    nc = _NC_CACHE[T]
    shared = _shared_maps(inputs)
    in_maps = [_core_map(inputs, shared, b) for b in range(B)]
    res = run_bass_kernel_spmd(nc, in_maps, core_ids=list(range(B)))
    return np.stack([np.asarray(r["out"], np.float32) for r in res.results], 0)
```

```python
from contextlib import ExitStack
import numpy as np
import concourse.bass as bass
import concourse.mybir as mybir
from concourse.bass_utils import run_bass_kernel_spmd

F32 = mybir.dt.float32
BF16 = mybir.dt.bfloat16
ALU = mybir.AluOpType
AF = mybir.ActivationFunctionType

D = 1024
CTX = 256
NCT = 34
ZC = NCT * 128
IN_COLS = 4320
DFF = 2816
NFT = DFF // 128
LWS = 0.6065306597126334
ALPHA = 2.0 ** 0.25

C_ID, C_M32F, C_M32B, C_MSF, C_MIF, C_MSB, C_MIB, C_BLK64, C_ONES, C_RST32, C_RST64, C_ROWM = range(12)
NCONST = 12
PT_L0 = 0
PT_L1 = 8
PT_NW = 16
PT_MU = 17
PT_ML = 31
PT_W0 = 115
PT_A0 = 123
PT_KK = 131
PT_KA = 135
PT_RK = 139
PT_LNW = 143
PT_LNB = 147
PT_BADA = 151
NPT = 199


class Buf:
    __slots__ = ("ap", "w", "r", "name", "tw", "tr")

    def __init__(self, ap, name=""):
        self.ap = ap
        self.w = {}
        self.r = {}
        self.name = name
        self.tw = 0.0
        self.tr = 0.0

    def __getitem__(self, k):
        return self.ap[k]


class KB:
    NR = 8

    def __init__(self, nc):
        self.nc = nc
        self.E = {"pe": nc.tensor, "dve": nc.vector, "act": nc.scalar, "pool": nc.gpsimd, "sp": nc.sync}
        self.sems = []
        self.semval = []
        self.esem = {e: self._newsem("c_" + e) for e in self.E}
        self.dsem = {"sp": [self._newsem(f"d_sp{i}") for i in range(self.NR)]}
        self.didx = {"sp": 0}
        self.seen = {e: {} for e in self.E}
        self.nbuf = 0
        self.nwait = 0
        self.ninst = 0
        self.stack = None
        self._st = None
        self.clk = {}

    def _newsem(self, name):
        self.sems.append(self.nc.alloc_semaphore(name))
        self.semval.append(0)
        return len(self.sems) - 1

    def sb(self, shape, dtype=F32, name=None, perm=False):
        self.nbuf += 1
        name = f"{name or 't'}_{self.nbuf}"
        if perm or self.stack is None:
            h = self.nc.alloc_sbuf_tensor(name, list(shape), dtype)
        else:
            h = self.stack.enter_context(self.nc.sbuf_tensor(name, list(shape), dtype))
        return Buf(h.ap(), name)

    def ps(self, shape, dtype=F32, name=None):
        self.nbuf += 1
        return Buf(self.nc.alloc_psum_tensor(f"{name or 'p'}_{self.nbuf}", list(shape), dtype).ap(), name)

    def dram(self, name, shape, dtype=F32, kind="Internal"):
        return Buf(self.nc.dram_tensor(name, list(shape), dtype, kind=kind).ap(), name)

    def _need(self, reads, writes):
        need = {}
        for b in reads:
            for k, v in b.w.items():
                if need.get(k, 0) < v:
                    need[k] = v
        for b in writes:
            for k, v in b.w.items():
                if need.get(k, 0) < v:
                    need[k] = v
            for k, v in b.r.items():
                if need.get(k, 0) < v:
                    need[k] = v
        return need

    def _wait(self, e, need):
        own = self.esem[e]
        seen = self.seen[e]
        eng = self.E[e]
        for k, v in need.items():
            if k == own and e == "pe":
                continue
            if seen.get(k, 0) < v:
                eng.wait_ge(self.sems[k], v)
                seen[k] = v
                self.nwait += 1

    def _commit(self, k, v, reads, writes):
        for b in writes:
            b.w = {k: v}
            b.r = {}
        for b in reads:
            if b.r.get(k, 0) < v:
                b.r[k] = v

    def op(self, e, fn, reads=(), writes=(), cost=0.5):
        self._yield(e, reads, writes)
        self._model(e, reads, writes, cost)
        self._wait(e, self._need(reads, writes))
        ins = fn()
        k = self.esem[e]
        self.semval[k] += 1
        ins.then_inc(self.sems[k], 1)
        self._commit(k, self.semval[k], reads, writes)
        self.ninst += 1
        return ins

    def dma(self, out, in_, reads=(), writes=(), q="sp"):
        self._yield(q, reads, writes)
        self._model(q, reads, writes, 1.0)
        i = self.didx[q]
        self.didx[q] += 1
        k = self.dsem[q][i % self.NR]
        need = self._need(reads, writes)
        if self.semval[k] > 0 and need.get(k, 0) < self.semval[k]:
            need[k] = self.semval[k]
        self._wait(q, need)
        self.semval[k] += 16
        self.E[q].dma_start(out=out, in_=in_).then_inc(self.sems[k], 16)
        self._commit(k, self.semval[k], reads, writes)
        self.ninst += 1


    def run_streams(self, fns):
        import threading
        n = len(fns)
        if n == 1:
            fns[0]()
            return
        st = {"turn": -1, "alive": [True] * n, "err": [], "pend": [None] * n, "started": 0}
        cv = threading.Condition()
        self._st, self._cv = st, cv
        self._tls = threading.local()

        def pick():
            best, bt_ = -1, None
            for i in range(n):
                if st["alive"][i] and st["pend"][i] is not None:
                    t = st["pend"][i]
                    if bt_ is None or t < bt_:
                        best, bt_ = i, t
            st["turn"] = best
            cv.notify_all()
        self._pick = pick

        def all_pending():
            return all((not st["alive"][i]) or st["pend"][i] is not None for i in range(n))
        self._all_pending = all_pending

        def runner(i):
            self._tls.sid = i
            self._tls.atomic = 0
            try:
                fns[i]()
            except BaseException as e:
                st["err"].append(e)
            finally:
                with cv:
                    st["alive"][i] = False
                    st["pend"][i] = None
                    if any(st["alive"]) and all_pending():
                        pick()
        ths = [threading.Thread(target=runner, args=(i,)) for i in range(n)]
        for t in ths:
            t.start()
        for t in ths:
            t.join()
        self._st = None
        if st["err"]:
            raise st["err"][0]

    def _est_start(self, e, reads, writes):
        t = self.clk.get(e, 0.0)
        for b in reads:
            if b.tw > t:
                t = b.tw
        for b in writes:
            if b.tw > t:
                t = b.tw
            if b.tr > t:
                t = b.tr
        return t

    def _model(self, e, reads, writes, cost):
        t = self._est_start(e, reads, writes) + 0.15
        f = t + cost
        if e == "sp":
            self.clk[e] = t + 0.05
            f = t + 2.0 + cost
        else:
            self.clk[e] = f
        for b in writes:
            b.tw = f
        for b in reads:
            if b.tr < f:
                b.tr = f

    def _yield(self, e, reads, writes):
        st = getattr(self, "_st", None)
        if st is None:
            return
        tls = self._tls
        i = getattr(tls, "sid", None)
        if i is None or tls.atomic:
            return
        cv = self._cv
        with cv:
            st["pend"][i] = self._est_start(e, reads, writes)
            if self._all_pending():
                self._pick()
            while st["turn"] != i:
                cv.wait()
            st["turn"] = -1
            st["pend"][i] = None

    def atomic(self):
        kb = self

        class _A:
            def __enter__(self_):
                if getattr(kb, "_st", None) is not None and getattr(kb._tls, "sid", None) is not None:
                    kb._tls.atomic += 1

            def __exit__(self_, *a):
                if getattr(kb, "_st", None) is not None and getattr(kb._tls, "sid", None) is not None:
                    kb._tls.atomic -= 1
        return _A()

    def barrier(self):
        need = {k: v for k, v in enumerate(self.semval) if v > 0}
        for e in self.E:
            self._wait(e, need)

    @staticmethod
    def _n(ap):
        n = 1
        for d in ap.shape[1:]:
            n *= d
        return n

    def mm(self, out, lhsT, rhs, reads, writes, start=True, stop=True):
        nc = self.nc
        c = 0.03 + self._n(rhs) * (4 if rhs.dtype == F32 else 1) / 2400.0
        return self.op("pe", lambda: nc.tensor.matmul(out, lhsT, rhs, start=start, stop=stop), reads, writes, cost=c)

    def tr(self, out, in_, ident, reads, writes):
        nc = self.nc
        return self.op("pe", lambda: nc.tensor.transpose(out, in_, ident), reads, writes, cost=0.09)

    def act(self, out, in_, func, reads, writes, bias=None, scale=None):
        nc = self.nc
        kw = {}
        if bias is not None:
            kw["bias"] = bias
        if scale is not None:
            kw["scale"] = scale
        c = 0.2 + self._n(out) / 1200.0
        return self.op("act", lambda: nc.scalar.activation(out, in_, func, **kw), reads, writes, cost=c)

    def _vc(self, e, out):
        n = self._n(out)
        return (0.1 + n / 500.0) if e == "pool" else (0.07 + n / 900.0)

    def tt(self, e, out, in0, in1, op, reads, writes):
        eng = self.E[e]
        return self.op(e, lambda: eng.tensor_tensor(out, in0, in1, op), reads, writes, cost=self._vc(e, out))

    def ts(self, e, out, in0, s1, s2, op0, op1, reads, writes):
        eng = self.E[e]
        if s2 is None:
            return self.op(e, lambda: eng.tensor_scalar(out, in0, s1, None, op0), reads, writes, cost=self._vc(e, out))
        return self.op(e, lambda: eng.tensor_scalar(out, in0, s1, s2, op0, op1), reads, writes, cost=self._vc(e, out))

    def stt(self, out, in0, scalar, in1, op0, op1, reads, writes):
        nc = self.nc
        return self.op("dve", lambda: nc.vector.scalar_tensor_tensor(out, in0, scalar, in1, op0, op1), reads, writes,
                       cost=0.07 + self._n(out) / 900.0)

    def cp(self, e, out, in_, reads, writes):
        if e == "act":
            nc = self.nc
            return self.op("act", lambda: nc.scalar.copy(out, in_), reads, writes, cost=0.2 + self._n(out) / 1200.0)
        eng = self.E[e]
        return self.op(e, lambda: eng.tensor_copy(out, in_), reads, writes, cost=self._vc(e, out))

    def memset(self, e, ap, val, writes):
        eng = self.E[e]
        return self.op(e, lambda: eng.memset(ap, val), [], writes, cost=self._vc(e, ap))


def _bc(ap, shape):
    return ap.to_broadcast(list(shape))


def build(T, debug=False, upto=9):
    NTX = T // 128
    NE = CTX + T
    nc = bass.Bass("TRN2", target_bir_lowering=False)
    K = KB(nc)
    x_d = K.dram("x", [T, D], kind="ExternalInput")
    ctx_d = K.dram("ctx", [CTX, D], kind="ExternalInput")
    cv_d = K.dram("cv", [128, 16], kind="ExternalInput")
    wada_d = K.dram("w_ada", [D, 6 * D], kind="ExternalInput")
    brow_d = K.dram("b_ada_row", [1, 6 * D], kind="ExternalInput")
    win_d = K.dram("w_in", [D, IN_COLS], kind="ExternalInput")
    ptab_d = K.dram("ptab", [128, NPT], kind="ExternalInput")
    const_d = K.dram("consts", [128, NCONST, 128], kind="ExternalInput")
    wl_d = K.dram("wl4", [128, 4, 512], kind="ExternalInput")
    g2_d = K.dram("g2", [96, 512], kind="ExternalInput")
    wout_d = K.dram("w_out", [D, D], kind="ExternalInput")
    lnrow_d = K.dram("lnrows", [4, D], kind="ExternalInput")
    wg_d = K.dram("w_gate", [D, DFF], kind="ExternalInput")
    wu_d = K.dram("w_up", [D, DFF], kind="ExternalInput")
    wd_d = K.dram("w_down", [DFF, D], kind="ExternalInput")
    out_d = K.dram("out", [T, D], kind="ExternalOutput")
    zT_d = K.dram("zT", [ZC, NE])
    of_d = K.dram("ofwd", [NTX, 128, 8, 128])
    x1_d = K.dram("x1s", [T, D])
    ob_d = K.dram("obwd", [NTX, 128, 8, 128])
    ex_d = K.dram("extra", [NTX, 128, 8, 128])
    dbg = {}
    if debug:
        dbg["zT"] = K.dram("dbg_zT", [ZC, NE], kind="ExternalOutput")
        dbg["yT"] = K.dram("dbg_yT", [NTX, 128, 8, 128], kind="ExternalOutput")
        dbg["x1"] = K.dram("dbg_x1", [T, D], kind="ExternalOutput")

    cst = K.sb([128, NCONST, 128], F32, "cst", perm=True)
    cstb = K.sb([128, NCONST, 128], BF16, "cstb", perm=True)
    ptab = K.sb([128, NPT], F32, "ptab", perm=True)
    modT = K.sb([128, 48, 2], F32, "modT", perm=True)
    opsc = K.sb([128, 3, 8], F32, "opsc", perm=True)
    epsc = K.sb([128, 4], F32, "epsc", perm=True)
    gb = K.sb([128, 2, D], F32, "gb", perm=True)
    drv = K.sb([128, 128], F32, "drv", perm=True)
    DV_LB, DV_OML, DV_NOML = 0, 8, 16
    DV_C0 = 24
    DV_CS = 38
    DV_OMKA = 122
    PS = [K.ps([128, 512], F32, f"bank{i}") for i in range(8)]

    def psb(i):
        return PS[i].ap.bitcast(BF16)

    ident = cst[:, C_ID, :]
    identb = cstb[:, C_ID, :]

    K.dma(cst[:], const_d[:, :, :], [const_d], [cst])
    K.dma(ptab[:], ptab_d[:, :], [ptab_d], [ptab])
    K.cp("dve", cstb[:], cst[:], [cst], [cstb])
    K.memset("pool", epsc[:, 0:1], 1e-6, [epsc])
    K.memset("pool", epsc[:, 1:2], 1e-5, [epsc])
    K.memset("pool", epsc[:, 2:3], 64e-5, [epsc])
    K.memset("pool", epsc[:, 3:4], 1e-24, [epsc])
    K.tt("dve", drv[:, 0:8], ptab[:, PT_L0:PT_L0 + 8], ptab[:, PT_L1:PT_L1 + 8], ALU.subtract, [ptab], [drv])
    K.act(drv[:, DV_LB:DV_LB + 8], drv[:, 0:8], AF.Sigmoid, [drv], [drv])
    K.ts("dve", drv[:, DV_OML:DV_OML + 8], drv[:, DV_LB:DV_LB + 8], -1.0, 1.0, ALU.mult, ALU.add, [drv], [drv])
    K.ts("dve", drv[:, DV_NOML:DV_NOML + 8], drv[:, DV_OML:DV_OML + 8], -1.0, None, ALU.mult, None, [drv], [drv])
    K.ts("dve", drv[:, DV_C0:DV_C0 + 14], ptab[:, PT_MU:PT_MU + 14], -1.0, 1.0, ALU.mult, ALU.add, [ptab], [drv])
    for i in range(6):
        K.tt("dve", drv[:, DV_CS + 14 * i:DV_CS + 14 * (i + 1)], ptab[:, PT_MU:PT_MU + 14],
             ptab[:, PT_ML + 14 * i:PT_ML + 14 * (i + 1)], ALU.mult, [ptab], [drv])
    K.ts("dve", drv[:, DV_OMKA:DV_OMKA + 4], ptab[:, PT_KA:PT_KA + 4], -1.0, 1.0, ALU.mult, ALU.add, [ptab], [drv])

    if upto == 0.1:
        K.barrier()
        return nc, K
    K.stack = ExitStack()
    winb = K.sb([128, 8, ZC], BF16, "winb")
    cv = K.sb([128, 16], F32, "cv")
    cvs = K.sb([128, 16], F32, "cvs")
    K.dma(cv[:], cv_d[:, :], [cv_d], [cv])
    outer0 = K.stack
    K.stack = ExitStack()
    brow = K.sb([1, 4, 512], F32, "brow")
    for i_, eg_ in enumerate((4, 5, 10, 11)):
        K.dma(brow[0:1, i_, :], brow_d[0:1, eg_ * 512:(eg_ + 1) * 512], [brow_d], [brow])
    K.act(cvs[:], cv[:], AF.Sigmoid, [cv], [cvs])
    K.tt("dve", cvs[:], cvs[:], cv[:], ALU.mult, [cvs, cv], [cvs])
    wa = [K.sb([128, 8, 512], F32, f"wa{i}") for i in range(2)]
    wada_v = wada_d.ap.rearrange("(k p) e -> p k e", p=128)
    grow = K.sb([1, 512], F32, "grow")
    for eg in range(12):
        w = wa[eg % 2]
        K.dma(w[:], wada_v[:, :, eg * 512:(eg + 1) * 512], [wada_d], [w])
        bank = PS[eg % 2]
        for j in range(4):
            for dk in range(8):
                K.mm(bank[:, 2 * j:2 * j + 2], w[:, dk, j * 128:(j + 1) * 128], cvs[:, 2 * dk:2 * dk + 2], [w, cvs], [bank],
                     start=(dk == 0), stop=(dk == 7))
        for j in range(4):
            et = eg * 4 + j
            K.ts("dve", modT[:, et, :], bank[:, 2 * j:2 * j + 2], ptab[:, PT_BADA + et:PT_BADA + et + 1], None, ALU.add, None,
                 [bank, ptab], [modT])
        if eg in (4, 5, 10, 11):
            gi = 0 if eg < 6 else 1
            half = eg % 2 if eg < 6 else (eg - 10)
            rb_ = PS[2]
            for dk in range(8):
                K.mm(rb_[0:1, 0:512], cvs[:, 2 * dk:2 * dk + 1], w[:, dk, :], [w, cvs], [rb_], start=(dk == 0), stop=(dk == 7))
            K.tt("dve", grow[:], rb_[0:1, 0:512], brow[0:1, (4, 5, 10, 11).index(eg), :], ALU.add, [rb_, brow], [grow])
            bb = PS[3]
            K.mm(bb[:, 0:512], cst[0:1, C_ONES, :], grow[:], [cst, grow], [bb])
            K.cp("act", gb[:, gi, half * 512:(half + 1) * 512], bb[:, 0:512], [bb], [gb])
    K.ts("dve", opsc[:, 0, :], modT[:, 8:16, 0], 1.0, None, ALU.add, None, [modT], [opsc])
    K.ts("dve", opsc[:, 1, :], modT[:, 8:16, 1], 1.0, None, ALU.add, None, [modT], [opsc])
    K.ts("dve", opsc[:, 2, :], modT[:, 32:40, 0], 1.0, None, ALU.add, None, [modT], [opsc])
    K.barrier()
    if upto == 0.2:
        return nc, K
    K.stack.close()
    K.stack = ExitStack()
    wst = [K.sb([128, IN_COLS], F32, f"wst{i}") for i in range(2)]
    win_v = win_d.ap.rearrange("(k p) c -> p k c", p=128)
    K.memset("pool", winb[:, :, IN_COLS:ZC], 0.0, [winb])
    for dk in range(8):
        s = wst[dk % 2]
        K.dma(s[:], win_v[:, dk, :], [win_d], [s])
        K.cp("act" if dk % 2 else "dve", winb[:, dk, 0:IN_COLS], s[:], [s], [winb])

    K.barrier()
    if upto == 0.3:
        return nc, K
    K.stack.close()
    K.stack = outer0
    xts = [K.sb([128, D], F32, f"xt{i}") for i in range(2)]
    xnb = [K.sb([128, D], BF16, f"xnb{i}") for i in range(2)]
    stt_ = [K.sb([128, 16], F32, f"st{i}") for i in range(2)]

    def ln_stats(src_ap, src_bufs, st, eps_col):
        K.op("dve", lambda: nc.vector.bn_stats(st[:, 0:6], src_ap[:, 0:512]), src_bufs, [st])
        K.op("dve", lambda: nc.vector.bn_stats(st[:, 6:12], src_ap[:, 512:1024]), src_bufs, [st])
        K.op("dve", lambda: nc.vector.bn_aggr(st[:, 12:14], st[:, 0:12]), [st], [st])
        K.act(st[:, 14:15], st[:, 13:14], AF.Ln, [st, epsc], [st], bias=epsc[:, eps_col:eps_col + 1])
        K.act(st[:, 15:16], st[:, 14:15], AF.Exp, [st], [st], scale=-0.5)

    def modulate_T(src, i, uT, col0, sc_ap, sh_ap, tbank):
        st = stt_[i % 2]
        xb = xnb[i % 2]
        ln_stats(src.ap, [src], st, 0)
        K.ts("dve", xb[:], src[:], st[:, 12:13], st[:, 15:16], ALU.subtract, ALU.mult, [src, st], [xb])
        tb = psb(tbank)
        for dk in range(8):
            K.tr(tb[:, dk * 128:(dk + 1) * 128], xb[:, dk * 128:(dk + 1) * 128], identb, [xb, cstb], [PS[tbank]])
        for dk in range(8):
            K.act(uT[:, dk, col0:col0 + 128], tb[:, dk * 128:(dk + 1) * 128], AF.Identity, [PS[tbank], opsc, modT], [uT],
                  bias=sh_ap(dk), scale=sc_ap(dk))

    uTs = [K.sb([128, 8, 512], BF16, f"uT{i}") for i in range(2)]
    zsb = [K.sb([128, 512], F32, f"zsb{i}") for i in range(4)]
    groups = [("ctx", 0, 2)] + [("x", g * 4, min(4, NTX - g * 4)) for g in range((NTX + 3) // 4)]
    ti = 0
    zi = 0
    for gi_, (kind, t0, nt) in enumerate(groups):
        uT = uTs[gi_ % 2]
        src_d = ctx_d if kind == "ctx" else x_d
        mj = 1 if kind == "ctx" else 0
        for i in range(nt):
            xt = xts[ti % 2]
            K.dma(xt[:], src_d[(t0 + i) * 128:(t0 + i + 1) * 128, :], [src_d], [xt])
            modulate_T(xt, ti, uT, i * 128, lambda dk: opsc[:, mj, dk:dk + 1], lambda dk: modT[:, dk, mj:mj + 1], 7)
            ti += 1
        ncol = nt * 128
        ecol0 = (0 if kind == "ctx" else CTX) + t0 * 128
        import os
        ZD = int(os.environ.get("ZDBG", "0"))
        if ZD == 1:
            continue
        for ct in range(NCT):
            bank = PS[ct % 4]
            for dk in range(8):
                K.mm(bank[:, 0:ncol], winb[:, dk, ct * 128:(ct + 1) * 128], uT[:, dk, 0:ncol], [winb, uT], [bank],
                     start=(dk == 0), stop=(dk == 7))
            z = zsb[zi % 4]
            zi += 1
            K.cp("act" if ct % 2 else "dve", z[:, 0:ncol], bank[:, 0:ncol], [bank], [z])
            if ZD == 2:
                continue
            if ZD != 4:
                K.dma(zT_d[ct * 128:(ct + 1) * 128, ecol0:ecol0 + ncol], z[:, 0:ncol], [z], [zT_d])
            if debug and ZD != 3:
                K.dma(dbg["zT"][ct * 128:(ct + 1) * 128, ecol0:ecol0 + ncol], z[:, 0:ncol], [z], [dbg["zT"]])
    K.barrier()
    K.stack.close()

    def run_pass(dirn):
        final = dirn == 1
        K.stack = ExitStack()
        wlb = K.sb([128, 4, 512], BF16, "wlb")
        g2b = K.sb([96, 512], BF16, "g2b")
        outer = K.stack
        K.stack = ExitStack()
        tmpw = K.sb([128, 4, 512], F32, "tmpw")
        K.dma(tmpw[:], wl_d[:, :, :], [wl_d], [tmpw])
        K.cp("dve", wlb[:], tmpw[:], [tmpw], [wlb])
        tmpg = K.sb([96, 512], F32, "tmpg")
        K.dma(tmpg[:], g2_d[:, :], [g2_d], [tmpg])
        K.cp("dve", g2b[:], tmpg[:], [tmpg], [g2b])
        K.barrier()
        K.stack.close()
        K.stack = outer
        S = K.sb([128, 4, 128], F32, "S")
        Sbf = [K.sb([128, 4, 128], BF16, f"Sbf{i}") for i in range(4)]
        H = [K.sb([128, 64], F32, f"H{j}") for j in range(4)]
        Hbd = [[K.sb([128, 128], BF16, f"Hbd{j}_{c}") for c in range(2)] for j in range(4)]
        MTbd = [[K.sb([128, 128], F32, f"MT{j}_{c}") for c in range(2)] for j in range(4)]
        K.memset("pool", S[:], 0.0, [S])
        for j in range(4):
            K.memset("pool", H[j][:], 0.0, [H[j]])
            for c in range(2):
                K.memset("pool", Hbd[j][c][:], 0.0, [Hbd[j][c]])
                K.memset("pool", MTbd[j][c][:], 0.0, [MTbd[j][c]])
        for i in range(4):
            K.memset("pool", Sbf[i][:], 0.0, [Sbf[i]])
        ld_q = K.sb([128, 4, 128], F32, "ldq")
        ld_f = K.sb([128, 4, 128], F32, "ldf")
        ld_i = K.sb([128, 4, 128], F32, "ldi")
        ld_rw = [K.sb([128, 4, 256], F32, f"ldrw{g}") for g in range(4)]
        zp = K.sb([128, 14, 4, 66], F32, "zp")
        zc = K.sb([128, 14, 130], F32, "zc")
        K.memset("pool", zp[:], 0.0, [zp])
        K.memset("pool", zc[:], 0.0, [zc])
        zrl = K.sb([128, 14, 128], F32, "zrl")
        mshg = cstb[:, C_M32B if dirn else C_M32F, :]
        msi = cstb[:, C_MSB:C_MSB + 2, :] if dirn else cstb[:, C_MSF:C_MSF + 2, :]
        ms32 = cst[:, C_MSB, :] if dirn else cst[:, C_MSF, :]
        mnt32 = cst[:, C_MSF, :] if dirn else cst[:, C_MSB, :]
        id32r = K.sb([128, 2, 128], F32, "id32r")
        msir = K.sb([128, 2, 2, 128], BF16, "msir")
        mshgr = K.sb([128, 4, 128], BF16, "mshgr")
        for e_ in range(2):
            K.cp("pool", id32r[:, e_, :], ident, [cst], [id32r])
            K.cp("pool", msir[:, e_, :, :], msi, [cstb], [msir])
        for h_ in range(4):
            K.cp("pool", mshgr[:, h_, :], mshg, [cstb], [mshgr])
        rst32 = cst[:, C_RST32, :]
        rst64 = cst[:, C_RST64, :]
        blk64 = cst[:, C_BLK64, :]

        if dirn == 0:
            order = [("ctx", 0), ("ctx", 1)] + [("x", j) for j in range(NTX)]
        else:
            order = [("ctx", 1), ("ctx", 0)] + [("x", j) for j in range(NTX - 1, -1, -1)]

        def issue_loads(n):
            kind, j = order[n]
            ec0 = (0 if kind == "ctx" else CTX) + j * 128

            def hv(ct0):
                return zT_d.ap[ct0 * 128:(ct0 + 4) * 128, ec0:ec0 + 128].rearrange("(h p) t -> p h t", p=128)
            K.dma(ld_q[:], hv(0), [zT_d], [ld_q])
            K.dma(ld_f[:], hv(4 + 4 * dirn), [zT_d], [ld_f])
            K.dma(ld_i[:], hv(12), [zT_d], [ld_i])
            if kind == "x":
                lo = 64 if j > 0 else 0
                hi = 64 if j < NTX - 1 else 0
            else:
                lo = 1 if j > 0 else 0
                hi = 1 if j < 1 else 0
            for g in range(4):
                nt_ = 4 if g < 3 else 2
                src = zT_d.ap[(20 + 4 * g) * 128:(20 + 4 * g + nt_) * 128, ec0 - lo:ec0 + 128 + hi].rearrange("(h p) t -> p h t", p=128)
                K.dma(ld_rw[g][:, 0:nt_, 64 - lo:192 + hi], src, [zT_d], [ld_rw[g]])

        def T32(name, shape=(128, 4, 128)):
            return K.sb(list(shape), F32, name)

        def T16(name, shape=(128, 4, 128)):
            return K.sb(list(shape), BF16, name)
        P32 = [T32(f"w32_{i}") for i in range(11)]
        sgq = qh = P32[0]
        sgf = P32[1]
        ff = lg = P32[2]
        kdh = P32[3]
        bcum = P32[4]
        tmp1 = P32[5]
        e1 = P32[6]
        e2 = P32[7]
        sw = sq = P32[0]
        asg = P32[1]
        cs = kdr = P32[2]
        csm = rn = P32[3]
        E1 = P32[4]
        E2 = P32[5]
        tmpa = P32[6]
        bvec = P32[7]
        E3 = P32[8]
        kk = P32[9]
        kkn = P32[10]
        qb, kb, ib16, ATm = [T16(n) for n in ("qb", "kb", "ib16", "ATm")]
        kbTm = [T16(f"kbTm{c}") for c in range(4)]
        iT = T16("iT")
        stmp = T32("stmp")
        Lb = K.sb([128, 128], BF16, "Lb")
        vb16 = T16("vb16")
        if final:
            sgd = K.sb([96, 128], BF16, "sgd")
            asg2, tmpa2, bonp = T32("asg2"), T32("tmpa2"), T32("bonp")
        osb = [K.sb([128, 8, 128], F32, f"osb{i}") for i in range(2)]
        exb = [K.sb([128, 8, 128], F32, f"exb{i}") for i in range(2)] if final else None
        AR = [K.sb([128, 4, 2, 128], BF16, f"AR{i}") for i in range(2)]
        bt = [T16(f"bt{i}") for i in range(2)]
        kt = [T16(f"kt{i}") for i in range(2)]
        aT = [T16(f"aT{i}") for i in range(2)]
        vT = [T16(f"vT{i}") for i in range(2)]
        btTm = [[T16(f"btTm{i}_{c}") for c in range(2)] for i in range(2)]
        ktTm = [[T16(f"ktTm{i}_{c}") for c in range(2)] for i in range(2)]
        gam = [K.sb([128, 4, 2], F32, f"gam{i}") for i in range(2)]
        Amat = [K.sb([128, 2, 2, 2, 128], BF16, f"Amat{i}") for i in range(2)]
        XP = [[K.sb([128, 2, 2, 128], F32, f"XP{i}_{p}") for p in range(2)] for i in range(2)]
        XT = [[K.sb([128, 2, 128], F32, f"XT{i}_{p}") for p in range(2)] for i in range(2)]
        Pbf = [K.sb([128, 2, 128], BF16, f"Pbf{i}") for i in range(2)]
        AW = [K.sb([128, 2, 128], BF16, f"AW{i}") for i in range(2)]
        AU = [K.sb([128, 2, 128], BF16, f"AU{i}") for i in range(2)]
        QT = [K.sb([128, 128], BF16, f"QT{i}") for i in range(2)]
        yb = PS[7]

        def front(n):
            kind, j = order[n]
            isx = kind == "x"
            fb = n % 2
            issue_loads(n)
            lrw = ld_rw
            if isx:
                if j == 0:
                    for g in range(4):
                        K.memset("pool", lrw[g][:, :, 0:64], 0.0, [lrw[g]])
                if j == NTX - 1:
                    for g in range(4):
                        K.memset("pool", lrw[g][:, :, 192:256], 0.0, [lrw[g]])
                for g in range(4):
                    nt_ = 4 if g < 3 else 2
                    for q_ in range(nt_):
                        K.cp("pool", zp[:, 4 * g + q_, :, 1:65], lrw[g][:, q_, :].rearrange("p (r c) -> p r c", c=64), [lrw[g]], [zp])
            else:
                if j == 0:
                    for g in range(4):
                        K.memset("pool", lrw[g][:, :, 63:64], 0.0, [lrw[g]])
                if j == 1:
                    for g in range(4):
                        K.memset("pool", lrw[g][:, :, 192:193], 0.0, [lrw[g]])
                for g in range(4):
                    nt_ = 4 if g < 3 else 2
                    K.cp("pool", zc[:, 4 * g:4 * g + nt_, :], lrw[g][:, 0:nt_, 63:193], [lrw[g]], [zc])
            zq, zf, zi_ = ld_q, ld_f, ld_i
            K.act(sgq[:], zq[:], AF.Sigmoid, [zq], [sgq])
            K.act(sgf[:], zf[:], AF.Sigmoid, [zf], [sgf])
            K.tt("pool", qh[:], zq[:], sgq[:], ALU.mult, [zq, sgq], [qh])
            for h in range(4):
                c_ = dirn * 4 + h
                K.ts("dve", ff[:, h, :], sgf[:, h, :], drv[:, DV_OML + c_:DV_OML + c_ + 1], drv[:, DV_LB + c_:DV_LB + c_ + 1],
                     ALU.mult, ALU.add, [sgf, drv], [ff])
                K.ts("pool", kdh[:, h, :], sgf[:, h, :], drv[:, DV_NOML + c_:DV_NOML + c_ + 1], drv[:, DV_OML + c_:DV_OML + c_ + 1],
                     ALU.mult, ALU.add, [sgf, drv], [kdh])
            K.act(lg[:], ff[:], AF.Ln, [ff], [lg])
            for h in range(4):
                K.op("dve", lambda h=h: nc.vector.tensor_tensor_scan(bcum[:, h, :], rst32, lg[:, h, :], 0.0, ALU.mult, ALU.add),
                     [lg, cst], [bcum])
            if dirn:
                bv4 = bcum[:].rearrange("p h (c t) -> p (h c) t", t=32)
                K.tt("pool", tmp1[:], lg[:], bcum[:], ALU.subtract, [lg, bcum], [tmp1])
                K.tt("dve", e2[:].rearrange("p h (c t) -> p (h c) t", t=32), tmp1[:].rearrange("p h (c t) -> p (h c) t", t=32),
                     _bc(bv4[:, :, 31:32], [128, 16, 32]), ALU.add, [tmp1, bcum], [e2])
                K.cp("pool", bcum[:], e2[:], [e2], [bcum])
            K.act(e1[:], bcum[:], AF.Exp, [bcum], [e1])
            K.act(e2[:], bcum[:], AF.Exp, [bcum], [e2], scale=-1.0)
            K.tt("dve", qb[:], qh[:], e1[:], ALU.mult, [qh, e1], [qb])
            K.tt("pool", kb[:], kdh[:], e2[:], ALU.mult, [kdh, e2], [kb])
            K.cp("pool", ib16[:], zi_[:], [zi_], [ib16])
            tb = psb(0)
            for h in range(4):
                K.tr(tb[:, h * 128:(h + 1) * 128], kb[:, h, :], identb, [kb, cstb], [PS[0]])
            for h in range(4):
                K.tr(tb[:, 512 + h * 128:512 + (h + 1) * 128], ib16[:, h, :], identb, [ib16, cstb], [PS[0]])
            for c in range(4):
                K.act(kbTm[c][:].rearrange("p h t -> p (h t)"), tb[:, 0:512], AF.Identity, [PS[0], cst], [kbTm[c]],
                      scale=cst[:, C_ROWM, c:c + 1])
            K.cp("act", iT[:].rearrange("p h t -> p (h t)"), tb[:, 512:1024], [PS[0]], [iT])
            if isx:
                for h in range(4):
                    K.mm(PS[1][:, h * 128:(h + 1) * 128], kb[:, h, :], qb[:, h, :], [kb, qb], [PS[1]])
                K.tt("dve", ATm[:], PS[1][:, :].rearrange("p (h t) -> p h t", h=4), mshgr[:], ALU.mult, [PS[1], mshgr], [ATm])
            corder = [0, 1, 2, 3] if dirn == 0 else [3, 2, 1, 0]
            for ci, c in enumerate(corder):
                K.cp("act", Sbf[c][:], S[:], [S], [Sbf[c]])
                kvb = PS[2]
                for h in range(4):
                    K.mm(kvb[:, h * 128:(h + 1) * 128], kbTm[c][:, h, :], iT[:, h, :], [kbTm[c], iT], [kvb])
                dcol = c * 32 + (0 if dirn else 31)
                K.tt("dve", stmp[:], kvb[:, :].rearrange("p (h v) -> p h v", h=4), S[:], ALU.add, [kvb, S], [stmp])
                K.tt("pool", S[:], stmp[:], _bc(e1[:, :, dcol:dcol + 1], [128, 4, 128]), ALU.mult, [stmp, e1], [S])
            if isx:
                ob = PS[1]
                for h in range(4):
                    with K.atomic():
                        K.mm(ob[:, h * 128:(h + 1) * 128], iT[:, h, :], ATm[:, h, :], [iT, ATm], [ob], start=True, stop=False)
                        for c in range(4):
                            K.mm(ob[:, h * 128 + c * 32:h * 128 + (c + 1) * 32], Sbf[c][:, h, :], qb[:, h, c * 32:(c + 1) * 32],
                                 [Sbf[c], qb], [ob], start=False, stop=(c == 3))
                K.cp("act", osb[fb][:, 0:4, :], ob[:, :].rearrange("p (h t) -> p h t", h=4), [ob], [osb[fb]])
            if isx:
                for ct in range(14):
                    views = {"L": zp[:, ct, 1:3, 0:64], "R": zp[:, ct, 1:3, 2:66], "U": zp[:, ct, 0:2, 1:65], "D": zp[:, ct, 2:4, 1:65]}
                    cen = zp[:, ct, 1:3, 1:65]
                    lo_, hi_ = ct * 128, ct * 128 + 128
                    kinds = []
                    if lo_ < 440: kinds.append(("L", 0))
                    if hi_ > 440 and lo_ < 880: kinds.append(("R", 1))
                    if hi_ > 880 and lo_ < 1320: kinds.append(("U", 2))
                    if hi_ > 1320: kinds.append(("D", 3))
                    o3 = zrl[:, ct, :].rearrange("p (r c) -> p r c", c=64)
                    K.act(o3, cen, AF.Identity, [zp, drv], [zrl], scale=drv[:, DV_C0 + ct:DV_C0 + ct + 1])
                    for (vn, ki) in kinds:
                        K.stt(o3, views[vn], drv[:, DV_CS + 14 * ki + ct:DV_CS + 14 * ki + ct + 1], o3, ALU.mult, ALU.add, [zp, drv, zrl], [zrl])
            else:
                for ct in range(14):
                    lo_, hi_ = ct * 128, ct * 128 + 128
                    kinds = []
                    if lo_ < 880: kinds.append((zc[:, ct, 0:128], 4))
                    if hi_ > 880: kinds.append((zc[:, ct, 2:130], 5))
                    K.act(zrl[:, ct, :], zc[:, ct, 1:129], AF.Identity, [zc, drv], [zrl], scale=drv[:, DV_C0 + ct:DV_C0 + ct + 1])
                    for (vw, ki) in kinds:
                        K.stt(zrl[:, ct, :], vw, drv[:, DV_CS + 14 * ki + ct:DV_CS + 14 * ki + ct + 1], zrl[:, ct, :], ALU.mult, ALU.add,
                              [zc, drv, zrl], [zrl])
            r_ = zrl[:, 0:4, :]
            k_ = zrl[:, 4:8, :]
            v_ = zrl[:, 8:12, :]
            K.act(Lb[0:64, :], zrl[0:64, 12, :], AF.Tanh, [zrl], [Lb])
            K.cp("pool", Lb[64:128, :], zrl[64:128, 12, :], [zrl], [Lb])
            pw, pa = PS[0], PS[1]
            for jj in range(4):
                K.mm(pw[:, jj * 128:(jj + 1) * 128], wlb[:, dirn, jj * 128:(jj + 1) * 128], Lb[:], [wlb, Lb], [pw])
            for jj in range(4):
                K.mm(pa[:, jj * 128:(jj + 1) * 128], wlb[:, 2 + dirn, jj * 128:(jj + 1) * 128], Lb[:], [wlb, Lb], [pa])
            for jj in range(4):
                K.act(sw[:, jj, :], pw[:, jj * 128:(jj + 1) * 128], AF.Sigmoid, [pw, ptab], [sw],
                      bias=ptab[:, PT_W0 + dirn * 4 + jj:PT_W0 + dirn * 4 + jj + 1])
                K.act(asg[:, jj, :], pa[:, jj * 128:(jj + 1) * 128], AF.Sigmoid, [pa, ptab], [asg],
                      bias=ptab[:, PT_A0 + dirn * 4 + jj:PT_A0 + dirn * 4 + jj + 1])
            if final and isx:
                for jj in range(4):
                    K.mm(pa[:, jj * 128:(jj + 1) * 128], wlb[:, 2, jj * 128:(jj + 1) * 128], Lb[:], [wlb, Lb], [pa])
                for jj in range(4):
                    K.act(asg2[:, jj, :], pa[:, jj * 128:(jj + 1) * 128], AF.Sigmoid, [pa, ptab], [asg2],
                          bias=ptab[:, PT_A0 + jj:PT_A0 + jj + 1])
            for jj in range(4):
                K.op("dve", lambda jj=jj: nc.vector.tensor_tensor_scan(cs[:, jj, :], rst64, sw[:, jj, :], 0.0, ALU.mult, ALU.add),
                     [sw, cst], [cs])
            if dirn == 0:
                K.tt("pool", csm[:], cs[:], sw[:], ALU.subtract, [cs, sw], [csm])
            else:
                c8 = cs[:].rearrange("p j (c t) -> p (j c) t", t=64)
                K.tt("dve", csm[:].rearrange("p j (c t) -> p (j c) t", t=64), _bc(c8[:, :, 63:64], [128, 8, 64]), c8, ALU.subtract,
                     [cs], [csm])
                K.tt("pool", cs[:], csm[:], sw[:], ALU.add, [csm, sw], [cs])
            K.act(E1[:], csm[:], AF.Exp, [csm], [E1], scale=-LWS)
            K.act(E2[:], cs[:], AF.Exp, [cs], [E2], scale=-LWS)
            K.act(E3[:], cs[:], AF.Exp, [cs], [E3], scale=LWS)
            goff = 0 if dirn else 63
            K.cp("pool", gam[fb][:], E2[:].rearrange("p j (c t) -> p j c t", t=64)[:, :, :, goff], [E2], [gam[fb]])
            for jj in range(4):
                K.ts("pool", kk[:, jj, :], k_[:, jj, :], ptab[:, PT_KK + jj:PT_KK + jj + 1], None, ALU.mult, None, [zrl, ptab], [kk])
            K.tt("pool", sq[:], kk[:], kk[:], ALU.mult, [kk], [sq])
            K.mm(PS[2][:, :], blk64, sq[:].rearrange("p j t -> p (j t)"), [cst, sq], [PS[2]])
            K.ts("dve", rn[:].rearrange("p j t -> p (j t)"), PS[2][:, :], epsc[:, 3:4], None, ALU.max, None, [PS[2], epsc], [rn])
            K.act(rn[:], rn[:], AF.Ln, [rn], [rn])
            K.act(rn[:], rn[:], AF.Exp, [rn], [rn], scale=-0.5)
            K.tt("dve", kkn[:], kk[:], rn[:], ALU.mult, [kk, rn], [kkn])
            for jj in range(4):
                K.ts("pool", tmpa[:, jj, :], asg[:, jj, :], ptab[:, PT_KA + jj:PT_KA + jj + 1], drv[:, DV_OMKA + jj:DV_OMKA + jj + 1],
                     ALU.mult, ALU.add, [asg, ptab, drv], [tmpa])
            K.tt("pool", kdr[:], k_, tmpa[:], ALU.mult, [zrl, tmpa], [kdr])
            K.tt("pool", bvec[:], kkn[:], asg[:], ALU.mult, [kkn, asg], [bvec])
            K.stt(AR[fb][:, :, 0, :], kkn[:], -1.0, E1[:], ALU.mult, ALU.mult, [kkn, E1], [AR[fb]])
            K.tt("dve", AR[fb][:, :, 1, :], r_, E2[:], ALU.mult, [zrl, E2], [AR[fb]])
            K.tt("dve", bt[fb][:], bvec[:], E3[:], ALU.mult, [bvec, E3], [bt[fb]])
            K.tt("pool", kt[fb][:], kdr[:], E3[:], ALU.mult, [kdr, E3], [kt[fb]])
            K.cp("pool", vb16[:], v_, [zrl], [vb16])
            tb0, tb1 = psb(0), psb(1)
            for jj in range(4):
                K.tr(tb0[:, jj * 128:(jj + 1) * 128], AR[fb][:, jj, 0, :], identb, [AR[fb], cstb], [PS[0]])
                K.tr(tb0[:, 512 + jj * 128:512 + (jj + 1) * 128], vb16[:, jj, :], identb, [vb16, cstb], [PS[0]])
                K.tr(tb1[:, jj * 128:(jj + 1) * 128], bt[fb][:, jj, :], identb, [bt[fb], cstb], [PS[1]])
                K.tr(tb1[:, 512 + jj * 128:512 + (jj + 1) * 128], kt[fb][:, jj, :], identb, [kt[fb], cstb], [PS[1]])
            K.cp("act", aT[fb][:].rearrange("p j t -> p (j t)"), tb0[:, 0:512], [PS[0]], [aT[fb]])
            K.cp("act", vT[fb][:].rearrange("p j t -> p (j t)"), tb0[:, 512:1024], [PS[0]], [vT[fb]])
            for c in range(2):
                K.act(btTm[fb][c][:].rearrange("p j t -> p (j t)"), tb1[:, 0:512], AF.Identity, [PS[1], cst], [btTm[fb][c]],
                      scale=cst[:, C_ROWM, 4 + c:5 + c])
                K.act(ktTm[fb][c][:].rearrange("p j t -> p (j t)"), tb1[:, 512:1024], AF.Identity, [PS[1], cst], [ktTm[fb][c]],
                      scale=cst[:, C_ROWM, 4 + c:5 + c])
            if final and isx:
                K.act(sgd[:], zrl[0:96, 13, :], AF.Sigmoid, [zrl], [sgd])
                for jj in range(4):
                    K.ts("pool", tmpa2[:, jj, :], asg2[:, jj, :], ptab[:, PT_KA + jj:PT_KA + jj + 1], drv[:, DV_OMKA + jj:DV_OMKA + jj + 1],
                         ALU.mult, ALU.add, [asg2, ptab, drv], [tmpa2])
                K.tt("pool", tmpa2[:], tmpa2[:], tmpa[:], ALU.add, [tmpa2, tmpa], [tmpa2])
                K.tt("pool", tmpa2[:], tmpa2[:], k_, ALU.mult, [tmpa2, zrl], [tmpa2])
                for jj in range(4):
                    K.stt(bonp[:, jj, :], r_[:, jj, :], ptab[:, PT_RK + jj:PT_RK + jj + 1], tmpa2[:, jj, :], ALU.mult, ALU.mult,
                          [zrl, ptab, tmpa2], [bonp])
                K.mm(PS[2][:, :], blk64, bonp[:].rearrange("p j t -> p (j t)"), [cst, bonp], [PS[2]])
                K.tt("dve", exb[fb][:, 0:4, :], PS[2][:, :].rearrange("p (j t) -> p j t", j=4), v_, ALU.mult, [PS[2], zrl], [exb[fb]])
                pg = PS[0]
                for jj in range(4):
                    K.mm(pg[:, jj * 128:(jj + 1) * 128], g2b[:, jj * 128:(jj + 1) * 128], sgd[:], [g2b, sgd], [pg])
                K.cp("act", exb[fb][:, 4:8, :], pg[:, :].rearrange("p (j t) -> p j t", j=4), [pg], [exb[fb]])
                K.dma(ex_d.ap[j], exb[fb][:], [exb[fb]], [ex_d])

        def pairs(n, s):
            kind, j = order[n]
            isx = kind == "x"
            fb = n % 2
            ba, bb = PS[3 + 2 * s], PS[4 + 2 * s]
            bke = (ba, bb)
            am, aw, au, qt, pbf = Amat[s], AW[s], AU[s], QT[s], Pbf[s]
            xp0, xp1 = XP[s]
            xt0, xt1 = XT[s]
            ar, bt_, kt_, aT_, vT_ = AR[fb], bt[fb], kt[fb], aT[fb], vT[fb]
            for jj in (s, s + 2):
                for e in range(2):
                    ep = slice(e * 64, (e + 1) * 64)
                    bk = bke[e]
                    arf = ar[ep, jj, :, :].rearrange("p a t -> p (a t)")
                    K.mm(bk[:, 0:256], bt_[ep, jj, :], arf, [bt_, ar], [bk])
                    K.mm(bk[:, 256:512], kt_[ep, jj, :], arf, [kt_, ar], [bk])
                for e in range(2):
                    bk = bke[e]
                    bv_ = bk[:, :].rearrange("p (w a t) -> p w a t", w=2, a=2)
                    K.tt("dve", xp0[:, e, 0, :], bk[:, 0:128], ms32, ALU.mult, [bk, cst], [xp0])
                    K.tt("dve", am[:, e, :, :, :], bv_, msir[:], ALU.mult, [bk, msir], [am])
                for e in range(2):
                    ep = slice(e * 64, (e + 1) * 64)
                    bk = bke[e]
                    K.mm(bk[:, 0:128], ar[ep, jj, 0, :], bt_[ep, jj, :], [ar, bt_], [bk])
                    K.tt("dve", xt0[:, e, :], bk[:, 0:128], mnt32, ALU.mult, [bk, cst], [xt0])
                K.cp("pool", xp0[:, :, 1, :], id32r[:], [id32r], [xp0])
                cur, nxt = (xp0, xt0), (xp1, xt1)
                for lvl in range(6):
                    cxp, cxt = cur
                    nxp, nxt_ = nxt
                    last = lvl == 5
                    for e in range(2):
                        if not last:
                            K.mm(ba[:, e * 256:(e + 1) * 256], cxt[:, e, :], cxp[:, e, :, :].rearrange("p a t -> p (a t)"), [cxt, cxp], [ba])
                        else:
                            K.mm(ba[:, e * 256 + 128:(e + 1) * 256], cxt[:, e, :], cxp[:, e, 1, :], [cxt, cxp], [ba])
                    if not last:
                        for e in range(2):
                            K.mm(bb[:, e * 128:(e + 1) * 128], cxp[:, e, 0, :], cxt[:, e, :], [cxp, cxt], [bb])
                    pav = ba[:, :].rearrange("p (e a t) -> p e a t", e=2, a=2)
                    K.tt("dve", nxp[:, :, 1, :], pav[:, :, 1, :], cxp[:, :, 1, :], ALU.add, [ba, cxp], [nxp])
                    if not last:
                        K.cp("act", nxp[:, :, 0, :], pav[:, :, 0, :], [ba], [nxp])
                        K.cp("act", nxt_[:], bb[:, 0:256].rearrange("p (e t) -> p e t", e=2), [bb], [nxt_])
                    cur, nxt = nxt, cur
                K.cp("pool", pbf[:], cur[0][:, :, 1, :], [cur[0]], [pbf])
                for e in range(2):
                    K.mm(ba[:, e * 64:(e + 1) * 64], am[:, e, 1, 0, :], vT_[:, jj, e * 64:(e + 1) * 64], [am, vT_], [ba])
                K.cp("pool", aw[:, :, 0:64], aT_[:, jj, :].rearrange("p (e k) -> p e k", e=2), [aT_], [aw])
                K.cp("act", aw[:, :, 64:128], ba[:, 0:128].rearrange("p (e v) -> p e v", e=2), [ba], [aw])
                for e in range(2):
                    K.mm(bb[:, e * 128:(e + 1) * 128], pbf[:, e, :], aw[:, e, :], [pbf, aw], [bb])
                K.cp("dve", au[:], bb[:, 0:256].rearrange("p (e c) -> p e c", e=2), [bb], [au])
                if isx:
                    for e in range(2):
                        K.mm(ba[e * 64:(e + 1) * 64, 128:256], au[:, e, 0:64], am[:, e, 0, 1, :], [au, am], [ba])
                    K.tt("dve", qt[:], ba[:, 128:256], ar[:, jj, 1, :], ALU.add, [ba, ar], [qt])
                corder2 = [0, 1] if dirn == 0 else [1, 0]
                Hj = H[jj]
                for ci, c in enumerate(corder2):
                    hb = Hbd[jj][c]
                    mt = MTbd[jj][c]
                    for e in range(2):
                        K.cp("act", hb[e * 64:(e + 1) * 64, e * 64:(e + 1) * 64], Hj[e * 64:(e + 1) * 64, :], [Hj], [hb])
                    for e in range(2):
                        K.mm(bb[e * 64:(e + 1) * 64, 256:320], au[:, e, 0:64], btTm[fb][c][:, jj, e * 64:(e + 1) * 64], [au, btTm[fb][c]], [bb])
                    for e in range(2):
                        K.tt("dve", mt[e * 64:(e + 1) * 64, e * 64:(e + 1) * 64], bb[e * 64:(e + 1) * 64, 256:320],
                             ident[e * 64:(e + 1) * 64, e * 64:(e + 1) * 64], ALU.add, [bb, cst], [mt])
                    with K.atomic():
                        K.mm(ba[:, 384:448], mt[:], Hj[:], [mt, Hj], [ba], start=True, stop=False)
                        for e in range(2):
                            ep = slice(e * 64, (e + 1) * 64)
                            K.mm(ba[ep, 384:448], btTm[fb][c][:, jj, ep], au[:, e, 64:128], [btTm[fb][c], au], [ba], start=False, stop=False)
                            K.mm(ba[ep, 384:448], ktTm[fb][c][:, jj, ep], vT_[:, jj, ep], [ktTm[fb][c], vT_], [ba], start=False, stop=True)
                    K.ts("dve", Hj[:], ba[:, 384:448], gam[fb][:, jj, c:c + 1], None, ALU.mult, None, [ba, gam[fb]], [Hj])
                if isx:
                    with K.atomic():
                        for e in range(2):
                            ep = slice(e * 64, (e + 1) * 64)
                            K.mm(yb[ep, jj * 128:(jj + 1) * 128], au[:, e, 64:128], am[:, e, 0, 1, :], [au, am], [yb], start=True, stop=False)
                            K.mm(yb[ep, jj * 128:(jj + 1) * 128], vT_[:, jj, ep], am[:, e, 1, 1, :], [vT_, am], [yb], start=False, stop=False)
                        for c in range(2):
                            K.mm(yb[:, jj * 128 + c * 64:jj * 128 + (c + 1) * 64], Hbd[jj][c][:], qt[:, c * 64:(c + 1) * 64], [Hbd[jj][c], qt], [yb],
                                 start=False, stop=(c == 1))

        def tail(n):
            kind, j = order[n]
            if kind != "x":
                return
            fb = n % 2
            K.cp("act", osb[fb][:, 4:8, :], yb[:, :].rearrange("p (j t) -> p j t", j=4), [yb], [osb[fb]])
            K.dma((ob_d if final else of_d).ap[j], osb[fb][:], [osb[fb]], [ob_d if final else of_d])

        import os
        NOIL = os.environ.get("NOIL", "0") == "1"
        NT_ = len(order)
        if NOIL:
            for n in range(NT_):
                front(n); pairs(n, 0); pairs(n, 1); tail(n)
        else:
            front(0)
            for n in range(NT_):
                fns = [lambda n=n: pairs(n, 0), lambda n=n: pairs(n, 1)]
                if n + 1 < NT_:
                    fns.append(lambda n=n: front(n + 1))
                K.run_streams(fns)
                tail(n)
        K.barrier()
        K.stack.close()

    def run_pc():
        K.stack = ExitStack()
        woutb = K.sb([128, 8, D], BF16, "woutb")
        lnb_ = K.sb([128, 2, D], F32, "ln1bc")
        K.dma(lnb_[:, 0, :], lnrow_d.ap[0:1, :].partition_broadcast(128), [lnrow_d], [lnb_])
        K.dma(lnb_[:, 1, :], lnrow_d.ap[1:2, :].partition_broadcast(128), [lnrow_d], [lnb_])
        wo_v = wout_d.ap.rearrange("(k p) c -> p k c", p=128)
        wos = [K.sb([128, D], F32, f"wos{i}") for i in range(2)]
        for dk in range(8):
            K.dma(wos[dk % 2][:], wo_v[:, dk, :], [wout_d], [wos[dk % 2]])
            K.cp("act" if dk % 2 else "dve", woutb[:, dk, :], wos[dk % 2][:], [wos[dk % 2]], [woutb])
        blk64 = cst[:, C_BLK64, :]
        onesf = cst[:, C_ONES, :]
        NS = 2
        bufs = []
        for s in range(NS):
            d_ = {}
            d_["of"] = K.sb([128, 8, 128], F32, f"pc_of{s}")
            d_["ob"] = K.sb([128, 8, 128], F32, f"pc_ob{s}")
            d_["ex"] = K.sb([128, 8, 128], F32, f"pc_ex{s}")
            d_["og"] = K.sb([128, 4, 128], F32, f"pc_og{s}")
            d_["x"] = K.sb([128, D], F32, f"pc_x{s}")
            for nm in ("ohg", "sqh", "rsth", "sog", "ysb", "ycen", "sq2", "rstd2", "yn2"):
                d_[nm] = K.sb([128, 4, 128], F32, f"pc_{nm}{s}")
            d_["yT"] = K.sb([128, 8, 128], BF16, f"pc_yT{s}")
            d_["h1"] = K.sb([128, D], F32, f"pc_h1{s}")
            d_["x1t"] = K.sb([128, D], F32, f"pc_x1t{s}")
            d_["st1"] = K.sb([128, 16], F32, f"pc_st{s}")
            d_["dbg"] = K.sb([128, 8, 128], F32, f"pc_dbg{s}") if debug else None
            bufs.append(d_)

        def pc_stream(s):
            B_ = bufs[s]
            pA, pB, pC, pD = PS[4 * s], PS[4 * s + 1], PS[4 * s + 2], PS[4 * s + 3]
            for j in range(s, NTX, NS):
                ec0 = CTX + j * 128
                K.dma(B_["of"][:], of_d.ap[j], [of_d], [B_["of"]])
                K.dma(B_["ob"][:], ob_d.ap[j], [ob_d], [B_["ob"]])
                K.dma(B_["ex"][:], ex_d.ap[j], [ex_d], [B_["ex"]])
                K.dma(B_["og"][:], zT_d.ap[16 * 128:20 * 128, ec0:ec0 + 128].rearrange("(h p) t -> p h t", p=128), [zT_d], [B_["og"]])
                K.dma(B_["x"][:], x_d[j * 128:(j + 1) * 128, :], [x_d], [B_["x"]])
                ohg, sqh, rsth, sog, ysb, ycen, sq2, rstd2, yn2 = [B_[k_] for k_ in ("ohg", "sqh", "rsth", "sog", "ysb", "ycen", "sq2", "rstd2", "yn2")]
                yT, h1, x1t, st1, xt = B_["yT"], B_["h1"], B_["x1t"], B_["st1"], B_["x"]
                K.tt("pool", ohg[:], B_["of"][:, 0:4, :], B_["ob"][:, 0:4, :], ALU.add, [B_["of"], B_["ob"]], [ohg])
                K.tt("pool", sqh[:], ohg[:], ohg[:], ALU.mult, [ohg], [sqh])
                K.mm(pA[:, :], onesf, sqh[:].rearrange("p h t -> p (h t)"), [cst, sqh], [pA])
                K.act(rsth[:].rearrange("p h t -> p (h t)"), pA[:, :], AF.Ln, [pA, epsc], [rsth], bias=epsc[:, 1:2], scale=1.0 / 128.0)
                K.act(rsth[:], rsth[:], AF.Exp, [rsth], [rsth], scale=-0.5)
                K.act(sog[:], B_["og"][:], AF.Sigmoid, [B_["og"]], [sog])
                K.tt("pool", sog[:], sog[:], B_["og"][:], ALU.mult, [sog, B_["og"]], [sog])
                K.tt("dve", ohg[:], ohg[:], rsth[:], ALU.mult, [ohg, rsth], [ohg])
                K.stt(yT[:, 0:4, :], ohg[:], ptab[:, PT_NW:PT_NW + 1], sog[:], ALU.mult, ALU.mult, [ohg, ptab, sog], [yT])
                K.tt("pool", ysb[:], B_["of"][:, 4:8, :], B_["ob"][:, 4:8, :], ALU.add, [B_["of"], B_["ob"]], [ysb])
                K.mm(pB[:, :], blk64, ysb[:].rearrange("p j t -> p (j t)"), [cst, ysb], [pB])
                K.stt(ycen[:], pB[:, :].rearrange("p (j t) -> p j t", j=4), -1.0 / 64.0, ysb[:], ALU.mult, ALU.add, [pB, ysb], [ycen])
                K.tt("pool", sq2[:], ycen[:], ycen[:], ALU.mult, [ycen], [sq2])
                K.mm(pB[:, :], blk64, sq2[:].rearrange("p j t -> p (j t)"), [cst, sq2], [pB])
                K.act(rstd2[:].rearrange("p j t -> p (j t)"), pB[:, :], AF.Ln, [pB, epsc], [rstd2], bias=epsc[:, 2:3], scale=1.0 / 64.0)
                K.act(rstd2[:], rstd2[:], AF.Exp, [rstd2], [rstd2], scale=-0.5)
                K.tt("dve", ycen[:], ycen[:], rstd2[:], ALU.mult, [ycen, rstd2], [ycen])
                for jj in range(4):
                    K.ts("pool", yn2[:, jj, :], ycen[:, jj, :], ptab[:, PT_LNW + jj:PT_LNW + jj + 1], ptab[:, PT_LNB + jj:PT_LNB + jj + 1],
                         ALU.mult, ALU.add, [ycen, ptab], [yn2])
                K.tt("pool", yn2[:], yn2[:], B_["ex"][:, 0:4, :], ALU.add, [yn2, B_["ex"]], [yn2])
                K.tt("dve", yT[:, 4:8, :], yn2[:], B_["ex"][:, 4:8, :], ALU.mult, [yn2, B_["ex"]], [yT])
                if debug:
                    K.cp("pool", B_["dbg"][:], yT[:], [yT], [B_["dbg"]])
                    K.dma(dbg["yT"].ap[j], B_["dbg"][:], [B_["dbg"]], [dbg["yT"]])
                for dh in range(2):
                    bank = pC if dh == 0 else pD
                    with K.atomic():
                        for m in range(8):
                            K.mm(bank[:, :], yT[:, m, :], woutb[:, m, dh * 512:(dh + 1) * 512], [yT, woutb], [bank], start=(m == 0), stop=(m == 7))
                    K.tt("dve", h1[:, dh * 512:(dh + 1) * 512], bank[:, :], gb[:, 0, dh * 512:(dh + 1) * 512], ALU.mult, [bank, gb], [h1])
                K.stt(h1[:], xt[:], ALPHA, h1[:], ALU.mult, ALU.add, [xt, h1], [h1])
                ln_stats2(K, nc, h1, h1.ap, st1, epsc, 1)
                K.ts("dve", x1t[:], h1[:], st1[:, 12:13], st1[:, 15:16], ALU.subtract, ALU.mult, [h1, st1], [x1t])
                K.tt("pool", x1t[:], x1t[:], lnb_[:, 0, :], ALU.mult, [x1t, lnb_], [x1t])
                K.tt("pool", x1t[:], x1t[:], lnb_[:, 1, :], ALU.add, [x1t, lnb_], [x1t])
                K.dma(x1_d[j * 128:(j + 1) * 128, :], x1t[:], [x1t], [x1_d])
                if debug:
                    K.dma(dbg["x1"][j * 128:(j + 1) * 128, :], x1t[:], [x1t], [dbg["x1"]])
        K.run_streams([lambda s=s: pc_stream(s) for s in range(NS)])
        K.barrier()
        K.stack.close()

    if upto < 1:
        return nc, K
    run_pass(0)
    if upto < 2:
        return nc, K
    run_pass(1)
    if upto < 2.5:
        return nc, K
    run_pc()
    if upto < 3:
        return nc, K

    GT = 2
    GC = GT * 128
    K.stack = ExitStack()
    wgb = K.sb([128, 8, DFF], BF16, "wgb")
    wub = K.sb([128, 8, DFF], BF16, "wub")
    wdb = K.sb([128, NFT, D], BF16, "wdb")
    outer4 = K.stack
    K.stack = ExitStack()
    stg = [K.sb([128, DFF], F32, f"stg{i}") for i in range(2)]
    si = 0
    for (wd_, wb_, nk, ncol) in ((wg_d, wgb, 8, DFF), (wu_d, wub, 8, DFF), (wd_d, wdb, NFT, D)):
        v = wd_.ap.rearrange("(k p) c -> p k c", p=128)
        for kk_ in range(nk):
            s = stg[si % 2]
            K.dma(s[:, 0:ncol], v[:, kk_, :], [wd_], [s])
            K.cp(("act", "dve", "pool")[si % 3], wb_[:, kk_, :], s[:, 0:ncol], [s], [wb_])
            si += 1
    K.barrier()
    K.stack.close()
    K.stack = outer4
    ln2bc = K.sb([128, 2, D], F32, "ln2bc")
    K.dma(ln2bc[:, 0, :], lnrow_d.ap[2:3, :].partition_broadcast(128), [lnrow_d], [ln2bc])
    K.dma(ln2bc[:, 1, :], lnrow_d.ap[3:4, :].partition_broadcast(128), [lnrow_d], [ln2bc])
    x1g = [K.sb([128, GT, D], F32, f"x1g{i}") for i in range(2)]
    u2T = [K.sb([128, 8, GC], BF16, f"u2T{i}") for i in range(2)]
    hT = K.sb([128, NFT, GC], BF16, "hT")
    xnb2 = [K.sb([128, D], BF16, "xnb2_0")] * 2
    st2 = [K.sb([128, 16], F32, f"st2_{i}") for i in range(2)]
    sgl = [K.sb([128, GC], F32, f"sgl{i}") for i in range(2)]
    h2 = [K.sb([128, D], F32, f"h2_{i}") for i in range(2)]
    st3 = [K.sb([128, 16], F32, f"st3_{i}") for i in range(2)]
    ngrp = (NTX + GT - 1) // GT
    tcount = 0
    for g in range(ngrp):
        nt = min(GT, NTX - g * GT)
        ncol = nt * 128
        xg = x1g[g % 2]
        ut = u2T[g % 2]
        for i in range(nt):
            t = g * GT + i
            K.dma(xg[:, i, :], x1_d[t * 128:(t + 1) * 128, :], [x1_d], [xg])
        for i in range(nt):
            st = st2[tcount % 2]
            xb = xnb2[tcount % 2]
            tcount += 1
            ln_stats2(K, nc, xg, xg[:, i, :], st, epsc, 0)
            K.ts("dve", xb[:], xg[:, i, :], st[:, 12:13], st[:, 15:16], ALU.subtract, ALU.mult, [xg, st], [xb])
            tb = psb(7)
            for dk in range(8):
                K.tr(tb[:, dk * 128:(dk + 1) * 128], xb[:, dk * 128:(dk + 1) * 128], identb, [xb, cstb], [PS[7]])
            for dk in range(8):
                K.act(ut[:, dk, i * 128:(i + 1) * 128], tb[:, dk * 128:(dk + 1) * 128], AF.Identity, [PS[7], opsc, modT], [ut],
                      bias=modT[:, 24 + dk, 0:1], scale=opsc[:, 2, dk:dk + 1])
        for ft in range(NFT):
            bg, bu = PS[(2 * ft) % 4], PS[(2 * ft + 1) % 4]
            for dk in range(8):
                K.mm(bg[:, 0:ncol], wgb[:, dk, ft * 128:(ft + 1) * 128], ut[:, dk, 0:ncol], [wgb, ut], [bg], start=(dk == 0), stop=(dk == 7))
            for dk in range(8):
                K.mm(bu[:, 0:ncol], wub[:, dk, ft * 128:(ft + 1) * 128], ut[:, dk, 0:ncol], [wub, ut], [bu], start=(dk == 0), stop=(dk == 7))
            sg_ = sgl[ft % 2]
            K.act(sg_[:, 0:ncol], bg[:, 0:ncol], AF.Sigmoid, [bg], [sg_])
            K.tt("dve", sg_[:, 0:ncol], sg_[:, 0:ncol], bg[:, 0:ncol], ALU.mult, [sg_, bg], [sg_])
            K.tt("dve", hT[:, ft, 0:ncol], sg_[:, 0:ncol], bu[:, 0:ncol], ALU.mult, [sg_, bu], [hT])
        for i in range(nt):
            t = g * GT + i
            hh = h2[t % 2]
            st = st3[t % 2]
            for dh in range(2):
                bank = PS[4 + dh]
                for ft in range(NFT):
                    K.mm(bank[:, :], hT[:, ft, i * 128:(i + 1) * 128], wdb[:, ft, dh * 512:(dh + 1) * 512], [hT, wdb], [bank],
                         start=(ft == 0), stop=(ft == NFT - 1))
                K.tt("dve", hh[:, dh * 512:(dh + 1) * 512], bank[:, :], gb[:, 1, dh * 512:(dh + 1) * 512], ALU.mult, [bank, gb], [hh])
            K.stt(hh[:], xg[:, i, :], ALPHA, hh[:], ALU.mult, ALU.add, [xg, hh], [hh])
            ln_stats2(K, nc, hh, hh.ap, st, epsc, 1)
            K.ts("dve", hh[:], hh[:], st[:, 12:13], st[:, 15:16], ALU.subtract, ALU.mult, [hh, st], [hh])
            K.tt("pool", hh[:], hh[:], ln2bc[:, 0, :], ALU.mult, [hh, ln2bc], [hh])
            K.tt("pool", hh[:], hh[:], ln2bc[:, 1, :], ALU.add, [hh, ln2bc], [hh])
            K.dma(out_d[t * 128:(t + 1) * 128, :], hh[:], [hh], [out_d])
    K.barrier()
    K.stack.close()
    return nc, K


def ln_stats2(K, nc, src, ap, st, epsc, eps_col):
    K.op("dve", lambda: nc.vector.bn_stats(st[:, 0:6], ap[:, 0:512]), [src], [st])
    K.op("dve", lambda: nc.vector.bn_stats(st[:, 6:12], ap[:, 512:1024]), [src], [st])
    K.op("dve", lambda: nc.vector.bn_aggr(st[:, 12:14], st[:, 0:12]), [st], [st])
    K.act(st[:, 14:15], st[:, 13:14], AF.Ln, [st, epsc], [st], bias=epsc[:, eps_col:eps_col + 1])
    K.act(st[:, 15:16], st[:, 14:15], AF.Exp, [st], [st], scale=-0.5)


def _consts():
    c = np.zeros((128, NCONST, 128), np.float32)
    s = np.arange(128)[:, None]
    t = np.arange(128)[None, :]
    c[:, C_ID] = (s == t)
    c[:, C_M32F] = (s // 32 == t // 32) & (s <= t)
    c[:, C_M32B] = (s // 32 == t // 32) & (s >= t)
    c[:, C_MSF] = (s // 64 == t // 64) & (s < t)
    c[:, C_MIF] = (s // 64 == t // 64) & (s <= t)
    c[:, C_MSB] = (s // 64 == t // 64) & (s > t)
    c[:, C_MIB] = (s // 64 == t // 64) & (s >= t)
    c[:, C_BLK64] = (s // 64 == t // 64)
    c[:, C_ONES] = 1.0
    c[:, C_RST32] = np.broadcast_to((t % 32 != 0), (128, 128))
    c[:, C_RST64] = np.broadcast_to((t % 64 != 0), (128, 128))
    rm = np.zeros((128, 128), np.float32)
    for k in range(4):
        rm[:, k] = (np.arange(128) // 32 == k)
    for k in range(2):
        rm[:, 4 + k] = (np.arange(128) // 64 == k)
    c[:, C_ROWM] = rm
    return c


def _fm(v, nt):
    return np.ascontiguousarray(np.asarray(v, np.float32).reshape(nt, 128).T)


def _ptab(inp):
    pt = np.zeros((128, NPT), np.float32)
    lbl = np.asarray(inp["hgrn_lb_logits"], np.float32)
    for d in range(2):
        pt[:, PT_L0 + 4 * d:PT_L0 + 4 * d + 4] = _fm(lbl[0, d], 4)
        pt[:, PT_L1 + 4 * d:PT_L1 + 4 * d + 4] = _fm(lbl[1, d], 4)
    pt[:, PT_NW] = np.asarray(inp["hgrn_norm_w"], np.float32)[0]
    mu = np.zeros(14 * 128, np.float32)
    mu[:1760] = np.asarray(inp["rwkv_mu"], np.float32)[0]
    pt[:, PT_MU:PT_MU + 14] = _fm(mu, 14)
    ch = np.arange(14 * 128)
    valid = ch < 1760
    masks = [ch < 440, (ch >= 440) & (ch < 880), (ch >= 880) & (ch < 1320), (ch >= 1320) & valid, ch < 880, (ch >= 880) & valid]
    for i, m in enumerate(masks):
        pt[:, PT_ML + 14 * i:PT_ML + 14 * (i + 1)] = _fm(m.astype(np.float32), 14)
    for d in range(2):
        pt[:, PT_W0 + 4 * d:PT_W0 + 4 * d + 4] = _fm(inp["rwkv_w0"][0, d], 4)
        pt[:, PT_A0 + 4 * d:PT_A0 + 4 * d + 4] = _fm(inp["rwkv_a0"][0, d], 4)
    pt[:, PT_KK:PT_KK + 4] = _fm(inp["rwkv_k_k"][0], 4)
    pt[:, PT_KA:PT_KA + 4] = _fm(inp["rwkv_k_a"][0], 4)
    pt[:, PT_RK:PT_RK + 4] = _fm(np.asarray(inp["rwkv_r_k"])[0].reshape(512), 4)
    pt[:, PT_LNW:PT_LNW + 4] = _fm(inp["rwkv_lnx_w"][0], 4)
    pt[:, PT_LNB:PT_LNB + 4] = _fm(inp["rwkv_lnx_b"][0], 4)
    pt[:, PT_BADA:PT_BADA + 48] = _fm(inp["b_ada"][0], 48)
    return pt


def _shared_maps(inp):
    f = lambda a: np.ascontiguousarray(np.asarray(a, np.float32))
    wl4 = np.zeros((128, 4, 512), np.float32)
    wl4[0:32, 0] = inp["rwkv_w2"][0, 0]
    wl4[32:64, 1] = inp["rwkv_w2"][0, 1]
    wl4[64:96, 2] = inp["rwkv_a2"][0, 0]
    wl4[96:128, 3] = inp["rwkv_a2"][0, 1]
    lnrows = np.stack([f(inp["ln1_g"])[0], f(inp["ln1_b"])[0], f(inp["ln2_g"])[0], f(inp["ln2_b"])[0]], 0)
    return {
        "w_ada": f(inp["w_ada"])[0], "b_ada_row": f(inp["b_ada"]), "w_in": f(inp["w_in"])[0], "ptab": _ptab(inp),
        "consts": _consts(), "wl4": wl4, "g2": f(inp["rwkv_g2"])[0], "w_out": f(inp["w_out"])[0],
        "lnrows": np.ascontiguousarray(lnrows), "w_gate": f(inp["w_ffn_gate"])[0], "w_up": f(inp["w_ffn_up"])[0],
        "w_down": f(inp["w_ffn_down"])[0],
    }


def _core_map(inp, shared, b):
    m = dict(shared)
    m["x"] = np.ascontiguousarray(np.asarray(inp["x"][b], np.float32))
    m["ctx"] = np.ascontiguousarray(np.asarray(inp["ctx"][b], np.float32))
    cv = np.zeros((128, 16), np.float32)
    cv[:, 0::2] = np.asarray(inp["c"][b], np.float32).reshape(8, 128).T
    cv[:, 1::2] = np.asarray(inp["c_ctx"], np.float32).reshape(8, 128).T
    m["cv"] = cv
    return m


_NC_CACHE = {}


def kernel(**inputs):
    x = np.asarray(inputs["x"])
    B, T, _ = x.shape
    if T not in _NC_CACHE:
        _NC_CACHE[T] = build(T)[0]
    nc = _NC_CACHE[T]
    shared = _shared_maps(inputs)
    in_maps = [_core_map(inputs, shared, b) for b in range(B)]
    res = run_bass_kernel_spmd(nc, in_maps, core_ids=list(range(B)))
    return np.stack([np.asarray(r["out"], np.float32) for r in res.results], 0)
```

```python
from contextlib import ExitStack
import numpy as np
import concourse.bass as bass
import concourse.mybir as mybir
from concourse.bass_utils import run_bass_kernel_spmd

F32 = mybir.dt.float32
BF16 = mybir.dt.bfloat16
ALU = mybir.AluOpType
AF = mybir.ActivationFunctionType

D = 1024
CTX = 256
NCT = 34
ZC = NCT * 128
IN_COLS = 4320
DFF = 2816
NFT = DFF // 128
LWS = 0.6065306597126334
ALPHA = 2.0 ** 0.25

C_ID, C_M32F, C_M32B, C_MSF, C_MIF, C_MSB, C_MIB, C_BLK64, C_ONES, C_RST32, C_RST64, C_ROWM = range(12)
NCONST = 12
PT_L0 = 0
PT_L1 = 8
PT_NW = 16
PT_MU = 17
PT_ML = 31
PT_W0 = 115
PT_A0 = 123
PT_KK = 131
PT_KA = 135
PT_RK = 139
PT_LNW = 143
PT_LNB = 147
PT_BADA = 151
NPT = 199


class Buf:
    __slots__ = ("ap", "w", "r", "name", "tw", "tr")

    def __init__(self, ap, name=""):
        self.ap = ap
        self.w = {}
        self.r = {}
        self.name = name
        self.tw = 0.0
        self.tr = 0.0

    def __getitem__(self, k):
        return self.ap[k]


class KB:
    NR = 8

    def __init__(self, nc):
        self.nc = nc
        self.E = {"pe": nc.tensor, "dve": nc.vector, "act": nc.scalar, "pool": nc.gpsimd, "sp": nc.sync}
        self.sems = []
        self.semval = []
        self.esem = {e: self._newsem("c_" + e) for e in self.E}
        self.dsem = {"sp": [self._newsem(f"d_sp{i}") for i in range(self.NR)]}
        self.didx = {"sp": 0}
        self.seen = {e: {} for e in self.E}
        self.nbuf = 0
        self.nwait = 0
        self.ninst = 0
        self.stack = None
        self._st = None
        self.clk = {}

    def _newsem(self, name):
        self.sems.append(self.nc.alloc_semaphore(name))
        self.semval.append(0)
        return len(self.sems) - 1

    def sb(self, shape, dtype=F32, name=None, perm=False):
        self.nbuf += 1
        name = f"{name or 't'}_{self.nbuf}"
        if perm or self.stack is None:
            h = self.nc.alloc_sbuf_tensor(name, list(shape), dtype)
        else:
            h = self.stack.enter_context(self.nc.sbuf_tensor(name, list(shape), dtype))
        return Buf(h.ap(), name)

    def ps(self, shape, dtype=F32, name=None):
        self.nbuf += 1
        return Buf(self.nc.alloc_psum_tensor(f"{name or 'p'}_{self.nbuf}", list(shape), dtype).ap(), name)

    def dram(self, name, shape, dtype=F32, kind="Internal"):
        return Buf(self.nc.dram_tensor(name, list(shape), dtype, kind=kind).ap(), name)

    def _need(self, reads, writes):
        need = {}
        for b in reads:
            for k, v in b.w.items():
                if need.get(k, 0) < v:
                    need[k] = v
        for b in writes:
            for k, v in b.w.items():
                if need.get(k, 0) < v:
                    need[k] = v
            for k, v in b.r.items():
                if need.get(k, 0) < v:
                    need[k] = v
        return need

    def _wait(self, e, need):
        own = self.esem[e]
        seen = self.seen[e]
        eng = self.E[e]
        for k, v in need.items():
            if k == own and e == "pe":
                continue
            if seen.get(k, 0) < v:
                eng.wait_ge(self.sems[k], v)
                seen[k] = v
                self.nwait += 1

    def _commit(self, k, v, reads, writes):
        for b in writes:
            b.w = {k: v}
            b.r = {}
        for b in reads:
            if b.r.get(k, 0) < v:
                b.r[k] = v

    def op(self, e, fn, reads=(), writes=(), cost=0.5):
        self._yield(e, reads, writes)
        self._model(e, reads, writes, cost)
        self._wait(e, self._need(reads, writes))
        ins = fn()
        k = self.esem[e]
        self.semval[k] += 1
        ins.then_inc(self.sems[k], 1)
        self._commit(k, self.semval[k], reads, writes)
        self.ninst += 1
        return ins

    def dma(self, out, in_, reads=(), writes=(), q="sp"):
        self._yield(q, reads, writes)
        self._model(q, reads, writes, 1.0)
        i = self.didx[q]
        self.didx[q] += 1
        k = self.dsem[q][i % self.NR]
        need = self._need(reads, writes)
        if self.semval[k] > 0 and need.get(k, 0) < self.semval[k]:
            need[k] = self.semval[k]
        self._wait(q, need)
        self.semval[k] += 16
        self.E[q].dma_start(out=out, in_=in_).then_inc(self.sems[k], 16)
        self._commit(k, self.semval[k], reads, writes)
        self.ninst += 1


    def run_streams(self, fns):
        import threading
        n = len(fns)
        if n == 1:
            fns[0]()
            return
        st = {"turn": -1, "alive": [True] * n, "err": [], "pend": [None] * n, "started": 0}
        cv = threading.Condition()
        self._st, self._cv = st, cv
        self._tls = threading.local()

        def pick():
            best, bt_ = -1, None
            for i in range(n):
                if st["alive"][i] and st["pend"][i] is not None:
                    t = st["pend"][i]
                    if bt_ is None or t < bt_:
                        best, bt_ = i, t
            st["turn"] = best
            cv.notify_all()
        self._pick = pick

        def all_pending():
            return all((not st["alive"][i]) or st["pend"][i] is not None for i in range(n))
        self._all_pending = all_pending

        def runner(i):
            self._tls.sid = i
            self._tls.atomic = 0
            try:
                fns[i]()
            except BaseException as e:
                st["err"].append(e)
            finally:
                with cv:
                    st["alive"][i] = False
                    st["pend"][i] = None
                    if any(st["alive"]) and all_pending():
                        pick()
        ths = [threading.Thread(target=runner, args=(i,)) for i in range(n)]
        for t in ths:
            t.start()
        for t in ths:
            t.join()
        self._st = None
        if st["err"]:
            raise st["err"][0]

    def _est_start(self, e, reads, writes):
        t = self.clk.get(e, 0.0)
        for b in reads:
            if b.tw > t:
                t = b.tw
        for b in writes:
            if b.tw > t:
                t = b.tw
            if b.tr > t:
                t = b.tr
        return t

    def _model(self, e, reads, writes, cost):
        t = self._est_start(e, reads, writes) + 0.15
        f = t + cost
        if e == "sp":
            self.clk[e] = t + 0.05
            f = t + 2.0 + cost
        else:
            self.clk[e] = f
        for b in writes:
            b.tw = f
        for b in reads:
            if b.tr < f:
                b.tr = f

    def _yield(self, e, reads, writes):
        st = getattr(self, "_st", None)
        if st is None:
            return
        tls = self._tls
        i = getattr(tls, "sid", None)
        if i is None or tls.atomic:
            return
        cv = self._cv
        with cv:
            st["pend"][i] = self._est_start(e, reads, writes)
            if self._all_pending():
                self._pick()
            while st["turn"] != i:
                cv.wait()
            st["turn"] = -1
            st["pend"][i] = None

    def atomic(self):
        kb = self

        class _A:
            def __enter__(self_):
                if getattr(kb, "_st", None) is not None and getattr(kb._tls, "sid", None) is not None:
                    kb._tls.atomic += 1

            def __exit__(self_, *a):
                if getattr(kb, "_st", None) is not None and getattr(kb._tls, "sid", None) is not None:
                    kb._tls.atomic -= 1
        return _A()

    def barrier(self):
        need = {k: v for k, v in enumerate(self.semval) if v > 0}
        for e in self.E:
            self._wait(e, need)

    @staticmethod
    def _n(ap):
        n = 1
        for d in ap.shape[1:]:
            n *= d
        return n

    def mm(self, out, lhsT, rhs, reads, writes, start=True, stop=True):
        nc = self.nc
        c = 0.03 + self._n(rhs) * (4 if rhs.dtype == F32 else 1) / 2400.0
        return self.op("pe", lambda: nc.tensor.matmul(out, lhsT, rhs, start=start, stop=stop), reads, writes, cost=c)

    def tr(self, out, in_, ident, reads, writes):
        nc = self.nc
        return self.op("pe", lambda: nc.tensor.transpose(out, in_, ident), reads, writes, cost=0.09)

    def act(self, out, in_, func, reads, writes, bias=None, scale=None):
        nc = self.nc
        kw = {}
        if bias is not None:
            kw["bias"] = bias
        if scale is not None:
            kw["scale"] = scale
        c = 0.2 + self._n(out) / 1200.0
        return self.op("act", lambda: nc.scalar.activation(out, in_, func, **kw), reads, writes, cost=c)

    def _vc(self, e, out):
        n = self._n(out)
        return (0.1 + n / 500.0) if e == "pool" else (0.07 + n / 900.0)

    def tt(self, e, out, in0, in1, op, reads, writes):
        eng = self.E[e]
        return self.op(e, lambda: eng.tensor_tensor(out, in0, in1, op), reads, writes, cost=self._vc(e, out))

    def ts(self, e, out, in0, s1, s2, op0, op1, reads, writes):
        eng = self.E[e]
        if s2 is None:
            return self.op(e, lambda: eng.tensor_scalar(out, in0, s1, None, op0), reads, writes, cost=self._vc(e, out))
        return self.op(e, lambda: eng.tensor_scalar(out, in0, s1, s2, op0, op1), reads, writes, cost=self._vc(e, out))

    def stt(self, out, in0, scalar, in1, op0, op1, reads, writes):
        nc = self.nc
        return self.op("dve", lambda: nc.vector.scalar_tensor_tensor(out, in0, scalar, in1, op0, op1), reads, writes,
                       cost=0.07 + self._n(out) / 900.0)

    def cp(self, e, out, in_, reads, writes):
        if e == "act":
            nc = self.nc
            return self.op("act", lambda: nc.scalar.copy(out, in_), reads, writes, cost=0.2 + self._n(out) / 1200.0)
        eng = self.E[e]
        return self.op(e, lambda: eng.tensor_copy(out, in_), reads, writes, cost=self._vc(e, out))

    def memset(self, e, ap, val, writes):
        eng = self.E[e]
        return self.op(e, lambda: eng.memset(ap, val), [], writes, cost=self._vc(e, ap))


def _bc(ap, shape):
    return ap.to_broadcast(list(shape))


def build(T, debug=False, upto=9):
    NTX = T // 128
    NE = CTX + T
    nc = bass.Bass("TRN2", target_bir_lowering=False)
    K = KB(nc)
    x_d = K.dram("x", [T, D], kind="ExternalInput")
    ctx_d = K.dram("ctx", [CTX, D], kind="ExternalInput")
    cv_d = K.dram("cv", [128, 16], kind="ExternalInput")
    wada_d = K.dram("w_ada", [D, 6 * D], kind="ExternalInput")
    brow_d = K.dram("b_ada_row", [1, 6 * D], kind="ExternalInput")
    win_d = K.dram("w_in", [D, IN_COLS], kind="ExternalInput")
    ptab_d = K.dram("ptab", [128, NPT], kind="ExternalInput")
    const_d = K.dram("consts", [128, NCONST, 128], kind="ExternalInput")
    wl_d = K.dram("wl4", [128, 4, 512], kind="ExternalInput")
    g2_d = K.dram("g2", [96, 512], kind="ExternalInput")
    wout_d = K.dram("w_out", [D, D], kind="ExternalInput")
    lnrow_d = K.dram("lnrows", [4, D], kind="ExternalInput")
    wg_d = K.dram("w_gate", [D, DFF], kind="ExternalInput")
    wu_d = K.dram("w_up", [D, DFF], kind="ExternalInput")
    wd_d = K.dram("w_down", [DFF, D], kind="ExternalInput")
    out_d = K.dram("out", [T, D], kind="ExternalOutput")
    zT_d = K.dram("zT", [ZC, NE])
    of_d = K.dram("ofwd", [NTX, 128, 8, 128])
    x1_d = K.dram("x1s", [T, D])
    ob_d = K.dram("obwd", [NTX, 128, 8, 128])
    ex_d = K.dram("extra", [NTX, 128, 8, 128])
    dbg = {}
    if debug:
        dbg["zT"] = K.dram("dbg_zT", [ZC, NE], kind="ExternalOutput")
        dbg["yT"] = K.dram("dbg_yT", [NTX, 128, 8, 128], kind="ExternalOutput")
        dbg["x1"] = K.dram("dbg_x1", [T, D], kind="ExternalOutput")

    cst = K.sb([128, NCONST, 128], F32, "cst", perm=True)
    cstb = K.sb([128, NCONST, 128], BF16, "cstb", perm=True)
    ptab = K.sb([128, NPT], F32, "ptab", perm=True)
    modT = K.sb([128, 48, 2], F32, "modT", perm=True)
    opsc = K.sb([128, 3, 8], F32, "opsc", perm=True)
    epsc = K.sb([128, 4], F32, "epsc", perm=True)
    gb = K.sb([128, 2, D], F32, "gb", perm=True)
    drv = K.sb([128, 128], F32, "drv", perm=True)
    DV_LB, DV_OML, DV_NOML = 0, 8, 16
    DV_C0 = 24
    DV_CS = 38
    DV_OMKA = 122
    PS = [K.ps([128, 512], F32, f"bank{i}") for i in range(8)]

    def psb(i):
        return PS[i].ap.bitcast(BF16)

    ident = cst[:, C_ID, :]
    identb = cstb[:, C_ID, :]

    K.dma(cst[:], const_d[:, :, :], [const_d], [cst])
    K.dma(ptab[:], ptab_d[:, :], [ptab_d], [ptab])
    K.cp("dve", cstb[:], cst[:], [cst], [cstb])
    K.memset("pool", epsc[:, 0:1], 1e-6, [epsc])
    K.memset("pool", epsc[:, 1:2], 1e-5, [epsc])
    K.memset("pool", epsc[:, 2:3], 64e-5, [epsc])
    K.memset("pool", epsc[:, 3:4], 1e-24, [epsc])
    K.tt("dve", drv[:, 0:8], ptab[:, PT_L0:PT_L0 + 8], ptab[:, PT_L1:PT_L1 + 8], ALU.subtract, [ptab], [drv])
    K.act(drv[:, DV_LB:DV_LB + 8], drv[:, 0:8], AF.Sigmoid, [drv], [drv])
    K.ts("dve", drv[:, DV_OML:DV_OML + 8], drv[:, DV_LB:DV_LB + 8], -1.0, 1.0, ALU.mult, ALU.add, [drv], [drv])
    K.ts("dve", drv[:, DV_NOML:DV_NOML + 8], drv[:, DV_OML:DV_OML + 8], -1.0, None, ALU.mult, None, [drv], [drv])
    K.ts("dve", drv[:, DV_C0:DV_C0 + 14], ptab[:, PT_MU:PT_MU + 14], -1.0, 1.0, ALU.mult, ALU.add, [ptab], [drv])
    for i in range(6):
        K.tt("dve", drv[:, DV_CS + 14 * i:DV_CS + 14 * (i + 1)], ptab[:, PT_MU:PT_MU + 14],
             ptab[:, PT_ML + 14 * i:PT_ML + 14 * (i + 1)], ALU.mult, [ptab], [drv])
    K.ts("dve", drv[:, DV_OMKA:DV_OMKA + 4], ptab[:, PT_KA:PT_KA + 4], -1.0, 1.0, ALU.mult, ALU.add, [ptab], [drv])

    if upto == 0.1:
        K.barrier()
        return nc, K
    K.stack = ExitStack()
    winb = K.sb([128, 8, ZC], BF16, "winb")
    cv = K.sb([128, 16], F32, "cv")
    cvs = K.sb([128, 16], F32, "cvs")
    K.dma(cv[:], cv_d[:, :], [cv_d], [cv])
    outer0 = K.stack
    K.stack = ExitStack()
    brow = K.sb([1, 4, 512], F32, "brow")
    for i_, eg_ in enumerate((4, 5, 10, 11)):
        K.dma(brow[0:1, i_, :], brow_d[0:1, eg_ * 512:(eg_ + 1) * 512], [brow_d], [brow])
    K.act(cvs[:], cv[:], AF.Sigmoid, [cv], [cvs])
    K.tt("dve", cvs[:], cvs[:], cv[:], ALU.mult, [cvs, cv], [cvs])
    wa = [K.sb([128, 8, 512], F32, f"wa{i}") for i in range(2)]
    wada_v = wada_d.ap.rearrange("(k p) e -> p k e", p=128)
    grow = K.sb([1, 512], F32, "grow")
    for eg in range(12):
        w = wa[eg % 2]
        K.dma(w[:], wada_v[:, :, eg * 512:(eg + 1) * 512], [wada_d], [w])
        bank = PS[eg % 2]
        for j in range(4):
            for dk in range(8):
                K.mm(bank[:, 2 * j:2 * j + 2], w[:, dk, j * 128:(j + 1) * 128], cvs[:, 2 * dk:2 * dk + 2], [w, cvs], [bank],
                     start=(dk == 0), stop=(dk == 7))
        for j in range(4):
            et = eg * 4 + j
            K.ts("dve", modT[:, et, :], bank[:, 2 * j:2 * j + 2], ptab[:, PT_BADA + et:PT_BADA + et + 1], None, ALU.add, None,
                 [bank, ptab], [modT])
        if eg in (4, 5, 10, 11):
            gi = 0 if eg < 6 else 1
            half = eg % 2 if eg < 6 else (eg - 10)
            rb_ = PS[2]
            for dk in range(8):
                K.mm(rb_[0:1, 0:512], cvs[:, 2 * dk:2 * dk + 1], w[:, dk, :], [w, cvs], [rb_], start=(dk == 0), stop=(dk == 7))
            K.tt("dve", grow[:], rb_[0:1, 0:512], brow[0:1, (4, 5, 10, 11).index(eg), :], ALU.add, [rb_, brow], [grow])
            bb = PS[3]
            K.mm(bb[:, 0:512], cst[0:1, C_ONES, :], grow[:], [cst, grow], [bb])
            K.cp("act", gb[:, gi, half * 512:(half + 1) * 512], bb[:, 0:512], [bb], [gb])
    K.ts("dve", opsc[:, 0, :], modT[:, 8:16, 0], 1.0, None, ALU.add, None, [modT], [opsc])
    K.ts("dve", opsc[:, 1, :], modT[:, 8:16, 1], 1.0, None, ALU.add, None, [modT], [opsc])
    K.ts("dve", opsc[:, 2, :], modT[:, 32:40, 0], 1.0, None, ALU.add, None, [modT], [opsc])
    K.barrier()
    if upto == 0.2:
        return nc, K
    K.stack.close()
    K.stack = ExitStack()
    wst = [K.sb([128, IN_COLS], F32, f"wst{i}") for i in range(2)]
    win_v = win_d.ap.rearrange("(k p) c -> p k c", p=128)
    K.memset("pool", winb[:, :, IN_COLS:ZC], 0.0, [winb])
    for dk in range(8):
        s = wst[dk % 2]
        K.dma(s[:], win_v[:, dk, :], [win_d], [s])
        K.cp("act" if dk % 2 else "dve", winb[:, dk, 0:IN_COLS], s[:], [s], [winb])

    K.barrier()
    if upto == 0.3:
        return nc, K
    K.stack.close()
    K.stack = outer0
    xts = [K.sb([128, D], F32, f"xt{i}") for i in range(2)]
    xnb = [K.sb([128, D], BF16, f"xnb{i}") for i in range(2)]
    stt_ = [K.sb([128, 16], F32, f"st{i}") for i in range(2)]

    def ln_stats(src_ap, src_bufs, st, eps_col):
        K.op("dve", lambda: nc.vector.bn_stats(st[:, 0:6], src_ap[:, 0:512]), src_bufs, [st])
        K.op("dve", lambda: nc.vector.bn_stats(st[:, 6:12], src_ap[:, 512:1024]), src_bufs, [st])
        K.op("dve", lambda: nc.vector.bn_aggr(st[:, 12:14], st[:, 0:12]), [st], [st])
        K.act(st[:, 14:15], st[:, 13:14], AF.Ln, [st, epsc], [st], bias=epsc[:, eps_col:eps_col + 1])
        K.act(st[:, 15:16], st[:, 14:15], AF.Exp, [st], [st], scale=-0.5)

    def modulate_T(src, i, uT, col0, sc_ap, sh_ap, tbank):
        st = stt_[i % 2]
        xb = xnb[i % 2]
        ln_stats(src.ap, [src], st, 0)
        K.ts("dve", xb[:], src[:], st[:, 12:13], st[:, 15:16], ALU.subtract, ALU.mult, [src, st], [xb])
        tb = psb(tbank)
        for dk in range(8):
            K.tr(tb[:, dk * 128:(dk + 1) * 128], xb[:, dk * 128:(dk + 1) * 128], identb, [xb, cstb], [PS[tbank]])
        for dk in range(8):
            K.act(uT[:, dk, col0:col0 + 128], tb[:, dk * 128:(dk + 1) * 128], AF.Identity, [PS[tbank], opsc, modT], [uT],
                  bias=sh_ap(dk), scale=sc_ap(dk))

    uTs = [K.sb([128, 8, 512], BF16, f"uT{i}") for i in range(2)]
    zsb = [K.sb([128, 512], F32, f"zsb{i}") for i in range(4)]
    groups = [("ctx", 0, 2)] + [("x", g * 4, min(4, NTX - g * 4)) for g in range((NTX + 3) // 4)]
    ti = 0
    zi = 0
    for gi_, (kind, t0, nt) in enumerate(groups):
        uT = uTs[gi_ % 2]
        src_d = ctx_d if kind == "ctx" else x_d
        mj = 1 if kind == "ctx" else 0
        for i in range(nt):
            xt = xts[ti % 2]
            K.dma(xt[:], src_d[(t0 + i) * 128:(t0 + i + 1) * 128, :], [src_d], [xt])
            modulate_T(xt, ti, uT, i * 128, lambda dk: opsc[:, mj, dk:dk + 1], lambda dk: modT[:, dk, mj:mj + 1], 7)
            ti += 1
        ncol = nt * 128
        ecol0 = (0 if kind == "ctx" else CTX) + t0 * 128
        import os
        ZD = int(os.environ.get("ZDBG", "0"))
        if ZD == 1:
            continue
        for ct in range(NCT):
            bank = PS[ct % 4]
            for dk in range(8):
                K.mm(bank[:, 0:ncol], winb[:, dk, ct * 128:(ct + 1) * 128], uT[:, dk, 0:ncol], [winb, uT], [bank],
                     start=(dk == 0), stop=(dk == 7))
            z = zsb[zi % 4]
            zi += 1
            K.cp("act" if ct % 2 else "dve", z[:, 0:ncol], bank[:, 0:ncol], [bank], [z])
            if ZD == 2:
                continue
            if ZD != 4:
                K.dma(zT_d[ct * 128:(ct + 1) * 128, ecol0:ecol0 + ncol], z[:, 0:ncol], [z], [zT_d])
            if debug and ZD != 3:
                K.dma(dbg["zT"][ct * 128:(ct + 1) * 128, ecol0:ecol0 + ncol], z[:, 0:ncol], [z], [dbg["zT"]])
    K.barrier()
    K.stack.close()

    def run_pass(dirn):
        final = dirn == 1
        K.stack = ExitStack()
        wlb = K.sb([128, 4, 512], BF16, "wlb")
        g2b = K.sb([96, 512], BF16, "g2b")
        outer = K.stack
        K.stack = ExitStack()
        tmpw = K.sb([128, 4, 512], F32, "tmpw")
        K.dma(tmpw[:], wl_d[:, :, :], [wl_d], [tmpw])
        K.cp("dve", wlb[:], tmpw[:], [tmpw], [wlb])
        tmpg = K.sb([96, 512], F32, "tmpg")
        K.dma(tmpg[:], g2_d[:, :], [g2_d], [tmpg])
        K.cp("dve", g2b[:], tmpg[:], [tmpg], [g2b])
        K.barrier()
        K.stack.close()
        K.stack = outer
        S = K.sb([128, 4, 128], F32, "S")
        Sbf = [K.sb([128, 4, 128], BF16, f"Sbf{i}") for i in range(4)]
        H = [K.sb([128, 64], F32, f"H{j}") for j in range(4)]
        Hbd = [[K.sb([128, 128], BF16, f"Hbd{j}_{c}") for c in range(2)] for j in range(4)]
        MTbd = [[K.sb([128, 128], F32, f"MT{j}_{c}") for c in range(2)] for j in range(4)]
        K.memset("pool", S[:], 0.0, [S])
        for j in range(4):
            K.memset("pool", H[j][:], 0.0, [H[j]])
            for c in range(2):
                K.memset("pool", Hbd[j][c][:], 0.0, [Hbd[j][c]])
                K.memset("pool", MTbd[j][c][:], 0.0, [MTbd[j][c]])
        for i in range(4):
            K.memset("pool", Sbf[i][:], 0.0, [Sbf[i]])
        ld_q = K.sb([128, 4, 128], F32, "ldq")
        ld_f = K.sb([128, 4, 128], F32, "ldf")
        ld_i = K.sb([128, 4, 128], F32, "ldi")
        ld_rw = [K.sb([128, 4, 256], F32, f"ldrw{g}") for g in range(4)]
        zp = K.sb([128, 14, 4, 66], F32, "zp")
        zc = Buf(zp[:].rearrange("p a r c -> p (a r c)")[:, 0:14 * 130].rearrange("p (a t) -> p a t", t=130), "zc")
        K.memset("pool", zp[:], 0.0, [zp])
        zrl = K.sb([128, 14, 128], F32, "zrl")
        mshg = cstb[:, C_M32B if dirn else C_M32F, :]
        msi = cstb[:, C_MSB:C_MSB + 2, :] if dirn else cstb[:, C_MSF:C_MSF + 2, :]
        ms32 = cst[:, C_MSB, :] if dirn else cst[:, C_MSF, :]
        mnt32 = cst[:, C_MSF, :] if dirn else cst[:, C_MSB, :]
        id32r = K.sb([128, 2, 128], F32, "id32r")
        msir = K.sb([128, 2, 2, 128], BF16, "msir")
        mshgr = K.sb([128, 4, 128], BF16, "mshgr")
        for e_ in range(2):
            K.cp("pool", id32r[:, e_, :], ident, [cst], [id32r])
            K.cp("pool", msir[:, e_, :, :], msi, [cstb], [msir])
        for h_ in range(4):
            K.cp("pool", mshgr[:, h_, :], mshg, [cstb], [mshgr])
        rst32 = cst[:, C_RST32, :]
        rst64 = cst[:, C_RST64, :]
        blk64 = cst[:, C_BLK64, :]

        if dirn == 0:
            order = [("ctx", 0), ("ctx", 1)] + [("x", j) for j in range(NTX)]
        else:
            order = [("ctx", 1), ("ctx", 0)] + [("x", j) for j in range(NTX - 1, -1, -1)]

        def issue_loads(n):
            kind, j = order[n]
            ec0 = (0 if kind == "ctx" else CTX) + j * 128

            def hv(ct0):
                return zT_d.ap[ct0 * 128:(ct0 + 4) * 128, ec0:ec0 + 128].rearrange("(h p) t -> p h t", p=128)
            K.dma(ld_q[:], hv(0), [zT_d], [ld_q])
            K.dma(ld_f[:], hv(4 + 4 * dirn), [zT_d], [ld_f])
            K.dma(ld_i[:], hv(12), [zT_d], [ld_i])
            if kind == "x":
                lo = 64 if j > 0 else 0
                hi = 64 if j < NTX - 1 else 0
            else:
                lo = 1 if j > 0 else 0
                hi = 1 if j < 1 else 0
            for g in range(4):
                nt_ = 4 if g < 3 else 2
                src = zT_d.ap[(20 + 4 * g) * 128:(20 + 4 * g + nt_) * 128, ec0 - lo:ec0 + 128 + hi].rearrange("(h p) t -> p h t", p=128)
                K.dma(ld_rw[g][:, 0:nt_, 64 - lo:192 + hi], src, [zT_d], [ld_rw[g]])

        def T32(name, shape=(128, 4, 128)):
            return K.sb(list(shape), F32, name)

        def T16(name, shape=(128, 4, 128)):
            return K.sb(list(shape), BF16, name)
        P32 = [T32(f"w32_{i}") for i in range(11)]
        H32 = [T32(f"h32_{i}") for i in range(8)]
        sgq = qh = H32[0]
        sgf = H32[1]
        ff = lg = H32[2]
        kdh = H32[3]
        bcum = H32[4]
        tmp1 = H32[5]
        e1 = H32[6]
        e2 = H32[7]
        sw = sq = P32[0]
        asg = P32[1]
        cs = kdr = P32[2]
        csm = rn = P32[3]
        E1 = P32[4]
        E2 = P32[5]
        tmpa = P32[6]
        bvec = P32[7]
        E3 = P32[8]
        kk = P32[9]
        kkn = P32[10]
        qb, kb, ib16, ATm = [T16(n) for n in ("qb", "kb", "ib16", "ATm")]
        kbTm = [T16(f"kbTm{c}") for c in range(4)]
        iT = T16("iT")
        stmp = T32("stmp")
        Lb = K.sb([128, 128], BF16, "Lb")
        vb16 = T16("vb16")
        if final:
            sgd = K.sb([96, 128], BF16, "sgd")
            asg2, tmpa2, bonp = T32("asg2"), T32("tmpa2"), T32("bonp")
        osb = [K.sb([128, 8, 128], F32, f"osb{i}") for i in range(2)]
        exb = [K.sb([128, 8, 128], F32, f"exb{i}") for i in range(2)] if final else None
        AR = [K.sb([128, 4, 2, 128], BF16, f"AR{i}") for i in range(2)]
        bt = [T16(f"bt{i}") for i in range(2)]
        kt = [T16(f"kt{i}") for i in range(2)]
        aT = [T16(f"aT{i}") for i in range(2)]
        vT = [T16(f"vT{i}") for i in range(2)]
        btTm = [[T16(f"btTm{i}_{c}") for c in range(2)] for i in range(2)]
        ktTm = [[T16(f"ktTm{i}_{c}") for c in range(2)] for i in range(2)]
        gam = [K.sb([128, 4, 2], F32, f"gam{i}") for i in range(2)]
        Amat = [K.sb([128, 2, 2, 2, 128], BF16, f"Amat{i}") for i in range(2)]
        XP = [[K.sb([128, 2, 2, 128], F32, f"XP{i}_{p}") for p in range(2)] for i in range(2)]
        XT = [[K.sb([128, 2, 128], F32, f"XT{i}_{p}") for p in range(2)] for i in range(2)]
        Pbf = [K.sb([128, 2, 128], BF16, f"Pbf{i}") for i in range(2)]
        AW = [K.sb([128, 2, 128], BF16, f"AW{i}") for i in range(2)]
        AU = [K.sb([128, 2, 128], BF16, f"AU{i}") for i in range(2)]
        QT = [K.sb([128, 128], BF16, f"QT{i}") for i in range(2)]

        def front_hg(n):
            kind, j = order[n]
            isx = kind == "x"
            fb = n % 2
            ec0 = (0 if kind == "ctx" else CTX) + j * 128

            def hv(ct0):
                return zT_d.ap[ct0 * 128:(ct0 + 4) * 128, ec0:ec0 + 128].rearrange("(h p) t -> p h t", p=128)
            K.dma(ld_q[:], hv(0), [zT_d], [ld_q])
            K.dma(ld_f[:], hv(4 + 4 * dirn), [zT_d], [ld_f])
            K.dma(ld_i[:], hv(12), [zT_d], [ld_i])
            zq, zf, zi_ = ld_q, ld_f, ld_i
            K.act(sgq[:], zq[:], AF.Sigmoid, [zq], [sgq])
            K.act(sgf[:], zf[:], AF.Sigmoid, [zf], [sgf])
            K.tt("pool", qh[:], zq[:], sgq[:], ALU.mult, [zq, sgq], [qh])
            for h in range(4):
                c_ = dirn * 4 + h
                K.ts("dve", ff[:, h, :], sgf[:, h, :], drv[:, DV_OML + c_:DV_OML + c_ + 1], drv[:, DV_LB + c_:DV_LB + c_ + 1],
                     ALU.mult, ALU.add, [sgf, drv], [ff])
                K.ts("pool", kdh[:, h, :], sgf[:, h, :], drv[:, DV_NOML + c_:DV_NOML + c_ + 1], drv[:, DV_OML + c_:DV_OML + c_ + 1],
                     ALU.mult, ALU.add, [sgf, drv], [kdh])
            K.act(lg[:], ff[:], AF.Ln, [ff], [lg])
            for h in range(4):
                K.op("dve", lambda h=h: nc.vector.tensor_tensor_scan(bcum[:, h, :], rst32, lg[:, h, :], 0.0, ALU.mult, ALU.add),
                     [lg, cst], [bcum])
            if dirn:
                bv4 = bcum[:].rearrange("p h (c t) -> p (h c) t", t=32)
                K.tt("pool", tmp1[:], lg[:], bcum[:], ALU.subtract, [lg, bcum], [tmp1])
                K.tt("dve", e2[:].rearrange("p h (c t) -> p (h c) t", t=32), tmp1[:].rearrange("p h (c t) -> p (h c) t", t=32),
                     _bc(bv4[:, :, 31:32], [128, 16, 32]), ALU.add, [tmp1, bcum], [e2])
                K.cp("pool", bcum[:], e2[:], [e2], [bcum])
            K.act(e1[:], bcum[:], AF.Exp, [bcum], [e1])
            K.act(e2[:], bcum[:], AF.Exp, [bcum], [e2], scale=-1.0)
            K.tt("dve", qb[:], qh[:], e1[:], ALU.mult, [qh, e1], [qb])
            K.tt("pool", kb[:], kdh[:], e2[:], ALU.mult, [kdh, e2], [kb])
            K.cp("pool", ib16[:], zi_[:], [zi_], [ib16])
            tb = psb(0)
            for h in range(4):
                K.tr(tb[:, h * 128:(h + 1) * 128], kb[:, h, :], identb, [kb, cstb], [PS[0]])
            for h in range(4):
                K.tr(tb[:, 512 + h * 128:512 + (h + 1) * 128], ib16[:, h, :], identb, [ib16, cstb], [PS[0]])
            for c in range(4):
                K.act(kbTm[c][:].rearrange("p h t -> p (h t)"), tb[:, 0:512], AF.Identity, [PS[0], cst], [kbTm[c]],
                      scale=cst[:, C_ROWM, c:c + 1])
            K.cp("act", iT[:].rearrange("p h t -> p (h t)"), tb[:, 512:1024], [PS[0]], [iT])
            if isx:
                for h in range(4):
                    K.mm(PS[1][:, h * 128:(h + 1) * 128], kb[:, h, :], qb[:, h, :], [kb, qb], [PS[1]])
                K.tt("dve", ATm[:], PS[1][:, :].rearrange("p (h t) -> p h t", h=4), mshgr[:], ALU.mult, [PS[1], mshgr], [ATm])
            corder = [0, 1, 2, 3] if dirn == 0 else [3, 2, 1, 0]
            for ci, c in enumerate(corder):
                K.cp("act", Sbf[c][:], S[:], [S], [Sbf[c]])
                kvb = PS[0]
                for h in range(4):
                    K.mm(kvb[:, h * 128:(h + 1) * 128], kbTm[c][:, h, :], iT[:, h, :], [kbTm[c], iT], [kvb])
                dcol = c * 32 + (0 if dirn else 31)
                K.tt("dve", stmp[:], kvb[:, :].rearrange("p (h v) -> p h v", h=4), S[:], ALU.add, [kvb, S], [stmp])
                K.tt("pool", S[:], stmp[:], _bc(e1[:, :, dcol:dcol + 1], [128, 4, 128]), ALU.mult, [stmp, e1], [S])
            if isx:
                ob = PS[1]
                for h in range(4):
                    with K.atomic():
                        K.mm(ob[:, h * 128:(h + 1) * 128], iT[:, h, :], ATm[:, h, :], [iT, ATm], [ob], start=True, stop=False)
                        for c in range(4):
                            K.mm(ob[:, h * 128 + c * 32:h * 128 + (c + 1) * 32], Sbf[c][:, h, :], qb[:, h, c * 32:(c + 1) * 32],
                                 [Sbf[c], qb], [ob], start=False, stop=(c == 3))
                K.cp("act", osb[fb][:, 0:4, :], ob[:, :].rearrange("p (h t) -> p h t", h=4), [ob], [osb[fb]])

        def front_rw(n):
            kind, j = order[n]
            isx = kind == "x"
            fb = n % 2
            ec0 = (0 if kind == "ctx" else CTX) + j * 128
            if isx and n == 2:
                K.memset("pool", zp[:], 0.0, [zp])
            if kind == "x":
                lo = 64 if j > 0 else 0
                hi = 64 if j < NTX - 1 else 0
            else:
                lo = 1 if j > 0 else 0
                hi = 1 if j < 1 else 0
            for g in range(4):
                nt_ = 4 if g < 3 else 2
                src_ = zT_d.ap[(20 + 4 * g) * 128:(20 + 4 * g + nt_) * 128, ec0 - lo:ec0 + 128 + hi].rearrange("(h p) t -> p h t", p=128)
                K.dma(ld_rw[g][:, 0:nt_, 64 - lo:192 + hi], src_, [zT_d], [ld_rw[g]])
            lrw = ld_rw
            if isx:
                if j == 0:
                    for g in range(4):
                        K.memset("pool", lrw[g][:, :, 0:64], 0.0, [lrw[g]])
                if j == NTX - 1:
                    for g in range(4):
                        K.memset("pool", lrw[g][:, :, 192:256], 0.0, [lrw[g]])
                for g in range(4):
                    nt_ = 4 if g < 3 else 2
                    for q_ in range(nt_):
                        K.cp("pool", zp[:, 4 * g + q_, :, 1:65], lrw[g][:, q_, :].rearrange("p (r c) -> p r c", c=64), [lrw[g]], [zp])
            else:
                if j == 0:
                    for g in range(4):
                        K.memset("pool", lrw[g][:, :, 63:64], 0.0, [lrw[g]])
                if j == 1:
                    for g in range(4):
                        K.memset("pool", lrw[g][:, :, 192:193], 0.0, [lrw[g]])
                for g in range(4):
                    nt_ = 4 if g < 3 else 2
                    K.cp("pool", zc[:, 4 * g:4 * g + nt_, :], lrw[g][:, 0:nt_, 63:193], [lrw[g]], [zp])
            if isx:
                for ct in range(14):
                    views = {"L": zp[:, ct, 1:3, 0:64], "R": zp[:, ct, 1:3, 2:66], "U": zp[:, ct, 0:2, 1:65], "D": zp[:, ct, 2:4, 1:65]}
                    cen = zp[:, ct, 1:3, 1:65]
                    lo_, hi_ = ct * 128, ct * 128 + 128
                    kinds = []
                    if lo_ < 440: kinds.append(("L", 0))
                    if hi_ > 440 and lo_ < 880: kinds.append(("R", 1))
                    if hi_ > 880 and lo_ < 1320: kinds.append(("U", 2))
                    if hi_ > 1320: kinds.append(("D", 3))
                    o3 = zrl[:, ct, :].rearrange("p (r c) -> p r c", c=64)
                    K.act(o3, cen, AF.Identity, [zp, drv], [zrl], scale=drv[:, DV_C0 + ct:DV_C0 + ct + 1])
                    for (vn, ki) in kinds:
                        K.stt(o3, views[vn], drv[:, DV_CS + 14 * ki + ct:DV_CS + 14 * ki + ct + 1], o3, ALU.mult, ALU.add, [zp, drv, zrl], [zrl])
            else:
                for ct in range(14):
                    lo_, hi_ = ct * 128, ct * 128 + 128
                    kinds = []
                    if lo_ < 880: kinds.append((zc[:, ct, 0:128], 4))
                    if hi_ > 880: kinds.append((zc[:, ct, 2:130], 5))
                    K.act(zrl[:, ct, :], zc[:, ct, 1:129], AF.Identity, [zp, drv], [zrl], scale=drv[:, DV_C0 + ct:DV_C0 + ct + 1])
                    for (vw, ki) in kinds:
                        K.stt(zrl[:, ct, :], vw, drv[:, DV_CS + 14 * ki + ct:DV_CS + 14 * ki + ct + 1], zrl[:, ct, :], ALU.mult, ALU.add,
                              [zp, drv, zrl], [zrl])
            r_ = zrl[:, 0:4, :]
            k_ = zrl[:, 4:8, :]
            v_ = zrl[:, 8:12, :]
            K.act(Lb[0:64, :], zrl[0:64, 12, :], AF.Tanh, [zrl], [Lb])
            K.cp("pool", Lb[64:128, :], zrl[64:128, 12, :], [zrl], [Lb])
            pw, pa = PS[2], PS[7]
            for jj in range(4):
                K.mm(pw[:, jj * 128:(jj + 1) * 128], wlb[:, dirn, jj * 128:(jj + 1) * 128], Lb[:], [wlb, Lb], [pw])
            for jj in range(4):
                K.mm(pa[:, jj * 128:(jj + 1) * 128], wlb[:, 2 + dirn, jj * 128:(jj + 1) * 128], Lb[:], [wlb, Lb], [pa])
            for jj in range(4):
                K.act(sw[:, jj, :], pw[:, jj * 128:(jj + 1) * 128], AF.Sigmoid, [pw, ptab], [sw],
                      bias=ptab[:, PT_W0 + dirn * 4 + jj:PT_W0 + dirn * 4 + jj + 1])
                K.act(asg[:, jj, :], pa[:, jj * 128:(jj + 1) * 128], AF.Sigmoid, [pa, ptab], [asg],
                      bias=ptab[:, PT_A0 + dirn * 4 + jj:PT_A0 + dirn * 4 + jj + 1])
            if final and isx:
                for jj in range(4):
                    K.mm(pa[:, jj * 128:(jj + 1) * 128], wlb[:, 2, jj * 128:(jj + 1) * 128], Lb[:], [wlb, Lb], [pa])
                for jj in range(4):
                    K.act(asg2[:, jj, :], pa[:, jj * 128:(jj + 1) * 128], AF.Sigmoid, [pa, ptab], [asg2],
                          bias=ptab[:, PT_A0 + jj:PT_A0 + jj + 1])
            for jj in range(4):
                K.op("dve", lambda jj=jj: nc.vector.tensor_tensor_scan(cs[:, jj, :], rst64, sw[:, jj, :], 0.0, ALU.mult, ALU.add),
                     [sw, cst], [cs])
            if dirn == 0:
                K.tt("pool", csm[:], cs[:], sw[:], ALU.subtract, [cs, sw], [csm])
            else:
                c8 = cs[:].rearrange("p j (c t) -> p (j c) t", t=64)
                K.tt("dve", csm[:].rearrange("p j (c t) -> p (j c) t", t=64), _bc(c8[:, :, 63:64], [128, 8, 64]), c8, ALU.subtract,
                     [cs], [csm])
                K.tt("pool", cs[:], csm[:], sw[:], ALU.add, [csm, sw], [cs])
            K.act(E1[:], csm[:], AF.Exp, [csm], [E1], scale=-LWS)
            K.act(E2[:], cs[:], AF.Exp, [cs], [E2], scale=-LWS)
            K.act(E3[:], cs[:], AF.Exp, [cs], [E3], scale=LWS)
            goff = 0 if dirn else 63
            K.cp("pool", gam[fb][:], E2[:].rearrange("p j (c t) -> p j c t", t=64)[:, :, :, goff], [E2], [gam[fb]])
            for jj in range(4):
                K.ts("pool", kk[:, jj, :], k_[:, jj, :], ptab[:, PT_KK + jj:PT_KK + jj + 1], None, ALU.mult, None, [zrl, ptab], [kk])
            K.tt("pool", sq[:], kk[:], kk[:], ALU.mult, [kk], [sq])
            K.mm(PS[2][:, :], blk64, sq[:].rearrange("p j t -> p (j t)"), [cst, sq], [PS[2]])
            K.ts("dve", rn[:].rearrange("p j t -> p (j t)"), PS[2][:, :], epsc[:, 3:4], None, ALU.max, None, [PS[2], epsc], [rn])
            K.act(rn[:], rn[:], AF.Ln, [rn], [rn])
            K.act(rn[:], rn[:], AF.Exp, [rn], [rn], scale=-0.5)
            K.tt("dve", kkn[:], kk[:], rn[:], ALU.mult, [kk, rn], [kkn])
            for jj in range(4):
                K.ts("pool", tmpa[:, jj, :], asg[:, jj, :], ptab[:, PT_KA + jj:PT_KA + jj + 1], drv[:, DV_OMKA + jj:DV_OMKA + jj + 1],
                     ALU.mult, ALU.add, [asg, ptab, drv], [tmpa])
            K.tt("pool", kdr[:], k_, tmpa[:], ALU.mult, [zrl, tmpa], [kdr])
            K.tt("pool", bvec[:], kkn[:], asg[:], ALU.mult, [kkn, asg], [bvec])
            K.stt(AR[fb][:, :, 0, :], kkn[:], -1.0, E1[:], ALU.mult, ALU.mult, [kkn, E1], [AR[fb]])
            K.tt("dve", AR[fb][:, :, 1, :], r_, E2[:], ALU.mult, [zrl, E2], [AR[fb]])
            K.tt("dve", bt[fb][:], bvec[:], E3[:], ALU.mult, [bvec, E3], [bt[fb]])
            K.tt("pool", kt[fb][:], kdr[:], E3[:], ALU.mult, [kdr, E3], [kt[fb]])
            K.cp("pool", vb16[:], v_, [zrl], [vb16])
            tb0, tb1 = psb(2), psb(7)
            for jj in range(4):
                K.tr(tb0[:, jj * 128:(jj + 1) * 128], AR[fb][:, jj, 0, :], identb, [AR[fb], cstb], [PS[2]])
                K.tr(tb0[:, 512 + jj * 128:512 + (jj + 1) * 128], vb16[:, jj, :], identb, [vb16, cstb], [PS[2]])
                K.tr(tb1[:, jj * 128:(jj + 1) * 128], bt[fb][:, jj, :], identb, [bt[fb], cstb], [PS[7]])
                K.tr(tb1[:, 512 + jj * 128:512 + (jj + 1) * 128], kt[fb][:, jj, :], identb, [kt[fb], cstb], [PS[7]])
            K.cp("act", aT[fb][:].rearrange("p j t -> p (j t)"), tb0[:, 0:512], [PS[2]], [aT[fb]])
            K.cp("act", vT[fb][:].rearrange("p j t -> p (j t)"), tb0[:, 512:1024], [PS[2]], [vT[fb]])
            for c in range(2):
                K.act(btTm[fb][c][:].rearrange("p j t -> p (j t)"), tb1[:, 0:512], AF.Identity, [PS[7], cst], [btTm[fb][c]],
                      scale=cst[:, C_ROWM, 4 + c:5 + c])
                K.act(ktTm[fb][c][:].rearrange("p j t -> p (j t)"), tb1[:, 512:1024], AF.Identity, [PS[7], cst], [ktTm[fb][c]],
                      scale=cst[:, C_ROWM, 4 + c:5 + c])
            if final and isx:
                K.act(sgd[:], zrl[0:96, 13, :], AF.Sigmoid, [zrl], [sgd])
                for jj in range(4):
                    K.ts("pool", tmpa2[:, jj, :], asg2[:, jj, :], ptab[:, PT_KA + jj:PT_KA + jj + 1], drv[:, DV_OMKA + jj:DV_OMKA + jj + 1],
                         ALU.mult, ALU.add, [asg2, ptab, drv], [tmpa2])
                K.tt("pool", tmpa2[:], tmpa2[:], tmpa[:], ALU.add, [tmpa2, tmpa], [tmpa2])
                K.tt("pool", tmpa2[:], tmpa2[:], k_, ALU.mult, [tmpa2, zrl], [tmpa2])
                for jj in range(4):
                    K.stt(bonp[:, jj, :], r_[:, jj, :], ptab[:, PT_RK + jj:PT_RK + jj + 1], tmpa2[:, jj, :], ALU.mult, ALU.mult,
                          [zrl, ptab, tmpa2], [bonp])
                K.mm(PS[2][:, :], blk64, bonp[:].rearrange("p j t -> p (j t)"), [cst, bonp], [PS[2]])
                K.tt("dve", exb[fb][:, 0:4, :], PS[2][:, :].rearrange("p (j t) -> p j t", j=4), v_, ALU.mult, [PS[2], zrl], [exb[fb]])
                pg = PS[7]
                for jj in range(4):
                    K.mm(pg[:, jj * 128:(jj + 1) * 128], g2b[:, jj * 128:(jj + 1) * 128], sgd[:], [g2b, sgd], [pg])
                K.cp("act", exb[fb][:, 4:8, :], pg[:, :].rearrange("p (j t) -> p j t", j=4), [pg], [exb[fb]])
                K.dma(ex_d.ap[j], exb[fb][:], [exb[fb]], [ex_d])

        def pairs(n, s):
            kind, j = order[n]
            isx = kind == "x"
            fb = n % 2
            ba, bb = PS[3 + 2 * s], PS[4 + 2 * s]
            bke = (ba, bb)
            am, aw, au, qt, pbf = Amat[s], AW[s], AU[s], QT[s], Pbf[s]
            xp0, xp1 = XP[s]
            xt0, xt1 = XT[s]
            ar, bt_, kt_, aT_, vT_ = AR[fb], bt[fb], kt[fb], aT[fb], vT[fb]
            for jj in (s, s + 2):
                for e in range(2):
                    ep = slice(e * 64, (e + 1) * 64)
                    bk = bke[e]
                    arf = ar[ep, jj, :, :].rearrange("p a t -> p (a t)")
                    K.mm(bk[:, 0:256], bt_[ep, jj, :], arf, [bt_, ar], [bk])
                    K.mm(bk[:, 256:512], kt_[ep, jj, :], arf, [kt_, ar], [bk])
                for e in range(2):
                    bk = bke[e]
                    bv_ = bk[:, :].rearrange("p (w a t) -> p w a t", w=2, a=2)
                    K.tt("dve", xp0[:, e, 0, :], bk[:, 0:128], ms32, ALU.mult, [bk, cst], [xp0])
                    K.tt("dve", am[:, e, :, :, :], bv_, msir[:], ALU.mult, [bk, msir], [am])
                for e in range(2):
                    ep = slice(e * 64, (e + 1) * 64)
                    bk = bke[e]
                    K.mm(bk[:, 0:128], ar[ep, jj, 0, :], bt_[ep, jj, :], [ar, bt_], [bk])
                    K.tt("dve", xt0[:, e, :], bk[:, 0:128], mnt32, ALU.mult, [bk, cst], [xt0])
                K.cp("pool", xp0[:, :, 1, :], id32r[:], [id32r], [xp0])
                cur, nxt = (xp0, xt0), (xp1, xt1)
                for lvl in range(6):
                    cxp, cxt = cur
                    nxp, nxt_ = nxt
                    last = lvl == 5
                    for e in range(2):
                        if not last:
                            K.mm(ba[:, e * 256:(e + 1) * 256], cxt[:, e, :], cxp[:, e, :, :].rearrange("p a t -> p (a t)"), [cxt, cxp], [ba])
                        else:
                            K.mm(ba[:, e * 256 + 128:(e + 1) * 256], cxt[:, e, :], cxp[:, e, 1, :], [cxt, cxp], [ba])
                    if not last:
                        for e in range(2):
                            K.mm(bb[:, e * 128:(e + 1) * 128], cxp[:, e, 0, :], cxt[:, e, :], [cxp, cxt], [bb])
                    pav = ba[:, :].rearrange("p (e a t) -> p e a t", e=2, a=2)
                    K.tt("dve", nxp[:, :, 1, :], pav[:, :, 1, :], cxp[:, :, 1, :], ALU.add, [ba, cxp], [nxp])
                    if not last:
                        K.cp("act", nxp[:, :, 0, :], pav[:, :, 0, :], [ba], [nxp])
                        K.cp("act", nxt_[:], bb[:, 0:256].rearrange("p (e t) -> p e t", e=2), [bb], [nxt_])
                    cur, nxt = nxt, cur
                K.cp("pool", pbf[:], cur[0][:, :, 1, :], [cur[0]], [pbf])
                for e in range(2):
                    K.mm(ba[:, e * 64:(e + 1) * 64], am[:, e, 1, 0, :], vT_[:, jj, e * 64:(e + 1) * 64], [am, vT_], [ba])
                K.cp("pool", aw[:, :, 0:64], aT_[:, jj, :].rearrange("p (e k) -> p e k", e=2), [aT_], [aw])
                K.cp("act", aw[:, :, 64:128], ba[:, 0:128].rearrange("p (e v) -> p e v", e=2), [ba], [aw])
                for e in range(2):
                    K.mm(bb[:, e * 128:(e + 1) * 128], pbf[:, e, :], aw[:, e, :], [pbf, aw], [bb])
                K.cp("dve", au[:], bb[:, 0:256].rearrange("p (e c) -> p e c", e=2), [bb], [au])
                if isx:
                    for e in range(2):
                        K.mm(ba[e * 64:(e + 1) * 64, 128:256], au[:, e, 0:64], am[:, e, 0, 1, :], [au, am], [ba])
                    K.tt("dve", qt[:], ba[:, 128:256], ar[:, jj, 1, :], ALU.add, [ba, ar], [qt])
                corder2 = [0, 1] if dirn == 0 else [1, 0]
                Hj = H[jj]
                for ci, c in enumerate(corder2):
                    hb = Hbd[jj][c]
                    mt = MTbd[jj][c]
                    for e in range(2):
                        K.cp("act", hb[e * 64:(e + 1) * 64, e * 64:(e + 1) * 64], Hj[e * 64:(e + 1) * 64, :], [Hj], [hb])
                    for e in range(2):
                        K.mm(bb[e * 64:(e + 1) * 64, 256:320], au[:, e, 0:64], btTm[fb][c][:, jj, e * 64:(e + 1) * 64], [au, btTm[fb][c]], [bb])
                    for e in range(2):
                        K.tt("dve", mt[e * 64:(e + 1) * 64, e * 64:(e + 1) * 64], bb[e * 64:(e + 1) * 64, 256:320],
                             ident[e * 64:(e + 1) * 64, e * 64:(e + 1) * 64], ALU.add, [bb, cst], [mt])
                    with K.atomic():
                        K.mm(ba[:, 384:448], mt[:], Hj[:], [mt, Hj], [ba], start=True, stop=False)
                        for e in range(2):
                            ep = slice(e * 64, (e + 1) * 64)
                            K.mm(ba[ep, 384:448], btTm[fb][c][:, jj, ep], au[:, e, 64:128], [btTm[fb][c], au], [ba], start=False, stop=False)
                            K.mm(ba[ep, 384:448], ktTm[fb][c][:, jj, ep], vT_[:, jj, ep], [ktTm[fb][c], vT_], [ba], start=False, stop=True)
                    K.ts("dve", Hj[:], ba[:, 384:448], gam[fb][:, jj, c:c + 1], None, ALU.mult, None, [ba, gam[fb]], [Hj])
                if isx:
                    yb = bb[:, 384:512]
                    with K.atomic():
                        for e in range(2):
                            ep = slice(e * 64, (e + 1) * 64)
                            K.mm(bb[ep, 384:512], au[:, e, 64:128], am[:, e, 0, 1, :], [au, am], [bb], start=True, stop=False)
                            K.mm(bb[ep, 384:512], vT_[:, jj, ep], am[:, e, 1, 1, :], [vT_, am], [bb], start=False, stop=False)
                        for c in range(2):
                            K.mm(bb[:, 384 + c * 64:384 + (c + 1) * 64], Hbd[jj][c][:], qt[:, c * 64:(c + 1) * 64], [Hbd[jj][c], qt], [bb],
                                 start=False, stop=(c == 1))
                    K.cp("act", osb[fb][:, 4 + jj, :], yb, [bb], [osb[fb]])

        def tail(n):
            kind, j = order[n]
            if kind != "x":
                return
            fb = n % 2
            K.dma((ob_d if final else of_d).ap[j], osb[fb][:], [osb[fb]], [ob_d if final else of_d])

        import os
        NOIL = os.environ.get("NOIL", "0") == "1"
        NT_ = len(order)
        if NOIL:
            for n in range(NT_):
                front_hg(n); front_rw(n); pairs(n, 0); pairs(n, 1); tail(n)
        else:
            K.run_streams([lambda: front_hg(0), lambda: front_rw(0)])
            for n in range(NT_):
                fns = [lambda n=n: pairs(n, 0), lambda n=n: pairs(n, 1)]
                if n + 1 < NT_:
                    fns.append(lambda n=n: front_hg(n + 1))
                    fns.append(lambda n=n: front_rw(n + 1))
                K.run_streams(fns)
                tail(n)
        K.barrier()
        K.stack.close()

    def run_pc():
        K.stack = ExitStack()
        woutb = K.sb([128, 8, D], BF16, "woutb")
        lnb_ = K.sb([128, 2, D], F32, "ln1bc")
        K.dma(lnb_[:, 0, :], lnrow_d.ap[0:1, :].partition_broadcast(128), [lnrow_d], [lnb_])
        K.dma(lnb_[:, 1, :], lnrow_d.ap[1:2, :].partition_broadcast(128), [lnrow_d], [lnb_])
        wo_v = wout_d.ap.rearrange("(k p) c -> p k c", p=128)
        wos = [K.sb([128, D], F32, f"wos{i}") for i in range(2)]
        for dk in range(8):
            K.dma(wos[dk % 2][:], wo_v[:, dk, :], [wout_d], [wos[dk % 2]])
            K.cp("act" if dk % 2 else "dve", woutb[:, dk, :], wos[dk % 2][:], [wos[dk % 2]], [woutb])
        blk64 = cst[:, C_BLK64, :]
        onesf = cst[:, C_ONES, :]
        NS = 2
        bufs = []
        for s in range(NS):
            d_ = {}
            d_["of"] = K.sb([128, 8, 128], F32, f"pc_of{s}")
            d_["ob"] = K.sb([128, 8, 128], F32, f"pc_ob{s}")
            d_["ex"] = K.sb([128, 8, 128], F32, f"pc_ex{s}")
            d_["og"] = K.sb([128, 4, 128], F32, f"pc_og{s}")
            d_["x"] = K.sb([128, D], F32, f"pc_x{s}")
            for nm in ("ohg", "sqh", "rsth", "sog", "ysb", "ycen", "sq2", "rstd2", "yn2"):
                d_[nm] = K.sb([128, 4, 128], F32, f"pc_{nm}{s}")
            d_["yT"] = K.sb([128, 8, 128], BF16, f"pc_yT{s}")
            d_["h1"] = K.sb([128, D], F32, f"pc_h1{s}")
            d_["x1t"] = K.sb([128, D], F32, f"pc_x1t{s}")
            d_["st1"] = K.sb([128, 16], F32, f"pc_st{s}")
            d_["dbg"] = K.sb([128, 8, 128], F32, f"pc_dbg{s}") if debug else None
            bufs.append(d_)

        def pc_stream(s):
            B_ = bufs[s]
            pA, pB, pC, pD = PS[4 * s], PS[4 * s + 1], PS[4 * s + 2], PS[4 * s + 3]
            for j in range(s, NTX, NS):
                ec0 = CTX + j * 128
                K.dma(B_["of"][:], of_d.ap[j], [of_d], [B_["of"]])
                K.dma(B_["ob"][:], ob_d.ap[j], [ob_d], [B_["ob"]])
                K.dma(B_["ex"][:], ex_d.ap[j], [ex_d], [B_["ex"]])
                K.dma(B_["og"][:], zT_d.ap[16 * 128:20 * 128, ec0:ec0 + 128].rearrange("(h p) t -> p h t", p=128), [zT_d], [B_["og"]])
                K.dma(B_["x"][:], x_d[j * 128:(j + 1) * 128, :], [x_d], [B_["x"]])
                ohg, sqh, rsth, sog, ysb, ycen, sq2, rstd2, yn2 = [B_[k_] for k_ in ("ohg", "sqh", "rsth", "sog", "ysb", "ycen", "sq2", "rstd2", "yn2")]
                yT, h1, x1t, st1, xt = B_["yT"], B_["h1"], B_["x1t"], B_["st1"], B_["x"]
                K.tt("pool", ohg[:], B_["of"][:, 0:4, :], B_["ob"][:, 0:4, :], ALU.add, [B_["of"], B_["ob"]], [ohg])
                K.tt("pool", sqh[:], ohg[:], ohg[:], ALU.mult, [ohg], [sqh])
                K.mm(pA[:, :], onesf, sqh[:].rearrange("p h t -> p (h t)"), [cst, sqh], [pA])
                K.act(rsth[:].rearrange("p h t -> p (h t)"), pA[:, :], AF.Ln, [pA, epsc], [rsth], bias=epsc[:, 1:2], scale=1.0 / 128.0)
                K.act(rsth[:], rsth[:], AF.Exp, [rsth], [rsth], scale=-0.5)
                K.act(sog[:], B_["og"][:], AF.Sigmoid, [B_["og"]], [sog])
                K.tt("pool", sog[:], sog[:], B_["og"][:], ALU.mult, [sog, B_["og"]], [sog])
                K.tt("dve", ohg[:], ohg[:], rsth[:], ALU.mult, [ohg, rsth], [ohg])
                K.stt(yT[:, 0:4, :], ohg[:], ptab[:, PT_NW:PT_NW + 1], sog[:], ALU.mult, ALU.mult, [ohg, ptab, sog], [yT])
                K.tt("pool", ysb[:], B_["of"][:, 4:8, :], B_["ob"][:, 4:8, :], ALU.add, [B_["of"], B_["ob"]], [ysb])
                K.mm(pB[:, :], blk64, ysb[:].rearrange("p j t -> p (j t)"), [cst, ysb], [pB])
                K.stt(ycen[:], pB[:, :].rearrange("p (j t) -> p j t", j=4), -1.0 / 64.0, ysb[:], ALU.mult, ALU.add, [pB, ysb], [ycen])
                K.tt("pool", sq2[:], ycen[:], ycen[:], ALU.mult, [ycen], [sq2])
                K.mm(pB[:, :], blk64, sq2[:].rearrange("p j t -> p (j t)"), [cst, sq2], [pB])
                K.act(rstd2[:].rearrange("p j t -> p (j t)"), pB[:, :], AF.Ln, [pB, epsc], [rstd2], bias=epsc[:, 2:3], scale=1.0 / 64.0)
                K.act(rstd2[:], rstd2[:], AF.Exp, [rstd2], [rstd2], scale=-0.5)
                K.tt("dve", ycen[:], ycen[:], rstd2[:], ALU.mult, [ycen, rstd2], [ycen])
                for jj in range(4):
                    K.ts("pool", yn2[:, jj, :], ycen[:, jj, :], ptab[:, PT_LNW + jj:PT_LNW + jj + 1], ptab[:, PT_LNB + jj:PT_LNB + jj + 1],
                         ALU.mult, ALU.add, [ycen, ptab], [yn2])
                K.tt("pool", yn2[:], yn2[:], B_["ex"][:, 0:4, :], ALU.add, [yn2, B_["ex"]], [yn2])
                K.tt("dve", yT[:, 4:8, :], yn2[:], B_["ex"][:, 4:8, :], ALU.mult, [yn2, B_["ex"]], [yT])
                if debug:
                    K.cp("pool", B_["dbg"][:], yT[:], [yT], [B_["dbg"]])
                    K.dma(dbg["yT"].ap[j], B_["dbg"][:], [B_["dbg"]], [dbg["yT"]])
                for dh in range(2):
                    bank = pC if dh == 0 else pD
                    with K.atomic():
                        for m in range(8):
                            K.mm(bank[:, :], yT[:, m, :], woutb[:, m, dh * 512:(dh + 1) * 512], [yT, woutb], [bank], start=(m == 0), stop=(m == 7))
                    K.tt("dve", h1[:, dh * 512:(dh + 1) * 512], bank[:, :], gb[:, 0, dh * 512:(dh + 1) * 512], ALU.mult, [bank, gb], [h1])
                K.stt(h1[:], xt[:], ALPHA, h1[:], ALU.mult, ALU.add, [xt, h1], [h1])
                ln_stats2(K, nc, h1, h1.ap, st1, epsc, 1)
                K.ts("dve", x1t[:], h1[:], st1[:, 12:13], st1[:, 15:16], ALU.subtract, ALU.mult, [h1, st1], [x1t])
                K.tt("pool", x1t[:], x1t[:], lnb_[:, 0, :], ALU.mult, [x1t, lnb_], [x1t])
                K.tt("pool", x1t[:], x1t[:], lnb_[:, 1, :], ALU.add, [x1t, lnb_], [x1t])
                K.dma(x1_d[j * 128:(j + 1) * 128, :], x1t[:], [x1t], [x1_d])
                if debug:
                    K.dma(dbg["x1"][j * 128:(j + 1) * 128, :], x1t[:], [x1t], [dbg["x1"]])
        K.run_streams([lambda s=s: pc_stream(s) for s in range(NS)])
        K.barrier()
        K.stack.close()

    if upto < 1:
        return nc, K
    run_pass(0)
    if upto < 2:
        return nc, K
    run_pass(1)
    if upto < 2.5:
        return nc, K
    run_pc()
    if upto < 3:
        return nc, K

    GT = 2
    GC = GT * 128
    K.stack = ExitStack()
    wgb = K.sb([128, 8, DFF], BF16, "wgb")
    wub = K.sb([128, 8, DFF], BF16, "wub")
    wdb = K.sb([128, NFT, D], BF16, "wdb")
    outer4 = K.stack
    K.stack = ExitStack()
    stg = [K.sb([128, DFF], F32, f"stg{i}") for i in range(2)]
    si = 0
    for (wd_, wb_, nk, ncol) in ((wg_d, wgb, 8, DFF), (wu_d, wub, 8, DFF), (wd_d, wdb, NFT, D)):
        v = wd_.ap.rearrange("(k p) c -> p k c", p=128)
        for kk_ in range(nk):
            s = stg[si % 2]
            K.dma(s[:, 0:ncol], v[:, kk_, :], [wd_], [s])
            K.cp(("act", "dve", "pool")[si % 3], wb_[:, kk_, :], s[:, 0:ncol], [s], [wb_])
            si += 1
    K.barrier()
    K.stack.close()
    K.stack = outer4
    ln2bc = K.sb([128, 2, D], F32, "ln2bc")
    K.dma(ln2bc[:, 0, :], lnrow_d.ap[2:3, :].partition_broadcast(128), [lnrow_d], [ln2bc])
    K.dma(ln2bc[:, 1, :], lnrow_d.ap[3:4, :].partition_broadcast(128), [lnrow_d], [ln2bc])
    x1g = [K.sb([128, GT, D], F32, f"x1g{i}") for i in range(2)]
    u2T = [K.sb([128, 8, GC], BF16, f"u2T{i}") for i in range(2)]
    hT = K.sb([128, NFT, GC], BF16, "hT")
    xnb2 = [K.sb([128, D], BF16, "xnb2_0")] * 2
    st2 = [K.sb([128, 16], F32, f"st2_{i}") for i in range(2)]
    sgl = [K.sb([128, GC], F32, f"sgl{i}") for i in range(2)]
    h2 = [K.sb([128, D], F32, f"h2_{i}") for i in range(2)]
    st3 = [K.sb([128, 16], F32, f"st3_{i}") for i in range(2)]
    ngrp = (NTX + GT - 1) // GT
    tcount = 0
    for g in range(ngrp):
        nt = min(GT, NTX - g * GT)
        ncol = nt * 128
        xg = x1g[g % 2]
        ut = u2T[g % 2]
        for i in range(nt):
            t = g * GT + i
            K.dma(xg[:, i, :], x1_d[t * 128:(t + 1) * 128, :], [x1_d], [xg])
        for i in range(nt):
            st = st2[tcount % 2]
            xb = xnb2[tcount % 2]
            tcount += 1
            ln_stats2(K, nc, xg, xg[:, i, :], st, epsc, 0)
            K.ts("dve", xb[:], xg[:, i, :], st[:, 12:13], st[:, 15:16], ALU.subtract, ALU.mult, [xg, st], [xb])
            tb = psb(7)
            for dk in range(8):
                K.tr(tb[:, dk * 128:(dk + 1) * 128], xb[:, dk * 128:(dk + 1) * 128], identb, [xb, cstb], [PS[7]])
            for dk in range(8):
                K.act(ut[:, dk, i * 128:(i + 1) * 128], tb[:, dk * 128:(dk + 1) * 128], AF.Identity, [PS[7], opsc, modT], [ut],
                      bias=modT[:, 24 + dk, 0:1], scale=opsc[:, 2, dk:dk + 1])
        for ft in range(NFT):
            bg, bu = PS[(2 * ft) % 4], PS[(2 * ft + 1) % 4]
            for dk in range(8):
                K.mm(bg[:, 0:ncol], wgb[:, dk, ft * 128:(ft + 1) * 128], ut[:, dk, 0:ncol], [wgb, ut], [bg], start=(dk == 0), stop=(dk == 7))
            for dk in range(8):
                K.mm(bu[:, 0:ncol], wub[:, dk, ft * 128:(ft + 1) * 128], ut[:, dk, 0:ncol], [wub, ut], [bu], start=(dk == 0), stop=(dk == 7))
            sg_ = sgl[ft % 2]
            K.act(sg_[:, 0:ncol], bg[:, 0:ncol], AF.Sigmoid, [bg], [sg_])
            K.tt("dve", sg_[:, 0:ncol], sg_[:, 0:ncol], bg[:, 0:ncol], ALU.mult, [sg_, bg], [sg_])
            K.tt("dve", hT[:, ft, 0:ncol], sg_[:, 0:ncol], bu[:, 0:ncol], ALU.mult, [sg_, bu], [hT])
        for i in range(nt):
            t = g * GT + i
            hh = h2[t % 2]
            st = st3[t % 2]
            for dh in range(2):
                bank = PS[4 + dh]
                for ft in range(NFT):
                    K.mm(bank[:, :], hT[:, ft, i * 128:(i + 1) * 128], wdb[:, ft, dh * 512:(dh + 1) * 512], [hT, wdb], [bank],
                         start=(ft == 0), stop=(ft == NFT - 1))
                K.tt("dve", hh[:, dh * 512:(dh + 1) * 512], bank[:, :], gb[:, 1, dh * 512:(dh + 1) * 512], ALU.mult, [bank, gb], [hh])
            K.stt(hh[:], xg[:, i, :], ALPHA, hh[:], ALU.mult, ALU.add, [xg, hh], [hh])
            ln_stats2(K, nc, hh, hh.ap, st, epsc, 1)
            K.ts("dve", hh[:], hh[:], st[:, 12:13], st[:, 15:16], ALU.subtract, ALU.mult, [hh, st], [hh])
            K.tt("pool", hh[:], hh[:], ln2bc[:, 0, :], ALU.mult, [hh, ln2bc], [hh])
            K.tt("pool", hh[:], hh[:], ln2bc[:, 1, :], ALU.add, [hh, ln2bc], [hh])
            K.dma(out_d[t * 128:(t + 1) * 128, :], hh[:], [hh], [out_d])
    K.barrier()
    K.stack.close()
    return nc, K


def ln_stats2(K, nc, src, ap, st, epsc, eps_col):
    K.op("dve", lambda: nc.vector.bn_stats(st[:, 0:6], ap[:, 0:512]), [src], [st])
    K.op("dve", lambda: nc.vector.bn_stats(st[:, 6:12], ap[:, 512:1024]), [src], [st])
    K.op("dve", lambda: nc.vector.bn_aggr(st[:, 12:14], st[:, 0:12]), [st], [st])
    K.act(st[:, 14:15], st[:, 13:14], AF.Ln, [st, epsc], [st], bias=epsc[:, eps_col:eps_col + 1])
    K.act(st[:, 15:16], st[:, 14:15], AF.Exp, [st], [st], scale=-0.5)


def _consts():
    c = np.zeros((128, NCONST, 128), np.float32)
    s = np.arange(128)[:, None]
    t = np.arange(128)[None, :]
    c[:, C_ID] = (s == t)
    c[:, C_M32F] = (s // 32 == t // 32) & (s <= t)
    c[:, C_M32B] = (s // 32 == t // 32) & (s >= t)
    c[:, C_MSF] = (s // 64 == t // 64) & (s < t)
    c[:, C_MIF] = (s // 64 == t // 64) & (s <= t)
    c[:, C_MSB] = (s // 64 == t // 64) & (s > t)
    c[:, C_MIB] = (s // 64 == t // 64) & (s >= t)
    c[:, C_BLK64] = (s // 64 == t // 64)
    c[:, C_ONES] = 1.0
    c[:, C_RST32] = np.broadcast_to((t % 32 != 0), (128, 128))
    c[:, C_RST64] = np.broadcast_to((t % 64 != 0), (128, 128))
    rm = np.zeros((128, 128), np.float32)
    for k in range(4):
        rm[:, k] = (np.arange(128) // 32 == k)
    for k in range(2):
        rm[:, 4 + k] = (np.arange(128) // 64 == k)
    c[:, C_ROWM] = rm
    return c


def _fm(v, nt):
    return np.ascontiguousarray(np.asarray(v, np.float32).reshape(nt, 128).T)


def _ptab(inp):
    pt = np.zeros((128, NPT), np.float32)
    lbl = np.asarray(inp["hgrn_lb_logits"], np.float32)
    for d in range(2):
        pt[:, PT_L0 + 4 * d:PT_L0 + 4 * d + 4] = _fm(lbl[0, d], 4)
        pt[:, PT_L1 + 4 * d:PT_L1 + 4 * d + 4] = _fm(lbl[1, d], 4)
    pt[:, PT_NW] = np.asarray(inp["hgrn_norm_w"], np.float32)[0]
    mu = np.zeros(14 * 128, np.float32)
    mu[:1760] = np.asarray(inp["rwkv_mu"], np.float32)[0]
    pt[:, PT_MU:PT_MU + 14] = _fm(mu, 14)
    ch = np.arange(14 * 128)
    valid = ch < 1760
    masks = [ch < 440, (ch >= 440) & (ch < 880), (ch >= 880) & (ch < 1320), (ch >= 1320) & valid, ch < 880, (ch >= 880) & valid]
    for i, m in enumerate(masks):
        pt[:, PT_ML + 14 * i:PT_ML + 14 * (i + 1)] = _fm(m.astype(np.float32), 14)
    for d in range(2):
        pt[:, PT_W0 + 4 * d:PT_W0 + 4 * d + 4] = _fm(inp["rwkv_w0"][0, d], 4)
        pt[:, PT_A0 + 4 * d:PT_A0 + 4 * d + 4] = _fm(inp["rwkv_a0"][0, d], 4)
    pt[:, PT_KK:PT_KK + 4] = _fm(inp["rwkv_k_k"][0], 4)
    pt[:, PT_KA:PT_KA + 4] = _fm(inp["rwkv_k_a"][0], 4)
    pt[:, PT_RK:PT_RK + 4] = _fm(np.asarray(inp["rwkv_r_k"])[0].reshape(512), 4)
    pt[:, PT_LNW:PT_LNW + 4] = _fm(inp["rwkv_lnx_w"][0], 4)
    pt[:, PT_LNB:PT_LNB + 4] = _fm(inp["rwkv_lnx_b"][0], 4)
    pt[:, PT_BADA:PT_BADA + 48] = _fm(inp["b_ada"][0], 48)
    return pt


def _shared_maps(inp):
    f = lambda a: np.ascontiguousarray(np.asarray(a, np.float32))
    wl4 = np.zeros((128, 4, 512), np.float32)
    wl4[0:32, 0] = inp["rwkv_w2"][0, 0]
    wl4[32:64, 1] = inp["rwkv_w2"][0, 1]
    wl4[64:96, 2] = inp["rwkv_a2"][0, 0]
    wl4[96:128, 3] = inp["rwkv_a2"][0, 1]
    lnrows = np.stack([f(inp["ln1_g"])[0], f(inp["ln1_b"])[0], f(inp["ln2_g"])[0], f(inp["ln2_b"])[0]], 0)
    return {
        "w_ada": f(inp["w_ada"])[0], "b_ada_row": f(inp["b_ada"]), "w_in": f(inp["w_in"])[0], "ptab": _ptab(inp),
        "consts": _consts(), "wl4": wl4, "g2": f(inp["rwkv_g2"])[0], "w_out": f(inp["w_out"])[0],
        "lnrows": np.ascontiguousarray(lnrows), "w_gate": f(inp["w_ffn_gate"])[0], "w_up": f(inp["w_ffn_up"])[0],
        "w_down": f(inp["w_ffn_down"])[0],
    }


def _core_map(inp, shared, b):
    m = dict(shared)
    m["x"] = np.ascontiguousarray(np.asarray(inp["x"][b], np.float32))
    m["ctx"] = np.ascontiguousarray(np.asarray(inp["ctx"][b], np.float32))
    cv = np.zeros((128, 16), np.float32)
    cv[:, 0::2] = np.asarray(inp["c"][b], np.float32).reshape(8, 128).T
    cv[:, 1::2] = np.asarray(inp["c_ctx"], np.float32).reshape(8, 128).T
    m["cv"] = cv
    return m


_NC_CACHE = {}


def kernel(**inputs):
    x = np.asarray(inputs["x"])
    B, T, _ = x.shape
    if T not in _NC_CACHE:
        _NC_CACHE[T] = build(T)[0]
    nc = _NC_CACHE[T]
    shared = _shared_maps(inputs)
    in_maps = [_core_map(inputs, shared, b) for b in range(B)]
    res = run_bass_kernel_spmd(nc, in_maps, core_ids=list(range(B)))
    return np.stack([np.asarray(r["out"], np.float32) for r in res.results], 0)
```

```python
from contextlib import ExitStack
import numpy as np
import concourse.bass as bass
import concourse.mybir as mybir
from concourse.bass_utils import run_bass_kernel_spmd

F32 = mybir.dt.float32
BF16 = mybir.dt.bfloat16
ALU = mybir.AluOpType
AF = mybir.ActivationFunctionType

D = 1024
CTX = 256
NCT = 34
ZC = NCT * 128
IN_COLS = 4320
DFF = 2816
NFT = DFF // 128
LWS = 0.6065306597126334
ALPHA = 2.0 ** 0.25

C_ID, C_M32F, C_M32B, C_MSF, C_MIF, C_MSB, C_MIB, C_BLK64, C_ONES, C_RST32, C_RST64, C_ROWM = range(12)
NCONST = 12
PT_L0 = 0
PT_L1 = 8
PT_NW = 16
PT_MU = 17
PT_ML = 31
PT_W0 = 115
PT_A0 = 123
PT_KK = 131
PT_KA = 135
PT_RK = 139
PT_LNW = 143
PT_LNB = 147
PT_BADA = 151
NPT = 199


class Buf:
    __slots__ = ("ap", "w", "r", "name", "tw", "tr", "root")

    def __init__(self, ap, name="", root=None):
        self.root = root if root is not None else self
        self.ap = ap
        self.w = {}
        self.r = {}
        self.name = name
        self.tw = 0.0
        self.tr = 0.0

    def __getitem__(self, k):
        return self.ap[k]


class KB:
    NR = 8

    def __init__(self, nc):
        self.nc = nc
        self.E = {"pe": nc.tensor, "dve": nc.vector, "act": nc.scalar, "pool": nc.gpsimd, "sp": nc.sync}
        self.sems = []
        self.semval = []
        self.esem = {e: self._newsem("c_" + e) for e in self.E}
        self.dsem = {"sp": [self._newsem(f"d_sp{i}") for i in range(self.NR)]}
        self.didx = {"sp": 0}
        self.seen = {e: {} for e in self.E}
        self.nbuf = 0
        self.nwait = 0
        self.ninst = 0
        self.stack = None
        self._st = None
        self.clk = {}

    def _newsem(self, name):
        self.sems.append(self.nc.alloc_semaphore(name))
        self.semval.append(0)
        return len(self.sems) - 1

    def sb(self, shape, dtype=F32, name=None, perm=False):
        self.nbuf += 1
        name = f"{name or 't'}_{self.nbuf}"
        if perm or self.stack is None:
            h = self.nc.alloc_sbuf_tensor(name, list(shape), dtype)
        else:
            h = self.stack.enter_context(self.nc.sbuf_tensor(name, list(shape), dtype))
        return Buf(h.ap(), name)

    def ps(self, shape, dtype=F32, name=None):
        self.nbuf += 1
        return Buf(self.nc.alloc_psum_tensor(f"{name or 'p'}_{self.nbuf}", list(shape), dtype).ap(), name)

    def dram(self, name, shape, dtype=F32, kind="Internal"):
        return Buf(self.nc.dram_tensor(name, list(shape), dtype, kind=kind).ap(), name)

    def _need(self, reads, writes):
        need = {}
        for b in reads:
            for k, v in b.w.items():
                if need.get(k, 0) < v:
                    need[k] = v
        for b in writes:
            for k, v in b.w.items():
                if need.get(k, 0) < v:
                    need[k] = v
            for k, v in b.r.items():
                if need.get(k, 0) < v:
                    need[k] = v
        return need

    def _wait(self, e, need):
        own = self.esem[e]
        seen = self.seen[e]
        eng = self.E[e]
        for k, v in need.items():
            if k == own and e == "pe":
                continue
            if seen.get(k, 0) < v:
                eng.wait_ge(self.sems[k], v)
                seen[k] = v
                self.nwait += 1

    def _commit(self, k, v, reads, writes):
        for b in writes:
            b.w = {k: v}
            b.r = {}
        for b in reads:
            if b.r.get(k, 0) < v:
                b.r[k] = v

    def op(self, e, fn, reads=(), writes=(), cost=0.5):
        reads = [b.root for b in reads]
        writes = [b.root for b in writes]
        self._yield(e, reads, writes)
        self._model(e, reads, writes, cost)
        self._wait(e, self._need(reads, writes))
        ins = fn()
        k = self.esem[e]
        self.semval[k] += 1
        ins.then_inc(self.sems[k], 1)
        self._commit(k, self.semval[k], reads, writes)
        self.ninst += 1
        return ins

    def dma(self, out, in_, reads=(), writes=(), q="sp"):
        reads = [b.root for b in reads]
        writes = [b.root for b in writes]
        self._yield(q, reads, writes)
        self._model(q, reads, writes, 1.0)
        i = self.didx[q]
        self.didx[q] += 1
        k = self.dsem[q][i % self.NR]
        need = self._need(reads, writes)
        if self.semval[k] > 0 and need.get(k, 0) < self.semval[k]:
            need[k] = self.semval[k]
        self._wait(q, need)
        self.semval[k] += 16
        self.E[q].dma_start(out=out, in_=in_).then_inc(self.sems[k], 16)
        self._commit(k, self.semval[k], reads, writes)
        self.ninst += 1


    def run_streams(self, fns):
        import threading
        n = len(fns)
        if n == 1:
            fns[0]()
            return
        st = {"turn": -1, "alive": [True] * n, "err": [], "pend": [None] * n, "started": 0}
        cv = threading.Condition()
        self._st, self._cv = st, cv
        self._tls = threading.local()

        def pick():
            best, bt_ = -1, None
            for i in range(n):
                if st["alive"][i] and st["pend"][i] is not None:
                    t = st["pend"][i]
                    if bt_ is None or t < bt_:
                        best, bt_ = i, t
            st["turn"] = best
            cv.notify_all()
        self._pick = pick

        def all_pending():
            return all((not st["alive"][i]) or st["pend"][i] is not None for i in range(n))
        self._all_pending = all_pending

        def runner(i):
            self._tls.sid = i
            self._tls.atomic = 0
            try:
                fns[i]()
            except BaseException as e:
                st["err"].append(e)
            finally:
                with cv:
                    st["alive"][i] = False
                    st["pend"][i] = None
                    if any(st["alive"]) and all_pending():
                        pick()
        ths = [threading.Thread(target=runner, args=(i,)) for i in range(n)]
        for t in ths:
            t.start()
        for t in ths:
            t.join()
        self._st = None
        if st["err"]:
            raise st["err"][0]

    def _est_start(self, e, reads, writes):
        t = self.clk.get(e, 0.0)
        for b in reads:
            if b.tw > t:
                t = b.tw
        for b in writes:
            if b.tw > t:
                t = b.tw
            if b.tr > t:
                t = b.tr
        return t

    def _model(self, e, reads, writes, cost):
        t = self._est_start(e, reads, writes) + 0.15
        f = t + cost
        if e == "sp":
            self.clk[e] = t + 0.05
            f = t + 2.0 + cost
        else:
            self.clk[e] = f
        for b in writes:
            b.tw = f
        for b in reads:
            if b.tr < f:
                b.tr = f

    def _yield(self, e, reads, writes):
        st = getattr(self, "_st", None)
        if st is None:
            return
        tls = self._tls
        i = getattr(tls, "sid", None)
        if i is None or tls.atomic:
            return
        cv = self._cv
        with cv:
            st["pend"][i] = self._est_start(e, reads, writes)
            if self._all_pending():
                self._pick()
            while st["turn"] != i:
                cv.wait()
            st["turn"] = -1
            st["pend"][i] = None

    def atomic(self):
        kb = self

        class _A:
            def __enter__(self_):
                if getattr(kb, "_st", None) is not None and getattr(kb._tls, "sid", None) is not None:
                    kb._tls.atomic += 1

            def __exit__(self_, *a):
                if getattr(kb, "_st", None) is not None and getattr(kb._tls, "sid", None) is not None:
                    kb._tls.atomic -= 1
        return _A()

    def barrier(self):
        need = {k: v for k, v in enumerate(self.semval) if v > 0}
        for e in self.E:
            self._wait(e, need)

    @staticmethod
    def _n(ap):
        n = 1
        for d in ap.shape[1:]:
            n *= d
        return n

    def mm(self, out, lhsT, rhs, reads, writes, start=True, stop=True):
        nc = self.nc
        c = 0.03 + self._n(rhs) * (4 if rhs.dtype == F32 else 1) / 2400.0
        return self.op("pe", lambda: nc.tensor.matmul(out, lhsT, rhs, start=start, stop=stop), reads, writes, cost=c)

    def tr(self, out, in_, ident, reads, writes):
        nc = self.nc
        return self.op("pe", lambda: nc.tensor.transpose(out, in_, ident), reads, writes, cost=0.09)

    def act(self, out, in_, func, reads, writes, bias=None, scale=None):
        nc = self.nc
        kw = {}
        if bias is not None:
            kw["bias"] = bias
        if scale is not None:
            kw["scale"] = scale
        c = 0.2 + self._n(out) / 1200.0
        return self.op("act", lambda: nc.scalar.activation(out, in_, func, **kw), reads, writes, cost=c)

    def _vc(self, e, out):
        n = self._n(out)
        return (0.1 + n / 500.0) if e == "pool" else (0.07 + n / 900.0)

    def tt(self, e, out, in0, in1, op, reads, writes):
        eng = self.E[e]
        return self.op(e, lambda: eng.tensor_tensor(out, in0, in1, op), reads, writes, cost=self._vc(e, out))

    def ts(self, e, out, in0, s1, s2, op0, op1, reads, writes):
        eng = self.E[e]
        if s2 is None:
            return self.op(e, lambda: eng.tensor_scalar(out, in0, s1, None, op0), reads, writes, cost=self._vc(e, out))
        return self.op(e, lambda: eng.tensor_scalar(out, in0, s1, s2, op0, op1), reads, writes, cost=self._vc(e, out))

    def stt(self, out, in0, scalar, in1, op0, op1, reads, writes):
        nc = self.nc
        return self.op("dve", lambda: nc.vector.scalar_tensor_tensor(out, in0, scalar, in1, op0, op1), reads, writes,
                       cost=0.07 + self._n(out) / 900.0)

    def cp(self, e, out, in_, reads, writes):
        if e == "act":
            nc = self.nc
            return self.op("act", lambda: nc.scalar.copy(out, in_), reads, writes, cost=0.2 + self._n(out) / 1200.0)
        eng = self.E[e]
        return self.op(e, lambda: eng.tensor_copy(out, in_), reads, writes, cost=self._vc(e, out))

    def memset(self, e, ap, val, writes):
        eng = self.E[e]
        return self.op(e, lambda: eng.memset(ap, val), [], writes, cost=self._vc(e, ap))


def _bc(ap, shape):
    return ap.to_broadcast(list(shape))


def build(T, debug=False, upto=9):
    NTX = T // 128
    NE = CTX + T
    nc = bass.Bass("TRN2", target_bir_lowering=False)
    K = KB(nc)
    x_d = K.dram("x", [T, D], kind="ExternalInput")
    ctx_d = K.dram("ctx", [CTX, D], kind="ExternalInput")
    cv_d = K.dram("cv", [128, 16], kind="ExternalInput")
    wada_d = K.dram("w_ada", [D, 6 * D], kind="ExternalInput")
    brow_d = K.dram("b_ada_row", [1, 6 * D], kind="ExternalInput")
    win_d = K.dram("w_in", [D, IN_COLS], kind="ExternalInput")
    ptab_d = K.dram("ptab", [128, NPT], kind="ExternalInput")
    const_d = K.dram("consts", [128, NCONST, 128], kind="ExternalInput")
    wl_d = K.dram("wl4", [128, 4, 512], kind="ExternalInput")
    g2_d = K.dram("g2", [96, 512], kind="ExternalInput")
    wout_d = K.dram("w_out", [D, D], kind="ExternalInput")
    lnrow_d = K.dram("lnrows", [4, D], kind="ExternalInput")
    wg_d = K.dram("w_gate", [D, DFF], kind="ExternalInput")
    wu_d = K.dram("w_up", [D, DFF], kind="ExternalInput")
    wd_d = K.dram("w_down", [DFF, D], kind="ExternalInput")
    out_d = K.dram("out", [T, D], kind="ExternalOutput")
    zT_d = K.dram("zT", [ZC, NE])
    of_d = K.dram("ofwd", [NTX, 128, 8, 128])
    x1_d = K.dram("x1s", [T, D])
    ob_d = K.dram("obwd", [NTX, 128, 8, 128])
    ex_d = K.dram("extra", [NTX, 128, 8, 128])
    dbg = {}
    if debug:
        dbg["zT"] = K.dram("dbg_zT", [ZC, NE], kind="ExternalOutput")
        dbg["yT"] = K.dram("dbg_yT", [NTX, 128, 8, 128], kind="ExternalOutput")
        dbg["x1"] = K.dram("dbg_x1", [T, D], kind="ExternalOutput")

    cst = K.sb([128, NCONST, 128], F32, "cst", perm=True)
    cstb = K.sb([128, NCONST, 128], BF16, "cstb", perm=True)
    ptab = K.sb([128, NPT], F32, "ptab", perm=True)
    modT = K.sb([128, 48, 2], F32, "modT", perm=True)
    opsc = K.sb([128, 3, 8], F32, "opsc", perm=True)
    epsc = K.sb([128, 4], F32, "epsc", perm=True)
    gb = K.sb([128, 2, D], F32, "gb", perm=True)
    drv = K.sb([128, 128], F32, "drv", perm=True)
    DV_LB, DV_OML, DV_NOML = 0, 8, 16
    DV_C0 = 24
    DV_CS = 38
    DV_OMKA = 122
    PS = [K.ps([128, 512], F32, f"bank{i}") for i in range(8)]

    def psb(i):
        return PS[i].ap.bitcast(BF16)

    ident = cst[:, C_ID, :]
    identb = cstb[:, C_ID, :]

    K.dma(cst[:], const_d[:, :, :], [const_d], [cst])
    K.dma(ptab[:], ptab_d[:, :], [ptab_d], [ptab])
    K.cp("dve", cstb[:], cst[:], [cst], [cstb])
    K.memset("pool", epsc[:, 0:1], 1e-6, [epsc])
    K.memset("pool", epsc[:, 1:2], 1e-5, [epsc])
    K.memset("pool", epsc[:, 2:3], 64e-5, [epsc])
    K.memset("pool", epsc[:, 3:4], 1e-24, [epsc])
    K.tt("dve", drv[:, 0:8], ptab[:, PT_L0:PT_L0 + 8], ptab[:, PT_L1:PT_L1 + 8], ALU.subtract, [ptab], [drv])
    K.act(drv[:, DV_LB:DV_LB + 8], drv[:, 0:8], AF.Sigmoid, [drv], [drv])
    K.ts("dve", drv[:, DV_OML:DV_OML + 8], drv[:, DV_LB:DV_LB + 8], -1.0, 1.0, ALU.mult, ALU.add, [drv], [drv])
    K.ts("dve", drv[:, DV_NOML:DV_NOML + 8], drv[:, DV_OML:DV_OML + 8], -1.0, None, ALU.mult, None, [drv], [drv])
    K.ts("dve", drv[:, DV_C0:DV_C0 + 14], ptab[:, PT_MU:PT_MU + 14], -1.0, 1.0, ALU.mult, ALU.add, [ptab], [drv])
    for i in range(6):
        K.tt("dve", drv[:, DV_CS + 14 * i:DV_CS + 14 * (i + 1)], ptab[:, PT_MU:PT_MU + 14],
             ptab[:, PT_ML + 14 * i:PT_ML + 14 * (i + 1)], ALU.mult, [ptab], [drv])
    K.ts("dve", drv[:, DV_OMKA:DV_OMKA + 4], ptab[:, PT_KA:PT_KA + 4], -1.0, 1.0, ALU.mult, ALU.add, [ptab], [drv])

    if upto == 0.1:
        K.barrier()
        return nc, K
    K.stack = ExitStack()
    winb = K.sb([128, 8, ZC], BF16, "winb")
    cv = K.sb([128, 16], F32, "cv")
    cvs = K.sb([128, 16], F32, "cvs")
    K.dma(cv[:], cv_d[:, :], [cv_d], [cv])
    outer0 = K.stack
    K.stack = ExitStack()
    brow = K.sb([1, 4, 512], F32, "brow")
    for i_, eg_ in enumerate((4, 5, 10, 11)):
        K.dma(brow[0:1, i_, :], brow_d[0:1, eg_ * 512:(eg_ + 1) * 512], [brow_d], [brow])
    K.act(cvs[:], cv[:], AF.Sigmoid, [cv], [cvs])
    K.tt("dve", cvs[:], cvs[:], cv[:], ALU.mult, [cvs, cv], [cvs])
    wa = [K.sb([128, 8, 512], F32, f"wa{i}") for i in range(2)]
    wada_v = wada_d.ap.rearrange("(k p) e -> p k e", p=128)
    grow = K.sb([1, 512], F32, "grow")
    for eg in range(12):
        w = wa[eg % 2]
        K.dma(w[:], wada_v[:, :, eg * 512:(eg + 1) * 512], [wada_d], [w])
        bank = PS[eg % 2]
        for j in range(4):
            for dk in range(8):
                K.mm(bank[:, 2 * j:2 * j + 2], w[:, dk, j * 128:(j + 1) * 128], cvs[:, 2 * dk:2 * dk + 2], [w, cvs], [bank],
                     start=(dk == 0), stop=(dk == 7))
        for j in range(4):
            et = eg * 4 + j
            K.ts("dve", modT[:, et, :], bank[:, 2 * j:2 * j + 2], ptab[:, PT_BADA + et:PT_BADA + et + 1], None, ALU.add, None,
                 [bank, ptab], [modT])
        if eg in (4, 5, 10, 11):
            gi = 0 if eg < 6 else 1
            half = eg % 2 if eg < 6 else (eg - 10)
            rb_ = PS[2]
            for dk in range(8):
                K.mm(rb_[0:1, 0:512], cvs[:, 2 * dk:2 * dk + 1], w[:, dk, :], [w, cvs], [rb_], start=(dk == 0), stop=(dk == 7))
            K.tt("dve", grow[:], rb_[0:1, 0:512], brow[0:1, (4, 5, 10, 11).index(eg), :], ALU.add, [rb_, brow], [grow])
            bb = PS[3]
            K.mm(bb[:, 0:512], cst[0:1, C_ONES, :], grow[:], [cst, grow], [bb])
            K.cp("act", gb[:, gi, half * 512:(half + 1) * 512], bb[:, 0:512], [bb], [gb])
    K.ts("dve", opsc[:, 0, :], modT[:, 8:16, 0], 1.0, None, ALU.add, None, [modT], [opsc])
    K.ts("dve", opsc[:, 1, :], modT[:, 8:16, 1], 1.0, None, ALU.add, None, [modT], [opsc])
    K.ts("dve", opsc[:, 2, :], modT[:, 32:40, 0], 1.0, None, ALU.add, None, [modT], [opsc])
    K.barrier()
    if upto == 0.2:
        return nc, K
    K.stack.close()
    K.stack = ExitStack()
    wst = [K.sb([128, IN_COLS], F32, f"wst{i}") for i in range(2)]
    win_v = win_d.ap.rearrange("(k p) c -> p k c", p=128)
    K.memset("pool", winb[:, :, IN_COLS:ZC], 0.0, [winb])
    for dk in range(8):
        s = wst[dk % 2]
        K.dma(s[:], win_v[:, dk, :], [win_d], [s])
        K.cp("act" if dk % 2 else "dve", winb[:, dk, 0:IN_COLS], s[:], [s], [winb])

    K.barrier()
    if upto == 0.3:
        return nc, K
    K.stack.close()
    K.stack = outer0
    xts = [K.sb([128, D], F32, f"xt{i}") for i in range(2)]
    xnb = [K.sb([128, D], BF16, f"xnb{i}") for i in range(2)]
    stt_ = [K.sb([128, 16], F32, f"st{i}") for i in range(2)]

    def ln_stats(src_ap, src_bufs, st, eps_col):
        K.op("dve", lambda: nc.vector.bn_stats(st[:, 0:6], src_ap[:, 0:512]), src_bufs, [st])
        K.op("dve", lambda: nc.vector.bn_stats(st[:, 6:12], src_ap[:, 512:1024]), src_bufs, [st])
        K.op("dve", lambda: nc.vector.bn_aggr(st[:, 12:14], st[:, 0:12]), [st], [st])
        K.act(st[:, 14:15], st[:, 13:14], AF.Ln, [st, epsc], [st], bias=epsc[:, eps_col:eps_col + 1])
        K.act(st[:, 15:16], st[:, 14:15], AF.Exp, [st], [st], scale=-0.5)

    def modulate_T(src, i, uT, col0, sc_ap, sh_ap, tbank):
        st = stt_[i % 2]
        xb = xnb[i % 2]
        ln_stats(src.ap, [src], st, 0)
        K.ts("dve", xb[:], src[:], st[:, 12:13], st[:, 15:16], ALU.subtract, ALU.mult, [src, st], [xb])
        tb = psb(tbank)
        for dk in range(8):
            K.tr(tb[:, dk * 128:(dk + 1) * 128], xb[:, dk * 128:(dk + 1) * 128], identb, [xb, cstb], [PS[tbank]])
        for dk in range(8):
            K.act(uT[:, dk, col0:col0 + 128], tb[:, dk * 128:(dk + 1) * 128], AF.Identity, [PS[tbank], opsc, modT], [uT],
                  bias=sh_ap(dk), scale=sc_ap(dk))

    uTs = [K.sb([128, 8, 512], BF16, f"uT{i}") for i in range(2)]
    zsb = [K.sb([128, 512], F32, f"zsb{i}") for i in range(4)]
    groups = [("ctx", 0, 2)] + [("x", g * 4, min(4, NTX - g * 4)) for g in range((NTX + 3) // 4)]
    ti = 0
    zi = 0
    for gi_, (kind, t0, nt) in enumerate(groups):
        uT = uTs[gi_ % 2]
        src_d = ctx_d if kind == "ctx" else x_d
        mj = 1 if kind == "ctx" else 0
        for i in range(nt):
            xt = xts[ti % 2]
            K.dma(xt[:], src_d[(t0 + i) * 128:(t0 + i + 1) * 128, :], [src_d], [xt])
            modulate_T(xt, ti, uT, i * 128, lambda dk: opsc[:, mj, dk:dk + 1], lambda dk: modT[:, dk, mj:mj + 1], 7)
            ti += 1
        ncol = nt * 128
        ecol0 = (0 if kind == "ctx" else CTX) + t0 * 128
        import os
        ZD = int(os.environ.get("ZDBG", "0"))
        if ZD == 1:
            continue
        for ct in range(NCT):
            bank = PS[ct % 4]
            for dk in range(8):
                K.mm(bank[:, 0:ncol], winb[:, dk, ct * 128:(ct + 1) * 128], uT[:, dk, 0:ncol], [winb, uT], [bank],
                     start=(dk == 0), stop=(dk == 7))
            z = zsb[zi % 4]
            zi += 1
            K.cp("act" if ct % 2 else "dve", z[:, 0:ncol], bank[:, 0:ncol], [bank], [z])
            if ZD == 2:
                continue
            if ZD != 4:
                K.dma(zT_d[ct * 128:(ct + 1) * 128, ecol0:ecol0 + ncol], z[:, 0:ncol], [z], [zT_d])
            if debug and ZD != 3:
                K.dma(dbg["zT"][ct * 128:(ct + 1) * 128, ecol0:ecol0 + ncol], z[:, 0:ncol], [z], [dbg["zT"]])
    K.barrier()
    K.stack.close()

    def run_pass(dirn):
        final = dirn == 1
        K.stack = ExitStack()
        wlb = K.sb([128, 4, 512], BF16, "wlb")
        g2b = K.sb([96, 512], BF16, "g2b")
        outer = K.stack
        K.stack = ExitStack()
        tmpw = K.sb([128, 4, 512], F32, "tmpw")
        K.dma(tmpw[:], wl_d[:, :, :], [wl_d], [tmpw])
        K.cp("dve", wlb[:], tmpw[:], [tmpw], [wlb])
        tmpg = K.sb([96, 512], F32, "tmpg")
        K.dma(tmpg[:], g2_d[:, :], [g2_d], [tmpg])
        K.cp("dve", g2b[:], tmpg[:], [tmpg], [g2b])
        K.barrier()
        K.stack.close()
        K.stack = outer
        S = K.sb([128, 4, 128], F32, "S")
        Sbf = [K.sb([128, 4, 128], BF16, f"Sbf{i}") for i in range(4)]
        H = [K.sb([128, 64], F32, f"H{j}") for j in range(4)]
        Hbd = [[K.sb([128, 128], BF16, f"Hbd{j}_{c}") for c in range(2)] for j in range(4)]
        MTbd = [[K.sb([128, 128], F32, f"MT{j}_{c}") for c in range(2)] for j in range(4)]
        K.memset("pool", S[:], 0.0, [S])
        for j in range(4):
            K.memset("pool", H[j][:], 0.0, [H[j]])
            for c in range(2):
                K.memset("pool", Hbd[j][c][:], 0.0, [Hbd[j][c]])
                K.memset("pool", MTbd[j][c][:], 0.0, [MTbd[j][c]])
        for i in range(4):
            K.memset("pool", Sbf[i][:], 0.0, [Sbf[i]])
        ld_q = K.sb([128, 4, 128], F32, "ldq")
        ld_f = K.sb([128, 4, 128], F32, "ldf")
        ld_i = K.sb([128, 4, 128], F32, "ldi")
        zp = K.sb([128, 14, 4, 66], F32, "zp")
        zc = Buf(zp[:].rearrange("p a r c -> p (a r c)")[:, 0:14 * 130].rearrange("p (a t) -> p a t", t=130), "zc")
        K.memset("pool", zp[:], 0.0, [zp])
        zrl = K.sb([128, 14, 128], F32, "zrl")
        mshg = cstb[:, C_M32B if dirn else C_M32F, :]
        msi = cstb[:, C_MSB:C_MSB + 2, :] if dirn else cstb[:, C_MSF:C_MSF + 2, :]
        ms32 = cst[:, C_MSB, :] if dirn else cst[:, C_MSF, :]
        mnt32 = cst[:, C_MSF, :] if dirn else cst[:, C_MSB, :]
        id32r = K.sb([128, 2, 128], F32, "id32r")
        msir = K.sb([128, 2, 2, 128], BF16, "msir")
        mshgr = K.sb([128, 4, 128], BF16, "mshgr")
        for e_ in range(2):
            K.cp("pool", id32r[:, e_, :], ident, [cst], [id32r])
            K.cp("pool", msir[:, e_, :, :], msi, [cstb], [msir])
        for h_ in range(4):
            K.cp("pool", mshgr[:, h_, :], mshg, [cstb], [mshgr])
        rst32 = cst[:, C_RST32, :]
        rst64 = cst[:, C_RST64, :]
        blk64 = cst[:, C_BLK64, :]

        if dirn == 0:
            order = [("ctx", 0), ("ctx", 1)] + [("x", j) for j in range(NTX)]
        else:
            order = [("ctx", 1), ("ctx", 0)] + [("x", j) for j in range(NTX - 1, -1, -1)]

        def T32(name, shape=(128, 4, 128)):
            return K.sb(list(shape), F32, name)

        def T16(name, shape=(128, 4, 128)):
            return K.sb(list(shape), BF16, name)
        P32 = [T32(f"w32_{i}") for i in range(11)]
        H32 = [T32(f"h32_{i}") for i in range(8)]
        sgq = qh = H32[0]
        sgf = H32[1]
        ff = lg = H32[2]
        kdh = H32[3]
        bcum = H32[4]
        tmp1 = H32[5]
        e1 = H32[6]
        e2 = H32[7]
        sw = sq = P32[0]
        asg = P32[1]
        cs = kdr = P32[2]
        csm = rn = P32[3]
        E1 = P32[4]
        E2 = P32[5]
        tmpa = P32[6]
        bvec = P32[7]
        E3 = P32[8]
        kk = P32[9]
        kkn = P32[10]
        qb, kb, ib16, ATm = [T16(n) for n in ("qb", "kb", "ib16", "ATm")]
        kbTm = [T16(f"kbTm{c}") for c in range(4)]
        iT = T16("iT")
        stmp = T32("stmp")
        Lb = K.sb([128, 128], BF16, "Lb")
        vb16 = T16("vb16")
        if final:
            sgd = K.sb([96, 128], BF16, "sgd")
            asg2, tmpa2, bonp = T32("asg2"), T32("tmpa2"), T32("bonp")
        osb = [K.sb([128, 8, 128], F32, f"osb{i}") for i in range(2)]
        exb = [K.sb([128, 8, 128], F32, f"exb{i}") for i in range(2)] if final else None
        AR = [K.sb([128, 4, 2, 128], BF16, f"AR{i}") for i in range(2)]
        bt = [T16(f"bt{i}") for i in range(2)]
        kt = [T16(f"kt{i}") for i in range(2)]
        aT = [T16(f"aT{i}") for i in range(2)]
        vT = [T16(f"vT{i}") for i in range(2)]
        btTm = [[T16(f"btTm{i}_{c}") for c in range(2)] for i in range(2)]
        ktTm = [[T16(f"ktTm{i}_{c}") for c in range(2)] for i in range(2)]
        gam = [K.sb([128, 4, 2], F32, f"gam{i}") for i in range(2)]
        Amat = [K.sb([128, 2, 2, 2, 128], BF16, f"Amat{i}") for i in range(4)]
        XP = [K.sb([128, 2, 2, 128], F32, f"XP{i}") for i in range(4)]
        XT = [K.sb([128, 2, 128], F32, f"XT{i}") for i in range(4)]
        Pbf = [K.sb([128, 2, 128], BF16, f"Pbf{i}") for i in range(4)]
        AW = [K.sb([128, 2, 128], BF16, f"AW{i}") for i in range(4)]
        AU = [K.sb([128, 2, 128], BF16, f"AU{i}") for i in range(4)]
        QT = [K.sb([128, 128], BF16, f"QT{i}") for i in range(4)]
        AUXH = [Buf(PS[6 + i // 2].ap[:, (i % 2) * 256:(i % 2) * 256 + 256], f"aux{i}", root=PS[6 + i // 2]) for i in range(4)]

        def front_hg(n):
            kind, j = order[n]
            isx = kind == "x"
            fb = n % 2
            ec0 = (0 if kind == "ctx" else CTX) + j * 128

            def hv(ct0):
                return zT_d.ap[ct0 * 128:(ct0 + 4) * 128, ec0:ec0 + 128].rearrange("(h p) t -> p h t", p=128)
            K.dma(ld_q[:], hv(0), [zT_d], [ld_q])
            K.dma(ld_f[:], hv(4 + 4 * dirn), [zT_d], [ld_f])
            K.dma(ld_i[:], hv(12), [zT_d], [ld_i])
            zq, zf, zi_ = ld_q, ld_f, ld_i
            K.act(sgq[:], zq[:], AF.Sigmoid, [zq], [sgq])
            K.act(sgf[:], zf[:], AF.Sigmoid, [zf], [sgf])
            K.tt("pool", qh[:], zq[:], sgq[:], ALU.mult, [zq, sgq], [qh])
            for h in range(4):
                c_ = dirn * 4 + h
                K.ts("dve", ff[:, h, :], sgf[:, h, :], drv[:, DV_OML + c_:DV_OML + c_ + 1], drv[:, DV_LB + c_:DV_LB + c_ + 1],
                     ALU.mult, ALU.add, [sgf, drv], [ff])
                K.act(kdh[:, h, :], sgf[:, h, :], AF.Identity, [sgf, drv], [kdh],
                      scale=drv[:, DV_NOML + c_:DV_NOML + c_ + 1], bias=drv[:, DV_OML + c_:DV_OML + c_ + 1])
            K.act(lg[:], ff[:], AF.Ln, [ff], [lg])
            for h in range(4):
                K.op("dve", lambda h=h: nc.vector.tensor_tensor_scan(bcum[:, h, :], rst32, lg[:, h, :], 0.0, ALU.mult, ALU.add),
                     [lg, cst], [bcum])
            if dirn:
                bv4 = bcum[:].rearrange("p h (c t) -> p (h c) t", t=32)
                K.tt("pool", tmp1[:], lg[:], bcum[:], ALU.subtract, [lg, bcum], [tmp1])
                K.tt("dve", e2[:].rearrange("p h (c t) -> p (h c) t", t=32), tmp1[:].rearrange("p h (c t) -> p (h c) t", t=32),
                     _bc(bv4[:, :, 31:32], [128, 16, 32]), ALU.add, [tmp1, bcum], [e2])
                K.cp("pool", bcum[:], e2[:], [e2], [bcum])
            K.act(e1[:], bcum[:], AF.Exp, [bcum], [e1])
            K.act(e2[:], bcum[:], AF.Exp, [bcum], [e2], scale=-1.0)
            K.tt("dve", qb[:], qh[:], e1[:], ALU.mult, [qh, e1], [qb])
            K.tt("pool", kb[:], kdh[:], e2[:], ALU.mult, [kdh, e2], [kb])
            K.cp("pool", ib16[:], zi_[:], [zi_], [ib16])
            tb = psb(0)
            for h in range(4):
                K.tr(tb[:, h * 128:(h + 1) * 128], kb[:, h, :], identb, [kb, cstb], [PS[0]])
            for h in range(4):
                K.tr(tb[:, 512 + h * 128:512 + (h + 1) * 128], ib16[:, h, :], identb, [ib16, cstb], [PS[0]])
            for c in range(4):
                K.act(kbTm[c][:].rearrange("p h t -> p (h t)"), tb[:, 0:512], AF.Identity, [PS[0], cst], [kbTm[c]],
                      scale=cst[:, C_ROWM, c:c + 1])
            K.cp("act", iT[:].rearrange("p h t -> p (h t)"), tb[:, 512:1024], [PS[0]], [iT])
            if isx:
                for h in range(4):
                    K.mm(PS[0][:, h * 128:(h + 1) * 128], kb[:, h, :], qb[:, h, :], [kb, qb], [PS[0]])
                K.tt("dve", ATm[:], PS[0][:, :].rearrange("p (h t) -> p h t", h=4), mshgr[:], ALU.mult, [PS[0], mshgr], [ATm])
            corder = [0, 1, 2, 3] if dirn == 0 else [3, 2, 1, 0]
            for ci, c in enumerate(corder):
                K.cp("act", Sbf[c][:], S[:], [S], [Sbf[c]])
                kvb = PS[0]
                for h in range(4):
                    K.mm(kvb[:, h * 128:(h + 1) * 128], kbTm[c][:, h, :], iT[:, h, :], [kbTm[c], iT], [kvb])
                dcol = c * 32 + (0 if dirn else 31)
                K.tt("dve", stmp[:], kvb[:, :].rearrange("p (h v) -> p h v", h=4), S[:], ALU.add, [kvb, S], [stmp])
                K.tt("pool", S[:], stmp[:], _bc(e1[:, :, dcol:dcol + 1], [128, 4, 128]), ALU.mult, [stmp, e1], [S])
            if isx:
                ob = PS[0]
                for h in range(4):
                    with K.atomic():
                        K.mm(ob[:, h * 128:(h + 1) * 128], iT[:, h, :], ATm[:, h, :], [iT, ATm], [ob], start=True, stop=False)
                        for c in range(4):
                            K.mm(ob[:, h * 128 + c * 32:h * 128 + (c + 1) * 32], Sbf[c][:, h, :], qb[:, h, c * 32:(c + 1) * 32],
                                 [Sbf[c], qb], [ob], start=False, stop=(c == 3))
                K.cp("act", osb[fb][:, 0:4, :], ob[:, :].rearrange("p (h t) -> p h t", h=4), [ob], [osb[fb]])

        def front_rw(n):
            kind, j = order[n]
            isx = kind == "x"
            fb = n % 2
            ec0 = (0 if kind == "ctx" else CTX) + j * 128
            if isx and n == 2:
                K.memset("pool", zp[:], 0.0, [zp])
            if isx:
                lo = 64 if j > 0 else 0
                hi = 64 if j < NTX - 1 else 0
                if lo == 0 and n != 2:
                    K.memset("pool", zp[:, :, 0, :], 0.0, [zp])
                if hi == 0 and n != 2:
                    K.memset("pool", zp[:, :, 3, :], 0.0, [zp])
                r0 = 1 - lo // 64
                r1 = 3 + hi // 64
                for ct in range(14):
                    src_ = zT_d.ap[(20 + ct) * 128:(21 + ct) * 128, ec0 - lo:ec0 + 128 + hi].rearrange("p (r c) -> p r c", c=64)
                    K.dma(zp[:, ct, r0:r1, 1:65], src_, [zT_d], [zp])
            else:
                lo = 1 if j > 0 else 0
                hi = 1 if j < 1 else 0
                if lo == 0:
                    K.memset("pool", zc[:, :, 0:1], 0.0, [zp])
                if hi == 0:
                    K.memset("pool", zc[:, :, 129:130], 0.0, [zp])
                for g in range(2):
                    src_ = zT_d.ap[(20 + 7 * g) * 128:(27 + 7 * g) * 128, ec0 - lo:ec0 + 128 + hi].rearrange("(h p) t -> p h t", p=128)
                    K.dma(zc[:, 7 * g:7 * g + 7, 1 - lo:129 + hi], src_, [zT_d], [zp])
            if isx:
                for ct in range(14):
                    views = {"L": zp[:, ct, 1:3, 0:64], "R": zp[:, ct, 1:3, 2:66], "U": zp[:, ct, 0:2, 1:65], "D": zp[:, ct, 2:4, 1:65]}
                    cen = zp[:, ct, 1:3, 1:65]
                    lo_, hi_ = ct * 128, ct * 128 + 128
                    kinds = []
                    if lo_ < 440: kinds.append(("L", 0))
                    if hi_ > 440 and lo_ < 880: kinds.append(("R", 1))
                    if hi_ > 880 and lo_ < 1320: kinds.append(("U", 2))
                    if hi_ > 1320: kinds.append(("D", 3))
                    o3 = zrl[:, ct, :].rearrange("p (r c) -> p r c", c=64)
                    K.act(o3, cen, AF.Identity, [zp, drv], [zrl], scale=drv[:, DV_C0 + ct:DV_C0 + ct + 1])
                    for (vn, ki) in kinds:
                        K.stt(o3, views[vn], drv[:, DV_CS + 14 * ki + ct:DV_CS + 14 * ki + ct + 1], o3, ALU.mult, ALU.add, [zp, drv, zrl], [zrl])
            else:
                for ct in range(14):
                    lo_, hi_ = ct * 128, ct * 128 + 128
                    kinds = []
                    if lo_ < 880: kinds.append((zc[:, ct, 0:128], 4))
                    if hi_ > 880: kinds.append((zc[:, ct, 2:130], 5))
                    K.act(zrl[:, ct, :], zc[:, ct, 1:129], AF.Identity, [zp, drv], [zrl], scale=drv[:, DV_C0 + ct:DV_C0 + ct + 1])
                    for (vw, ki) in kinds:
                        K.stt(zrl[:, ct, :], vw, drv[:, DV_CS + 14 * ki + ct:DV_CS + 14 * ki + ct + 1], zrl[:, ct, :], ALU.mult, ALU.add,
                              [zp, drv, zrl], [zrl])
            r_ = zrl[:, 0:4, :]
            k_ = zrl[:, 4:8, :]
            v_ = zrl[:, 8:12, :]
            K.act(Lb[0:64, :], zrl[0:64, 12, :], AF.Tanh, [zrl], [Lb])
            K.cp("pool", Lb[64:128, :], zrl[64:128, 12, :], [zrl], [Lb])
            pw = pa = PS[1]
            for jj in range(4):
                K.mm(pw[:, jj * 128:(jj + 1) * 128], wlb[:, dirn, jj * 128:(jj + 1) * 128], Lb[:], [wlb, Lb], [pw])
            for jj in range(4):
                K.act(sw[:, jj, :], pw[:, jj * 128:(jj + 1) * 128], AF.Sigmoid, [pw, ptab], [sw],
                      bias=ptab[:, PT_W0 + dirn * 4 + jj:PT_W0 + dirn * 4 + jj + 1])
            for jj in range(4):
                K.mm(pa[:, jj * 128:(jj + 1) * 128], wlb[:, 2 + dirn, jj * 128:(jj + 1) * 128], Lb[:], [wlb, Lb], [pa])
            for jj in range(4):
                K.act(asg[:, jj, :], pa[:, jj * 128:(jj + 1) * 128], AF.Sigmoid, [pa, ptab], [asg],
                      bias=ptab[:, PT_A0 + dirn * 4 + jj:PT_A0 + dirn * 4 + jj + 1])
            if final and isx:
                for jj in range(4):
                    K.mm(pa[:, jj * 128:(jj + 1) * 128], wlb[:, 2, jj * 128:(jj + 1) * 128], Lb[:], [wlb, Lb], [pa])
                for jj in range(4):
                    K.act(asg2[:, jj, :], pa[:, jj * 128:(jj + 1) * 128], AF.Sigmoid, [pa, ptab], [asg2],
                          bias=ptab[:, PT_A0 + jj:PT_A0 + jj + 1])
            for jj in range(4):
                K.op("dve", lambda jj=jj: nc.vector.tensor_tensor_scan(cs[:, jj, :], rst64, sw[:, jj, :], 0.0, ALU.mult, ALU.add),
                     [sw, cst], [cs])
            if dirn == 0:
                K.tt("pool", csm[:], cs[:], sw[:], ALU.subtract, [cs, sw], [csm])
            else:
                c8 = cs[:].rearrange("p j (c t) -> p (j c) t", t=64)
                K.tt("dve", csm[:].rearrange("p j (c t) -> p (j c) t", t=64), _bc(c8[:, :, 63:64], [128, 8, 64]), c8, ALU.subtract,
                     [cs], [csm])
                K.tt("pool", cs[:], csm[:], sw[:], ALU.add, [csm, sw], [cs])
            K.act(E1[:], csm[:], AF.Exp, [csm], [E1], scale=-LWS)
            K.act(E2[:], cs[:], AF.Exp, [cs], [E2], scale=-LWS)
            K.act(E3[:], cs[:], AF.Exp, [cs], [E3], scale=LWS)
            goff = 0 if dirn else 63
            K.cp("pool", gam[fb][:], E2[:].rearrange("p j (c t) -> p j c t", t=64)[:, :, :, goff], [E2], [gam[fb]])
            for jj in range(4):
                K.act(kk[:, jj, :], k_[:, jj, :], AF.Identity, [zrl, ptab], [kk], scale=ptab[:, PT_KK + jj:PT_KK + jj + 1])
            K.act(sq[:], kk[:], AF.Square, [kk], [sq])
            K.mm(PS[1][:, :], blk64, sq[:].rearrange("p j t -> p (j t)"), [cst, sq], [PS[1]])
            K.ts("dve", rn[:].rearrange("p j t -> p (j t)"), PS[1][:, :], epsc[:, 3:4], None, ALU.max, None, [PS[1], epsc], [rn])
            K.act(rn[:], rn[:], AF.Ln, [rn], [rn])
            K.act(rn[:], rn[:], AF.Exp, [rn], [rn], scale=-0.5)
            K.tt("dve", kkn[:], kk[:], rn[:], ALU.mult, [kk, rn], [kkn])
            for jj in range(4):
                K.act(tmpa[:, jj, :], asg[:, jj, :], AF.Identity, [asg, ptab, drv], [tmpa],
                      scale=ptab[:, PT_KA + jj:PT_KA + jj + 1], bias=drv[:, DV_OMKA + jj:DV_OMKA + jj + 1])
            K.tt("pool", kdr[:], k_, tmpa[:], ALU.mult, [zrl, tmpa], [kdr])
            K.tt("pool", bvec[:], kkn[:], asg[:], ALU.mult, [kkn, asg], [bvec])
            K.stt(AR[fb][:, :, 0, :], kkn[:], -1.0, E1[:], ALU.mult, ALU.mult, [kkn, E1], [AR[fb]])
            K.tt("dve", AR[fb][:, :, 1, :], r_, E2[:], ALU.mult, [zrl, E2], [AR[fb]])
            K.tt("dve", bt[fb][:], bvec[:], E3[:], ALU.mult, [bvec, E3], [bt[fb]])
            K.tt("pool", kt[fb][:], kdr[:], E3[:], ALU.mult, [kdr, E3], [kt[fb]])
            K.cp("pool", vb16[:], v_, [zrl], [vb16])
            tb0 = psb(1)
            for jj in range(4):
                K.tr(tb0[:, jj * 128:(jj + 1) * 128], AR[fb][:, jj, 0, :], identb, [AR[fb], cstb], [PS[1]])
                K.tr(tb0[:, 512 + jj * 128:512 + (jj + 1) * 128], vb16[:, jj, :], identb, [vb16, cstb], [PS[1]])
            K.cp("act", aT[fb][:].rearrange("p j t -> p (j t)"), tb0[:, 0:512], [PS[1]], [aT[fb]])
            K.cp("act", vT[fb][:].rearrange("p j t -> p (j t)"), tb0[:, 512:1024], [PS[1]], [vT[fb]])
            for jj in range(4):
                K.tr(tb0[:, jj * 128:(jj + 1) * 128], bt[fb][:, jj, :], identb, [bt[fb], cstb], [PS[1]])
                K.tr(tb0[:, 512 + jj * 128:512 + (jj + 1) * 128], kt[fb][:, jj, :], identb, [kt[fb], cstb], [PS[1]])
            for c in range(2):
                K.act(btTm[fb][c][:].rearrange("p j t -> p (j t)"), tb0[:, 0:512], AF.Identity, [PS[1], cst], [btTm[fb][c]],
                      scale=cst[:, C_ROWM, 4 + c:5 + c])
                K.act(ktTm[fb][c][:].rearrange("p j t -> p (j t)"), tb0[:, 512:1024], AF.Identity, [PS[1], cst], [ktTm[fb][c]],
                      scale=cst[:, C_ROWM, 4 + c:5 + c])
            if final and isx:
                K.act(sgd[:], zrl[0:96, 13, :], AF.Sigmoid, [zrl], [sgd])
                for jj in range(4):
                    K.act(tmpa2[:, jj, :], asg2[:, jj, :], AF.Identity, [asg2, ptab, drv], [tmpa2],
                          scale=ptab[:, PT_KA + jj:PT_KA + jj + 1], bias=drv[:, DV_OMKA + jj:DV_OMKA + jj + 1])
                K.tt("pool", tmpa2[:], tmpa2[:], tmpa[:], ALU.add, [tmpa2, tmpa], [tmpa2])
                K.tt("pool", tmpa2[:], tmpa2[:], k_, ALU.mult, [tmpa2, zrl], [tmpa2])
                for jj in range(4):
                    K.stt(bonp[:, jj, :], r_[:, jj, :], ptab[:, PT_RK + jj:PT_RK + jj + 1], tmpa2[:, jj, :], ALU.mult, ALU.mult,
                          [zrl, ptab, tmpa2], [bonp])
                K.mm(PS[1][:, :], blk64, bonp[:].rearrange("p j t -> p (j t)"), [cst, bonp], [PS[1]])
                K.tt("dve", exb[fb][:, 0:4, :], PS[1][:, :].rearrange("p (j t) -> p j t", j=4), v_, ALU.mult, [PS[1], zrl], [exb[fb]])
                pg = PS[1]
                for jj in range(4):
                    K.mm(pg[:, jj * 128:(jj + 1) * 128], g2b[:, jj * 128:(jj + 1) * 128], sgd[:], [g2b, sgd], [pg])
                K.cp("act", exb[fb][:, 4:8, :], pg[:, :].rearrange("p (j t) -> p j t", j=4), [pg], [exb[fb]])
                K.dma(ex_d.ap[j], exb[fb][:], [exb[fb]], [ex_d])

        def pairs(n, k):
            kind, j = order[n]
            isx = kind == "x"
            fb = n % 2
            jj = k
            ba = PS[2 + k]
            bb = AUXH[k]
            am, aw, au, qt, pbf = Amat[k], AW[k], AU[k], QT[k], Pbf[k]
            xp, xt = XP[k], XT[k]
            ar, bt_, kt_, aT_, vT_ = AR[fb], bt[fb], kt[fb], aT[fb], vT[fb]
            for e in range(2):
                ep = slice(e * 64, (e + 1) * 64)
                arf = ar[ep, jj, :, :].rearrange("p a t -> p (a t)")
                K.mm(ba[:, 0:256], bt_[ep, jj, :], arf, [bt_, ar], [ba])
                K.mm(ba[:, 256:512], kt_[ep, jj, :], arf, [kt_, ar], [ba])
                bv_ = ba[:, :].rearrange("p (w a t) -> p w a t", w=2, a=2)
                K.tt("dve", xp[:, e, 0, :], ba[:, 0:128], ms32, ALU.mult, [ba, cst], [xp])
                K.tt("dve", am[:, e, :, :, :], bv_, msir[:], ALU.mult, [ba, msir], [am])
            for e in range(2):
                ep = slice(e * 64, (e + 1) * 64)
                bk = ba if e == 0 else bb
                K.mm(bk[:, 0:128], ar[ep, jj, 0, :], bt_[ep, jj, :], [ar, bt_], [bk])
                K.tt("dve", xt[:, e, :], bk[:, 0:128], mnt32, ALU.mult, [bk, cst], [xt])
            K.cp("pool", xp[:, :, 1, :], id32r[:], [id32r], [xp])
            pav = ba[:, :].rearrange("p (e a t) -> p e a t", e=2, a=2)
            for lvl in range(6):
                last = lvl == 5
                for e in range(2):
                    if not last:
                        K.mm(ba[:, e * 256:(e + 1) * 256], xt[:, e, :], xp[:, e, :, :].rearrange("p a t -> p (a t)"), [xt, xp], [ba])
                    else:
                        K.mm(ba[:, e * 256 + 128:(e + 1) * 256], xt[:, e, :], xp[:, e, 1, :], [xt, xp], [ba])
                if not last:
                    for e in range(2):
                        K.mm(bb[:, e * 128:(e + 1) * 128], xp[:, e, 0, :], xt[:, e, :], [xp, xt], [bb])
                K.tt("dve", xp[:, :, 1, :], pav[:, :, 1, :], xp[:, :, 1, :], ALU.add, [ba, xp], [xp])
                if not last:
                    K.cp("act", xp[:, :, 0, :], pav[:, :, 0, :], [ba], [xp])
                    K.cp("act", xt[:], bb[:, 0:256].rearrange("p (e t) -> p e t", e=2), [bb], [xt])
            K.cp("pool", pbf[:], xp[:, :, 1, :], [xp], [pbf])
            for e in range(2):
                K.mm(ba[:, e * 64:(e + 1) * 64], am[:, e, 1, 0, :], vT_[:, jj, e * 64:(e + 1) * 64], [am, vT_], [ba])
            K.cp("pool", aw[:, :, 0:64], aT_[:, jj, :].rearrange("p (e k) -> p e k", e=2), [aT_], [aw])
            K.cp("act", aw[:, :, 64:128], ba[:, 0:128].rearrange("p (e v) -> p e v", e=2), [ba], [aw])
            for e in range(2):
                K.mm(bb[:, e * 128:(e + 1) * 128], pbf[:, e, :], aw[:, e, :], [pbf, aw], [bb])
            K.cp("dve", au[:], bb[:, 0:256].rearrange("p (e c) -> p e c", e=2), [bb], [au])
            if isx:
                for e in range(2):
                    K.mm(ba[e * 64:(e + 1) * 64, 128:256], au[:, e, 0:64], am[:, e, 0, 1, :], [au, am], [ba])
                K.tt("dve", qt[:], ba[:, 128:256], ar[:, jj, 1, :], ALU.add, [ba, ar], [qt])
            corder2 = [0, 1] if dirn == 0 else [1, 0]
            Hj = H[jj]
            for ci, c in enumerate(corder2):
                hb = Hbd[jj][c]
                mt = MTbd[jj][c]
                for e in range(2):
                    K.cp("act", hb[e * 64:(e + 1) * 64, e * 64:(e + 1) * 64], Hj[e * 64:(e + 1) * 64, :], [Hj], [hb])
                for e in range(2):
                    K.mm(bb[e * 64:(e + 1) * 64, 0:64], au[:, e, 0:64], btTm[fb][c][:, jj, e * 64:(e + 1) * 64], [au, btTm[fb][c]], [bb])
                for e in range(2):
                    K.tt("dve", mt[e * 64:(e + 1) * 64, e * 64:(e + 1) * 64], bb[e * 64:(e + 1) * 64, 0:64],
                         ident[e * 64:(e + 1) * 64, e * 64:(e + 1) * 64], ALU.add, [bb, cst], [mt])
                with K.atomic():
                    K.mm(ba[:, 384:448], mt[:], Hj[:], [mt, Hj], [ba], start=True, stop=False)
                    for e in range(2):
                        ep = slice(e * 64, (e + 1) * 64)
                        K.mm(ba[ep, 384:448], btTm[fb][c][:, jj, ep], au[:, e, 64:128], [btTm[fb][c], au], [ba], start=False, stop=False)
                        K.mm(ba[ep, 384:448], ktTm[fb][c][:, jj, ep], vT_[:, jj, ep], [ktTm[fb][c], vT_], [ba], start=False, stop=True)
                K.ts("dve", Hj[:], ba[:, 384:448], gam[fb][:, jj, c:c + 1], None, ALU.mult, None, [ba, gam[fb]], [Hj])
            if isx:
                with K.atomic():
                    for e in range(2):
                        ep = slice(e * 64, (e + 1) * 64)
                        K.mm(ba[ep, 256:384], au[:, e, 64:128], am[:, e, 0, 1, :], [au, am], [ba], start=True, stop=False)
                        K.mm(ba[ep, 256:384], vT_[:, jj, ep], am[:, e, 1, 1, :], [vT_, am], [ba], start=False, stop=False)
                    for c in range(2):
                        K.mm(ba[:, 256 + c * 64:256 + (c + 1) * 64], Hbd[jj][c][:], qt[:, c * 64:(c + 1) * 64], [Hbd[jj][c], qt], [ba],
                             start=False, stop=(c == 1))
                K.cp("act", osb[fb][:, 4 + jj, :], ba[:, 256:384], [ba], [osb[fb]])

        def tail(n):
            kind, j = order[n]
            if kind != "x":
                return
            fb = n % 2
            K.dma((ob_d if final else of_d).ap[j], osb[fb][:], [osb[fb]], [ob_d if final else of_d])

        import os
        NOIL = os.environ.get("NOIL", "0") == "1"
        NT_ = len(order)
        if NOIL:
            for n in range(NT_):
                front_hg(n); front_rw(n); pairs(n, 0); pairs(n, 1); pairs(n, 2); pairs(n, 3); tail(n)
        else:
            K.run_streams([lambda: front_hg(0), lambda: front_rw(0)])
            for n in range(NT_):
                fns = [lambda n=n, k=k: pairs(n, k) for k in range(4)]
                if n + 1 < NT_:
                    fns.append(lambda n=n: front_hg(n + 1))
                    fns.append(lambda n=n: front_rw(n + 1))
                K.run_streams(fns)
                tail(n)
        K.barrier()
        K.stack.close()

    def run_pc():
        K.stack = ExitStack()
        woutb = K.sb([128, 8, D], BF16, "woutb")
        lnb_ = K.sb([128, 2, D], F32, "ln1bc")
        K.dma(lnb_[:, 0, :], lnrow_d.ap[0:1, :].partition_broadcast(128), [lnrow_d], [lnb_])
        K.dma(lnb_[:, 1, :], lnrow_d.ap[1:2, :].partition_broadcast(128), [lnrow_d], [lnb_])
        wo_v = wout_d.ap.rearrange("(k p) c -> p k c", p=128)
        wos = [K.sb([128, D], F32, f"wos{i}") for i in range(2)]
        for dk in range(8):
            K.dma(wos[dk % 2][:], wo_v[:, dk, :], [wout_d], [wos[dk % 2]])
            K.cp("act" if dk % 2 else "dve", woutb[:, dk, :], wos[dk % 2][:], [wos[dk % 2]], [woutb])
        blk64 = cst[:, C_BLK64, :]
        onesf = cst[:, C_ONES, :]
        NS = 2
        bufs = []
        for s in range(NS):
            d_ = {}
            d_["of"] = K.sb([128, 8, 128], F32, f"pc_of{s}")
            d_["ob"] = K.sb([128, 8, 128], F32, f"pc_ob{s}")
            d_["ex"] = K.sb([128, 8, 128], F32, f"pc_ex{s}")
            d_["og"] = K.sb([128, 4, 128], F32, f"pc_og{s}")
            d_["x"] = K.sb([128, D], F32, f"pc_x{s}")
            for nm in ("ohg", "sqh", "rsth", "sog", "ysb", "ycen", "sq2", "rstd2", "yn2"):
                d_[nm] = K.sb([128, 4, 128], F32, f"pc_{nm}{s}")
            d_["yT"] = K.sb([128, 8, 128], BF16, f"pc_yT{s}")
            d_["h1"] = K.sb([128, D], F32, f"pc_h1{s}")
            d_["x1t"] = K.sb([128, D], F32, f"pc_x1t{s}")
            d_["st1"] = K.sb([128, 16], F32, f"pc_st{s}")
            d_["dbg"] = K.sb([128, 8, 128], F32, f"pc_dbg{s}") if debug else None
            bufs.append(d_)

        def pc_stream(s):
            B_ = bufs[s]
            pA, pB, pC, pD = PS[4 * s], PS[4 * s + 1], PS[4 * s + 2], PS[4 * s + 3]
            for j in range(s, NTX, NS):
                ec0 = CTX + j * 128
                K.dma(B_["of"][:], of_d.ap[j], [of_d], [B_["of"]])
                K.dma(B_["ob"][:], ob_d.ap[j], [ob_d], [B_["ob"]])
                K.dma(B_["ex"][:], ex_d.ap[j], [ex_d], [B_["ex"]])
                K.dma(B_["og"][:], zT_d.ap[16 * 128:20 * 128, ec0:ec0 + 128].rearrange("(h p) t -> p h t", p=128), [zT_d], [B_["og"]])
                K.dma(B_["x"][:], x_d[j * 128:(j + 1) * 128, :], [x_d], [B_["x"]])
                ohg, sqh, rsth, sog, ysb, ycen, sq2, rstd2, yn2 = [B_[k_] for k_ in ("ohg", "sqh", "rsth", "sog", "ysb", "ycen", "sq2", "rstd2", "yn2")]
                yT, h1, x1t, st1, xt = B_["yT"], B_["h1"], B_["x1t"], B_["st1"], B_["x"]
                K.tt("pool", ohg[:], B_["of"][:, 0:4, :], B_["ob"][:, 0:4, :], ALU.add, [B_["of"], B_["ob"]], [ohg])
                K.tt("pool", sqh[:], ohg[:], ohg[:], ALU.mult, [ohg], [sqh])
                K.mm(pA[:, :], onesf, sqh[:].rearrange("p h t -> p (h t)"), [cst, sqh], [pA])
                K.act(rsth[:].rearrange("p h t -> p (h t)"), pA[:, :], AF.Ln, [pA, epsc], [rsth], bias=epsc[:, 1:2], scale=1.0 / 128.0)
                K.act(rsth[:], rsth[:], AF.Exp, [rsth], [rsth], scale=-0.5)
                K.act(sog[:], B_["og"][:], AF.Sigmoid, [B_["og"]], [sog])
                K.tt("pool", sog[:], sog[:], B_["og"][:], ALU.mult, [sog, B_["og"]], [sog])
                K.tt("dve", ohg[:], ohg[:], rsth[:], ALU.mult, [ohg, rsth], [ohg])
                K.stt(yT[:, 0:4, :], ohg[:], ptab[:, PT_NW:PT_NW + 1], sog[:], ALU.mult, ALU.mult, [ohg, ptab, sog], [yT])
                K.tt("pool", ysb[:], B_["of"][:, 4:8, :], B_["ob"][:, 4:8, :], ALU.add, [B_["of"], B_["ob"]], [ysb])
                K.mm(pB[:, :], blk64, ysb[:].rearrange("p j t -> p (j t)"), [cst, ysb], [pB])
                K.stt(ycen[:], pB[:, :].rearrange("p (j t) -> p j t", j=4), -1.0 / 64.0, ysb[:], ALU.mult, ALU.add, [pB, ysb], [ycen])
                K.tt("pool", sq2[:], ycen[:], ycen[:], ALU.mult, [ycen], [sq2])
                K.mm(pB[:, :], blk64, sq2[:].rearrange("p j t -> p (j t)"), [cst, sq2], [pB])
                K.act(rstd2[:].rearrange("p j t -> p (j t)"), pB[:, :], AF.Ln, [pB, epsc], [rstd2], bias=epsc[:, 2:3], scale=1.0 / 64.0)
                K.act(rstd2[:], rstd2[:], AF.Exp, [rstd2], [rstd2], scale=-0.5)
                K.tt("dve", ycen[:], ycen[:], rstd2[:], ALU.mult, [ycen, rstd2], [ycen])
                for jj in range(4):
                    K.ts("pool", yn2[:, jj, :], ycen[:, jj, :], ptab[:, PT_LNW + jj:PT_LNW + jj + 1], ptab[:, PT_LNB + jj:PT_LNB + jj + 1],
                         ALU.mult, ALU.add, [ycen, ptab], [yn2])
                K.tt("pool", yn2[:], yn2[:], B_["ex"][:, 0:4, :], ALU.add, [yn2, B_["ex"]], [yn2])
                K.tt("dve", yT[:, 4:8, :], yn2[:], B_["ex"][:, 4:8, :], ALU.mult, [yn2, B_["ex"]], [yT])
                if debug:
                    K.cp("pool", B_["dbg"][:], yT[:], [yT], [B_["dbg"]])
                    K.dma(dbg["yT"].ap[j], B_["dbg"][:], [B_["dbg"]], [dbg["yT"]])
                for dh in range(2):
                    bank = pC if dh == 0 else pD
                    with K.atomic():
                        for m in range(8):
                            K.mm(bank[:, :], yT[:, m, :], woutb[:, m, dh * 512:(dh + 1) * 512], [yT, woutb], [bank], start=(m == 0), stop=(m == 7))
                    K.tt("dve", h1[:, dh * 512:(dh + 1) * 512], bank[:, :], gb[:, 0, dh * 512:(dh + 1) * 512], ALU.mult, [bank, gb], [h1])
                K.stt(h1[:], xt[:], ALPHA, h1[:], ALU.mult, ALU.add, [xt, h1], [h1])
                ln_stats2(K, nc, h1, h1.ap, st1, epsc, 1)
                K.ts("dve", x1t[:], h1[:], st1[:, 12:13], st1[:, 15:16], ALU.subtract, ALU.mult, [h1, st1], [x1t])
                K.tt("pool", x1t[:], x1t[:], lnb_[:, 0, :], ALU.mult, [x1t, lnb_], [x1t])
                K.tt("pool", x1t[:], x1t[:], lnb_[:, 1, :], ALU.add, [x1t, lnb_], [x1t])
                K.dma(x1_d[j * 128:(j + 1) * 128, :], x1t[:], [x1t], [x1_d])
                if debug:
                    K.dma(dbg["x1"][j * 128:(j + 1) * 128, :], x1t[:], [x1t], [dbg["x1"]])
        K.run_streams([lambda s=s: pc_stream(s) for s in range(NS)])
        K.barrier()
        K.stack.close()

    if upto < 1:
        return nc, K
    run_pass(0)
    if upto < 2:
        return nc, K
    run_pass(1)
    if upto < 2.5:
        return nc, K
    run_pc()
    if upto < 3:
        return nc, K

    GT = 2
    GC = GT * 128
    K.stack = ExitStack()
    wgb = K.sb([128, 8, DFF], BF16, "wgb")
    wub = K.sb([128, 8, DFF], BF16, "wub")
    wdb = K.sb([128, NFT, D], BF16, "wdb")
    outer4 = K.stack
    K.stack = ExitStack()
    stg = [K.sb([128, DFF], F32, f"stg{i}") for i in range(2)]
    si = 0
    for (wd_, wb_, nk, ncol) in ((wg_d, wgb, 8, DFF), (wu_d, wub, 8, DFF), (wd_d, wdb, NFT, D)):
        v = wd_.ap.rearrange("(k p) c -> p k c", p=128)
        for kk_ in range(nk):
            s = stg[si % 2]
            K.dma(s[:, 0:ncol], v[:, kk_, :], [wd_], [s])
            K.cp(("act", "dve", "pool")[si % 3], wb_[:, kk_, :], s[:, 0:ncol], [s], [wb_])
            si += 1
    K.barrier()
    K.stack.close()
    K.stack = outer4
    ln2bc = K.sb([128, 2, D], F32, "ln2bc")
    K.dma(ln2bc[:, 0, :], lnrow_d.ap[2:3, :].partition_broadcast(128), [lnrow_d], [ln2bc])
    K.dma(ln2bc[:, 1, :], lnrow_d.ap[3:4, :].partition_broadcast(128), [lnrow_d], [ln2bc])
    x1g = [K.sb([128, GT, D], F32, f"x1g{i}") for i in range(2)]
    u2T = [K.sb([128, 8, GC], BF16, f"u2T{i}") for i in range(2)]
    hT = K.sb([128, NFT, GC], BF16, "hT")
    xnb2 = [K.sb([128, D], BF16, "xnb2_0")] * 2
    st2 = [K.sb([128, 16], F32, f"st2_{i}") for i in range(2)]
    sgl = [K.sb([128, GC], F32, f"sgl{i}") for i in range(2)]
    h2 = [K.sb([128, D], F32, f"h2_{i}") for i in range(2)]
    st3 = [K.sb([128, 16], F32, f"st3_{i}") for i in range(2)]
    ngrp = (NTX + GT - 1) // GT
    tcount = 0
    for g in range(ngrp):
        nt = min(GT, NTX - g * GT)
        ncol = nt * 128
        xg = x1g[g % 2]
        ut = u2T[g % 2]
        for i in range(nt):
            t = g * GT + i
            K.dma(xg[:, i, :], x1_d[t * 128:(t + 1) * 128, :], [x1_d], [xg])
        for i in range(nt):
            st = st2[tcount % 2]
            xb = xnb2[tcount % 2]
            tcount += 1
            ln_stats2(K, nc, xg, xg[:, i, :], st, epsc, 0)
            K.ts("dve", xb[:], xg[:, i, :], st[:, 12:13], st[:, 15:16], ALU.subtract, ALU.mult, [xg, st], [xb])
            tb = psb(7)
            for dk in range(8):
                K.tr(tb[:, dk * 128:(dk + 1) * 128], xb[:, dk * 128:(dk + 1) * 128], identb, [xb, cstb], [PS[7]])
            for dk in range(8):
                K.act(ut[:, dk, i * 128:(i + 1) * 128], tb[:, dk * 128:(dk + 1) * 128], AF.Identity, [PS[7], opsc, modT], [ut],
                      bias=modT[:, 24 + dk, 0:1], scale=opsc[:, 2, dk:dk + 1])
        for ft in range(NFT):
            bg, bu = PS[(2 * ft) % 4], PS[(2 * ft + 1) % 4]
            for dk in range(8):
                K.mm(bg[:, 0:ncol], wgb[:, dk, ft * 128:(ft + 1) * 128], ut[:, dk, 0:ncol], [wgb, ut], [bg], start=(dk == 0), stop=(dk == 7))
            for dk in range(8):
                K.mm(bu[:, 0:ncol], wub[:, dk, ft * 128:(ft + 1) * 128], ut[:, dk, 0:ncol], [wub, ut], [bu], start=(dk == 0), stop=(dk == 7))
            sg_ = sgl[ft % 2]
            K.act(sg_[:, 0:ncol], bg[:, 0:ncol], AF.Sigmoid, [bg], [sg_])
            K.tt("dve", sg_[:, 0:ncol], sg_[:, 0:ncol], bg[:, 0:ncol], ALU.mult, [sg_, bg], [sg_])
            K.tt("dve", hT[:, ft, 0:ncol], sg_[:, 0:ncol], bu[:, 0:ncol], ALU.mult, [sg_, bu], [hT])
        for i in range(nt):
            t = g * GT + i
            hh = h2[t % 2]
            st = st3[t % 2]
            for dh in range(2):
                bank = PS[4 + dh]
                for ft in range(NFT):
                    K.mm(bank[:, :], hT[:, ft, i * 128:(i + 1) * 128], wdb[:, ft, dh * 512:(dh + 1) * 512], [hT, wdb], [bank],
                         start=(ft == 0), stop=(ft == NFT - 1))
                K.tt("dve", hh[:, dh * 512:(dh + 1) * 512], bank[:, :], gb[:, 1, dh * 512:(dh + 1) * 512], ALU.mult, [bank, gb], [hh])
            K.stt(hh[:], xg[:, i, :], ALPHA, hh[:], ALU.mult, ALU.add, [xg, hh], [hh])
            ln_stats2(K, nc, hh, hh.ap, st, epsc, 1)
            K.ts("dve", hh[:], hh[:], st[:, 12:13], st[:, 15:16], ALU.subtract, ALU.mult, [hh, st], [hh])
            K.tt("pool", hh[:], hh[:], ln2bc[:, 0, :], ALU.mult, [hh, ln2bc], [hh])
            K.tt("pool", hh[:], hh[:], ln2bc[:, 1, :], ALU.add, [hh, ln2bc], [hh])
            K.dma(out_d[t * 128:(t + 1) * 128, :], hh[:], [hh], [out_d])
    K.barrier()
    K.stack.close()
    return nc, K


def ln_stats2(K, nc, src, ap, st, epsc, eps_col):
    K.op("dve", lambda: nc.vector.bn_stats(st[:, 0:6], ap[:, 0:512]), [src], [st])
    K.op("dve", lambda: nc.vector.bn_stats(st[:, 6:12], ap[:, 512:1024]), [src], [st])
    K.op("dve", lambda: nc.vector.bn_aggr(st[:, 12:14], st[:, 0:12]), [st], [st])
    K.act(st[:, 14:15], st[:, 13:14], AF.Ln, [st, epsc], [st], bias=epsc[:, eps_col:eps_col + 1])
    K.act(st[:, 15:16], st[:, 14:15], AF.Exp, [st], [st], scale=-0.5)


def _consts():
    c = np.zeros((128, NCONST, 128), np.float32)
    s = np.arange(128)[:, None]
    t = np.arange(128)[None, :]
    c[:, C_ID] = (s == t)
    c[:, C_M32F] = (s // 32 == t // 32) & (s <= t)
    c[:, C_M32B] = (s // 32 == t // 32) & (s >= t)
    c[:, C_MSF] = (s // 64 == t // 64) & (s < t)
    c[:, C_MIF] = (s // 64 == t // 64) & (s <= t)
    c[:, C_MSB] = (s // 64 == t // 64) & (s > t)
    c[:, C_MIB] = (s // 64 == t // 64) & (s >= t)
    c[:, C_BLK64] = (s // 64 == t // 64)
    c[:, C_ONES] = 1.0
    c[:, C_RST32] = np.broadcast_to((t % 32 != 0), (128, 128))
    c[:, C_RST64] = np.broadcast_to((t % 64 != 0), (128, 128))
    rm = np.zeros((128, 128), np.float32)
    for k in range(4):
        rm[:, k] = (np.arange(128) // 32 == k)
    for k in range(2):
        rm[:, 4 + k] = (np.arange(128) // 64 == k)
    c[:, C_ROWM] = rm
    return c


def _fm(v, nt):
    return np.ascontiguousarray(np.asarray(v, np.float32).reshape(nt, 128).T)


def _ptab(inp):
    pt = np.zeros((128, NPT), np.float32)
    lbl = np.asarray(inp["hgrn_lb_logits"], np.float32)
    for d in range(2):
        pt[:, PT_L0 + 4 * d:PT_L0 + 4 * d + 4] = _fm(lbl[0, d], 4)
        pt[:, PT_L1 + 4 * d:PT_L1 + 4 * d + 4] = _fm(lbl[1, d], 4)
    pt[:, PT_NW] = np.asarray(inp["hgrn_norm_w"], np.float32)[0]
    mu = np.zeros(14 * 128, np.float32)
    mu[:1760] = np.asarray(inp["rwkv_mu"], np.float32)[0]
    pt[:, PT_MU:PT_MU + 14] = _fm(mu, 14)
    ch = np.arange(14 * 128)
    valid = ch < 1760
    masks = [ch < 440, (ch >= 440) & (ch < 880), (ch >= 880) & (ch < 1320), (ch >= 1320) & valid, ch < 880, (ch >= 880) & valid]
    for i, m in enumerate(masks):
        pt[:, PT_ML + 14 * i:PT_ML + 14 * (i + 1)] = _fm(m.astype(np.float32), 14)
    for d in range(2):
        pt[:, PT_W0 + 4 * d:PT_W0 + 4 * d + 4] = _fm(inp["rwkv_w0"][0, d], 4)
        pt[:, PT_A0 + 4 * d:PT_A0 + 4 * d + 4] = _fm(inp["rwkv_a0"][0, d], 4)
    pt[:, PT_KK:PT_KK + 4] = _fm(inp["rwkv_k_k"][0], 4)
    pt[:, PT_KA:PT_KA + 4] = _fm(inp["rwkv_k_a"][0], 4)
    pt[:, PT_RK:PT_RK + 4] = _fm(np.asarray(inp["rwkv_r_k"])[0].reshape(512), 4)
    pt[:, PT_LNW:PT_LNW + 4] = _fm(inp["rwkv_lnx_w"][0], 4)
    pt[:, PT_LNB:PT_LNB + 4] = _fm(inp["rwkv_lnx_b"][0], 4)
    pt[:, PT_BADA:PT_BADA + 48] = _fm(inp["b_ada"][0], 48)
    return pt


def _shared_maps(inp):
    f = lambda a: np.ascontiguousarray(np.asarray(a, np.float32))
    wl4 = np.zeros((128, 4, 512), np.float32)
    wl4[0:32, 0] = inp["rwkv_w2"][0, 0]
    wl4[32:64, 1] = inp["rwkv_w2"][0, 1]
    wl4[64:96, 2] = inp["rwkv_a2"][0, 0]
    wl4[96:128, 3] = inp["rwkv_a2"][0, 1]
    lnrows = np.stack([f(inp["ln1_g"])[0], f(inp["ln1_b"])[0], f(inp["ln2_g"])[0], f(inp["ln2_b"])[0]], 0)
    return {
        "w_ada": f(inp["w_ada"])[0], "b_ada_row": f(inp["b_ada"]), "w_in": f(inp["w_in"])[0], "ptab": _ptab(inp),
        "consts": _consts(), "wl4": wl4, "g2": f(inp["rwkv_g2"])[0], "w_out": f(inp["w_out"])[0],
        "lnrows": np.ascontiguousarray(lnrows), "w_gate": f(inp["w_ffn_gate"])[0], "w_up": f(inp["w_ffn_up"])[0],
        "w_down": f(inp["w_ffn_down"])[0],
    }


def _core_map(inp, shared, b):
    m = dict(shared)
    m["x"] = np.ascontiguousarray(np.asarray(inp["x"][b], np.float32))
    m["ctx"] = np.ascontiguousarray(np.asarray(inp["ctx"][b], np.float32))
    cv = np.zeros((128, 16), np.float32)
    cv[:, 0::2] = np.asarray(inp["c"][b], np.float32).reshape(8, 128).T
    cv[:, 1::2] = np.asarray(inp["c_ctx"], np.float32).reshape(8, 128).T
    m["cv"] = cv
    return m


_NC_CACHE = {}


def kernel(**inputs):
    x = np.asarray(inputs["x"])
    B, T, _ = x.shape
    if T not in _NC_CACHE:
        _NC_CACHE[T] = build(T)[0]
    nc = _NC_CACHE[T]
    shared = _shared_maps(inputs)
    in_maps = [_core_map(inputs, shared, b) for b in range(B)]
    res = run_bass_kernel_spmd(nc, in_maps, core_ids=list(range(B)))
    return np.stack([np.asarray(r["out"], np.float32) for r in res.results], 0)
```

```python
from contextlib import ExitStack
import numpy as np
import concourse.bass as bass
import concourse.mybir as mybir
from concourse.bass_utils import run_bass_kernel_spmd

F32 = mybir.dt.float32
BF16 = mybir.dt.bfloat16
ALU = mybir.AluOpType
AF = mybir.ActivationFunctionType

D = 1024
CTX = 256
NCT = 34
ZC = NCT * 128
IN_COLS = 4320
DFF = 2816
NFT = DFF // 128
LWS = 0.6065306597126334
ALPHA = 2.0 ** 0.25

C_ID, C_M32F, C_M32B, C_MSF, C_MIF, C_MSB, C_MIB, C_BLK64, C_ONES, C_RST32, C_RST64, C_ROWM = range(12)
NCONST = 12
PT_L0 = 0
PT_L1 = 8
PT_NW = 16
PT_MU = 17
PT_ML = 31
PT_W0 = 115
PT_A0 = 123
PT_KK = 131
PT_KA = 135
PT_RK = 139
PT_LNW = 143
PT_LNB = 147
PT_BADA = 151
NPT = 199


class Buf:
    __slots__ = ("ap", "w", "r", "name", "tw", "tr", "root")

    def __init__(self, ap, name="", root=None):
        self.root = root if root is not None else self
        self.ap = ap
        self.w = {}
        self.r = {}
        self.name = name
        self.tw = 0.0
        self.tr = 0.0

    def __getitem__(self, k):
        return self.ap[k]


class KB:
    NR = 8

    def __init__(self, nc):
        self.nc = nc
        self.E = {"pe": nc.tensor, "dve": nc.vector, "act": nc.scalar, "pool": nc.gpsimd, "sp": nc.sync}
        self.sems = []
        self.semval = []
        self.esem = {e: self._newsem("c_" + e) for e in self.E}
        self.dsem = {"sp": [self._newsem(f"d_sp{i}") for i in range(self.NR)]}
        self.didx = {"sp": 0}
        self.seen = {e: {} for e in self.E}
        self.nbuf = 0
        self.nwait = 0
        self.ninst = 0
        self.stack = None
        self._st = None
        self.clk = {}

    def _newsem(self, name):
        self.sems.append(self.nc.alloc_semaphore(name))
        self.semval.append(0)
        return len(self.sems) - 1

    def sb(self, shape, dtype=F32, name=None, perm=False):
        self.nbuf += 1
        name = f"{name or 't'}_{self.nbuf}"
        if perm or self.stack is None:
            h = self.nc.alloc_sbuf_tensor(name, list(shape), dtype)
        else:
            h = self.stack.enter_context(self.nc.sbuf_tensor(name, list(shape), dtype))
        return Buf(h.ap(), name)

    def ps(self, shape, dtype=F32, name=None):
        self.nbuf += 1
        return Buf(self.nc.alloc_psum_tensor(f"{name or 'p'}_{self.nbuf}", list(shape), dtype).ap(), name)

    def dram(self, name, shape, dtype=F32, kind="Internal"):
        return Buf(self.nc.dram_tensor(name, list(shape), dtype, kind=kind).ap(), name)

    def _need(self, reads, writes):
        need = {}
        for b in reads:
            for k, v in b.w.items():
                if need.get(k, 0) < v:
                    need[k] = v
        for b in writes:
            for k, v in b.w.items():
                if need.get(k, 0) < v:
                    need[k] = v
            for k, v in b.r.items():
                if need.get(k, 0) < v:
                    need[k] = v
        return need

    def _wait(self, e, need):
        own = self.esem[e]
        seen = self.seen[e]
        eng = self.E[e]
        for k, v in need.items():
            if k == own and e == "pe":
                continue
            if seen.get(k, 0) < v:
                eng.wait_ge(self.sems[k], v)
                seen[k] = v
                self.nwait += 1

    def _commit(self, k, v, reads, writes):
        for b in writes:
            b.w = {k: v}
            b.r = {}
        for b in reads:
            if b.r.get(k, 0) < v:
                b.r[k] = v

    def op(self, e, fn, reads=(), writes=(), cost=0.5):
        reads = [b.root for b in reads]
        writes = [b.root for b in writes]
        self._yield(e, reads, writes)
        self._model(e, reads, writes, cost)
        self._wait(e, self._need(reads, writes))
        ins = fn()
        k = self.esem[e]
        self.semval[k] += 1
        ins.then_inc(self.sems[k], 1)
        self._commit(k, self.semval[k], reads, writes)
        self.ninst += 1
        return ins

    def dma(self, out, in_, reads=(), writes=(), q="sp"):
        reads = [b.root for b in reads]
        writes = [b.root for b in writes]
        self._yield(q, reads, writes)
        self._model(q, reads, writes, 1.0)
        i = self.didx[q]
        self.didx[q] += 1
        k = self.dsem[q][i % self.NR]
        need = self._need(reads, writes)
        if self.semval[k] > 0 and need.get(k, 0) < self.semval[k]:
            need[k] = self.semval[k]
        self._wait(q, need)
        self.semval[k] += 16
        self.E[q].dma_start(out=out, in_=in_).then_inc(self.sems[k], 16)
        self._commit(k, self.semval[k], reads, writes)
        self.ninst += 1


    def run_streams(self, fns):
        import threading
        n = len(fns)
        if n == 1:
            fns[0]()
            return
        st = {"turn": -1, "alive": [True] * n, "err": [], "pend": [None] * n, "started": 0}
        cv = threading.Condition()
        self._st, self._cv = st, cv
        self._tls = threading.local()

        def pick():
            best, bt_ = -1, None
            for i in range(n):
                if st["alive"][i] and st["pend"][i] is not None:
                    t = st["pend"][i]
                    if bt_ is None or t < bt_:
                        best, bt_ = i, t
            st["turn"] = best
            cv.notify_all()
        self._pick = pick

        def all_pending():
            return all((not st["alive"][i]) or st["pend"][i] is not None for i in range(n))
        self._all_pending = all_pending

        def runner(i):
            self._tls.sid = i
            self._tls.atomic = 0
            try:
                fns[i]()
            except BaseException as e:
                st["err"].append(e)
            finally:
                with cv:
                    st["alive"][i] = False
                    st["pend"][i] = None
                    if any(st["alive"]) and all_pending():
                        pick()
        ths = [threading.Thread(target=runner, args=(i,)) for i in range(n)]
        for t in ths:
            t.start()
        for t in ths:
            t.join()
        self._st = None
        if st["err"]:
            raise st["err"][0]

    def _est_start(self, e, reads, writes):
        t = self.clk.get(e, 0.0)
        for b in reads:
            if b.tw > t:
                t = b.tw
        for b in writes:
            if b.tw > t:
                t = b.tw
            if b.tr > t:
                t = b.tr
        return t

    def _model(self, e, reads, writes, cost):
        t = self._est_start(e, reads, writes) + 0.15
        f = t + cost
        if e == "sp":
            self.clk[e] = t + 0.05
            f = t + 2.0 + cost
        else:
            self.clk[e] = f
        for b in writes:
            b.tw = f
        for b in reads:
            if b.tr < f:
                b.tr = f

    def _yield(self, e, reads, writes):
        st = getattr(self, "_st", None)
        if st is None:
            return
        tls = self._tls
        i = getattr(tls, "sid", None)
        if i is None or tls.atomic:
            return
        cv = self._cv
        with cv:
            st["pend"][i] = self._est_start(e, reads, writes)
            if self._all_pending():
                self._pick()
            while st["turn"] != i:
                cv.wait()
            st["turn"] = -1
            st["pend"][i] = None

    def atomic(self):
        kb = self

        class _A:
            def __enter__(self_):
                if getattr(kb, "_st", None) is not None and getattr(kb._tls, "sid", None) is not None:
                    kb._tls.atomic += 1

            def __exit__(self_, *a):
                if getattr(kb, "_st", None) is not None and getattr(kb._tls, "sid", None) is not None:
                    kb._tls.atomic -= 1
        return _A()

    def barrier(self):
        need = {k: v for k, v in enumerate(self.semval) if v > 0}
        for e in self.E:
            self._wait(e, need)

    @staticmethod
    def _n(ap):
        n = 1
        for d in ap.shape[1:]:
            n *= d
        return n

    def mm(self, out, lhsT, rhs, reads, writes, start=True, stop=True):
        nc = self.nc
        c = 0.03 + self._n(rhs) * (4 if rhs.dtype == F32 else 1) / 2400.0
        return self.op("pe", lambda: nc.tensor.matmul(out, lhsT, rhs, start=start, stop=stop), reads, writes, cost=c)

    def tr(self, out, in_, ident, reads, writes):
        nc = self.nc
        return self.op("pe", lambda: nc.tensor.transpose(out, in_, ident), reads, writes, cost=0.09)

    def act(self, out, in_, func, reads, writes, bias=None, scale=None):
        nc = self.nc
        kw = {}
        if bias is not None:
            kw["bias"] = bias
        if scale is not None:
            kw["scale"] = scale
        c = 0.2 + self._n(out) / 1200.0
        return self.op("act", lambda: nc.scalar.activation(out, in_, func, **kw), reads, writes, cost=c)

    def _vc(self, e, out):
        n = self._n(out)
        return (0.1 + n / 500.0) if e == "pool" else (0.07 + n / 900.0)

    def tt(self, e, out, in0, in1, op, reads, writes):
        eng = self.E[e]
        return self.op(e, lambda: eng.tensor_tensor(out, in0, in1, op), reads, writes, cost=self._vc(e, out))

    def ts(self, e, out, in0, s1, s2, op0, op1, reads, writes):
        eng = self.E[e]
        if s2 is None:
            return self.op(e, lambda: eng.tensor_scalar(out, in0, s1, None, op0), reads, writes, cost=self._vc(e, out))
        return self.op(e, lambda: eng.tensor_scalar(out, in0, s1, s2, op0, op1), reads, writes, cost=self._vc(e, out))

    def stt(self, out, in0, scalar, in1, op0, op1, reads, writes):
        nc = self.nc
        return self.op("dve", lambda: nc.vector.scalar_tensor_tensor(out, in0, scalar, in1, op0, op1), reads, writes,
                       cost=0.07 + self._n(out) / 900.0)

    def cp(self, e, out, in_, reads, writes):
        if e == "act":
            nc = self.nc
            return self.op("act", lambda: nc.scalar.copy(out, in_), reads, writes, cost=0.2 + self._n(out) / 1200.0)
        eng = self.E[e]
        return self.op(e, lambda: eng.tensor_copy(out, in_), reads, writes, cost=self._vc(e, out))

    def memset(self, e, ap, val, writes):
        eng = self.E[e]
        return self.op(e, lambda: eng.memset(ap, val), [], writes, cost=self._vc(e, ap))


def _bc(ap, shape):
    return ap.to_broadcast(list(shape))


def build(T, debug=False, upto=9):
    NTX = T // 128
    NE = CTX + T
    nc = bass.Bass("TRN2", target_bir_lowering=False)
    K = KB(nc)
    x_d = K.dram("x", [T, D], kind="ExternalInput")
    ctx_d = K.dram("ctx", [CTX, D], kind="ExternalInput")
    cv_d = K.dram("cv", [128, 16], kind="ExternalInput")
    wada_d = K.dram("w_ada", [D, 6 * D], kind="ExternalInput")
    brow_d = K.dram("b_ada_row", [1, 6 * D], kind="ExternalInput")
    win_d = K.dram("w_in", [D, IN_COLS], kind="ExternalInput")
    ptab_d = K.dram("ptab", [128, NPT], kind="ExternalInput")
    const_d = K.dram("consts", [128, NCONST, 128], kind="ExternalInput")
    wl_d = K.dram("wl4", [128, 4, 512], kind="ExternalInput")
    g2_d = K.dram("g2", [96, 512], kind="ExternalInput")
    wout_d = K.dram("w_out", [D, D], kind="ExternalInput")
    lnrow_d = K.dram("lnrows", [4, D], kind="ExternalInput")
    wg_d = K.dram("w_gate", [D, DFF], kind="ExternalInput")
    wu_d = K.dram("w_up", [D, DFF], kind="ExternalInput")
    wd_d = K.dram("w_down", [DFF, D], kind="ExternalInput")
    out_d = K.dram("out", [T, D], kind="ExternalOutput")
    zT_d = K.dram("zT", [ZC, NE])
    of_d = K.dram("ofwd", [NTX, 128, 8, 128])
    x1_d = K.dram("x1s", [T, D])
    ob_d = K.dram("obwd", [NTX, 128, 8, 128])
    zr_d = K.dram("zrl", [14 * 128, NE])
    ex_d = K.dram("extra", [NTX, 128, 8, 128])
    dbg = {}
    if debug:
        dbg["zT"] = K.dram("dbg_zT", [ZC, NE], kind="ExternalOutput")
        dbg["yT"] = K.dram("dbg_yT", [NTX, 128, 8, 128], kind="ExternalOutput")
        dbg["x1"] = K.dram("dbg_x1", [T, D], kind="ExternalOutput")

    cst = K.sb([128, NCONST, 128], F32, "cst", perm=True)
    cstb = K.sb([128, NCONST, 128], BF16, "cstb", perm=True)
    ptab = K.sb([128, NPT], F32, "ptab", perm=True)
    modT = K.sb([128, 48, 2], F32, "modT", perm=True)
    opsc = K.sb([128, 3, 8], F32, "opsc", perm=True)
    epsc = K.sb([128, 4], F32, "epsc", perm=True)
    gb = K.sb([128, 2, D], F32, "gb", perm=True)
    drv = K.sb([128, 128], F32, "drv", perm=True)
    DV_LB, DV_OML, DV_NOML = 0, 8, 16
    DV_C0 = 24
    DV_CS = 38
    DV_OMKA = 122
    PS = [K.ps([128, 512], F32, f"bank{i}") for i in range(8)]

    def psb(i):
        return PS[i].ap.bitcast(BF16)

    ident = cst[:, C_ID, :]
    identb = cstb[:, C_ID, :]

    K.dma(cst[:], const_d[:, :, :], [const_d], [cst])
    K.dma(ptab[:], ptab_d[:, :], [ptab_d], [ptab])
    K.cp("dve", cstb[:], cst[:], [cst], [cstb])
    K.memset("pool", epsc[:, 0:1], 1e-6, [epsc])
    K.memset("pool", epsc[:, 1:2], 1e-5, [epsc])
    K.memset("pool", epsc[:, 2:3], 64e-5, [epsc])
    K.memset("pool", epsc[:, 3:4], 1e-24, [epsc])
    K.tt("dve", drv[:, 0:8], ptab[:, PT_L0:PT_L0 + 8], ptab[:, PT_L1:PT_L1 + 8], ALU.subtract, [ptab], [drv])
    K.act(drv[:, DV_LB:DV_LB + 8], drv[:, 0:8], AF.Sigmoid, [drv], [drv])
    K.ts("dve", drv[:, DV_OML:DV_OML + 8], drv[:, DV_LB:DV_LB + 8], -1.0, 1.0, ALU.mult, ALU.add, [drv], [drv])
    K.ts("dve", drv[:, DV_NOML:DV_NOML + 8], drv[:, DV_OML:DV_OML + 8], -1.0, None, ALU.mult, None, [drv], [drv])
    K.ts("dve", drv[:, DV_C0:DV_C0 + 14], ptab[:, PT_MU:PT_MU + 14], -1.0, 1.0, ALU.mult, ALU.add, [ptab], [drv])
    for i in range(6):
        K.tt("dve", drv[:, DV_CS + 14 * i:DV_CS + 14 * (i + 1)], ptab[:, PT_MU:PT_MU + 14],
             ptab[:, PT_ML + 14 * i:PT_ML + 14 * (i + 1)], ALU.mult, [ptab], [drv])
    K.ts("dve", drv[:, DV_OMKA:DV_OMKA + 4], ptab[:, PT_KA:PT_KA + 4], -1.0, 1.0, ALU.mult, ALU.add, [ptab], [drv])

    if upto == 0.1:
        K.barrier()
        return nc, K
    K.stack = ExitStack()
    winb = K.sb([128, 8, ZC], BF16, "winb")
    cv = K.sb([128, 16], F32, "cv")
    cvs = K.sb([128, 16], F32, "cvs")
    K.dma(cv[:], cv_d[:, :], [cv_d], [cv])
    outer0 = K.stack
    K.stack = ExitStack()
    brow = K.sb([1, 4, 512], F32, "brow")
    for i_, eg_ in enumerate((4, 5, 10, 11)):
        K.dma(brow[0:1, i_, :], brow_d[0:1, eg_ * 512:(eg_ + 1) * 512], [brow_d], [brow])
    K.act(cvs[:], cv[:], AF.Sigmoid, [cv], [cvs])
    K.tt("dve", cvs[:], cvs[:], cv[:], ALU.mult, [cvs, cv], [cvs])
    wa = [K.sb([128, 8, 512], F32, f"wa{i}") for i in range(2)]
    wada_v = wada_d.ap.rearrange("(k p) e -> p k e", p=128)
    grow = K.sb([1, 512], F32, "grow")
    for eg in range(12):
        w = wa[eg % 2]
        K.dma(w[:], wada_v[:, :, eg * 512:(eg + 1) * 512], [wada_d], [w])
        bank = PS[eg % 2]
        for j in range(4):
            for dk in range(8):
                K.mm(bank[:, 2 * j:2 * j + 2], w[:, dk, j * 128:(j + 1) * 128], cvs[:, 2 * dk:2 * dk + 2], [w, cvs], [bank],
                     start=(dk == 0), stop=(dk == 7))
        for j in range(4):
            et = eg * 4 + j
            K.ts("dve", modT[:, et, :], bank[:, 2 * j:2 * j + 2], ptab[:, PT_BADA + et:PT_BADA + et + 1], None, ALU.add, None,
                 [bank, ptab], [modT])
        if eg in (4, 5, 10, 11):
            gi = 0 if eg < 6 else 1
            half = eg % 2 if eg < 6 else (eg - 10)
            rb_ = PS[2]
            for dk in range(8):
                K.mm(rb_[0:1, 0:512], cvs[:, 2 * dk:2 * dk + 1], w[:, dk, :], [w, cvs], [rb_], start=(dk == 0), stop=(dk == 7))
            K.tt("dve", grow[:], rb_[0:1, 0:512], brow[0:1, (4, 5, 10, 11).index(eg), :], ALU.add, [rb_, brow], [grow])
            bb = PS[3]
            K.mm(bb[:, 0:512], cst[0:1, C_ONES, :], grow[:], [cst, grow], [bb])
            K.cp("act", gb[:, gi, half * 512:(half + 1) * 512], bb[:, 0:512], [bb], [gb])
    K.ts("dve", opsc[:, 0, :], modT[:, 8:16, 0], 1.0, None, ALU.add, None, [modT], [opsc])
    K.ts("dve", opsc[:, 1, :], modT[:, 8:16, 1], 1.0, None, ALU.add, None, [modT], [opsc])
    K.ts("dve", opsc[:, 2, :], modT[:, 32:40, 0], 1.0, None, ALU.add, None, [modT], [opsc])
    K.barrier()
    if upto == 0.2:
        return nc, K
    K.stack.close()
    K.stack = ExitStack()
    wst = [K.sb([128, IN_COLS], F32, f"wst{i}") for i in range(2)]
    win_v = win_d.ap.rearrange("(k p) c -> p k c", p=128)
    K.memset("pool", winb[:, :, IN_COLS:ZC], 0.0, [winb])
    for dk in range(8):
        s = wst[dk % 2]
        K.dma(s[:], win_v[:, dk, :], [win_d], [s])
        K.cp("act" if dk % 2 else "dve", winb[:, dk, 0:IN_COLS], s[:], [s], [winb])

    K.barrier()
    if upto == 0.3:
        return nc, K
    K.stack.close()
    K.stack = outer0
    xts = [K.sb([128, D], F32, f"xt{i}") for i in range(2)]
    xnb = [K.sb([128, D], BF16, f"xnb{i}") for i in range(2)]
    stt_ = [K.sb([128, 16], F32, f"st{i}") for i in range(2)]

    def ln_stats(src_ap, src_bufs, st, eps_col):
        K.op("dve", lambda: nc.vector.bn_stats(st[:, 0:6], src_ap[:, 0:512]), src_bufs, [st])
        K.op("dve", lambda: nc.vector.bn_stats(st[:, 6:12], src_ap[:, 512:1024]), src_bufs, [st])
        K.op("dve", lambda: nc.vector.bn_aggr(st[:, 12:14], st[:, 0:12]), [st], [st])
        K.act(st[:, 14:15], st[:, 13:14], AF.Ln, [st, epsc], [st], bias=epsc[:, eps_col:eps_col + 1])
        K.act(st[:, 15:16], st[:, 14:15], AF.Exp, [st], [st], scale=-0.5)

    def modulate_T(src, i, uT, col0, sc_ap, sh_ap, tbank):
        st = stt_[i % 2]
        xb = xnb[i % 2]
        ln_stats(src.ap, [src], st, 0)
        K.ts("dve", xb[:], src[:], st[:, 12:13], st[:, 15:16], ALU.subtract, ALU.mult, [src, st], [xb])
        tb = psb(tbank)
        for dk in range(8):
            K.tr(tb[:, dk * 128:(dk + 1) * 128], xb[:, dk * 128:(dk + 1) * 128], identb, [xb, cstb], [PS[tbank]])
        for dk in range(8):
            K.act(uT[:, dk, col0:col0 + 128], tb[:, dk * 128:(dk + 1) * 128], AF.Identity, [PS[tbank], opsc, modT], [uT],
                  bias=sh_ap(dk), scale=sc_ap(dk))

    uTs = [K.sb([128, 8, 512], BF16, f"uT{i}") for i in range(2)]
    zsb = [K.sb([128, 512], F32, f"zsb{i}") for i in range(4)]
    groups = [("ctx", 0, 2)] + [("x", g * 4, min(4, NTX - g * 4)) for g in range((NTX + 3) // 4)]
    ti = 0
    zi = 0
    for gi_, (kind, t0, nt) in enumerate(groups):
        uT = uTs[gi_ % 2]
        src_d = ctx_d if kind == "ctx" else x_d
        mj = 1 if kind == "ctx" else 0
        for i in range(nt):
            xt = xts[ti % 2]
            K.dma(xt[:], src_d[(t0 + i) * 128:(t0 + i + 1) * 128, :], [src_d], [xt])
            modulate_T(xt, ti, uT, i * 128, lambda dk: opsc[:, mj, dk:dk + 1], lambda dk: modT[:, dk, mj:mj + 1], 7)
            ti += 1
        ncol = nt * 128
        ecol0 = (0 if kind == "ctx" else CTX) + t0 * 128
        import os
        ZD = int(os.environ.get("ZDBG", "0"))
        if ZD == 1:
            continue
        for ct in range(NCT):
            bank = PS[ct % 4]
            for dk in range(8):
                K.mm(bank[:, 0:ncol], winb[:, dk, ct * 128:(ct + 1) * 128], uT[:, dk, 0:ncol], [winb, uT], [bank],
                     start=(dk == 0), stop=(dk == 7))
            z = zsb[zi % 4]
            zi += 1
            K.cp("act" if ct % 2 else "dve", z[:, 0:ncol], bank[:, 0:ncol], [bank], [z])
            if ZD == 2:
                continue
            if ZD != 4:
                K.dma(zT_d[ct * 128:(ct + 1) * 128, ecol0:ecol0 + ncol], z[:, 0:ncol], [z], [zT_d])
            if debug and ZD != 3:
                K.dma(dbg["zT"][ct * 128:(ct + 1) * 128, ecol0:ecol0 + ncol], z[:, 0:ncol], [z], [dbg["zT"]])
    K.barrier()
    K.stack.close()

    def run_lerp():
        K.stack = ExitStack()
        NS = 3
        tiles = [("ctx", 0), ("ctx", 1)] + [("x", j) for j in range(NTX)]
        bufs = []
        for s in range(NS):
            zp = K.sb([128, 14, 4, 66], F32, f"lzp{s}")
            zc = K.sb([128, 14, 130], F32, f"lzc{s}")
            zo = [K.sb([128, 14, 128], F32, f"lzo{s}_{i}") for i in range(2)]
            K.memset("pool", zp[:], 0.0, [zp])
            K.memset("pool", zc[:], 0.0, [zc])
            bufs.append((zp, zc, zo))

        def lstream(s):
            zp, zc, zo = bufs[s]
            for ti, (kind, j) in enumerate(tiles[s::NS]):
                zrl = zo[ti % 2]
                isx = kind == "x"
                ec0 = (0 if kind == "ctx" else CTX) + j * 128
                if isx:
                    lo = 64 if j > 0 else 0
                    hi = 64 if j < NTX - 1 else 0
                    if lo == 0:
                        K.memset("pool", zp[:, :, 0, :], 0.0, [zp])
                    if hi == 0:
                        K.memset("pool", zp[:, :, 3, :], 0.0, [zp])
                    r0 = 1 - lo // 64
                    r1 = 3 + hi // 64
                    for ct in range(14):
                        src_ = zT_d.ap[(20 + ct) * 128:(21 + ct) * 128, ec0 - lo:ec0 + 128 + hi].rearrange("p (r c) -> p r c", c=64)
                        K.dma(zp[:, ct, r0:r1, 1:65], src_, [zT_d], [zp])
                    for ct in range(14):
                        views = {"L": zp[:, ct, 1:3, 0:64], "R": zp[:, ct, 1:3, 2:66], "U": zp[:, ct, 0:2, 1:65], "D": zp[:, ct, 2:4, 1:65]}
                        cen = zp[:, ct, 1:3, 1:65]
                        lo_, hi_ = ct * 128, ct * 128 + 128
                        kinds = []
                        if lo_ < 440: kinds.append(("L", 0))
                        if hi_ > 440 and lo_ < 880: kinds.append(("R", 1))
                        if hi_ > 880 and lo_ < 1320: kinds.append(("U", 2))
                        if hi_ > 1320: kinds.append(("D", 3))
                        o3 = zrl[:, ct, :].rearrange("p (r c) -> p r c", c=64)
                        K.act(o3, cen, AF.Identity, [zp, drv], [zrl], scale=drv[:, DV_C0 + ct:DV_C0 + ct + 1])
                        for (vn, ki) in kinds:
                            K.stt(o3, views[vn], drv[:, DV_CS + 14 * ki + ct:DV_CS + 14 * ki + ct + 1], o3, ALU.mult, ALU.add, [zp, drv, zrl], [zrl])
                else:
                    lo = 1 if j > 0 else 0
                    hi = 1 if j < 1 else 0
                    if lo == 0:
                        K.memset("pool", zc[:, :, 0:1], 0.0, [zc])
                    if hi == 0:
                        K.memset("pool", zc[:, :, 129:130], 0.0, [zc])
                    for g in range(2):
                        src_ = zT_d.ap[(20 + 7 * g) * 128:(27 + 7 * g) * 128, ec0 - lo:ec0 + 128 + hi].rearrange("(h p) t -> p h t", p=128)
                        K.dma(zc[:, 7 * g:7 * g + 7, 1 - lo:129 + hi], src_, [zT_d], [zc])
                    for ct in range(14):
                        lo_, hi_ = ct * 128, ct * 128 + 128
                        kinds = []
                        if lo_ < 880: kinds.append((zc[:, ct, 0:128], 4))
                        if hi_ > 880: kinds.append((zc[:, ct, 2:130], 5))
                        K.act(zrl[:, ct, :], zc[:, ct, 1:129], AF.Identity, [zc, drv], [zrl], scale=drv[:, DV_C0 + ct:DV_C0 + ct + 1])
                        for (vw, ki) in kinds:
                            K.stt(zrl[:, ct, :], vw, drv[:, DV_CS + 14 * ki + ct:DV_CS + 14 * ki + ct + 1], zrl[:, ct, :], ALU.mult, ALU.add,
                                  [zc, drv, zrl], [zrl])
                for g in range(2):
                    dst = zr_d.ap[7 * g * 128:(7 * g + 7) * 128, ec0:ec0 + 128].rearrange("(h p) t -> p h t", p=128)
                    K.dma(dst, zrl[:, 7 * g:7 * g + 7, :], [zrl], [zr_d])
        K.run_streams([lambda s=s: lstream(s) for s in range(NS)])
        K.barrier()
        K.stack.close()

    def run_pass(dirn):
        final = dirn == 1
        K.stack = ExitStack()
        wlb = K.sb([128, 4, 512], BF16, "wlb")
        g2b = K.sb([96, 512], BF16, "g2b")
        outer = K.stack
        K.stack = ExitStack()
        tmpw = K.sb([128, 4, 512], F32, "tmpw")
        K.dma(tmpw[:], wl_d[:, :, :], [wl_d], [tmpw])
        K.cp("dve", wlb[:], tmpw[:], [tmpw], [wlb])
        tmpg = K.sb([96, 512], F32, "tmpg")
        K.dma(tmpg[:], g2_d[:, :], [g2_d], [tmpg])
        K.cp("dve", g2b[:], tmpg[:], [tmpg], [g2b])
        K.barrier()
        K.stack.close()
        K.stack = outer
        S = K.sb([128, 4, 128], F32, "S")
        Sbf = [K.sb([128, 4, 128], BF16, f"Sbf{i}") for i in range(4)]
        H = [K.sb([128, 64], F32, f"H{j}") for j in range(4)]
        Hbd = [[K.sb([128, 128], BF16, f"Hbd{j}_{c}") for c in range(2)] for j in range(4)]
        MTbd = [[K.sb([128, 128], F32, f"MT{j}_{c}") for c in range(2)] for j in range(4)]
        K.memset("pool", S[:], 0.0, [S])
        for j in range(4):
            K.memset("pool", H[j][:], 0.0, [H[j]])
            for c in range(2):
                K.memset("pool", Hbd[j][c][:], 0.0, [Hbd[j][c]])
                K.memset("pool", MTbd[j][c][:], 0.0, [MTbd[j][c]])
        for i in range(4):
            K.memset("pool", Sbf[i][:], 0.0, [Sbf[i]])
        ld_q = K.sb([128, 4, 128], F32, "ldq")
        ld_f = K.sb([128, 4, 128], F32, "ldf")
        ld_i = K.sb([128, 4, 128], F32, "ldi")
        zrl = K.sb([128, 14, 128], F32, "zrl")
        zrl_g = [zrl, zrl, zrl, zrl]
        mshg = cstb[:, C_M32B if dirn else C_M32F, :]
        msi = cstb[:, C_MSB:C_MSB + 2, :] if dirn else cstb[:, C_MSF:C_MSF + 2, :]
        ms32 = cst[:, C_MSB, :] if dirn else cst[:, C_MSF, :]
        mnt32 = cst[:, C_MSF, :] if dirn else cst[:, C_MSB, :]
        id32r = K.sb([128, 2, 128], F32, "id32r")
        msir = K.sb([128, 2, 2, 128], BF16, "msir")
        mshgr = K.sb([128, 4, 128], BF16, "mshgr")
        for e_ in range(2):
            K.cp("pool", id32r[:, e_, :], ident, [cst], [id32r])
            K.cp("pool", msir[:, e_, :, :], msi, [cstb], [msir])
        for h_ in range(4):
            K.cp("pool", mshgr[:, h_, :], mshg, [cstb], [mshgr])
        rst32 = cst[:, C_RST32, :]
        rst64 = cst[:, C_RST64, :]
        blk64 = cst[:, C_BLK64, :]

        if dirn == 0:
            order = [("ctx", 0), ("ctx", 1)] + [("x", j) for j in range(NTX)]
        else:
            order = [("ctx", 1), ("ctx", 0)] + [("x", j) for j in range(NTX - 1, -1, -1)]

        def T32(name, shape=(128, 4, 128)):
            return K.sb(list(shape), F32, name)

        def T16(name, shape=(128, 4, 128)):
            return K.sb(list(shape), BF16, name)
        P32 = [T32(f"w32_{i}") for i in range(11)]
        H32 = [T32(f"h32_{i}") for i in range(8)]
        sgq = qh = H32[0]
        sgf = H32[1]
        ff = lg = H32[2]
        kdh = H32[3]
        bcum = H32[4]
        tmp1 = H32[5]
        e1 = H32[6]
        e2 = H32[7]
        sw = sq = P32[0]
        asg = P32[1]
        cs = kdr = P32[2]
        csm = rn = P32[3]
        E1 = P32[4]
        E2 = P32[5]
        tmpa = P32[6]
        bvec = P32[7]
        E3 = P32[8]
        kk = P32[9]
        kkn = P32[10]
        qb, kb, ib16, ATm = [T16(n) for n in ("qb", "kb", "ib16", "ATm")]
        kbTm = [T16(f"kbTm{c}") for c in range(4)]
        iT = T16("iT")
        stmp = T32("stmp")
        Lb = K.sb([128, 128], BF16, "Lb")
        vb16 = T16("vb16")
        if final:
            sgd = K.sb([96, 128], BF16, "sgd")
            asg2, tmpa2, bonp = T32("asg2"), T32("tmpa2"), T32("bonp")
        osb = [K.sb([128, 8, 128], F32, f"osb{i}") for i in range(2)]
        exb = [K.sb([128, 8, 128], F32, f"exb{i}") for i in range(2)] if final else None
        AR = [K.sb([128, 4, 2, 128], BF16, f"AR{i}") for i in range(2)]
        bt = [T16(f"bt{i}") for i in range(2)]
        kt = [T16(f"kt{i}") for i in range(2)]
        aT = [T16(f"aT{i}") for i in range(2)]
        vT = [T16(f"vT{i}") for i in range(2)]
        btTm = [[T16(f"btTm{i}_{c}") for c in range(2)] for i in range(2)]
        ktTm = [[T16(f"ktTm{i}_{c}") for c in range(2)] for i in range(2)]
        gam = [K.sb([128, 4, 2], F32, f"gam{i}") for i in range(2)]
        Amat = [K.sb([128, 2, 2, 2, 128], BF16, f"Amat{i}") for i in range(4)]
        XP = [K.sb([128, 2, 2, 128], F32, f"XP{i}") for i in range(4)]
        XT = [K.sb([128, 2, 128], F32, f"XT{i}") for i in range(4)]
        Pbf = [K.sb([128, 2, 128], BF16, f"Pbf{i}") for i in range(4)]
        AW = [K.sb([128, 2, 128], BF16, f"AW{i}") for i in range(4)]
        AU = [K.sb([128, 2, 128], BF16, f"AU{i}") for i in range(4)]
        QT = [K.sb([128, 128], BF16, f"QT{i}") for i in range(4)]
        AUXH = [Buf(PS[6 + i // 2].ap[:, (i % 2) * 256:(i % 2) * 256 + 256], f"aux{i}", root=PS[6 + i // 2]) for i in range(4)]

        def front_hg(n):
            kind, j = order[n]
            isx = kind == "x"
            fb = n % 2
            ec0 = (0 if kind == "ctx" else CTX) + j * 128

            def hv(ct0):
                return zT_d.ap[ct0 * 128:(ct0 + 4) * 128, ec0:ec0 + 128].rearrange("(h p) t -> p h t", p=128)
            K.dma(ld_q[:], hv(0), [zT_d], [ld_q])
            K.dma(ld_f[:], hv(4 + 4 * dirn), [zT_d], [ld_f])
            K.dma(ld_i[:], hv(12), [zT_d], [ld_i])
            zq, zf, zi_ = ld_q, ld_f, ld_i
            K.act(sgq[:], zq[:], AF.Sigmoid, [zq], [sgq])
            K.act(sgf[:], zf[:], AF.Sigmoid, [zf], [sgf])
            K.tt("pool", qh[:], zq[:], sgq[:], ALU.mult, [zq, sgq], [qh])
            for h in range(4):
                c_ = dirn * 4 + h
                K.ts("dve", ff[:, h, :], sgf[:, h, :], drv[:, DV_OML + c_:DV_OML + c_ + 1], drv[:, DV_LB + c_:DV_LB + c_ + 1],
                     ALU.mult, ALU.add, [sgf, drv], [ff])
                K.act(kdh[:, h, :], sgf[:, h, :], AF.Identity, [sgf, drv], [kdh],
                      scale=drv[:, DV_NOML + c_:DV_NOML + c_ + 1], bias=drv[:, DV_OML + c_:DV_OML + c_ + 1])
            K.act(lg[:], ff[:], AF.Ln, [ff], [lg])
            for h in range(4):
                K.op("dve", lambda h=h: nc.vector.tensor_tensor_scan(bcum[:, h, :], rst32, lg[:, h, :], 0.0, ALU.mult, ALU.add),
                     [lg, cst], [bcum])
            if dirn:
                bv4 = bcum[:].rearrange("p h (c t) -> p (h c) t", t=32)
                K.tt("pool", tmp1[:], lg[:], bcum[:], ALU.subtract, [lg, bcum], [tmp1])
                K.tt("dve", e2[:].rearrange("p h (c t) -> p (h c) t", t=32), tmp1[:].rearrange("p h (c t) -> p (h c) t", t=32),
                     _bc(bv4[:, :, 31:32], [128, 16, 32]), ALU.add, [tmp1, bcum], [e2])
                K.cp("pool", bcum[:], e2[:], [e2], [bcum])
            K.act(e1[:], bcum[:], AF.Exp, [bcum], [e1])
            K.act(e2[:], bcum[:], AF.Exp, [bcum], [e2], scale=-1.0)
            K.tt("dve", qb[:], qh[:], e1[:], ALU.mult, [qh, e1], [qb])
            K.tt("pool", kb[:], kdh[:], e2[:], ALU.mult, [kdh, e2], [kb])
            K.cp("pool", ib16[:], zi_[:], [zi_], [ib16])
            tb = psb(0)
            for h in range(4):
                K.tr(tb[:, h * 128:(h + 1) * 128], kb[:, h, :], identb, [kb, cstb], [PS[0]])
            for h in range(4):
                K.tr(tb[:, 512 + h * 128:512 + (h + 1) * 128], ib16[:, h, :], identb, [ib16, cstb], [PS[0]])
            for c in range(4):
                K.act(kbTm[c][:].rearrange("p h t -> p (h t)"), tb[:, 0:512], AF.Identity, [PS[0], cst], [kbTm[c]],
                      scale=cst[:, C_ROWM, c:c + 1])
            K.cp("act", iT[:].rearrange("p h t -> p (h t)"), tb[:, 512:1024], [PS[0]], [iT])
            if isx:
                for h in range(4):
                    K.mm(PS[0][:, h * 128:(h + 1) * 128], kb[:, h, :], qb[:, h, :], [kb, qb], [PS[0]])
                K.tt("dve", ATm[:], PS[0][:, :].rearrange("p (h t) -> p h t", h=4), mshgr[:], ALU.mult, [PS[0], mshgr], [ATm])
            corder = [0, 1, 2, 3] if dirn == 0 else [3, 2, 1, 0]
            for ci, c in enumerate(corder):
                K.cp("act", Sbf[c][:], S[:], [S], [Sbf[c]])
                kvb = PS[0]
                for h in range(4):
                    K.mm(kvb[:, h * 128:(h + 1) * 128], kbTm[c][:, h, :], iT[:, h, :], [kbTm[c], iT], [kvb])
                dcol = c * 32 + (0 if dirn else 31)
                K.tt("dve", stmp[:], kvb[:, :].rearrange("p (h v) -> p h v", h=4), S[:], ALU.add, [kvb, S], [stmp])
                K.tt("pool", S[:], stmp[:], _bc(e1[:, :, dcol:dcol + 1], [128, 4, 128]), ALU.mult, [stmp, e1], [S])
            if isx:
                ob = PS[0]
                for h in range(4):
                    with K.atomic():
                        K.mm(ob[:, h * 128:(h + 1) * 128], iT[:, h, :], ATm[:, h, :], [iT, ATm], [ob], start=True, stop=False)
                        for c in range(4):
                            K.mm(ob[:, h * 128 + c * 32:h * 128 + (c + 1) * 32], Sbf[c][:, h, :], qb[:, h, c * 32:(c + 1) * 32],
                                 [Sbf[c], qb], [ob], start=False, stop=(c == 3))
                K.cp("act", osb[fb][:, 0:4, :], ob[:, :].rearrange("p (h t) -> p h t", h=4), [ob], [osb[fb]])

        def front_rw(n):
            kind, j = order[n]
            isx = kind == "x"
            fb = n % 2
            ec0 = (0 if kind == "ctx" else CTX) + j * 128
            for g in (3, 1, 0, 2):
                nt_ = 4 if g < 3 else 2
                src_ = zr_d.ap[4 * g * 128:(4 * g + nt_) * 128, ec0:ec0 + 128].rearrange("(h p) t -> p h t", p=128)
                K.dma(zrl[:, 4 * g:4 * g + nt_, :], src_, [zr_d], [zrl_g[g]])
            r_ = zrl[:, 0:4, :]
            k_ = zrl[:, 4:8, :]
            v_ = zrl[:, 8:12, :]
            K.act(Lb[0:64, :], zrl[0:64, 12, :], AF.Tanh, [zrl], [Lb])
            K.cp("pool", Lb[64:128, :], zrl[64:128, 12, :], [zrl], [Lb])
            pw = pa = PS[1]
            for jj in range(4):
                K.mm(pw[:, jj * 128:(jj + 1) * 128], wlb[:, dirn, jj * 128:(jj + 1) * 128], Lb[:], [wlb, Lb], [pw])
            for jj in range(4):
                K.act(sw[:, jj, :], pw[:, jj * 128:(jj + 1) * 128], AF.Sigmoid, [pw, ptab], [sw],
                      bias=ptab[:, PT_W0 + dirn * 4 + jj:PT_W0 + dirn * 4 + jj + 1])
            for jj in range(4):
                K.mm(pa[:, jj * 128:(jj + 1) * 128], wlb[:, 2 + dirn, jj * 128:(jj + 1) * 128], Lb[:], [wlb, Lb], [pa])
            for jj in range(4):
                K.act(asg[:, jj, :], pa[:, jj * 128:(jj + 1) * 128], AF.Sigmoid, [pa, ptab], [asg],
                      bias=ptab[:, PT_A0 + dirn * 4 + jj:PT_A0 + dirn * 4 + jj + 1])
            if final and isx:
                for jj in range(4):
                    K.mm(pa[:, jj * 128:(jj + 1) * 128], wlb[:, 2, jj * 128:(jj + 1) * 128], Lb[:], [wlb, Lb], [pa])
                for jj in range(4):
                    K.act(asg2[:, jj, :], pa[:, jj * 128:(jj + 1) * 128], AF.Sigmoid, [pa, ptab], [asg2],
                          bias=ptab[:, PT_A0 + jj:PT_A0 + jj + 1])
            for jj in range(4):
                K.op("dve", lambda jj=jj: nc.vector.tensor_tensor_scan(cs[:, jj, :], rst64, sw[:, jj, :], 0.0, ALU.mult, ALU.add),
                     [sw, cst], [cs])
            if dirn == 0:
                K.tt("pool", csm[:], cs[:], sw[:], ALU.subtract, [cs, sw], [csm])
            else:
                c8 = cs[:].rearrange("p j (c t) -> p (j c) t", t=64)
                K.tt("dve", csm[:].rearrange("p j (c t) -> p (j c) t", t=64), _bc(c8[:, :, 63:64], [128, 8, 64]), c8, ALU.subtract,
                     [cs], [csm])
                K.tt("pool", cs[:], csm[:], sw[:], ALU.add, [csm, sw], [cs])
            K.act(E1[:], csm[:], AF.Exp, [csm], [E1], scale=-LWS)
            K.act(E2[:], cs[:], AF.Exp, [cs], [E2], scale=-LWS)
            K.act(E3[:], cs[:], AF.Exp, [cs], [E3], scale=LWS)
            goff = 0 if dirn else 63
            K.cp("pool", gam[fb][:], E2[:].rearrange("p j (c t) -> p j c t", t=64)[:, :, :, goff], [E2], [gam[fb]])
            for jj in range(4):
                K.act(kk[:, jj, :], k_[:, jj, :], AF.Identity, [zrl, ptab], [kk], scale=ptab[:, PT_KK + jj:PT_KK + jj + 1])
            K.act(sq[:], kk[:], AF.Square, [kk], [sq])
            K.mm(PS[1][:, :], blk64, sq[:].rearrange("p j t -> p (j t)"), [cst, sq], [PS[1]])
            K.ts("dve", rn[:].rearrange("p j t -> p (j t)"), PS[1][:, :], epsc[:, 3:4], None, ALU.max, None, [PS[1], epsc], [rn])
            K.act(rn[:], rn[:], AF.Ln, [rn], [rn])
            K.act(rn[:], rn[:], AF.Exp, [rn], [rn], scale=-0.5)
            K.tt("dve", kkn[:], kk[:], rn[:], ALU.mult, [kk, rn], [kkn])
            for jj in range(4):
                K.act(tmpa[:, jj, :], asg[:, jj, :], AF.Identity, [asg, ptab, drv], [tmpa],
                      scale=ptab[:, PT_KA + jj:PT_KA + jj + 1], bias=drv[:, DV_OMKA + jj:DV_OMKA + jj + 1])
            K.tt("pool", kdr[:], k_, tmpa[:], ALU.mult, [zrl, tmpa], [kdr])
            K.tt("pool", bvec[:], kkn[:], asg[:], ALU.mult, [kkn, asg], [bvec])
            K.stt(AR[fb][:, :, 0, :], kkn[:], -1.0, E1[:], ALU.mult, ALU.mult, [kkn, E1], [AR[fb]])
            K.tt("dve", AR[fb][:, :, 1, :], r_, E2[:], ALU.mult, [zrl, E2], [AR[fb]])
            K.tt("dve", bt[fb][:], bvec[:], E3[:], ALU.mult, [bvec, E3], [bt[fb]])
            K.tt("pool", kt[fb][:], kdr[:], E3[:], ALU.mult, [kdr, E3], [kt[fb]])
            K.cp("pool", vb16[:], v_, [zrl], [vb16])
            tb0 = psb(1)
            for jj in range(4):
                K.tr(tb0[:, jj * 128:(jj + 1) * 128], AR[fb][:, jj, 0, :], identb, [AR[fb], cstb], [PS[1]])
                K.tr(tb0[:, 512 + jj * 128:512 + (jj + 1) * 128], vb16[:, jj, :], identb, [vb16, cstb], [PS[1]])
            K.cp("act", aT[fb][:].rearrange("p j t -> p (j t)"), tb0[:, 0:512], [PS[1]], [aT[fb]])
            K.cp("act", vT[fb][:].rearrange("p j t -> p (j t)"), tb0[:, 512:1024], [PS[1]], [vT[fb]])
            for jj in range(4):
                K.tr(tb0[:, jj * 128:(jj + 1) * 128], bt[fb][:, jj, :], identb, [bt[fb], cstb], [PS[1]])
                K.tr(tb0[:, 512 + jj * 128:512 + (jj + 1) * 128], kt[fb][:, jj, :], identb, [kt[fb], cstb], [PS[1]])
            for c in range(2):
                K.act(btTm[fb][c][:].rearrange("p j t -> p (j t)"), tb0[:, 0:512], AF.Identity, [PS[1], cst], [btTm[fb][c]],
                      scale=cst[:, C_ROWM, 4 + c:5 + c])
                K.act(ktTm[fb][c][:].rearrange("p j t -> p (j t)"), tb0[:, 512:1024], AF.Identity, [PS[1], cst], [ktTm[fb][c]],
                      scale=cst[:, C_ROWM, 4 + c:5 + c])
            if final and isx:
                K.act(sgd[:], zrl[0:96, 13, :], AF.Sigmoid, [zrl], [sgd])
                for jj in range(4):
                    K.act(tmpa2[:, jj, :], asg2[:, jj, :], AF.Identity, [asg2, ptab, drv], [tmpa2],
                          scale=ptab[:, PT_KA + jj:PT_KA + jj + 1], bias=drv[:, DV_OMKA + jj:DV_OMKA + jj + 1])
                K.tt("pool", tmpa2[:], tmpa2[:], tmpa[:], ALU.add, [tmpa2, tmpa], [tmpa2])
                K.tt("pool", tmpa2[:], tmpa2[:], k_, ALU.mult, [tmpa2, zrl], [tmpa2])
                for jj in range(4):
                    K.stt(bonp[:, jj, :], r_[:, jj, :], ptab[:, PT_RK + jj:PT_RK + jj + 1], tmpa2[:, jj, :], ALU.mult, ALU.mult,
                          [zrl, ptab, tmpa2], [bonp])
                K.mm(PS[1][:, :], blk64, bonp[:].rearrange("p j t -> p (j t)"), [cst, bonp], [PS[1]])
                K.tt("dve", exb[fb][:, 0:4, :], PS[1][:, :].rearrange("p (j t) -> p j t", j=4), v_, ALU.mult, [PS[1], zrl], [exb[fb]])
                pg = PS[1]
                for jj in range(4):
                    K.mm(pg[:, jj * 128:(jj + 1) * 128], g2b[:, jj * 128:(jj + 1) * 128], sgd[:], [g2b, sgd], [pg])
                K.cp("act", exb[fb][:, 4:8, :], pg[:, :].rearrange("p (j t) -> p j t", j=4), [pg], [exb[fb]])
                K.dma(ex_d.ap[j], exb[fb][:], [exb[fb]], [ex_d])

        def pairs(n, k):
            kind, j = order[n]
            isx = kind == "x"
            fb = n % 2
            jj = k
            ba = PS[2 + k]
            bb = AUXH[k]
            am, aw, au, qt, pbf = Amat[k], AW[k], AU[k], QT[k], Pbf[k]
            xp, xt = XP[k], XT[k]
            ar, bt_, kt_, aT_, vT_ = AR[fb], bt[fb], kt[fb], aT[fb], vT[fb]
            for e in range(2):
                ep = slice(e * 64, (e + 1) * 64)
                arf = ar[ep, jj, :, :].rearrange("p a t -> p (a t)")
                K.mm(ba[:, 0:256], bt_[ep, jj, :], arf, [bt_, ar], [ba])
                K.mm(ba[:, 256:512], kt_[ep, jj, :], arf, [kt_, ar], [ba])
                bv_ = ba[:, :].rearrange("p (w a t) -> p w a t", w=2, a=2)
                K.tt("dve", xp[:, e, 0, :], ba[:, 0:128], ms32, ALU.mult, [ba, cst], [xp])
                K.tt("dve", am[:, e, :, :, :], bv_, msir[:], ALU.mult, [ba, msir], [am])
            for e in range(2):
                ep = slice(e * 64, (e + 1) * 64)
                bk = ba if e == 0 else bb
                K.mm(bk[:, 0:128], ar[ep, jj, 0, :], bt_[ep, jj, :], [ar, bt_], [bk])
                K.tt("dve", xt[:, e, :], bk[:, 0:128], mnt32, ALU.mult, [bk, cst], [xt])
            K.cp("pool", xp[:, :, 1, :], id32r[:], [id32r], [xp])
            pav = ba[:, :].rearrange("p (e a t) -> p e a t", e=2, a=2)
            for lvl in range(6):
                last = lvl == 5
                for e in range(2):
                    if not last:
                        K.mm(ba[:, e * 256:(e + 1) * 256], xt[:, e, :], xp[:, e, :, :].rearrange("p a t -> p (a t)"), [xt, xp], [ba])
                    else:
                        K.mm(ba[:, e * 256 + 128:(e + 1) * 256], xt[:, e, :], xp[:, e, 1, :], [xt, xp], [ba])
                if not last:
                    for e in range(2):
                        K.mm(bb[:, e * 128:(e + 1) * 128], xp[:, e, 0, :], xt[:, e, :], [xp, xt], [bb])
                K.tt("dve", xp[:, :, 1, :], pav[:, :, 1, :], xp[:, :, 1, :], ALU.add, [ba, xp], [xp])
                if not last:
                    K.cp("act", xp[:, :, 0, :], pav[:, :, 0, :], [ba], [xp])
                    K.cp("act", xt[:], bb[:, 0:256].rearrange("p (e t) -> p e t", e=2), [bb], [xt])
            K.cp("pool", pbf[:], xp[:, :, 1, :], [xp], [pbf])
            for e in range(2):
                K.mm(ba[:, e * 64:(e + 1) * 64], am[:, e, 1, 0, :], vT_[:, jj, e * 64:(e + 1) * 64], [am, vT_], [ba])
            K.cp("pool", aw[:, :, 0:64], aT_[:, jj, :].rearrange("p (e k) -> p e k", e=2), [aT_], [aw])
            K.cp("act", aw[:, :, 64:128], ba[:, 0:128].rearrange("p (e v) -> p e v", e=2), [ba], [aw])
            for e in range(2):
                K.mm(bb[:, e * 128:(e + 1) * 128], pbf[:, e, :], aw[:, e, :], [pbf, aw], [bb])
            K.cp("dve", au[:], bb[:, 0:256].rearrange("p (e c) -> p e c", e=2), [bb], [au])
            if isx:
                for e in range(2):
                    K.mm(ba[e * 64:(e + 1) * 64, 128:256], au[:, e, 0:64], am[:, e, 0, 1, :], [au, am], [ba])
                K.tt("dve", qt[:], ba[:, 128:256], ar[:, jj, 1, :], ALU.add, [ba, ar], [qt])
            corder2 = [0, 1] if dirn == 0 else [1, 0]
            Hj = H[jj]
            for ci, c in enumerate(corder2):
                hb = Hbd[jj][c]
                mt = MTbd[jj][c]
                for e in range(2):
                    K.cp("act", hb[e * 64:(e + 1) * 64, e * 64:(e + 1) * 64], Hj[e * 64:(e + 1) * 64, :], [Hj], [hb])
                for e in range(2):
                    K.mm(bb[e * 64:(e + 1) * 64, 0:64], au[:, e, 0:64], btTm[fb][c][:, jj, e * 64:(e + 1) * 64], [au, btTm[fb][c]], [bb])
                for e in range(2):
                    K.tt("dve", mt[e * 64:(e + 1) * 64, e * 64:(e + 1) * 64], bb[e * 64:(e + 1) * 64, 0:64],
                         ident[e * 64:(e + 1) * 64, e * 64:(e + 1) * 64], ALU.add, [bb, cst], [mt])
                with K.atomic():
                    K.mm(ba[:, 384:448], mt[:], Hj[:], [mt, Hj], [ba], start=True, stop=False)
                    for e in range(2):
                        ep = slice(e * 64, (e + 1) * 64)
                        K.mm(ba[ep, 384:448], btTm[fb][c][:, jj, ep], au[:, e, 64:128], [btTm[fb][c], au], [ba], start=False, stop=False)
                        K.mm(ba[ep, 384:448], ktTm[fb][c][:, jj, ep], vT_[:, jj, ep], [ktTm[fb][c], vT_], [ba], start=False, stop=True)
                K.ts("dve", Hj[:], ba[:, 384:448], gam[fb][:, jj, c:c + 1], None, ALU.mult, None, [ba, gam[fb]], [Hj])
            if isx:
                with K.atomic():
                    for e in range(2):
                        ep = slice(e * 64, (e + 1) * 64)
                        K.mm(ba[ep, 256:384], au[:, e, 64:128], am[:, e, 0, 1, :], [au, am], [ba], start=True, stop=False)
                        K.mm(ba[ep, 256:384], vT_[:, jj, ep], am[:, e, 1, 1, :], [vT_, am], [ba], start=False, stop=False)
                    for c in range(2):
                        K.mm(ba[:, 256 + c * 64:256 + (c + 1) * 64], Hbd[jj][c][:], qt[:, c * 64:(c + 1) * 64], [Hbd[jj][c], qt], [ba],
                             start=False, stop=(c == 1))
                K.cp("act", osb[fb][:, 4 + jj, :], ba[:, 256:384], [ba], [osb[fb]])

        def tail(n):
            kind, j = order[n]
            if kind != "x":
                return
            fb = n % 2
            K.dma((ob_d if final else of_d).ap[j], osb[fb][:], [osb[fb]], [ob_d if final else of_d])

        import os
        NOIL = os.environ.get("NOIL", "0") == "1"
        NT_ = len(order)
        if NOIL:
            for n in range(NT_):
                front_hg(n); front_rw(n); pairs(n, 0); pairs(n, 1); pairs(n, 2); pairs(n, 3); tail(n)
        else:
            K.run_streams([lambda: front_hg(0), lambda: front_rw(0)])
            for n in range(NT_):
                fns = [lambda n=n, k=k: pairs(n, k) for k in range(4)]
                if n + 1 < NT_:
                    fns.append(lambda n=n: front_hg(n + 1))
                    fns.append(lambda n=n: front_rw(n + 1))
                K.run_streams(fns)
                tail(n)
        K.barrier()
        K.stack.close()

    def run_pc():
        K.stack = ExitStack()
        woutb = K.sb([128, 8, D], BF16, "woutb")
        lnb_ = K.sb([128, 2, D], F32, "ln1bc")
        K.dma(lnb_[:, 0, :], lnrow_d.ap[0:1, :].partition_broadcast(128), [lnrow_d], [lnb_])
        K.dma(lnb_[:, 1, :], lnrow_d.ap[1:2, :].partition_broadcast(128), [lnrow_d], [lnb_])
        wo_v = wout_d.ap.rearrange("(k p) c -> p k c", p=128)
        wos = [K.sb([128, D], F32, f"wos{i}") for i in range(2)]
        for dk in range(8):
            K.dma(wos[dk % 2][:], wo_v[:, dk, :], [wout_d], [wos[dk % 2]])
            K.cp("act" if dk % 2 else "dve", woutb[:, dk, :], wos[dk % 2][:], [wos[dk % 2]], [woutb])
        blk64 = cst[:, C_BLK64, :]
        onesf = cst[:, C_ONES, :]
        NS = 2
        bufs = []
        for s in range(NS):
            d_ = {}
            d_["of"] = K.sb([128, 8, 128], F32, f"pc_of{s}")
            d_["ob"] = K.sb([128, 8, 128], F32, f"pc_ob{s}")
            d_["ex"] = K.sb([128, 8, 128], F32, f"pc_ex{s}")
            d_["og"] = K.sb([128, 4, 128], F32, f"pc_og{s}")
            d_["x"] = K.sb([128, D], F32, f"pc_x{s}")
            for nm in ("ohg", "sqh", "rsth", "sog", "ysb", "ycen", "sq2", "rstd2", "yn2"):
                d_[nm] = K.sb([128, 4, 128], F32, f"pc_{nm}{s}")
            d_["yT"] = K.sb([128, 8, 128], BF16, f"pc_yT{s}")
            d_["h1"] = K.sb([128, D], F32, f"pc_h1{s}")
            d_["x1t"] = K.sb([128, D], F32, f"pc_x1t{s}")
            d_["st1"] = K.sb([128, 16], F32, f"pc_st{s}")
            d_["dbg"] = K.sb([128, 8, 128], F32, f"pc_dbg{s}") if debug else None
            bufs.append(d_)

        def pc_stream(s):
            B_ = bufs[s]
            pA, pB, pC, pD = PS[4 * s], PS[4 * s + 1], PS[4 * s + 2], PS[4 * s + 3]
            for j in range(s, NTX, NS):
                ec0 = CTX + j * 128
                K.dma(B_["of"][:], of_d.ap[j], [of_d], [B_["of"]])
                K.dma(B_["ob"][:], ob_d.ap[j], [ob_d], [B_["ob"]])
                K.dma(B_["ex"][:], ex_d.ap[j], [ex_d], [B_["ex"]])
                K.dma(B_["og"][:], zT_d.ap[16 * 128:20 * 128, ec0:ec0 + 128].rearrange("(h p) t -> p h t", p=128), [zT_d], [B_["og"]])
                K.dma(B_["x"][:], x_d[j * 128:(j + 1) * 128, :], [x_d], [B_["x"]])
                ohg, sqh, rsth, sog, ysb, ycen, sq2, rstd2, yn2 = [B_[k_] for k_ in ("ohg", "sqh", "rsth", "sog", "ysb", "ycen", "sq2", "rstd2", "yn2")]
                yT, h1, x1t, st1, xt = B_["yT"], B_["h1"], B_["x1t"], B_["st1"], B_["x"]
                K.tt("pool", ohg[:], B_["of"][:, 0:4, :], B_["ob"][:, 0:4, :], ALU.add, [B_["of"], B_["ob"]], [ohg])
                K.tt("pool", sqh[:], ohg[:], ohg[:], ALU.mult, [ohg], [sqh])
                K.mm(pA[:, :], onesf, sqh[:].rearrange("p h t -> p (h t)"), [cst, sqh], [pA])
                K.act(rsth[:].rearrange("p h t -> p (h t)"), pA[:, :], AF.Ln, [pA, epsc], [rsth], bias=epsc[:, 1:2], scale=1.0 / 128.0)
                K.act(rsth[:], rsth[:], AF.Exp, [rsth], [rsth], scale=-0.5)
                K.act(sog[:], B_["og"][:], AF.Sigmoid, [B_["og"]], [sog])
                K.tt("pool", sog[:], sog[:], B_["og"][:], ALU.mult, [sog, B_["og"]], [sog])
                K.tt("dve", ohg[:], ohg[:], rsth[:], ALU.mult, [ohg, rsth], [ohg])
                K.stt(yT[:, 0:4, :], ohg[:], ptab[:, PT_NW:PT_NW + 1], sog[:], ALU.mult, ALU.mult, [ohg, ptab, sog], [yT])
                K.tt("pool", ysb[:], B_["of"][:, 4:8, :], B_["ob"][:, 4:8, :], ALU.add, [B_["of"], B_["ob"]], [ysb])
                K.mm(pB[:, :], blk64, ysb[:].rearrange("p j t -> p (j t)"), [cst, ysb], [pB])
                K.stt(ycen[:], pB[:, :].rearrange("p (j t) -> p j t", j=4), -1.0 / 64.0, ysb[:], ALU.mult, ALU.add, [pB, ysb], [ycen])
                K.tt("pool", sq2[:], ycen[:], ycen[:], ALU.mult, [ycen], [sq2])
                K.mm(pB[:, :], blk64, sq2[:].rearrange("p j t -> p (j t)"), [cst, sq2], [pB])
                K.act(rstd2[:].rearrange("p j t -> p (j t)"), pB[:, :], AF.Ln, [pB, epsc], [rstd2], bias=epsc[:, 2:3], scale=1.0 / 64.0)
                K.act(rstd2[:], rstd2[:], AF.Exp, [rstd2], [rstd2], scale=-0.5)
                K.tt("dve", ycen[:], ycen[:], rstd2[:], ALU.mult, [ycen, rstd2], [ycen])
                for jj in range(4):
                    K.ts("pool", yn2[:, jj, :], ycen[:, jj, :], ptab[:, PT_LNW + jj:PT_LNW + jj + 1], ptab[:, PT_LNB + jj:PT_LNB + jj + 1],
                         ALU.mult, ALU.add, [ycen, ptab], [yn2])
                K.tt("pool", yn2[:], yn2[:], B_["ex"][:, 0:4, :], ALU.add, [yn2, B_["ex"]], [yn2])
                K.tt("dve", yT[:, 4:8, :], yn2[:], B_["ex"][:, 4:8, :], ALU.mult, [yn2, B_["ex"]], [yT])
                if debug:
                    K.cp("pool", B_["dbg"][:], yT[:], [yT], [B_["dbg"]])
                    K.dma(dbg["yT"].ap[j], B_["dbg"][:], [B_["dbg"]], [dbg["yT"]])
                for dh in range(2):
                    bank = pC if dh == 0 else pD
                    with K.atomic():
                        for m in range(8):
                            K.mm(bank[:, :], yT[:, m, :], woutb[:, m, dh * 512:(dh + 1) * 512], [yT, woutb], [bank], start=(m == 0), stop=(m == 7))
                    K.tt("dve", h1[:, dh * 512:(dh + 1) * 512], bank[:, :], gb[:, 0, dh * 512:(dh + 1) * 512], ALU.mult, [bank, gb], [h1])
                K.stt(h1[:], xt[:], ALPHA, h1[:], ALU.mult, ALU.add, [xt, h1], [h1])
                ln_stats2(K, nc, h1, h1.ap, st1, epsc, 1)
                K.ts("dve", x1t[:], h1[:], st1[:, 12:13], st1[:, 15:16], ALU.subtract, ALU.mult, [h1, st1], [x1t])
                K.tt("pool", x1t[:], x1t[:], lnb_[:, 0, :], ALU.mult, [x1t, lnb_], [x1t])
                K.tt("pool", x1t[:], x1t[:], lnb_[:, 1, :], ALU.add, [x1t, lnb_], [x1t])
                K.dma(x1_d[j * 128:(j + 1) * 128, :], x1t[:], [x1t], [x1_d])
                if debug:
                    K.dma(dbg["x1"][j * 128:(j + 1) * 128, :], x1t[:], [x1t], [dbg["x1"]])
        K.run_streams([lambda s=s: pc_stream(s) for s in range(NS)])
        K.barrier()
        K.stack.close()

    if upto < 1:
        return nc, K
    run_lerp()
    run_pass(0)
    if upto < 2:
        return nc, K
    run_pass(1)
    if upto < 2.5:
        return nc, K
    run_pc()
    if upto < 3:
        return nc, K

    GT = 2
    GC = GT * 128
    K.stack = ExitStack()
    wgb = K.sb([128, 8, DFF], BF16, "wgb")
    wub = K.sb([128, 8, DFF], BF16, "wub")
    wdb = K.sb([128, NFT, D], BF16, "wdb")
    outer4 = K.stack
    K.stack = ExitStack()
    stg = [K.sb([128, DFF], F32, f"stg{i}") for i in range(2)]
    si = 0
    for (wd_, wb_, nk, ncol) in ((wg_d, wgb, 8, DFF), (wu_d, wub, 8, DFF), (wd_d, wdb, NFT, D)):
        v = wd_.ap.rearrange("(k p) c -> p k c", p=128)
        for kk_ in range(nk):
            s = stg[si % 2]
            K.dma(s[:, 0:ncol], v[:, kk_, :], [wd_], [s])
            K.cp(("act", "dve", "pool")[si % 3], wb_[:, kk_, :], s[:, 0:ncol], [s], [wb_])
            si += 1
    K.barrier()
    K.stack.close()
    K.stack = outer4
    ln2bc = K.sb([128, 2, D], F32, "ln2bc")
    K.dma(ln2bc[:, 0, :], lnrow_d.ap[2:3, :].partition_broadcast(128), [lnrow_d], [ln2bc])
    K.dma(ln2bc[:, 1, :], lnrow_d.ap[3:4, :].partition_broadcast(128), [lnrow_d], [ln2bc])
    x1g = [K.sb([128, GT, D], F32, f"x1g{i}") for i in range(2)]
    u2T = [K.sb([128, 8, GC], BF16, f"u2T{i}") for i in range(2)]
    hT = K.sb([128, NFT, GC], BF16, "hT")
    xnb2 = [K.sb([128, D], BF16, "xnb2_0")] * 2
    st2 = [K.sb([128, 16], F32, f"st2_{i}") for i in range(2)]
    sgl = [K.sb([128, GC], F32, f"sgl{i}") for i in range(2)]
    h2 = [K.sb([128, D], F32, f"h2_{i}") for i in range(2)]
    st3 = [K.sb([128, 16], F32, f"st3_{i}") for i in range(2)]
    ngrp = (NTX + GT - 1) // GT
    GUB = [PS[i] for i in (0, 1, 2, 3, 6)]

    def ffn_pro(g):
        nt = min(GT, NTX - g * GT)
        xg = x1g[g % 2]
        ut = u2T[g % 2]
        for i in range(nt):
            t = g * GT + i
            K.dma(xg[:, i, :], x1_d[t * 128:(t + 1) * 128, :], [x1_d], [xg])
        for i in range(nt):
            st = st2[i % 2]
            xb = xnb2[i % 2]
            ln_stats2(K, nc, xg, xg[:, i, :], st, epsc, 0)
            K.ts("dve", xb[:], xg[:, i, :], st[:, 12:13], st[:, 15:16], ALU.subtract, ALU.mult, [xg, st], [xb])
            tb = psb(7)
            with K.atomic():
                for dk in range(8):
                    K.tr(tb[:, dk * 128:(dk + 1) * 128], xb[:, dk * 128:(dk + 1) * 128], identb, [xb, cstb], [PS[7]])
            for dk in range(8):
                K.act(ut[:, dk, i * 128:(i + 1) * 128], tb[:, dk * 128:(dk + 1) * 128], AF.Identity, [PS[7], opsc, modT], [ut],
                      bias=modT[:, 24 + dk, 0:1], scale=opsc[:, 2, dk:dk + 1])

    def ffn_main(g):
        nt = min(GT, NTX - g * GT)
        ncol = nt * 128
        xg = x1g[g % 2]
        ut = u2T[g % 2]
        for ft in range(NFT):
            bg, bu = GUB[(2 * ft) % 5], GUB[(2 * ft + 1) % 5]
            with K.atomic():
                for dk in range(8):
                    K.mm(bg[:, 0:ncol], wgb[:, dk, ft * 128:(ft + 1) * 128], ut[:, dk, 0:ncol], [wgb, ut], [bg], start=(dk == 0), stop=(dk == 7))
            with K.atomic():
                for dk in range(8):
                    K.mm(bu[:, 0:ncol], wub[:, dk, ft * 128:(ft + 1) * 128], ut[:, dk, 0:ncol], [wub, ut], [bu], start=(dk == 0), stop=(dk == 7))
            sg_ = sgl[ft % 2]
            K.act(sg_[:, 0:ncol], bg[:, 0:ncol], AF.Sigmoid, [bg], [sg_])
            K.tt("dve", sg_[:, 0:ncol], sg_[:, 0:ncol], bg[:, 0:ncol], ALU.mult, [sg_, bg], [sg_])
            K.tt("dve", hT[:, ft, 0:ncol], sg_[:, 0:ncol], bu[:, 0:ncol], ALU.mult, [sg_, bu], [hT])
        for i in range(nt):
            t = g * GT + i
            hh = h2[t % 2]
            st = st3[t % 2]
            for dh in range(2):
                bank = PS[4 + dh]
                with K.atomic():
                    for ft in range(NFT):
                        K.mm(bank[:, :], hT[:, ft, i * 128:(i + 1) * 128], wdb[:, ft, dh * 512:(dh + 1) * 512], [hT, wdb], [bank],
                             start=(ft == 0), stop=(ft == NFT - 1))
                K.tt("dve", hh[:, dh * 512:(dh + 1) * 512], bank[:, :], gb[:, 1, dh * 512:(dh + 1) * 512], ALU.mult, [bank, gb], [hh])
            K.stt(hh[:], xg[:, i, :], ALPHA, hh[:], ALU.mult, ALU.add, [xg, hh], [hh])
            ln_stats2(K, nc, hh, hh.ap, st, epsc, 1)
            K.ts("dve", hh[:], hh[:], st[:, 12:13], st[:, 15:16], ALU.subtract, ALU.mult, [hh, st], [hh])
            K.tt("pool", hh[:], hh[:], ln2bc[:, 0, :], ALU.mult, [hh, ln2bc], [hh])
            K.tt("pool", hh[:], hh[:], ln2bc[:, 1, :], ALU.add, [hh, ln2bc], [hh])
            K.dma(out_d[t * 128:(t + 1) * 128, :], hh[:], [hh], [out_d])

    ffn_pro(0)
    for g in range(ngrp):
        fns = [lambda g=g: ffn_main(g)]
        if g + 1 < ngrp:
            fns.append(lambda g=g: ffn_pro(g + 1))
        K.run_streams(fns)
    K.barrier()
    K.stack.close()
    return nc, K


def ln_stats2(K, nc, src, ap, st, epsc, eps_col):
    K.op("dve", lambda: nc.vector.bn_stats(st[:, 0:6], ap[:, 0:512]), [src], [st])
    K.op("dve", lambda: nc.vector.bn_stats(st[:, 6:12], ap[:, 512:1024]), [src], [st])
    K.op("dve", lambda: nc.vector.bn_aggr(st[:, 12:14], st[:, 0:12]), [st], [st])
    K.act(st[:, 14:15], st[:, 13:14], AF.Ln, [st, epsc], [st], bias=epsc[:, eps_col:eps_col + 1])
    K.act(st[:, 15:16], st[:, 14:15], AF.Exp, [st], [st], scale=-0.5)


def _consts():
    c = np.zeros((128, NCONST, 128), np.float32)
    s = np.arange(128)[:, None]
    t = np.arange(128)[None, :]
    c[:, C_ID] = (s == t)
    c[:, C_M32F] = (s // 32 == t // 32) & (s <= t)
    c[:, C_M32B] = (s // 32 == t // 32) & (s >= t)
    c[:, C_MSF] = (s // 64 == t // 64) & (s < t)
    c[:, C_MIF] = (s // 64 == t // 64) & (s <= t)
    c[:, C_MSB] = (s // 64 == t // 64) & (s > t)
    c[:, C_MIB] = (s // 64 == t // 64) & (s >= t)
    c[:, C_BLK64] = (s // 64 == t // 64)
    c[:, C_ONES] = 1.0
    c[:, C_RST32] = np.broadcast_to((t % 32 != 0), (128, 128))
    c[:, C_RST64] = np.broadcast_to((t % 64 != 0), (128, 128))
    rm = np.zeros((128, 128), np.float32)
    for k in range(4):
        rm[:, k] = (np.arange(128) // 32 == k)
    for k in range(2):
        rm[:, 4 + k] = (np.arange(128) // 64 == k)
    c[:, C_ROWM] = rm
    return c


def _fm(v, nt):
    return np.ascontiguousarray(np.asarray(v, np.float32).reshape(nt, 128).T)


def _ptab(inp):
    pt = np.zeros((128, NPT), np.float32)
    lbl = np.asarray(inp["hgrn_lb_logits"], np.float32)
    for d in range(2):
        pt[:, PT_L0 + 4 * d:PT_L0 + 4 * d + 4] = _fm(lbl[0, d], 4)
        pt[:, PT_L1 + 4 * d:PT_L1 + 4 * d + 4] = _fm(lbl[1, d], 4)
    pt[:, PT_NW] = np.asarray(inp["hgrn_norm_w"], np.float32)[0]
    mu = np.zeros(14 * 128, np.float32)
    mu[:1760] = np.asarray(inp["rwkv_mu"], np.float32)[0]
    pt[:, PT_MU:PT_MU + 14] = _fm(mu, 14)
    ch = np.arange(14 * 128)
    valid = ch < 1760
    masks = [ch < 440, (ch >= 440) & (ch < 880), (ch >= 880) & (ch < 1320), (ch >= 1320) & valid, ch < 880, (ch >= 880) & valid]
    for i, m in enumerate(masks):
        pt[:, PT_ML + 14 * i:PT_ML + 14 * (i + 1)] = _fm(m.astype(np.float32), 14)
    for d in range(2):
        pt[:, PT_W0 + 4 * d:PT_W0 + 4 * d + 4] = _fm(inp["rwkv_w0"][0, d], 4)
        pt[:, PT_A0 + 4 * d:PT_A0 + 4 * d + 4] = _fm(inp["rwkv_a0"][0, d], 4)
    pt[:, PT_KK:PT_KK + 4] = _fm(inp["rwkv_k_k"][0], 4)
    pt[:, PT_KA:PT_KA + 4] = _fm(inp["rwkv_k_a"][0], 4)
    pt[:, PT_RK:PT_RK + 4] = _fm(np.asarray(inp["rwkv_r_k"])[0].reshape(512), 4)
    pt[:, PT_LNW:PT_LNW + 4] = _fm(inp["rwkv_lnx_w"][0], 4)
    pt[:, PT_LNB:PT_LNB + 4] = _fm(inp["rwkv_lnx_b"][0], 4)
    pt[:, PT_BADA:PT_BADA + 48] = _fm(inp["b_ada"][0], 48)
    return pt


def _shared_maps(inp):
    f = lambda a: np.ascontiguousarray(np.asarray(a, np.float32))
    wl4 = np.zeros((128, 4, 512), np.float32)
    wl4[0:32, 0] = inp["rwkv_w2"][0, 0]
    wl4[32:64, 1] = inp["rwkv_w2"][0, 1]
    wl4[64:96, 2] = inp["rwkv_a2"][0, 0]
    wl4[96:128, 3] = inp["rwkv_a2"][0, 1]
    lnrows = np.stack([f(inp["ln1_g"])[0], f(inp["ln1_b"])[0], f(inp["ln2_g"])[0], f(inp["ln2_b"])[0]], 0)
    return {
        "w_ada": f(inp["w_ada"])[0], "b_ada_row": f(inp["b_ada"]), "w_in": f(inp["w_in"])[0], "ptab": _ptab(inp),
        "consts": _consts(), "wl4": wl4, "g2": f(inp["rwkv_g2"])[0], "w_out": f(inp["w_out"])[0],
        "lnrows": np.ascontiguousarray(lnrows), "w_gate": f(inp["w_ffn_gate"])[0], "w_up": f(inp["w_ffn_up"])[0],
        "w_down": f(inp["w_ffn_down"])[0],
    }


def _core_map(inp, shared, b):
    m = dict(shared)
    m["x"] = np.ascontiguousarray(np.asarray(inp["x"][b], np.float32))
    m["ctx"] = np.ascontiguousarray(np.asarray(inp["ctx"][b], np.float32))
    cv = np.zeros((128, 16), np.float32)
    cv[:, 0::2] = np.asarray(inp["c"][b], np.float32).reshape(8, 128).T
    cv[:, 1::2] = np.asarray(inp["c_ctx"], np.float32).reshape(8, 128).T
    m["cv"] = cv
    return m


_NC_CACHE = {}


def kernel(**inputs):
    x = np.asarray(inputs["x"])
    B, T, _ = x.shape
    if T not in _NC_CACHE:
        _NC_CACHE[T] = build(T)[0]
    nc = _NC_CACHE[T]
    shared = _shared_maps(inputs)
    in_maps = [_core_map(inputs, shared, b) for b in range(B)]
    res = run_bass_kernel_spmd(nc, in_maps, core_ids=list(range(B)))
    return np.stack([np.asarray(r["out"], np.float32) for r in res.results], 0)
```

```python
from contextlib import ExitStack
import numpy as np
import concourse.bass as bass
import concourse.mybir as mybir
from concourse.bass_utils import run_bass_kernel_spmd

F32 = mybir.dt.float32
BF16 = mybir.dt.bfloat16
ALU = mybir.AluOpType
AF = mybir.ActivationFunctionType

D = 1024
CTX = 256
NCT = 34
ZC = NCT * 128
IN_COLS = 4320
DFF = 2816
NFT = DFF // 128
LWS = 0.6065306597126334
ALPHA = 2.0 ** 0.25

C_ID, C_M32F, C_M32B, C_MSF, C_MIF, C_MSB, C_MIB, C_BLK64, C_ONES, C_RST32, C_RST64, C_ROWM = range(12)
NCONST = 12
PT_L0 = 0
PT_L1 = 8
PT_NW = 16
PT_MU = 17
PT_ML = 31
PT_W0 = 115
PT_A0 = 123
PT_KK = 131
PT_KA = 135
PT_RK = 139
PT_LNW = 143
PT_LNB = 147
PT_BADA = 151
NPT = 199


class Buf:
    __slots__ = ("ap", "w", "r", "name", "tw", "tr", "root")

    def __init__(self, ap, name="", root=None):
        self.root = root if root is not None else self
        self.ap = ap
        self.w = {}
        self.r = {}
        self.name = name
        self.tw = 0.0
        self.tr = 0.0

    def __getitem__(self, k):
        return self.ap[k]


class KB:
    NR = 8

    def __init__(self, nc):
        self.nc = nc
        self.E = {"pe": nc.tensor, "dve": nc.vector, "act": nc.scalar, "pool": nc.gpsimd, "sp": nc.sync}
        self.sems = []
        self.semval = []
        self.esem = {e: self._newsem("c_" + e) for e in self.E}
        self.dsem = {"sp": [self._newsem(f"d_sp{i}") for i in range(self.NR)]}
        self.didx = {"sp": 0}
        self.seen = {e: {} for e in self.E}
        self.nbuf = 0
        self.nwait = 0
        self.ninst = 0
        self.stack = None
        self._st = None
        self.clk = {}

    def _newsem(self, name):
        self.sems.append(self.nc.alloc_semaphore(name))
        self.semval.append(0)
        return len(self.sems) - 1

    def sb(self, shape, dtype=F32, name=None, perm=False):
        self.nbuf += 1
        name = f"{name or 't'}_{self.nbuf}"
        if perm or self.stack is None:
            h = self.nc.alloc_sbuf_tensor(name, list(shape), dtype)
        else:
            h = self.stack.enter_context(self.nc.sbuf_tensor(name, list(shape), dtype))
        return Buf(h.ap(), name)

    def ps(self, shape, dtype=F32, name=None):
        self.nbuf += 1
        return Buf(self.nc.alloc_psum_tensor(f"{name or 'p'}_{self.nbuf}", list(shape), dtype).ap(), name)

    def dram(self, name, shape, dtype=F32, kind="Internal"):
        return Buf(self.nc.dram_tensor(name, list(shape), dtype, kind=kind).ap(), name)

    def _need(self, reads, writes):
        need = {}
        for b in reads:
            for k, v in b.w.items():
                if need.get(k, 0) < v:
                    need[k] = v
        for b in writes:
            for k, v in b.w.items():
                if need.get(k, 0) < v:
                    need[k] = v
            for k, v in b.r.items():
                if need.get(k, 0) < v:
                    need[k] = v
        return need

    def _wait(self, e, need):
        own = self.esem[e]
        seen = self.seen[e]
        eng = self.E[e]
        for k, v in need.items():
            if k == own and e == "pe":
                continue
            if seen.get(k, 0) < v:
                eng.wait_ge(self.sems[k], v)
                seen[k] = v
                self.nwait += 1

    def _commit(self, k, v, reads, writes):
        for b in writes:
            b.w = {k: v}
            b.r = {}
        for b in reads:
            if b.r.get(k, 0) < v:
                b.r[k] = v

    def op(self, e, fn, reads=(), writes=(), cost=0.5):
        reads = [b.root for b in reads]
        writes = [b.root for b in writes]
        self._yield(e, reads, writes)
        self._model(e, reads, writes, cost)
        self._wait(e, self._need(reads, writes))
        ins = fn()
        k = self.esem[e]
        self.semval[k] += 1
        ins.then_inc(self.sems[k], 1)
        self._commit(k, self.semval[k], reads, writes)
        self.ninst += 1
        return ins

    def dma(self, out, in_, reads=(), writes=(), q="sp"):
        reads = [b.root for b in reads]
        writes = [b.root for b in writes]
        self._yield(q, reads, writes)
        self._model(q, reads, writes, 1.0)
        i = self.didx[q]
        self.didx[q] += 1
        k = self.dsem[q][i % self.NR]
        need = self._need(reads, writes)
        if self.semval[k] > 0 and need.get(k, 0) < self.semval[k]:
            need[k] = self.semval[k]
        self._wait(q, need)
        self.semval[k] += 16
        self.E[q].dma_start(out=out, in_=in_).then_inc(self.sems[k], 16)
        self._commit(k, self.semval[k], reads, writes)
        self.ninst += 1


    def run_streams(self, fns):
        import threading
        n = len(fns)
        if n == 1:
            fns[0]()
            return
        st = {"turn": -1, "alive": [True] * n, "err": [], "pend": [None] * n, "started": 0}
        cv = threading.Condition()
        self._st, self._cv = st, cv
        self._tls = threading.local()

        def pick():
            best, bt_ = -1, None
            for i in range(n):
                if st["alive"][i] and st["pend"][i] is not None:
                    t = st["pend"][i]
                    if bt_ is None or t < bt_:
                        best, bt_ = i, t
            st["turn"] = best
            cv.notify_all()
        self._pick = pick

        def all_pending():
            return all((not st["alive"][i]) or st["pend"][i] is not None for i in range(n))
        self._all_pending = all_pending

        def runner(i):
            self._tls.sid = i
            self._tls.atomic = 0
            try:
                fns[i]()
            except BaseException as e:
                st["err"].append(e)
            finally:
                with cv:
                    st["alive"][i] = False
                    st["pend"][i] = None
                    if any(st["alive"]) and all_pending():
                        pick()
        ths = [threading.Thread(target=runner, args=(i,)) for i in range(n)]
        for t in ths:
            t.start()
        for t in ths:
            t.join()
        self._st = None
        if st["err"]:
            raise st["err"][0]

    def _est_start(self, e, reads, writes):
        t = self.clk.get(e, 0.0)
        for b in reads:
            if b.tw > t:
                t = b.tw
        for b in writes:
            if b.tw > t:
                t = b.tw
            if b.tr > t:
                t = b.tr
        return t

    def _model(self, e, reads, writes, cost):
        t = self._est_start(e, reads, writes) + 0.15
        f = t + cost
        if e == "sp":
            self.clk[e] = t + 0.05
            f = t + 2.0 + cost
        else:
            self.clk[e] = f
        for b in writes:
            b.tw = f
        for b in reads:
            if b.tr < f:
                b.tr = f

    def _yield(self, e, reads, writes):
        st = getattr(self, "_st", None)
        if st is None:
            return
        tls = self._tls
        i = getattr(tls, "sid", None)
        if i is None or tls.atomic:
            return
        cv = self._cv
        with cv:
            st["pend"][i] = self._est_start(e, reads, writes)
            if self._all_pending():
                self._pick()
            while st["turn"] != i:
                cv.wait()
            st["turn"] = -1
            st["pend"][i] = None

    def atomic(self):
        kb = self

        class _A:
            def __enter__(self_):
                if getattr(kb, "_st", None) is not None and getattr(kb._tls, "sid", None) is not None:
                    kb._tls.atomic += 1

            def __exit__(self_, *a):
                if getattr(kb, "_st", None) is not None and getattr(kb._tls, "sid", None) is not None:
                    kb._tls.atomic -= 1
        return _A()

    def barrier(self):
        need = {k: v for k, v in enumerate(self.semval) if v > 0}
        for e in self.E:
            self._wait(e, need)

    @staticmethod
    def _n(ap):
        n = 1
        for d in ap.shape[1:]:
            n *= d
        return n

    def mm(self, out, lhsT, rhs, reads, writes, start=True, stop=True):
        nc = self.nc
        c = 0.03 + self._n(rhs) * (4 if rhs.dtype == F32 else 1) / 2400.0
        return self.op("pe", lambda: nc.tensor.matmul(out, lhsT, rhs, start=start, stop=stop), reads, writes, cost=c)

    def tr(self, out, in_, ident, reads, writes):
        nc = self.nc
        return self.op("pe", lambda: nc.tensor.transpose(out, in_, ident), reads, writes, cost=0.09)

    def act(self, out, in_, func, reads, writes, bias=None, scale=None):
        nc = self.nc
        kw = {}
        if bias is not None:
            kw["bias"] = bias
        if scale is not None:
            kw["scale"] = scale
        c = 0.2 + self._n(out) / 1200.0
        return self.op("act", lambda: nc.scalar.activation(out, in_, func, **kw), reads, writes, cost=c)

    def _vc(self, e, out):
        n = self._n(out)
        return (0.1 + n / 500.0) if e == "pool" else (0.07 + n / 900.0)

    def tt(self, e, out, in0, in1, op, reads, writes):
        eng = self.E[e]
        return self.op(e, lambda: eng.tensor_tensor(out, in0, in1, op), reads, writes, cost=self._vc(e, out))

    def ts(self, e, out, in0, s1, s2, op0, op1, reads, writes):
        eng = self.E[e]
        if s2 is None:
            return self.op(e, lambda: eng.tensor_scalar(out, in0, s1, None, op0), reads, writes, cost=self._vc(e, out))
        return self.op(e, lambda: eng.tensor_scalar(out, in0, s1, s2, op0, op1), reads, writes, cost=self._vc(e, out))

    def stt(self, out, in0, scalar, in1, op0, op1, reads, writes):
        nc = self.nc
        return self.op("dve", lambda: nc.vector.scalar_tensor_tensor(out, in0, scalar, in1, op0, op1), reads, writes,
                       cost=0.07 + self._n(out) / 900.0)

    def cp(self, e, out, in_, reads, writes):
        if e == "act":
            nc = self.nc
            return self.op("act", lambda: nc.scalar.copy(out, in_), reads, writes, cost=0.2 + self._n(out) / 1200.0)
        eng = self.E[e]
        return self.op(e, lambda: eng.tensor_copy(out, in_), reads, writes, cost=self._vc(e, out))

    def memset(self, e, ap, val, writes):
        eng = self.E[e]
        return self.op(e, lambda: eng.memset(ap, val), [], writes, cost=self._vc(e, ap))


def _bc(ap, shape):
    return ap.to_broadcast(list(shape))


def build(T, debug=False, upto=9):
    NTX = T // 128
    NE = CTX + T
    nc = bass.Bass("TRN2", target_bir_lowering=False)
    K = KB(nc)
    x_d = K.dram("x", [T, D], kind="ExternalInput")
    ctx_d = K.dram("ctx", [CTX, D], kind="ExternalInput")
    cv_d = K.dram("cv", [128, 16], kind="ExternalInput")
    wada_d = K.dram("w_ada", [D, 6 * D], kind="ExternalInput")
    brow_d = K.dram("b_ada_row", [1, 6 * D], kind="ExternalInput")
    win_d = K.dram("w_in", [D, IN_COLS], kind="ExternalInput")
    ptab_d = K.dram("ptab", [128, NPT], kind="ExternalInput")
    const_d = K.dram("consts", [128, NCONST, 128], kind="ExternalInput")
    wl_d = K.dram("wl4", [128, 4, 512], kind="ExternalInput")
    g2_d = K.dram("g2", [96, 512], kind="ExternalInput")
    wout_d = K.dram("w_out", [D, D], kind="ExternalInput")
    lnrow_d = K.dram("lnrows", [4, D], kind="ExternalInput")
    wg_d = K.dram("w_gate", [D, DFF], kind="ExternalInput")
    wu_d = K.dram("w_up", [D, DFF], kind="ExternalInput")
    wd_d = K.dram("w_down", [DFF, D], kind="ExternalInput")
    out_d = K.dram("out", [T, D], kind="ExternalOutput")
    zT_d = K.dram("zT", [ZC, NE])
    of_d = K.dram("ofwd", [NTX, 128, 8, 128])
    x1_d = K.dram("x1s", [T, D])
    ob_d = K.dram("obwd", [NTX, 128, 8, 128])
    zr_d = K.dram("zrl", [14 * 128, NE])
    ex_d = K.dram("extra", [NTX, 128, 8, 128])
    dbg = {}
    if debug:
        dbg["zT"] = K.dram("dbg_zT", [ZC, NE], kind="ExternalOutput")
        dbg["yT"] = K.dram("dbg_yT", [NTX, 128, 8, 128], kind="ExternalOutput")
        dbg["x1"] = K.dram("dbg_x1", [T, D], kind="ExternalOutput")

    cst = K.sb([128, NCONST, 128], F32, "cst", perm=True)
    cstb = K.sb([128, NCONST, 128], BF16, "cstb", perm=True)
    ptab = K.sb([128, NPT], F32, "ptab", perm=True)
    modT = K.sb([128, 48, 2], F32, "modT", perm=True)
    opsc = K.sb([128, 3, 8], F32, "opsc", perm=True)
    epsc = K.sb([128, 4], F32, "epsc", perm=True)
    gb = K.sb([128, 2, D], F32, "gb", perm=True)
    drv = K.sb([128, 128], F32, "drv", perm=True)
    DV_LB, DV_OML, DV_NOML = 0, 8, 16
    DV_C0 = 24
    DV_CS = 38
    DV_OMKA = 122
    PS = [K.ps([128, 512], F32, f"bank{i}") for i in range(8)]

    def psb(i):
        return PS[i].ap.bitcast(BF16)

    ident = cst[:, C_ID, :]
    identb = cstb[:, C_ID, :]

    K.dma(cst[:], const_d[:, :, :], [const_d], [cst])
    K.dma(ptab[:], ptab_d[:, :], [ptab_d], [ptab])
    K.cp("dve", cstb[:], cst[:], [cst], [cstb])
    K.memset("pool", epsc[:, 0:1], 1e-6, [epsc])
    K.memset("pool", epsc[:, 1:2], 1e-5, [epsc])
    K.memset("pool", epsc[:, 2:3], 64e-5, [epsc])
    K.memset("pool", epsc[:, 3:4], 1e-24, [epsc])
    K.tt("dve", drv[:, 0:8], ptab[:, PT_L0:PT_L0 + 8], ptab[:, PT_L1:PT_L1 + 8], ALU.subtract, [ptab], [drv])
    K.act(drv[:, DV_LB:DV_LB + 8], drv[:, 0:8], AF.Sigmoid, [drv], [drv])
    K.ts("dve", drv[:, DV_OML:DV_OML + 8], drv[:, DV_LB:DV_LB + 8], -1.0, 1.0, ALU.mult, ALU.add, [drv], [drv])
    K.ts("dve", drv[:, DV_NOML:DV_NOML + 8], drv[:, DV_OML:DV_OML + 8], -1.0, None, ALU.mult, None, [drv], [drv])
    K.ts("dve", drv[:, DV_C0:DV_C0 + 14], ptab[:, PT_MU:PT_MU + 14], -1.0, 1.0, ALU.mult, ALU.add, [ptab], [drv])
    for i in range(6):
        K.tt("dve", drv[:, DV_CS + 14 * i:DV_CS + 14 * (i + 1)], ptab[:, PT_MU:PT_MU + 14],
             ptab[:, PT_ML + 14 * i:PT_ML + 14 * (i + 1)], ALU.mult, [ptab], [drv])
    K.ts("dve", drv[:, DV_OMKA:DV_OMKA + 4], ptab[:, PT_KA:PT_KA + 4], -1.0, 1.0, ALU.mult, ALU.add, [ptab], [drv])

    if upto == 0.1:
        K.barrier()
        return nc, K
    K.stack = ExitStack()
    winb = K.sb([128, 8, ZC], BF16, "winb")
    cv = K.sb([128, 16], F32, "cv")
    cvs = K.sb([128, 16], F32, "cvs")
    K.dma(cv[:], cv_d[:, :], [cv_d], [cv])
    outer0 = K.stack
    K.stack = ExitStack()
    brow = K.sb([1, 4, 512], F32, "brow")
    for i_, eg_ in enumerate((4, 5, 10, 11)):
        K.dma(brow[0:1, i_, :], brow_d[0:1, eg_ * 512:(eg_ + 1) * 512], [brow_d], [brow])
    K.act(cvs[:], cv[:], AF.Sigmoid, [cv], [cvs])
    K.tt("dve", cvs[:], cvs[:], cv[:], ALU.mult, [cvs, cv], [cvs])
    wa = [K.sb([128, 8, 512], F32, f"wa{i}") for i in range(2)]
    wada_v = wada_d.ap.rearrange("(k p) e -> p k e", p=128)
    grow = K.sb([1, 512], F32, "grow")
    for eg in range(12):
        w = wa[eg % 2]
        K.dma(w[:], wada_v[:, :, eg * 512:(eg + 1) * 512], [wada_d], [w])
        bank = PS[eg % 2]
        for j in range(4):
            for dk in range(8):
                K.mm(bank[:, 2 * j:2 * j + 2], w[:, dk, j * 128:(j + 1) * 128], cvs[:, 2 * dk:2 * dk + 2], [w, cvs], [bank],
                     start=(dk == 0), stop=(dk == 7))
        for j in range(4):
            et = eg * 4 + j
            K.ts("dve", modT[:, et, :], bank[:, 2 * j:2 * j + 2], ptab[:, PT_BADA + et:PT_BADA + et + 1], None, ALU.add, None,
                 [bank, ptab], [modT])
        if eg in (4, 5, 10, 11):
            gi = 0 if eg < 6 else 1
            half = eg % 2 if eg < 6 else (eg - 10)
            rb_ = PS[2]
            for dk in range(8):
                K.mm(rb_[0:1, 0:512], cvs[:, 2 * dk:2 * dk + 1], w[:, dk, :], [w, cvs], [rb_], start=(dk == 0), stop=(dk == 7))
            K.tt("dve", grow[:], rb_[0:1, 0:512], brow[0:1, (4, 5, 10, 11).index(eg), :], ALU.add, [rb_, brow], [grow])
            bb = PS[3]
            K.mm(bb[:, 0:512], cst[0:1, C_ONES, :], grow[:], [cst, grow], [bb])
            K.cp("act", gb[:, gi, half * 512:(half + 1) * 512], bb[:, 0:512], [bb], [gb])
    K.ts("dve", opsc[:, 0, :], modT[:, 8:16, 0], 1.0, None, ALU.add, None, [modT], [opsc])
    K.ts("dve", opsc[:, 1, :], modT[:, 8:16, 1], 1.0, None, ALU.add, None, [modT], [opsc])
    K.ts("dve", opsc[:, 2, :], modT[:, 32:40, 0], 1.0, None, ALU.add, None, [modT], [opsc])
    K.barrier()
    if upto == 0.2:
        return nc, K
    K.stack.close()
    K.stack = ExitStack()
    wst = [K.sb([128, IN_COLS], F32, f"wst{i}") for i in range(2)]
    win_v = win_d.ap.rearrange("(k p) c -> p k c", p=128)
    K.memset("pool", winb[:, :, IN_COLS:ZC], 0.0, [winb])
    for dk in range(8):
        s = wst[dk % 2]
        K.dma(s[:], win_v[:, dk, :], [win_d], [s])
        K.cp("act" if dk % 2 else "dve", winb[:, dk, 0:IN_COLS], s[:], [s], [winb])

    K.barrier()
    if upto == 0.3:
        return nc, K
    K.stack.close()
    K.stack = outer0
    xts = [K.sb([128, D], F32, f"xt{i}") for i in range(2)]
    xnb = [K.sb([128, D], BF16, f"xnb{i}") for i in range(2)]
    stt_ = [K.sb([128, 16], F32, f"st{i}") for i in range(2)]

    def ln_stats(src_ap, src_bufs, st, eps_col):
        K.op("dve", lambda: nc.vector.bn_stats(st[:, 0:6], src_ap[:, 0:512]), src_bufs, [st])
        K.op("dve", lambda: nc.vector.bn_stats(st[:, 6:12], src_ap[:, 512:1024]), src_bufs, [st])
        K.op("dve", lambda: nc.vector.bn_aggr(st[:, 12:14], st[:, 0:12]), [st], [st])
        K.act(st[:, 14:15], st[:, 13:14], AF.Ln, [st, epsc], [st], bias=epsc[:, eps_col:eps_col + 1])
        K.act(st[:, 15:16], st[:, 14:15], AF.Exp, [st], [st], scale=-0.5)

    def modulate_T(src, i, uT, col0, sc_ap, sh_ap, tbank):
        st = stt_[i % 2]
        xb = xnb[i % 2]
        ln_stats(src.ap, [src], st, 0)
        K.ts("dve", xb[:], src[:], st[:, 12:13], st[:, 15:16], ALU.subtract, ALU.mult, [src, st], [xb])
        tb = psb(tbank)
        for dk in range(8):
            K.tr(tb[:, dk * 128:(dk + 1) * 128], xb[:, dk * 128:(dk + 1) * 128], identb, [xb, cstb], [PS[tbank]])
        for dk in range(8):
            K.act(uT[:, dk, col0:col0 + 128], tb[:, dk * 128:(dk + 1) * 128], AF.Identity, [PS[tbank], opsc, modT], [uT],
                  bias=sh_ap(dk), scale=sc_ap(dk))

    uTs = [K.sb([128, 8, 512], BF16, f"uT{i}") for i in range(2)]
    zsb = [K.sb([128, 512], F32, f"zsb{i}") for i in range(4)]
    groups = [("ctx", 0, 2)] + [("x", g * 4, min(4, NTX - g * 4)) for g in range((NTX + 3) // 4)]
    ti = 0
    zi = 0
    for gi_, (kind, t0, nt) in enumerate(groups):
        uT = uTs[gi_ % 2]
        src_d = ctx_d if kind == "ctx" else x_d
        mj = 1 if kind == "ctx" else 0
        for i in range(nt):
            xt = xts[ti % 2]
            K.dma(xt[:], src_d[(t0 + i) * 128:(t0 + i + 1) * 128, :], [src_d], [xt])
            modulate_T(xt, ti, uT, i * 128, lambda dk: opsc[:, mj, dk:dk + 1], lambda dk: modT[:, dk, mj:mj + 1], 7)
            ti += 1
        ncol = nt * 128
        ecol0 = (0 if kind == "ctx" else CTX) + t0 * 128
        import os
        ZD = int(os.environ.get("ZDBG", "0"))
        if ZD == 1:
            continue
        for ct in range(NCT):
            bank = PS[ct % 4]
            for dk in range(8):
                K.mm(bank[:, 0:ncol], winb[:, dk, ct * 128:(ct + 1) * 128], uT[:, dk, 0:ncol], [winb, uT], [bank],
                     start=(dk == 0), stop=(dk == 7))
            z = zsb[zi % 4]
            zi += 1
            K.cp("act" if ct % 2 else "dve", z[:, 0:ncol], bank[:, 0:ncol], [bank], [z])
            if ZD == 2:
                continue
            if ZD != 4:
                K.dma(zT_d[ct * 128:(ct + 1) * 128, ecol0:ecol0 + ncol], z[:, 0:ncol], [z], [zT_d])
            if debug and ZD != 3:
                K.dma(dbg["zT"][ct * 128:(ct + 1) * 128, ecol0:ecol0 + ncol], z[:, 0:ncol], [z], [dbg["zT"]])
    K.barrier()
    K.stack.close()

    def run_lerp():
        K.stack = ExitStack()
        NS = 4
        tiles = [("ctx", 0), ("ctx", 1)] + [("x", j) for j in range(NTX)]
        bufs = []
        for s in range(NS):
            zp = K.sb([128, 14, 4, 66], F32, f"lzp{s}")
            zo = [K.sb([128, 14, 128], F32, f"lzo{s}_{i}") for i in range(2)]
            stg_ = [K.sb([128, 4, 256], F32, f"lst{s}_{g}") for g in range(4)]
            K.memset("pool", zp[:], 0.0, [zp])
            bufs.append((zp, stg_, zo))

        def lstream(s):
            zp, stg_, zo = bufs[s]
            for ti, (kind, j) in enumerate(tiles[s::NS]):
                zrl = zo[ti % 2]
                isx = kind == "x"
                ec0 = (0 if kind == "ctx" else CTX) + j * 128
                if isx:
                    lo = 64 if j > 0 else 0
                    hi = 64 if j < NTX - 1 else 0
                else:
                    lo = 1 if j > 0 else 0
                    hi = 1 if j < 1 else 0
                for g in range(4):
                    nt_ = 4 if g < 3 else 2
                    src_ = zT_d.ap[(20 + 4 * g) * 128:(20 + 4 * g + nt_) * 128, ec0 - lo:ec0 + 128 + hi].rearrange("(h p) t -> p h t", p=128)
                    K.dma(stg_[g][:, 0:nt_, 64 - lo:192 + hi], src_, [zT_d], [stg_[g]])
                if isx:
                    if lo == 0:
                        K.memset("pool", zp[:, :, 0, :], 0.0, [zp])
                    if hi == 0:
                        K.memset("pool", zp[:, :, 3, :], 0.0, [zp])
                    r0 = 1 - lo // 64
                    r1 = 3 + hi // 64
                    for g in range(4):
                        nt_ = 4 if g < 3 else 2
                        for q_ in range(nt_):
                            K.cp("pool", zp[:, 4 * g + q_, r0:r1, 1:65],
                                 stg_[g][:, q_, 64 - lo:192 + hi].rearrange("p (r c) -> p r c", c=64), [stg_[g]], [zp])
                    for ct in range(14):
                        views = {"L": zp[:, ct, 1:3, 0:64], "R": zp[:, ct, 1:3, 2:66], "U": zp[:, ct, 0:2, 1:65], "D": zp[:, ct, 2:4, 1:65]}
                        cen = zp[:, ct, 1:3, 1:65]
                        lo_, hi_ = ct * 128, ct * 128 + 128
                        kinds = []
                        if lo_ < 440: kinds.append(("L", 0))
                        if hi_ > 440 and lo_ < 880: kinds.append(("R", 1))
                        if hi_ > 880 and lo_ < 1320: kinds.append(("U", 2))
                        if hi_ > 1320: kinds.append(("D", 3))
                        o3 = zrl[:, ct, :].rearrange("p (r c) -> p r c", c=64)
                        K.act(o3, cen, AF.Identity, [zp, drv], [zrl], scale=drv[:, DV_C0 + ct:DV_C0 + ct + 1])
                        for (vn, ki) in kinds:
                            K.stt(o3, views[vn], drv[:, DV_CS + 14 * ki + ct:DV_CS + 14 * ki + ct + 1], o3, ALU.mult, ALU.add, [zp, drv, zrl], [zrl])
                else:
                    for g in range(4):
                        if lo == 0:
                            K.memset("pool", stg_[g][:, :, 63:64], 0.0, [stg_[g]])
                        if hi == 0:
                            K.memset("pool", stg_[g][:, :, 192:193], 0.0, [stg_[g]])
                    for ct in range(14):
                        g, q_ = ct // 4, ct % 4
                        lo_, hi_ = ct * 128, ct * 128 + 128
                        kinds = []
                        if lo_ < 880: kinds.append((stg_[g][:, q_, 63:191], 4))
                        if hi_ > 880: kinds.append((stg_[g][:, q_, 65:193], 5))
                        K.act(zrl[:, ct, :], stg_[g][:, q_, 64:192], AF.Identity, [stg_[g], drv], [zrl], scale=drv[:, DV_C0 + ct:DV_C0 + ct + 1])
                        for (vw, ki) in kinds:
                            K.stt(zrl[:, ct, :], vw, drv[:, DV_CS + 14 * ki + ct:DV_CS + 14 * ki + ct + 1], zrl[:, ct, :], ALU.mult, ALU.add,
                                  [stg_[g], drv, zrl], [zrl])
                for g in range(2):
                    dst = zr_d.ap[7 * g * 128:(7 * g + 7) * 128, ec0:ec0 + 128].rearrange("(h p) t -> p h t", p=128)
                    K.dma(dst, zrl[:, 7 * g:7 * g + 7, :], [zrl], [zr_d])
        K.run_streams([lambda s=s: lstream(s) for s in range(NS)])
        K.barrier()
        K.stack.close()

    def run_pass(dirn):
        final = dirn == 1
        K.stack = ExitStack()
        wlb = K.sb([128, 4, 512], BF16, "wlb")
        g2b = K.sb([96, 512], BF16, "g2b")
        outer = K.stack
        K.stack = ExitStack()
        tmpw = K.sb([128, 4, 512], F32, "tmpw")
        K.dma(tmpw[:], wl_d[:, :, :], [wl_d], [tmpw])
        K.cp("dve", wlb[:], tmpw[:], [tmpw], [wlb])
        tmpg = K.sb([96, 512], F32, "tmpg")
        K.dma(tmpg[:], g2_d[:, :], [g2_d], [tmpg])
        K.cp("dve", g2b[:], tmpg[:], [tmpg], [g2b])
        K.barrier()
        K.stack.close()
        K.stack = outer
        S = K.sb([128, 4, 128], F32, "S")
        Sbf = [K.sb([128, 4, 128], BF16, f"Sbf{i}") for i in range(4)]
        H = [K.sb([128, 64], F32, f"H{j}") for j in range(4)]
        Hbd = [[K.sb([128, 128], BF16, f"Hbd{j}_{c}") for c in range(2)] for j in range(4)]
        MTbd = [[K.sb([128, 128], F32, f"MT{j}_{c}") for c in range(2)] for j in range(4)]
        K.memset("pool", S[:], 0.0, [S])
        for j in range(4):
            K.memset("pool", H[j][:], 0.0, [H[j]])
            for c in range(2):
                K.memset("pool", Hbd[j][c][:], 0.0, [Hbd[j][c]])
                K.memset("pool", MTbd[j][c][:], 0.0, [MTbd[j][c]])
        for i in range(4):
            K.memset("pool", Sbf[i][:], 0.0, [Sbf[i]])
        ld_q = K.sb([128, 4, 128], F32, "ldq")
        ld_f = K.sb([128, 4, 128], F32, "ldf")
        ld_i = K.sb([128, 4, 128], F32, "ldi")
        zrl = K.sb([128, 14, 128], F32, "zrl")
        zrl_g = [zrl, zrl, zrl, zrl]
        mshg = cstb[:, C_M32B if dirn else C_M32F, :]
        msi = cstb[:, C_MSB:C_MSB + 2, :] if dirn else cstb[:, C_MSF:C_MSF + 2, :]
        ms32 = cst[:, C_MSB, :] if dirn else cst[:, C_MSF, :]
        mnt32 = cst[:, C_MSF, :] if dirn else cst[:, C_MSB, :]
        id32r = K.sb([128, 2, 128], F32, "id32r")
        msir = K.sb([128, 2, 2, 128], BF16, "msir")
        mshgr = K.sb([128, 4, 128], BF16, "mshgr")
        for e_ in range(2):
            K.cp("pool", id32r[:, e_, :], ident, [cst], [id32r])
            K.cp("pool", msir[:, e_, :, :], msi, [cstb], [msir])
        for h_ in range(4):
            K.cp("pool", mshgr[:, h_, :], mshg, [cstb], [mshgr])
        rst32 = cst[:, C_RST32, :]
        rst64 = cst[:, C_RST64, :]
        blk64 = cst[:, C_BLK64, :]

        if dirn == 0:
            order = [("ctx", 0), ("ctx", 1)] + [("x", j) for j in range(NTX)]
        else:
            order = [("ctx", 1), ("ctx", 0)] + [("x", j) for j in range(NTX - 1, -1, -1)]

        def T32(name, shape=(128, 4, 128)):
            return K.sb(list(shape), F32, name)

        def T16(name, shape=(128, 4, 128)):
            return K.sb(list(shape), BF16, name)
        P32 = [T32(f"w32_{i}") for i in range(11)]
        H32 = [T32(f"h32_{i}") for i in range(8)]
        sgq = qh = H32[0]
        sgf = H32[1]
        ff = lg = H32[2]
        kdh = H32[3]
        bcum = H32[4]
        tmp1 = H32[5]
        e1 = H32[6]
        e2 = H32[7]
        sw = sq = P32[0]
        asg = P32[1]
        cs = kdr = P32[2]
        csm = rn = P32[3]
        E1 = P32[4]
        E2 = P32[5]
        tmpa = P32[6]
        bvec = P32[7]
        E3 = P32[8]
        kk = P32[9]
        kkn = P32[10]
        qb, kb, ib16, ATm = [T16(n) for n in ("qb", "kb", "ib16", "ATm")]
        kbTm = [T16(f"kbTm{c}") for c in range(4)]
        iT = T16("iT")
        stmp = T32("stmp")
        Lb = K.sb([128, 128], BF16, "Lb")
        vb16 = T16("vb16")
        if final:
            sgd = K.sb([96, 128], BF16, "sgd")
            asg2, tmpa2, bonp = T32("asg2"), T32("tmpa2"), T32("bonp")
        osb = [K.sb([128, 8, 128], F32, f"osb{i}") for i in range(2)]
        exb = [K.sb([128, 8, 128], F32, f"exb{i}") for i in range(2)] if final else None
        AR = [K.sb([128, 4, 2, 128], BF16, f"AR{i}") for i in range(2)]
        bt = [T16(f"bt{i}") for i in range(2)]
        kt = [T16(f"kt{i}") for i in range(2)]
        aT = [T16(f"aT{i}") for i in range(2)]
        vT = [T16(f"vT{i}") for i in range(2)]
        btTm = [[T16(f"btTm{i}_{c}") for c in range(2)] for i in range(2)]
        ktTm = [[T16(f"ktTm{i}_{c}") for c in range(2)] for i in range(2)]
        gam = [K.sb([128, 4, 2], F32, f"gam{i}") for i in range(2)]
        Amat = [K.sb([128, 2, 2, 2, 128], BF16, f"Amat{i}") for i in range(4)]
        XP = [K.sb([128, 2, 2, 128], F32, f"XP{i}") for i in range(4)]
        XT = [K.sb([128, 2, 128], F32, f"XT{i}") for i in range(4)]
        Pbf = [K.sb([128, 2, 128], BF16, f"Pbf{i}") for i in range(4)]
        AW = [K.sb([128, 2, 128], BF16, f"AW{i}") for i in range(4)]
        AU = [K.sb([128, 2, 128], BF16, f"AU{i}") for i in range(4)]
        QT = [K.sb([128, 128], BF16, f"QT{i}") for i in range(4)]
        AUXH = [Buf(PS[6 + i // 2].ap[:, (i % 2) * 256:(i % 2) * 256 + 256], f"aux{i}", root=PS[6 + i // 2]) for i in range(4)]

        def front_hg(n):
            kind, j = order[n]
            isx = kind == "x"
            fb = n % 2
            ec0 = (0 if kind == "ctx" else CTX) + j * 128

            def hv(ct0):
                return zT_d.ap[ct0 * 128:(ct0 + 4) * 128, ec0:ec0 + 128].rearrange("(h p) t -> p h t", p=128)
            K.dma(ld_q[:], hv(0), [zT_d], [ld_q])
            K.dma(ld_f[:], hv(4 + 4 * dirn), [zT_d], [ld_f])
            K.dma(ld_i[:], hv(12), [zT_d], [ld_i])
            zq, zf, zi_ = ld_q, ld_f, ld_i
            K.act(sgq[:], zq[:], AF.Sigmoid, [zq], [sgq])
            K.act(sgf[:], zf[:], AF.Sigmoid, [zf], [sgf])
            K.tt("pool", qh[:], zq[:], sgq[:], ALU.mult, [zq, sgq], [qh])
            for h in range(4):
                c_ = dirn * 4 + h
                K.ts("dve", ff[:, h, :], sgf[:, h, :], drv[:, DV_OML + c_:DV_OML + c_ + 1], drv[:, DV_LB + c_:DV_LB + c_ + 1],
                     ALU.mult, ALU.add, [sgf, drv], [ff])
                K.act(kdh[:, h, :], sgf[:, h, :], AF.Identity, [sgf, drv], [kdh],
                      scale=drv[:, DV_NOML + c_:DV_NOML + c_ + 1], bias=drv[:, DV_OML + c_:DV_OML + c_ + 1])
            K.act(lg[:], ff[:], AF.Ln, [ff], [lg])
            for h in range(4):
                K.op("dve", lambda h=h: nc.vector.tensor_tensor_scan(bcum[:, h, :], rst32, lg[:, h, :], 0.0, ALU.mult, ALU.add),
                     [lg, cst], [bcum])
            if dirn:
                bv4 = bcum[:].rearrange("p h (c t) -> p (h c) t", t=32)
                K.tt("pool", tmp1[:], lg[:], bcum[:], ALU.subtract, [lg, bcum], [tmp1])
                K.tt("dve", e2[:].rearrange("p h (c t) -> p (h c) t", t=32), tmp1[:].rearrange("p h (c t) -> p (h c) t", t=32),
                     _bc(bv4[:, :, 31:32], [128, 16, 32]), ALU.add, [tmp1, bcum], [e2])
                K.cp("pool", bcum[:], e2[:], [e2], [bcum])
            K.act(e1[:], bcum[:], AF.Exp, [bcum], [e1])
            K.act(e2[:], bcum[:], AF.Exp, [bcum], [e2], scale=-1.0)
            K.tt("dve", qb[:], qh[:], e1[:], ALU.mult, [qh, e1], [qb])
            K.tt("pool", kb[:], kdh[:], e2[:], ALU.mult, [kdh, e2], [kb])
            K.cp("pool", ib16[:], zi_[:], [zi_], [ib16])
            tb = psb(0)
            for h in range(4):
                K.tr(tb[:, h * 128:(h + 1) * 128], kb[:, h, :], identb, [kb, cstb], [PS[0]])
            for h in range(4):
                K.tr(tb[:, 512 + h * 128:512 + (h + 1) * 128], ib16[:, h, :], identb, [ib16, cstb], [PS[0]])
            for c in range(4):
                K.act(kbTm[c][:].rearrange("p h t -> p (h t)"), tb[:, 0:512], AF.Identity, [PS[0], cst], [kbTm[c]],
                      scale=cst[:, C_ROWM, c:c + 1])
            K.cp("act", iT[:].rearrange("p h t -> p (h t)"), tb[:, 512:1024], [PS[0]], [iT])
            if isx:
                for h in range(4):
                    K.mm(PS[0][:, h * 128:(h + 1) * 128], kb[:, h, :], qb[:, h, :], [kb, qb], [PS[0]])
                K.tt("dve", ATm[:], PS[0][:, :].rearrange("p (h t) -> p h t", h=4), mshgr[:], ALU.mult, [PS[0], mshgr], [ATm])
            corder = [0, 1, 2, 3] if dirn == 0 else [3, 2, 1, 0]
            for ci, c in enumerate(corder):
                K.cp("act", Sbf[c][:], S[:], [S], [Sbf[c]])
                kvb = PS[0]
                for h in range(4):
                    K.mm(kvb[:, h * 128:(h + 1) * 128], kbTm[c][:, h, :], iT[:, h, :], [kbTm[c], iT], [kvb])
                dcol = c * 32 + (0 if dirn else 31)
                K.tt("dve", stmp[:], kvb[:, :].rearrange("p (h v) -> p h v", h=4), S[:], ALU.add, [kvb, S], [stmp])
                K.tt("pool", S[:], stmp[:], _bc(e1[:, :, dcol:dcol + 1], [128, 4, 128]), ALU.mult, [stmp, e1], [S])
            if isx:
                ob = PS[0]
                for h in range(4):
                    with K.atomic():
                        K.mm(ob[:, h * 128:(h + 1) * 128], iT[:, h, :], ATm[:, h, :], [iT, ATm], [ob], start=True, stop=False)
                        for c in range(4):
                            K.mm(ob[:, h * 128 + c * 32:h * 128 + (c + 1) * 32], Sbf[c][:, h, :], qb[:, h, c * 32:(c + 1) * 32],
                                 [Sbf[c], qb], [ob], start=False, stop=(c == 3))
                K.cp("act", osb[fb][:, 0:4, :], ob[:, :].rearrange("p (h t) -> p h t", h=4), [ob], [osb[fb]])

        def front_rw(n):
            kind, j = order[n]
            isx = kind == "x"
            fb = n % 2
            ec0 = (0 if kind == "ctx" else CTX) + j * 128
            for g in (3, 1, 0, 2):
                nt_ = 4 if g < 3 else 2
                src_ = zr_d.ap[4 * g * 128:(4 * g + nt_) * 128, ec0:ec0 + 128].rearrange("(h p) t -> p h t", p=128)
                K.dma(zrl[:, 4 * g:4 * g + nt_, :], src_, [zr_d], [zrl_g[g]])
            r_ = zrl[:, 0:4, :]
            k_ = zrl[:, 4:8, :]
            v_ = zrl[:, 8:12, :]
            K.act(Lb[0:64, :], zrl[0:64, 12, :], AF.Tanh, [zrl], [Lb])
            K.cp("pool", Lb[64:128, :], zrl[64:128, 12, :], [zrl], [Lb])
            pw = pa = PS[1]
            for jj in range(4):
                K.mm(pw[:, jj * 128:(jj + 1) * 128], wlb[:, dirn, jj * 128:(jj + 1) * 128], Lb[:], [wlb, Lb], [pw])
            for jj in range(4):
                K.act(sw[:, jj, :], pw[:, jj * 128:(jj + 1) * 128], AF.Sigmoid, [pw, ptab], [sw],
                      bias=ptab[:, PT_W0 + dirn * 4 + jj:PT_W0 + dirn * 4 + jj + 1])
            for jj in range(4):
                K.mm(pa[:, jj * 128:(jj + 1) * 128], wlb[:, 2 + dirn, jj * 128:(jj + 1) * 128], Lb[:], [wlb, Lb], [pa])
            for jj in range(4):
                K.act(asg[:, jj, :], pa[:, jj * 128:(jj + 1) * 128], AF.Sigmoid, [pa, ptab], [asg],
                      bias=ptab[:, PT_A0 + dirn * 4 + jj:PT_A0 + dirn * 4 + jj + 1])
            if final and isx:
                for jj in range(4):
                    K.mm(pa[:, jj * 128:(jj + 1) * 128], wlb[:, 2, jj * 128:(jj + 1) * 128], Lb[:], [wlb, Lb], [pa])
                for jj in range(4):
                    K.act(asg2[:, jj, :], pa[:, jj * 128:(jj + 1) * 128], AF.Sigmoid, [pa, ptab], [asg2],
                          bias=ptab[:, PT_A0 + jj:PT_A0 + jj + 1])
            for jj in range(4):
                K.op("dve", lambda jj=jj: nc.vector.tensor_tensor_scan(cs[:, jj, :], rst64, sw[:, jj, :], 0.0, ALU.mult, ALU.add),
                     [sw, cst], [cs])
            if dirn == 0:
                K.tt("pool", csm[:], cs[:], sw[:], ALU.subtract, [cs, sw], [csm])
            else:
                c8 = cs[:].rearrange("p j (c t) -> p (j c) t", t=64)
                K.tt("dve", csm[:].rearrange("p j (c t) -> p (j c) t", t=64), _bc(c8[:, :, 63:64], [128, 8, 64]), c8, ALU.subtract,
                     [cs], [csm])
                K.tt("pool", cs[:], csm[:], sw[:], ALU.add, [csm, sw], [cs])
            K.act(E1[:], csm[:], AF.Exp, [csm], [E1], scale=-LWS)
            K.act(E2[:], cs[:], AF.Exp, [cs], [E2], scale=-LWS)
            K.act(E3[:], cs[:], AF.Exp, [cs], [E3], scale=LWS)
            goff = 0 if dirn else 63
            K.cp("pool", gam[fb][:], E2[:].rearrange("p j (c t) -> p j c t", t=64)[:, :, :, goff], [E2], [gam[fb]])
            for jj in range(4):
                K.act(kk[:, jj, :], k_[:, jj, :], AF.Identity, [zrl, ptab], [kk], scale=ptab[:, PT_KK + jj:PT_KK + jj + 1])
            K.act(sq[:], kk[:], AF.Square, [kk], [sq])
            K.mm(PS[1][:, :], blk64, sq[:].rearrange("p j t -> p (j t)"), [cst, sq], [PS[1]])
            K.ts("dve", rn[:].rearrange("p j t -> p (j t)"), PS[1][:, :], epsc[:, 3:4], None, ALU.max, None, [PS[1], epsc], [rn])
            K.act(rn[:], rn[:], AF.Ln, [rn], [rn])
            K.act(rn[:], rn[:], AF.Exp, [rn], [rn], scale=-0.5)
            K.tt("dve", kkn[:], kk[:], rn[:], ALU.mult, [kk, rn], [kkn])
            for jj in range(4):
                K.act(tmpa[:, jj, :], asg[:, jj, :], AF.Identity, [asg, ptab, drv], [tmpa],
                      scale=ptab[:, PT_KA + jj:PT_KA + jj + 1], bias=drv[:, DV_OMKA + jj:DV_OMKA + jj + 1])
            K.tt("pool", kdr[:], k_, tmpa[:], ALU.mult, [zrl, tmpa], [kdr])
            K.tt("pool", bvec[:], kkn[:], asg[:], ALU.mult, [kkn, asg], [bvec])
            K.stt(AR[fb][:, :, 0, :], kkn[:], -1.0, E1[:], ALU.mult, ALU.mult, [kkn, E1], [AR[fb]])
            K.tt("dve", AR[fb][:, :, 1, :], r_, E2[:], ALU.mult, [zrl, E2], [AR[fb]])
            K.tt("dve", bt[fb][:], bvec[:], E3[:], ALU.mult, [bvec, E3], [bt[fb]])
            K.tt("pool", kt[fb][:], kdr[:], E3[:], ALU.mult, [kdr, E3], [kt[fb]])
            K.cp("pool", vb16[:], v_, [zrl], [vb16])
            tb0 = psb(1)
            for jj in range(4):
                K.tr(tb0[:, jj * 128:(jj + 1) * 128], AR[fb][:, jj, 0, :], identb, [AR[fb], cstb], [PS[1]])
                K.tr(tb0[:, 512 + jj * 128:512 + (jj + 1) * 128], vb16[:, jj, :], identb, [vb16, cstb], [PS[1]])
            K.cp("act", aT[fb][:].rearrange("p j t -> p (j t)"), tb0[:, 0:512], [PS[1]], [aT[fb]])
            K.cp("act", vT[fb][:].rearrange("p j t -> p (j t)"), tb0[:, 512:1024], [PS[1]], [vT[fb]])
            for jj in range(4):
                K.tr(tb0[:, jj * 128:(jj + 1) * 128], bt[fb][:, jj, :], identb, [bt[fb], cstb], [PS[1]])
                K.tr(tb0[:, 512 + jj * 128:512 + (jj + 1) * 128], kt[fb][:, jj, :], identb, [kt[fb], cstb], [PS[1]])
            for c in range(2):
                K.act(btTm[fb][c][:].rearrange("p j t -> p (j t)"), tb0[:, 0:512], AF.Identity, [PS[1], cst], [btTm[fb][c]],
                      scale=cst[:, C_ROWM, 4 + c:5 + c])
                K.act(ktTm[fb][c][:].rearrange("p j t -> p (j t)"), tb0[:, 512:1024], AF.Identity, [PS[1], cst], [ktTm[fb][c]],
                      scale=cst[:, C_ROWM, 4 + c:5 + c])
            if final and isx:
                K.act(sgd[:], zrl[0:96, 13, :], AF.Sigmoid, [zrl], [sgd])
                for jj in range(4):
                    K.act(tmpa2[:, jj, :], asg2[:, jj, :], AF.Identity, [asg2, ptab, drv], [tmpa2],
                          scale=ptab[:, PT_KA + jj:PT_KA + jj + 1], bias=drv[:, DV_OMKA + jj:DV_OMKA + jj + 1])
                K.tt("pool", tmpa2[:], tmpa2[:], tmpa[:], ALU.add, [tmpa2, tmpa], [tmpa2])
                K.tt("pool", tmpa2[:], tmpa2[:], k_, ALU.mult, [tmpa2, zrl], [tmpa2])
                for jj in range(4):
                    K.stt(bonp[:, jj, :], r_[:, jj, :], ptab[:, PT_RK + jj:PT_RK + jj + 1], tmpa2[:, jj, :], ALU.mult, ALU.mult,
                          [zrl, ptab, tmpa2], [bonp])
                K.mm(PS[1][:, :], blk64, bonp[:].rearrange("p j t -> p (j t)"), [cst, bonp], [PS[1]])
                K.tt("dve", exb[fb][:, 0:4, :], PS[1][:, :].rearrange("p (j t) -> p j t", j=4), v_, ALU.mult, [PS[1], zrl], [exb[fb]])
                pg = PS[1]
                for jj in range(4):
                    K.mm(pg[:, jj * 128:(jj + 1) * 128], g2b[:, jj * 128:(jj + 1) * 128], sgd[:], [g2b, sgd], [pg])
                K.cp("act", exb[fb][:, 4:8, :], pg[:, :].rearrange("p (j t) -> p j t", j=4), [pg], [exb[fb]])
                K.dma(ex_d.ap[j], exb[fb][:], [exb[fb]], [ex_d])

        def pairs(n, k):
            kind, j = order[n]
            isx = kind == "x"
            fb = n % 2
            jj = k
            ba = PS[2 + k]
            bb = AUXH[k]
            am, aw, au, qt, pbf = Amat[k], AW[k], AU[k], QT[k], Pbf[k]
            xp, xt = XP[k], XT[k]
            ar, bt_, kt_, aT_, vT_ = AR[fb], bt[fb], kt[fb], aT[fb], vT[fb]
            for e in range(2):
                ep = slice(e * 64, (e + 1) * 64)
                arf = ar[ep, jj, :, :].rearrange("p a t -> p (a t)")
                K.mm(ba[:, 0:256], bt_[ep, jj, :], arf, [bt_, ar], [ba])
                K.mm(ba[:, 256:512], kt_[ep, jj, :], arf, [kt_, ar], [ba])
                bv_ = ba[:, :].rearrange("p (w a t) -> p w a t", w=2, a=2)
                K.tt("dve", xp[:, e, 0, :], ba[:, 0:128], ms32, ALU.mult, [ba, cst], [xp])
                K.tt("dve", am[:, e, :, :, :], bv_, msir[:], ALU.mult, [ba, msir], [am])
            for e in range(2):
                ep = slice(e * 64, (e + 1) * 64)
                bk = ba if e == 0 else bb
                K.mm(bk[:, 0:128], ar[ep, jj, 0, :], bt_[ep, jj, :], [ar, bt_], [bk])
                K.tt("dve", xt[:, e, :], bk[:, 0:128], mnt32, ALU.mult, [bk, cst], [xt])
            K.cp("pool", xp[:, :, 1, :], id32r[:], [id32r], [xp])
            pav = ba[:, :].rearrange("p (e a t) -> p e a t", e=2, a=2)
            for lvl in range(6):
                last = lvl == 5
                for e in range(2):
                    if not last:
                        K.mm(ba[:, e * 256:(e + 1) * 256], xt[:, e, :], xp[:, e, :, :].rearrange("p a t -> p (a t)"), [xt, xp], [ba])
                    else:
                        K.mm(ba[:, e * 256 + 128:(e + 1) * 256], xt[:, e, :], xp[:, e, 1, :], [xt, xp], [ba])
                if not last:
                    for e in range(2):
                        K.mm(bb[:, e * 128:(e + 1) * 128], xp[:, e, 0, :], xt[:, e, :], [xp, xt], [bb])
                K.tt("dve", xp[:, :, 1, :], pav[:, :, 1, :], xp[:, :, 1, :], ALU.add, [ba, xp], [xp])
                if not last:
                    K.cp("act", xp[:, :, 0, :], pav[:, :, 0, :], [ba], [xp])
                    K.cp("act", xt[:], bb[:, 0:256].rearrange("p (e t) -> p e t", e=2), [bb], [xt])
            K.cp("pool", pbf[:], xp[:, :, 1, :], [xp], [pbf])
            for e in range(2):
                K.mm(ba[:, e * 64:(e + 1) * 64], am[:, e, 1, 0, :], vT_[:, jj, e * 64:(e + 1) * 64], [am, vT_], [ba])
            K.cp("pool", aw[:, :, 0:64], aT_[:, jj, :].rearrange("p (e k) -> p e k", e=2), [aT_], [aw])
            K.cp("act", aw[:, :, 64:128], ba[:, 0:128].rearrange("p (e v) -> p e v", e=2), [ba], [aw])
            for e in range(2):
                K.mm(bb[:, e * 128:(e + 1) * 128], pbf[:, e, :], aw[:, e, :], [pbf, aw], [bb])
            K.cp("dve", au[:], bb[:, 0:256].rearrange("p (e c) -> p e c", e=2), [bb], [au])
            if isx:
                for e in range(2):
                    K.mm(ba[e * 64:(e + 1) * 64, 128:256], au[:, e, 0:64], am[:, e, 0, 1, :], [au, am], [ba])
                K.tt("dve", qt[:], ba[:, 128:256], ar[:, jj, 1, :], ALU.add, [ba, ar], [qt])
            corder2 = [0, 1] if dirn == 0 else [1, 0]
            Hj = H[jj]
            for ci, c in enumerate(corder2):
                hb = Hbd[jj][c]
                mt = MTbd[jj][c]
                for e in range(2):
                    K.cp("act", hb[e * 64:(e + 1) * 64, e * 64:(e + 1) * 64], Hj[e * 64:(e + 1) * 64, :], [Hj], [hb])
                for e in range(2):
                    K.mm(bb[e * 64:(e + 1) * 64, 0:64], au[:, e, 0:64], btTm[fb][c][:, jj, e * 64:(e + 1) * 64], [au, btTm[fb][c]], [bb])
                for e in range(2):
                    K.tt("dve", mt[e * 64:(e + 1) * 64, e * 64:(e + 1) * 64], bb[e * 64:(e + 1) * 64, 0:64],
                         ident[e * 64:(e + 1) * 64, e * 64:(e + 1) * 64], ALU.add, [bb, cst], [mt])
                with K.atomic():
                    K.mm(ba[:, 384:448], mt[:], Hj[:], [mt, Hj], [ba], start=True, stop=False)
                    for e in range(2):
                        ep = slice(e * 64, (e + 1) * 64)
                        K.mm(ba[ep, 384:448], btTm[fb][c][:, jj, ep], au[:, e, 64:128], [btTm[fb][c], au], [ba], start=False, stop=False)
                        K.mm(ba[ep, 384:448], ktTm[fb][c][:, jj, ep], vT_[:, jj, ep], [ktTm[fb][c], vT_], [ba], start=False, stop=True)
                K.ts("dve", Hj[:], ba[:, 384:448], gam[fb][:, jj, c:c + 1], None, ALU.mult, None, [ba, gam[fb]], [Hj])
            if isx:
                with K.atomic():
                    for e in range(2):
                        ep = slice(e * 64, (e + 1) * 64)
                        K.mm(ba[ep, 256:384], au[:, e, 64:128], am[:, e, 0, 1, :], [au, am], [ba], start=True, stop=False)
                        K.mm(ba[ep, 256:384], vT_[:, jj, ep], am[:, e, 1, 1, :], [vT_, am], [ba], start=False, stop=False)
                    for c in range(2):
                        K.mm(ba[:, 256 + c * 64:256 + (c + 1) * 64], Hbd[jj][c][:], qt[:, c * 64:(c + 1) * 64], [Hbd[jj][c], qt], [ba],
                             start=False, stop=(c == 1))
                K.cp("act", osb[fb][:, 4 + jj, :], ba[:, 256:384], [ba], [osb[fb]])

        def tail(n):
            kind, j = order[n]
            if kind != "x":
                return
            fb = n % 2
            K.dma((ob_d if final else of_d).ap[j], osb[fb][:], [osb[fb]], [ob_d if final else of_d])

        import os
        NOIL = os.environ.get("NOIL", "0") == "1"
        NT_ = len(order)
        if NOIL:
            for n in range(NT_):
                front_hg(n); front_rw(n); pairs(n, 0); pairs(n, 1); pairs(n, 2); pairs(n, 3); tail(n)
        else:
            K.run_streams([lambda: front_hg(0), lambda: front_rw(0)])
            for n in range(NT_):
                fns = [lambda n=n, k=k: pairs(n, k) for k in range(4)]
                if n + 1 < NT_:
                    fns.append(lambda n=n: front_hg(n + 1))
                    fns.append(lambda n=n: front_rw(n + 1))
                K.run_streams(fns)
                tail(n)
        K.barrier()
        K.stack.close()

    def run_pc():
        K.stack = ExitStack()
        woutb = K.sb([128, 8, D], BF16, "woutb")
        lnb_ = K.sb([128, 2, D], F32, "ln1bc")
        K.dma(lnb_[:, 0, :], lnrow_d.ap[0:1, :].partition_broadcast(128), [lnrow_d], [lnb_])
        K.dma(lnb_[:, 1, :], lnrow_d.ap[1:2, :].partition_broadcast(128), [lnrow_d], [lnb_])
        wo_v = wout_d.ap.rearrange("(k p) c -> p k c", p=128)
        wos = [K.sb([128, D], F32, f"wos{i}") for i in range(2)]
        for dk in range(8):
            K.dma(wos[dk % 2][:], wo_v[:, dk, :], [wout_d], [wos[dk % 2]])
            K.cp("act" if dk % 2 else "dve", woutb[:, dk, :], wos[dk % 2][:], [wos[dk % 2]], [woutb])
        blk64 = cst[:, C_BLK64, :]
        onesf = cst[:, C_ONES, :]
        NS = 3
        bufs = []
        for s in range(NS):
            d_ = {}
            d_["of"] = K.sb([128, 8, 128], F32, f"pc_of{s}")
            d_["ob"] = K.sb([128, 8, 128], F32, f"pc_ob{s}")
            d_["ex"] = K.sb([128, 8, 128], F32, f"pc_ex{s}")
            d_["og"] = K.sb([128, 4, 128], F32, f"pc_og{s}")
            d_["x"] = K.sb([128, D], F32, f"pc_x{s}")
            for nm in ("ohg", "sqh", "rsth", "sog", "ysb", "ycen", "yn2"):
                d_[nm] = K.sb([128, 4, 128], F32, f"pc_{nm}{s}")
            d_["sq2"], d_["rstd2"] = d_["sqh"], d_["rsth"]
            d_["yT"] = K.sb([128, 8, 128], BF16, f"pc_yT{s}")
            d_["h1"] = K.sb([128, D], F32, f"pc_h1{s}")
            d_["x1t"] = K.sb([128, D], F32, f"pc_x1t{s}")
            d_["st1"] = K.sb([128, 16], F32, f"pc_st{s}")
            d_["dbg"] = K.sb([128, 8, 128], F32, f"pc_dbg{s}") if debug else None
            bufs.append(d_)

        def pc_stream(s):
            B_ = bufs[s]
            pA = pB = PS[2 * s]
            pC = pD = PS[2 * s + 1]
            for j in range(s, NTX, NS):
                ec0 = CTX + j * 128
                K.dma(B_["of"][:], of_d.ap[j], [of_d], [B_["of"]])
                K.dma(B_["ob"][:], ob_d.ap[j], [ob_d], [B_["ob"]])
                K.dma(B_["ex"][:], ex_d.ap[j], [ex_d], [B_["ex"]])
                K.dma(B_["og"][:], zT_d.ap[16 * 128:20 * 128, ec0:ec0 + 128].rearrange("(h p) t -> p h t", p=128), [zT_d], [B_["og"]])
                K.dma(B_["x"][:], x_d[j * 128:(j + 1) * 128, :], [x_d], [B_["x"]])
                ohg, sqh, rsth, sog, ysb, ycen, sq2, rstd2, yn2 = [B_[k_] for k_ in ("ohg", "sqh", "rsth", "sog", "ysb", "ycen", "sq2", "rstd2", "yn2")]
                yT, h1, x1t, st1, xt = B_["yT"], B_["h1"], B_["x1t"], B_["st1"], B_["x"]
                K.tt("pool", ohg[:], B_["of"][:, 0:4, :], B_["ob"][:, 0:4, :], ALU.add, [B_["of"], B_["ob"]], [ohg])
                K.tt("pool", sqh[:], ohg[:], ohg[:], ALU.mult, [ohg], [sqh])
                K.mm(pA[:, :], onesf, sqh[:].rearrange("p h t -> p (h t)"), [cst, sqh], [pA])
                K.act(rsth[:].rearrange("p h t -> p (h t)"), pA[:, :], AF.Ln, [pA, epsc], [rsth], bias=epsc[:, 1:2], scale=1.0 / 128.0)
                K.act(rsth[:], rsth[:], AF.Exp, [rsth], [rsth], scale=-0.5)
                K.act(sog[:], B_["og"][:], AF.Sigmoid, [B_["og"]], [sog])
                K.tt("pool", sog[:], sog[:], B_["og"][:], ALU.mult, [sog, B_["og"]], [sog])
                K.tt("dve", ohg[:], ohg[:], rsth[:], ALU.mult, [ohg, rsth], [ohg])
                K.stt(yT[:, 0:4, :], ohg[:], ptab[:, PT_NW:PT_NW + 1], sog[:], ALU.mult, ALU.mult, [ohg, ptab, sog], [yT])
                K.tt("pool", ysb[:], B_["of"][:, 4:8, :], B_["ob"][:, 4:8, :], ALU.add, [B_["of"], B_["ob"]], [ysb])
                K.mm(pB[:, :], blk64, ysb[:].rearrange("p j t -> p (j t)"), [cst, ysb], [pB])
                K.stt(ycen[:], pB[:, :].rearrange("p (j t) -> p j t", j=4), -1.0 / 64.0, ysb[:], ALU.mult, ALU.add, [pB, ysb], [ycen])
                K.tt("pool", sq2[:], ycen[:], ycen[:], ALU.mult, [ycen], [sq2])
                K.mm(pB[:, :], blk64, sq2[:].rearrange("p j t -> p (j t)"), [cst, sq2], [pB])
                K.act(rstd2[:].rearrange("p j t -> p (j t)"), pB[:, :], AF.Ln, [pB, epsc], [rstd2], bias=epsc[:, 2:3], scale=1.0 / 64.0)
                K.act(rstd2[:], rstd2[:], AF.Exp, [rstd2], [rstd2], scale=-0.5)
                K.tt("dve", ycen[:], ycen[:], rstd2[:], ALU.mult, [ycen, rstd2], [ycen])
                for jj in range(4):
                    K.ts("pool", yn2[:, jj, :], ycen[:, jj, :], ptab[:, PT_LNW + jj:PT_LNW + jj + 1], ptab[:, PT_LNB + jj:PT_LNB + jj + 1],
                         ALU.mult, ALU.add, [ycen, ptab], [yn2])
                K.tt("pool", yn2[:], yn2[:], B_["ex"][:, 0:4, :], ALU.add, [yn2, B_["ex"]], [yn2])
                K.tt("dve", yT[:, 4:8, :], yn2[:], B_["ex"][:, 4:8, :], ALU.mult, [yn2, B_["ex"]], [yT])
                if debug:
                    K.cp("pool", B_["dbg"][:], yT[:], [yT], [B_["dbg"]])
                    K.dma(dbg["yT"].ap[j], B_["dbg"][:], [B_["dbg"]], [dbg["yT"]])
                for dh in range(2):
                    bank = pC if dh == 0 else pD
                    with K.atomic():
                        for m in range(8):
                            K.mm(bank[:, :], yT[:, m, :], woutb[:, m, dh * 512:(dh + 1) * 512], [yT, woutb], [bank], start=(m == 0), stop=(m == 7))
                    K.tt("dve", h1[:, dh * 512:(dh + 1) * 512], bank[:, :], gb[:, 0, dh * 512:(dh + 1) * 512], ALU.mult, [bank, gb], [h1])
                K.stt(h1[:], xt[:], ALPHA, h1[:], ALU.mult, ALU.add, [xt, h1], [h1])
                ln_stats2(K, nc, h1, h1.ap, st1, epsc, 1)
                K.ts("dve", x1t[:], h1[:], st1[:, 12:13], st1[:, 15:16], ALU.subtract, ALU.mult, [h1, st1], [x1t])
                K.tt("pool", x1t[:], x1t[:], lnb_[:, 0, :], ALU.mult, [x1t, lnb_], [x1t])
                K.tt("pool", x1t[:], x1t[:], lnb_[:, 1, :], ALU.add, [x1t, lnb_], [x1t])
                K.dma(x1_d[j * 128:(j + 1) * 128, :], x1t[:], [x1t], [x1_d])
                if debug:
                    K.dma(dbg["x1"][j * 128:(j + 1) * 128, :], x1t[:], [x1t], [dbg["x1"]])
        K.run_streams([lambda s=s: pc_stream(s) for s in range(NS)])
        K.barrier()
        K.stack.close()

    if upto < 1:
        return nc, K
    run_lerp()
    run_pass(0)
    if upto < 2:
        return nc, K
    run_pass(1)
    if upto < 2.5:
        return nc, K
    run_pc()
    if upto < 3:
        return nc, K

    GT = 2
    GC = GT * 128
    K.stack = ExitStack()
    wgb = K.sb([128, 8, DFF], BF16, "wgb")
    wub = K.sb([128, 8, DFF], BF16, "wub")
    wdb = K.sb([128, NFT, D], BF16, "wdb")
    outer4 = K.stack
    K.stack = ExitStack()
    stg = [K.sb([128, DFF], F32, f"stg{i}") for i in range(2)]
    si = 0
    for (wd_, wb_, nk, ncol) in ((wg_d, wgb, 8, DFF), (wu_d, wub, 8, DFF), (wd_d, wdb, NFT, D)):
        v = wd_.ap.rearrange("(k p) c -> p k c", p=128)
        for kk_ in range(nk):
            s = stg[si % 2]
            K.dma(s[:, 0:ncol], v[:, kk_, :], [wd_], [s])
            K.cp(("act", "dve", "pool")[si % 3], wb_[:, kk_, :], s[:, 0:ncol], [s], [wb_])
            si += 1
    K.barrier()
    K.stack.close()
    K.stack = outer4
    ln2bc = K.sb([128, 2, D], F32, "ln2bc")
    K.dma(ln2bc[:, 0, :], lnrow_d.ap[2:3, :].partition_broadcast(128), [lnrow_d], [ln2bc])
    K.dma(ln2bc[:, 1, :], lnrow_d.ap[3:4, :].partition_broadcast(128), [lnrow_d], [ln2bc])
    x1g = [K.sb([128, GT, D], F32, f"x1g{i}") for i in range(2)]
    u2T = [K.sb([128, 8, GC], BF16, f"u2T{i}") for i in range(2)]
    hT = K.sb([128, NFT, GC], BF16, "hT")
    xnb2 = [K.sb([128, D], BF16, "xnb2_0")] * 2
    st2 = [K.sb([128, 16], F32, f"st2_{i}") for i in range(2)]
    sgl = [K.sb([128, GC], F32, f"sgl{i}") for i in range(2)]
    h2 = [K.sb([128, D], F32, f"h2_{i}") for i in range(2)]
    st3 = [K.sb([128, 16], F32, f"st3_{i}") for i in range(2)]
    ngrp = (NTX + GT - 1) // GT
    GUB = [PS[i] for i in (0, 1, 2, 3, 6)]

    def ffn_pro(g):
        nt = min(GT, NTX - g * GT)
        xg = x1g[g % 2]
        ut = u2T[g % 2]
        for i in range(nt):
            t = g * GT + i
            K.dma(xg[:, i, :], x1_d[t * 128:(t + 1) * 128, :], [x1_d], [xg])
        for i in range(nt):
            st = st2[i % 2]
            xb = xnb2[i % 2]
            ln_stats2(K, nc, xg, xg[:, i, :], st, epsc, 0)
            K.ts("dve", xb[:], xg[:, i, :], st[:, 12:13], st[:, 15:16], ALU.subtract, ALU.mult, [xg, st], [xb])
            tb = psb(7)
            with K.atomic():
                for dk in range(8):
                    K.tr(tb[:, dk * 128:(dk + 1) * 128], xb[:, dk * 128:(dk + 1) * 128], identb, [xb, cstb], [PS[7]])
            for dk in range(8):
                K.act(ut[:, dk, i * 128:(i + 1) * 128], tb[:, dk * 128:(dk + 1) * 128], AF.Identity, [PS[7], opsc, modT], [ut],
                      bias=modT[:, 24 + dk, 0:1], scale=opsc[:, 2, dk:dk + 1])

    def ffn_main(g):
        nt = min(GT, NTX - g * GT)
        ncol = nt * 128
        xg = x1g[g % 2]
        ut = u2T[g % 2]
        for ft in range(NFT):
            bg, bu = GUB[(2 * ft) % 5], GUB[(2 * ft + 1) % 5]
            with K.atomic():
                for dk in range(8):
                    K.mm(bg[:, 0:ncol], wgb[:, dk, ft * 128:(ft + 1) * 128], ut[:, dk, 0:ncol], [wgb, ut], [bg], start=(dk == 0), stop=(dk == 7))
            with K.atomic():
                for dk in range(8):
                    K.mm(bu[:, 0:ncol], wub[:, dk, ft * 128:(ft + 1) * 128], ut[:, dk, 0:ncol], [wub, ut], [bu], start=(dk == 0), stop=(dk == 7))
            sg_ = sgl[ft % 2]
            K.act(sg_[:, 0:ncol], bg[:, 0:ncol], AF.Sigmoid, [bg], [sg_])
            K.tt("dve", sg_[:, 0:ncol], sg_[:, 0:ncol], bg[:, 0:ncol], ALU.mult, [sg_, bg], [sg_])
            K.tt("dve", hT[:, ft, 0:ncol], sg_[:, 0:ncol], bu[:, 0:ncol], ALU.mult, [sg_, bu], [hT])
        for i in range(nt):
            t = g * GT + i
            hh = h2[t % 2]
            st = st3[t % 2]
            for dh in range(2):
                bank = PS[4 + dh]
                with K.atomic():
                    for ft in range(NFT):
                        K.mm(bank[:, :], hT[:, ft, i * 128:(i + 1) * 128], wdb[:, ft, dh * 512:(dh + 1) * 512], [hT, wdb], [bank],
                             start=(ft == 0), stop=(ft == NFT - 1))
                K.tt("dve", hh[:, dh * 512:(dh + 1) * 512], bank[:, :], gb[:, 1, dh * 512:(dh + 1) * 512], ALU.mult, [bank, gb], [hh])
            K.stt(hh[:], xg[:, i, :], ALPHA, hh[:], ALU.mult, ALU.add, [xg, hh], [hh])
            ln_stats2(K, nc, hh, hh.ap, st, epsc, 1)
            K.ts("dve", hh[:], hh[:], st[:, 12:13], st[:, 15:16], ALU.subtract, ALU.mult, [hh, st], [hh])
            K.tt("pool", hh[:], hh[:], ln2bc[:, 0, :], ALU.mult, [hh, ln2bc], [hh])
            K.tt("pool", hh[:], hh[:], ln2bc[:, 1, :], ALU.add, [hh, ln2bc], [hh])
            K.dma(out_d[t * 128:(t + 1) * 128, :], hh[:], [hh], [out_d])

    ffn_pro(0)
    for g in range(ngrp):
        fns = [lambda g=g: ffn_main(g)]
        if g + 1 < ngrp:
            fns.append(lambda g=g: ffn_pro(g + 1))
        K.run_streams(fns)
    K.barrier()
    K.stack.close()
    return nc, K


def ln_stats2(K, nc, src, ap, st, epsc, eps_col):
    K.op("dve", lambda: nc.vector.bn_stats(st[:, 0:6], ap[:, 0:512]), [src], [st])
    K.op("dve", lambda: nc.vector.bn_stats(st[:, 6:12], ap[:, 512:1024]), [src], [st])
    K.op("dve", lambda: nc.vector.bn_aggr(st[:, 12:14], st[:, 0:12]), [st], [st])
    K.act(st[:, 14:15], st[:, 13:14], AF.Ln, [st, epsc], [st], bias=epsc[:, eps_col:eps_col + 1])
    K.act(st[:, 15:16], st[:, 14:15], AF.Exp, [st], [st], scale=-0.5)


def _consts():
    c = np.zeros((128, NCONST, 128), np.float32)
    s = np.arange(128)[:, None]
    t = np.arange(128)[None, :]
    c[:, C_ID] = (s == t)
    c[:, C_M32F] = (s // 32 == t // 32) & (s <= t)
    c[:, C_M32B] = (s // 32 == t // 32) & (s >= t)
    c[:, C_MSF] = (s // 64 == t // 64) & (s < t)
    c[:, C_MIF] = (s // 64 == t // 64) & (s <= t)
    c[:, C_MSB] = (s // 64 == t // 64) & (s > t)
    c[:, C_MIB] = (s // 64 == t // 64) & (s >= t)
    c[:, C_BLK64] = (s // 64 == t // 64)
    c[:, C_ONES] = 1.0
    c[:, C_RST32] = np.broadcast_to((t % 32 != 0), (128, 128))
    c[:, C_RST64] = np.broadcast_to((t % 64 != 0), (128, 128))
    rm = np.zeros((128, 128), np.float32)
    for k in range(4):
        rm[:, k] = (np.arange(128) // 32 == k)
    for k in range(2):
        rm[:, 4 + k] = (np.arange(128) // 64 == k)
    c[:, C_ROWM] = rm
    return c


def _fm(v, nt):
    return np.ascontiguousarray(np.asarray(v, np.float32).reshape(nt, 128).T)


def _ptab(inp):
    pt = np.zeros((128, NPT), np.float32)
    lbl = np.asarray(inp["hgrn_lb_logits"], np.float32)
    for d in range(2):
        pt[:, PT_L0 + 4 * d:PT_L0 + 4 * d + 4] = _fm(lbl[0, d], 4)
        pt[:, PT_L1 + 4 * d:PT_L1 + 4 * d + 4] = _fm(lbl[1, d], 4)
    pt[:, PT_NW] = np.asarray(inp["hgrn_norm_w"], np.float32)[0]
    mu = np.zeros(14 * 128, np.float32)
    mu[:1760] = np.asarray(inp["rwkv_mu"], np.float32)[0]
    pt[:, PT_MU:PT_MU + 14] = _fm(mu, 14)
    ch = np.arange(14 * 128)
    valid = ch < 1760
    masks = [ch < 440, (ch >= 440) & (ch < 880), (ch >= 880) & (ch < 1320), (ch >= 1320) & valid, ch < 880, (ch >= 880) & valid]
    for i, m in enumerate(masks):
        pt[:, PT_ML + 14 * i:PT_ML + 14 * (i + 1)] = _fm(m.astype(np.float32), 14)
    for d in range(2):
        pt[:, PT_W0 + 4 * d:PT_W0 + 4 * d + 4] = _fm(inp["rwkv_w0"][0, d], 4)
        pt[:, PT_A0 + 4 * d:PT_A0 + 4 * d + 4] = _fm(inp["rwkv_a0"][0, d], 4)
    pt[:, PT_KK:PT_KK + 4] = _fm(inp["rwkv_k_k"][0], 4)
    pt[:, PT_KA:PT_KA + 4] = _fm(inp["rwkv_k_a"][0], 4)
    pt[:, PT_RK:PT_RK + 4] = _fm(np.asarray(inp["rwkv_r_k"])[0].reshape(512), 4)
    pt[:, PT_LNW:PT_LNW + 4] = _fm(inp["rwkv_lnx_w"][0], 4)
    pt[:, PT_LNB:PT_LNB + 4] = _fm(inp["rwkv_lnx_b"][0], 4)
    pt[:, PT_BADA:PT_BADA + 48] = _fm(inp["b_ada"][0], 48)
    return pt


def _shared_maps(inp):
    f = lambda a: np.ascontiguousarray(np.asarray(a, np.float32))
    wl4 = np.zeros((128, 4, 512), np.float32)
    wl4[0:32, 0] = inp["rwkv_w2"][0, 0]
    wl4[32:64, 1] = inp["rwkv_w2"][0, 1]
    wl4[64:96, 2] = inp["rwkv_a2"][0, 0]
    wl4[96:128, 3] = inp["rwkv_a2"][0, 1]
    lnrows = np.stack([f(inp["ln1_g"])[0], f(inp["ln1_b"])[0], f(inp["ln2_g"])[0], f(inp["ln2_b"])[0]], 0)
    return {
        "w_ada": f(inp["w_ada"])[0], "b_ada_row": f(inp["b_ada"]), "w_in": f(inp["w_in"])[0], "ptab": _ptab(inp),
        "consts": _consts(), "wl4": wl4, "g2": f(inp["rwkv_g2"])[0], "w_out": f(inp["w_out"])[0],
        "lnrows": np.ascontiguousarray(lnrows), "w_gate": f(inp["w_ffn_gate"])[0], "w_up": f(inp["w_ffn_up"])[0],
        "w_down": f(inp["w_ffn_down"])[0],
    }


def _core_map(inp, shared, b):
    m = dict(shared)
    m["x"] = np.ascontiguousarray(np.asarray(inp["x"][b], np.float32))
    m["ctx"] = np.ascontiguousarray(np.asarray(inp["ctx"][b], np.float32))
    cv = np.zeros((128, 16), np.float32)
    cv[:, 0::2] = np.asarray(inp["c"][b], np.float32).reshape(8, 128).T
    cv[:, 1::2] = np.asarray(inp["c_ctx"], np.float32).reshape(8, 128).T
    m["cv"] = cv
    return m


_NC_CACHE = {}


def kernel(**inputs):
    x = np.asarray(inputs["x"])
    B, T, _ = x.shape
    if T not in _NC_CACHE:
        _NC_CACHE[T] = build(T)[0]
    nc = _NC_CACHE[T]
    shared = _shared_maps(inputs)
    in_maps = [_core_map(inputs, shared, b) for b in range(B)]
    res = run_bass_kernel_spmd(nc, in_maps, core_ids=list(range(B)))
    return np.stack([np.asarray(r["out"], np.float32) for r in res.results], 0)
```

```python
from contextlib import ExitStack
import numpy as np
import concourse.bass as bass
import concourse.mybir as mybir
from concourse.bass_utils import run_bass_kernel_spmd

F32 = mybir.dt.float32
BF16 = mybir.dt.bfloat16
ALU = mybir.AluOpType
AF = mybir.ActivationFunctionType

D = 1024
CTX = 256
NCT = 34
ZC = NCT * 128
IN_COLS = 4320
DFF = 2816
NFT = DFF // 128
LWS = 0.6065306597126334
ALPHA = 2.0 ** 0.25

C_ID, C_M32F, C_M32B, C_MSF, C_MIF, C_MSB, C_MIB, C_BLK64, C_ONES, C_RST32, C_RST64, C_ROWM = range(12)
NCONST = 12
PT_L0 = 0
PT_L1 = 8
PT_NW = 16
PT_MU = 17
PT_ML = 31
PT_W0 = 115
PT_A0 = 123
PT_KK = 131
PT_KA = 135
PT_RK = 139
PT_LNW = 143
PT_LNB = 147
PT_BADA = 151
NPT = 199


class Buf:
    __slots__ = ("ap", "w", "r", "name", "tw", "tr", "root")

    def __init__(self, ap, name="", root=None):
        self.root = root if root is not None else self
        self.ap = ap
        self.w = {}
        self.r = {}
        self.name = name
        self.tw = 0.0
        self.tr = 0.0

    def __getitem__(self, k):
        return self.ap[k]


class KB:
    NR = 8

    def __init__(self, nc):
        self.nc = nc
        self.E = {"pe": nc.tensor, "dve": nc.vector, "act": nc.scalar, "pool": nc.gpsimd, "sp": nc.sync}
        self.sems = []
        self.semval = []
        self.esem = {e: self._newsem("c_" + e) for e in self.E}
        self.dsem = {"sp": [self._newsem(f"d_sp{i}") for i in range(self.NR)]}
        self.didx = {"sp": 0}
        self.seen = {e: {} for e in self.E}
        self.nbuf = 0
        self.nwait = 0
        self.ninst = 0
        self.stack = None
        self._st = None
        self.clk = {}

    def _newsem(self, name):
        self.sems.append(self.nc.alloc_semaphore(name))
        self.semval.append(0)
        return len(self.sems) - 1

    def sb(self, shape, dtype=F32, name=None, perm=False):
        self.nbuf += 1
        name = f"{name or 't'}_{self.nbuf}"
        if perm or self.stack is None:
            h = self.nc.alloc_sbuf_tensor(name, list(shape), dtype)
        else:
            h = self.stack.enter_context(self.nc.sbuf_tensor(name, list(shape), dtype))
        return Buf(h.ap(), name)

    def ps(self, shape, dtype=F32, name=None):
        self.nbuf += 1
        return Buf(self.nc.alloc_psum_tensor(f"{name or 'p'}_{self.nbuf}", list(shape), dtype).ap(), name)

    def dram(self, name, shape, dtype=F32, kind="Internal"):
        return Buf(self.nc.dram_tensor(name, list(shape), dtype, kind=kind).ap(), name)

    def _need(self, reads, writes):
        need = {}
        for b in reads:
            for k, v in b.w.items():
                if need.get(k, 0) < v:
                    need[k] = v
        for b in writes:
            for k, v in b.w.items():
                if need.get(k, 0) < v:
                    need[k] = v
            for k, v in b.r.items():
                if need.get(k, 0) < v:
                    need[k] = v
        return need

    def _wait(self, e, need):
        own = self.esem[e]
        seen = self.seen[e]
        eng = self.E[e]
        for k, v in need.items():
            if k == own and e == "pe":
                continue
            if seen.get(k, 0) < v:
                eng.wait_ge(self.sems[k], v)
                seen[k] = v
                self.nwait += 1

    def _commit(self, k, v, reads, writes):
        for b in writes:
            b.w = {k: v}
            b.r = {}
        for b in reads:
            if b.r.get(k, 0) < v:
                b.r[k] = v

    def op(self, e, fn, reads=(), writes=(), cost=0.5):
        reads = [b.root for b in reads]
        writes = [b.root for b in writes]
        self._yield(e, reads, writes)
        self._model(e, reads, writes, cost)
        self._wait(e, self._need(reads, writes))
        ins = fn()
        k = self.esem[e]
        self.semval[k] += 1
        ins.then_inc(self.sems[k], 1)
        self._commit(k, self.semval[k], reads, writes)
        self.ninst += 1
        return ins

    def dma(self, out, in_, reads=(), writes=(), q="sp"):
        reads = [b.root for b in reads]
        writes = [b.root for b in writes]
        self._yield(q, reads, writes)
        self._model(q, reads, writes, 1.0)
        i = self.didx[q]
        self.didx[q] += 1
        k = self.dsem[q][i % self.NR]
        need = self._need(reads, writes)
        if self.semval[k] > 0 and need.get(k, 0) < self.semval[k]:
            need[k] = self.semval[k]
        self._wait(q, need)
        self.semval[k] += 16
        self.E[q].dma_start(out=out, in_=in_).then_inc(self.sems[k], 16)
        self._commit(k, self.semval[k], reads, writes)
        self.ninst += 1


    def run_streams(self, fns):
        import threading
        n = len(fns)
        if n == 1:
            fns[0]()
            return
        st = {"turn": -1, "alive": [True] * n, "err": [], "pend": [None] * n, "started": 0}
        cv = threading.Condition()
        self._st, self._cv = st, cv
        self._tls = threading.local()

        def pick():
            best, bt_ = -1, None
            for i in range(n):
                if st["alive"][i] and st["pend"][i] is not None:
                    t = st["pend"][i]
                    if bt_ is None or t < bt_:
                        best, bt_ = i, t
            st["turn"] = best
            cv.notify_all()
        self._pick = pick

        def all_pending():
            return all((not st["alive"][i]) or st["pend"][i] is not None for i in range(n))
        self._all_pending = all_pending

        def runner(i):
            self._tls.sid = i
            self._tls.atomic = 0
            try:
                fns[i]()
            except BaseException as e:
                st["err"].append(e)
            finally:
                with cv:
                    st["alive"][i] = False
                    st["pend"][i] = None
                    if any(st["alive"]) and all_pending():
                        pick()
        ths = [threading.Thread(target=runner, args=(i,)) for i in range(n)]
        for t in ths:
            t.start()
        for t in ths:
            t.join()
        self._st = None
        if st["err"]:
            raise st["err"][0]

    def _est_start(self, e, reads, writes):
        t = self.clk.get(e, 0.0)
        for b in reads:
            if b.tw > t:
                t = b.tw
        for b in writes:
            if b.tw > t:
                t = b.tw
            if b.tr > t:
                t = b.tr
        return t

    def _model(self, e, reads, writes, cost):
        t = self._est_start(e, reads, writes) + 0.15
        f = t + cost
        if e == "sp":
            self.clk[e] = t + 0.05
            f = t + 2.0 + cost
        else:
            self.clk[e] = f
        for b in writes:
            b.tw = f
        for b in reads:
            if b.tr < f:
                b.tr = f

    def _yield(self, e, reads, writes):
        st = getattr(self, "_st", None)
        if st is None:
            return
        tls = self._tls
        i = getattr(tls, "sid", None)
        if i is None or tls.atomic:
            return
        cv = self._cv
        with cv:
            st["pend"][i] = self._est_start(e, reads, writes)
            if self._all_pending():
                self._pick()
            while st["turn"] != i:
                cv.wait()
            st["turn"] = -1
            st["pend"][i] = None

    def atomic(self):
        kb = self

        class _A:
            def __enter__(self_):
                if getattr(kb, "_st", None) is not None and getattr(kb._tls, "sid", None) is not None:
                    kb._tls.atomic += 1

            def __exit__(self_, *a):
                if getattr(kb, "_st", None) is not None and getattr(kb._tls, "sid", None) is not None:
                    kb._tls.atomic -= 1
        return _A()

    def barrier(self):
        need = {k: v for k, v in enumerate(self.semval) if v > 0}
        for e in self.E:
            self._wait(e, need)

    @staticmethod
    def _n(ap):
        n = 1
        for d in ap.shape[1:]:
            n *= d
        return n

    def mm(self, out, lhsT, rhs, reads, writes, start=True, stop=True):
        nc = self.nc
        c = 0.03 + self._n(rhs) * (4 if rhs.dtype == F32 else 1) / 2400.0
        return self.op("pe", lambda: nc.tensor.matmul(out, lhsT, rhs, start=start, stop=stop), reads, writes, cost=c)

    def tr(self, out, in_, ident, reads, writes):
        nc = self.nc
        return self.op("pe", lambda: nc.tensor.transpose(out, in_, ident), reads, writes, cost=0.09)

    def act(self, out, in_, func, reads, writes, bias=None, scale=None):
        nc = self.nc
        kw = {}
        if bias is not None:
            kw["bias"] = bias
        if scale is not None:
            kw["scale"] = scale
        c = 0.2 + self._n(out) / 1200.0
        return self.op("act", lambda: nc.scalar.activation(out, in_, func, **kw), reads, writes, cost=c)

    def _vc(self, e, out):
        n = self._n(out)
        return (0.1 + n / 500.0) if e == "pool" else (0.07 + n / 900.0)

    def tt(self, e, out, in0, in1, op, reads, writes):
        eng = self.E[e]
        return self.op(e, lambda: eng.tensor_tensor(out, in0, in1, op), reads, writes, cost=self._vc(e, out))

    def ts(self, e, out, in0, s1, s2, op0, op1, reads, writes):
        eng = self.E[e]
        if s2 is None:
            return self.op(e, lambda: eng.tensor_scalar(out, in0, s1, None, op0), reads, writes, cost=self._vc(e, out))
        return self.op(e, lambda: eng.tensor_scalar(out, in0, s1, s2, op0, op1), reads, writes, cost=self._vc(e, out))

    def stt(self, out, in0, scalar, in1, op0, op1, reads, writes):
        nc = self.nc
        return self.op("dve", lambda: nc.vector.scalar_tensor_tensor(out, in0, scalar, in1, op0, op1), reads, writes,
                       cost=0.07 + self._n(out) / 900.0)

    def cp(self, e, out, in_, reads, writes):
        if e == "act":
            nc = self.nc
            return self.op("act", lambda: nc.scalar.copy(out, in_), reads, writes, cost=0.2 + self._n(out) / 1200.0)
        eng = self.E[e]
        return self.op(e, lambda: eng.tensor_copy(out, in_), reads, writes, cost=self._vc(e, out))

    def memset(self, e, ap, val, writes):
        eng = self.E[e]
        return self.op(e, lambda: eng.memset(ap, val), [], writes, cost=self._vc(e, ap))


def _bc(ap, shape):
    return ap.to_broadcast(list(shape))


def build(T, debug=False, upto=9):
    NTX = T // 128
    NE = CTX + T
    nc = bass.Bass("TRN2", target_bir_lowering=False)
    K = KB(nc)
    x_d = K.dram("x", [T, D], kind="ExternalInput")
    ctx_d = K.dram("ctx", [CTX, D], kind="ExternalInput")
    cv_d = K.dram("cv", [128, 16], kind="ExternalInput")
    wada_d = K.dram("w_ada", [D, 6 * D], kind="ExternalInput")
    brow_d = K.dram("b_ada_row", [1, 6 * D], kind="ExternalInput")
    win_d = K.dram("w_in", [D, IN_COLS], kind="ExternalInput")
    ptab_d = K.dram("ptab", [128, NPT], kind="ExternalInput")
    const_d = K.dram("consts", [128, NCONST, 128], kind="ExternalInput")
    wl_d = K.dram("wl4", [128, 4, 512], kind="ExternalInput")
    g2_d = K.dram("g2", [96, 512], kind="ExternalInput")
    wout_d = K.dram("w_out", [D, D], kind="ExternalInput")
    lnrow_d = K.dram("lnrows", [4, D], kind="ExternalInput")
    wg_d = K.dram("w_gate", [D, DFF], kind="ExternalInput")
    wu_d = K.dram("w_up", [D, DFF], kind="ExternalInput")
    wd_d = K.dram("w_down", [DFF, D], kind="ExternalInput")
    out_d = K.dram("out", [T, D], kind="ExternalOutput")
    zT_d = K.dram("zT", [ZC, NE])
    of_d = K.dram("ofwd", [NTX, 128, 8, 128])
    x1_d = K.dram("x1s", [T, D])
    ob_d = K.dram("obwd", [NTX, 128, 8, 128])
    zr_d = K.dram("zrl", [14 * 128, NE])
    ex_d = K.dram("extra", [NTX, 128, 8, 128])
    dbg = {}
    if debug:
        dbg["zT"] = K.dram("dbg_zT", [ZC, NE], kind="ExternalOutput")
        dbg["yT"] = K.dram("dbg_yT", [NTX, 128, 8, 128], kind="ExternalOutput")
        dbg["x1"] = K.dram("dbg_x1", [T, D], kind="ExternalOutput")

    cst = K.sb([128, NCONST, 128], F32, "cst", perm=True)
    cstb = K.sb([128, NCONST, 128], BF16, "cstb", perm=True)
    ptab = K.sb([128, NPT], F32, "ptab", perm=True)
    modT = K.sb([128, 48, 2], F32, "modT", perm=True)
    opsc = K.sb([128, 3, 8], F32, "opsc", perm=True)
    epsc = K.sb([128, 4], F32, "epsc", perm=True)
    gb = K.sb([128, 2, D], F32, "gb", perm=True)
    drv = K.sb([128, 128], F32, "drv", perm=True)
    DV_LB, DV_OML, DV_NOML = 0, 8, 16
    DV_C0 = 24
    DV_CS = 38
    DV_OMKA = 122
    PS = [K.ps([128, 512], F32, f"bank{i}") for i in range(8)]

    def psb(i):
        return PS[i].ap.bitcast(BF16)

    ident = cst[:, C_ID, :]
    identb = cstb[:, C_ID, :]

    K.dma(cst[:], const_d[:, :, :], [const_d], [cst])
    K.dma(ptab[:], ptab_d[:, :], [ptab_d], [ptab])
    K.cp("dve", cstb[:], cst[:], [cst], [cstb])
    K.memset("pool", epsc[:, 0:1], 1e-6, [epsc])
    K.memset("pool", epsc[:, 1:2], 1e-5, [epsc])
    K.memset("pool", epsc[:, 2:3], 64e-5, [epsc])
    K.memset("pool", epsc[:, 3:4], 1e-24, [epsc])
    K.tt("dve", drv[:, 0:8], ptab[:, PT_L0:PT_L0 + 8], ptab[:, PT_L1:PT_L1 + 8], ALU.subtract, [ptab], [drv])
    K.act(drv[:, DV_LB:DV_LB + 8], drv[:, 0:8], AF.Sigmoid, [drv], [drv])
    K.ts("dve", drv[:, DV_OML:DV_OML + 8], drv[:, DV_LB:DV_LB + 8], -1.0, 1.0, ALU.mult, ALU.add, [drv], [drv])
    K.ts("dve", drv[:, DV_NOML:DV_NOML + 8], drv[:, DV_OML:DV_OML + 8], -1.0, None, ALU.mult, None, [drv], [drv])
    K.ts("dve", drv[:, DV_C0:DV_C0 + 14], ptab[:, PT_MU:PT_MU + 14], -1.0, 1.0, ALU.mult, ALU.add, [ptab], [drv])
    for i in range(6):
        K.tt("dve", drv[:, DV_CS + 14 * i:DV_CS + 14 * (i + 1)], ptab[:, PT_MU:PT_MU + 14],
             ptab[:, PT_ML + 14 * i:PT_ML + 14 * (i + 1)], ALU.mult, [ptab], [drv])
    K.ts("dve", drv[:, DV_OMKA:DV_OMKA + 4], ptab[:, PT_KA:PT_KA + 4], -1.0, 1.0, ALU.mult, ALU.add, [ptab], [drv])

    if upto == 0.1:
        K.barrier()
        return nc, K
    K.stack = ExitStack()
    winb = K.sb([128, 8, ZC], BF16, "winb")
    cv = K.sb([128, 16], F32, "cv")
    cvs = K.sb([128, 16], F32, "cvs")
    K.dma(cv[:], cv_d[:, :], [cv_d], [cv])
    outer0 = K.stack
    K.stack = ExitStack()
    brow = K.sb([1, 4, 512], F32, "brow")
    for i_, eg_ in enumerate((4, 5, 10, 11)):
        K.dma(brow[0:1, i_, :], brow_d[0:1, eg_ * 512:(eg_ + 1) * 512], [brow_d], [brow])
    K.act(cvs[:], cv[:], AF.Sigmoid, [cv], [cvs])
    K.tt("dve", cvs[:], cvs[:], cv[:], ALU.mult, [cvs, cv], [cvs])
    wa = [K.sb([128, 8, 512], F32, f"wa{i}") for i in range(2)]
    wada_v = wada_d.ap.rearrange("(k p) e -> p k e", p=128)
    grow = K.sb([1, 512], F32, "grow")
    wst = [K.sb([128, IN_COLS], F32, f"wst{i}") for i in range(2)]
    win_v = win_d.ap.rearrange("(k p) c -> p k c", p=128)

    def adaln_stream():
        for eg in range(12):
            w = wa[eg % 2]
            K.dma(w[:], wada_v[:, :, eg * 512:(eg + 1) * 512], [wada_d], [w])
            bank = PS[eg % 2]
            for j in range(4):
                with K.atomic():
                    for dk in range(8):
                        K.mm(bank[:, 2 * j:2 * j + 2], w[:, dk, j * 128:(j + 1) * 128], cvs[:, 2 * dk:2 * dk + 2], [w, cvs], [bank],
                             start=(dk == 0), stop=(dk == 7))
            for j in range(4):
                et = eg * 4 + j
                K.ts("dve", modT[:, et, :], bank[:, 2 * j:2 * j + 2], ptab[:, PT_BADA + et:PT_BADA + et + 1], None, ALU.add, None,
                     [bank, ptab], [modT])
            if eg in (4, 5, 10, 11):
                gi = 0 if eg < 6 else 1
                half = eg % 2 if eg < 6 else (eg - 10)
                rb_ = PS[2]
                with K.atomic():
                    for dk in range(8):
                        K.mm(rb_[0:1, 0:512], cvs[:, 2 * dk:2 * dk + 1], w[:, dk, :], [w, cvs], [rb_], start=(dk == 0), stop=(dk == 7))
                K.tt("dve", grow[:], rb_[0:1, 0:512], brow[0:1, (4, 5, 10, 11).index(eg), :], ALU.add, [rb_, brow], [grow])
                bb = PS[3]
                K.mm(bb[:, 0:512], cst[0:1, C_ONES, :], grow[:], [cst, grow], [bb])
                K.cp("act", gb[:, gi, half * 512:(half + 1) * 512], bb[:, 0:512], [bb], [gb])
        K.ts("dve", opsc[:, 0, :], modT[:, 8:16, 0], 1.0, None, ALU.add, None, [modT], [opsc])
        K.ts("dve", opsc[:, 1, :], modT[:, 8:16, 1], 1.0, None, ALU.add, None, [modT], [opsc])
        K.ts("dve", opsc[:, 2, :], modT[:, 32:40, 0], 1.0, None, ALU.add, None, [modT], [opsc])

    def wincast_stream():
        K.memset("pool", winb[:, :, IN_COLS:ZC], 0.0, [winb])
        for dk in range(8):
            s = wst[dk % 2]
            K.dma(s[:], win_v[:, dk, :], [win_d], [s])
            K.cp("act" if dk % 2 else "pool", winb[:, dk, 0:IN_COLS], s[:], [s], [winb])

    K.run_streams([adaln_stream, wincast_stream])
    K.barrier()
    if upto == 0.3:
        return nc, K
    K.stack.close()
    K.stack = outer0
    xts = [K.sb([128, D], F32, f"xt{i}") for i in range(2)]
    xnb = [K.sb([128, D], BF16, f"xnb{i}") for i in range(2)]
    stt_ = [K.sb([128, 16], F32, f"st{i}") for i in range(2)]

    def ln_stats(src_ap, src_bufs, st, eps_col):
        K.op("dve", lambda: nc.vector.bn_stats(st[:, 0:6], src_ap[:, 0:512]), src_bufs, [st])
        K.op("dve", lambda: nc.vector.bn_stats(st[:, 6:12], src_ap[:, 512:1024]), src_bufs, [st])
        K.op("dve", lambda: nc.vector.bn_aggr(st[:, 12:14], st[:, 0:12]), [st], [st])
        K.act(st[:, 14:15], st[:, 13:14], AF.Ln, [st, epsc], [st], bias=epsc[:, eps_col:eps_col + 1])
        K.act(st[:, 15:16], st[:, 14:15], AF.Exp, [st], [st], scale=-0.5)

    def modulate_T(src, i, uT, col0, sc_ap, sh_ap, tbank):
        st = stt_[i % 2]
        xb = xnb[i % 2]
        ln_stats(src.ap, [src], st, 0)
        K.ts("dve", xb[:], src[:], st[:, 12:13], st[:, 15:16], ALU.subtract, ALU.mult, [src, st], [xb])
        tb = psb(tbank)
        with K.atomic():
            for dk in range(8):
                K.tr(tb[:, dk * 128:(dk + 1) * 128], xb[:, dk * 128:(dk + 1) * 128], identb, [xb, cstb], [PS[tbank]])
        for dk in range(8):
            K.act(uT[:, dk, col0:col0 + 128], tb[:, dk * 128:(dk + 1) * 128], AF.Identity, [PS[tbank], opsc, modT], [uT],
                  bias=sh_ap(dk), scale=sc_ap(dk))

    uTs = [K.sb([128, 8, 512], BF16, f"uT{i}") for i in range(2)]
    zsb = [K.sb([128, 512], F32, f"zsb{i}") for i in range(4)]
    groups = [("ctx", 0, 2)] + [("x", g * 4, min(4, NTX - g * 4)) for g in range((NTX + 3) // 4)]
    tbase = [0]
    for (_, _, nt_) in groups:
        tbase.append(tbase[-1] + nt_)

    def z_pro(gi_):
        kind, t0, nt = groups[gi_]
        uT = uTs[gi_ % 2]
        src_d = ctx_d if kind == "ctx" else x_d
        mj = 1 if kind == "ctx" else 0
        for i in range(nt):
            ti = tbase[gi_] + i
            xt = xts[ti % 2]
            K.dma(xt[:], src_d[(t0 + i) * 128:(t0 + i + 1) * 128, :], [src_d], [xt])
            modulate_T(xt, ti, uT, i * 128, lambda dk: opsc[:, mj, dk:dk + 1], lambda dk: modT[:, dk, mj:mj + 1], 7)

    def z_main(gi_):
        kind, t0, nt = groups[gi_]
        uT = uTs[gi_ % 2]
        ncol = nt * 128
        ecol0 = (0 if kind == "ctx" else CTX) + t0 * 128
        for ct in range(NCT):
            bank = PS[ct % 6]
            with K.atomic():
                for dk in range(8):
                    K.mm(bank[:, 0:ncol], winb[:, dk, ct * 128:(ct + 1) * 128], uT[:, dk, 0:ncol], [winb, uT], [bank],
                         start=(dk == 0), stop=(dk == 7))
            z = zsb[ct % 4]
            K.cp("act" if ct % 2 else "dve", z[:, 0:ncol], bank[:, 0:ncol], [bank], [z])
            K.dma(zT_d[ct * 128:(ct + 1) * 128, ecol0:ecol0 + ncol], z[:, 0:ncol], [z], [zT_d])
            if debug:
                K.dma(dbg["zT"][ct * 128:(ct + 1) * 128, ecol0:ecol0 + ncol], z[:, 0:ncol], [z], [dbg["zT"]])

    z_pro(0)
    for gi_ in range(len(groups)):
        fns = [lambda gi_=gi_: z_main(gi_)]
        if gi_ + 1 < len(groups):
            fns.append(lambda gi_=gi_: z_pro(gi_ + 1))
        K.run_streams(fns)
    K.barrier()
    K.stack.close()

    def run_lerp():
        K.stack = ExitStack()
        NS = 4
        tiles = [("ctx", 0), ("ctx", 1)] + [("x", j) for j in range(NTX)]
        bufs = []
        for s in range(NS):
            zp = K.sb([128, 14, 4, 66], F32, f"lzp{s}")
            zo = [K.sb([128, 14, 128], F32, f"lzo{s}_{i}") for i in range(2)]
            stg_ = [K.sb([128, 4, 256], F32, f"lst{s}_{g}") for g in range(4)]
            K.memset("pool", zp[:], 0.0, [zp])
            bufs.append((zp, stg_, zo))

        def lstream(s):
            zp, stg_, zo = bufs[s]
            for ti, (kind, j) in enumerate(tiles[s::NS]):
                zrl = zo[ti % 2]
                isx = kind == "x"
                ec0 = (0 if kind == "ctx" else CTX) + j * 128
                if isx:
                    lo = 64 if j > 0 else 0
                    hi = 64 if j < NTX - 1 else 0
                else:
                    lo = 1 if j > 0 else 0
                    hi = 1 if j < 1 else 0
                for g in range(4):
                    nt_ = 4 if g < 3 else 2
                    src_ = zT_d.ap[(20 + 4 * g) * 128:(20 + 4 * g + nt_) * 128, ec0 - lo:ec0 + 128 + hi].rearrange("(h p) t -> p h t", p=128)
                    K.dma(stg_[g][:, 0:nt_, 64 - lo:192 + hi], src_, [zT_d], [stg_[g]])
                if isx:
                    if lo == 0:
                        K.memset("pool", zp[:, :, 0, :], 0.0, [zp])
                    if hi == 0:
                        K.memset("pool", zp[:, :, 3, :], 0.0, [zp])
                    r0 = 1 - lo // 64
                    r1 = 3 + hi // 64
                    for g in range(4):
                        nt_ = 4 if g < 3 else 2
                        for q_ in range(nt_):
                            K.cp(("pool", "pool", "act", "dve")[(4 * g + q_) % 4], zp[:, 4 * g + q_, r0:r1, 1:65],
                                 stg_[g][:, q_, 64 - lo:192 + hi].rearrange("p (r c) -> p r c", c=64), [stg_[g]], [zp])
                    for ct in range(14):
                        views = {"L": zp[:, ct, 1:3, 0:64], "R": zp[:, ct, 1:3, 2:66], "U": zp[:, ct, 0:2, 1:65], "D": zp[:, ct, 2:4, 1:65]}
                        cen = zp[:, ct, 1:3, 1:65]
                        lo_, hi_ = ct * 128, ct * 128 + 128
                        kinds = []
                        if lo_ < 440: kinds.append(("L", 0))
                        if hi_ > 440 and lo_ < 880: kinds.append(("R", 1))
                        if hi_ > 880 and lo_ < 1320: kinds.append(("U", 2))
                        if hi_ > 1320: kinds.append(("D", 3))
                        o3 = zrl[:, ct, :].rearrange("p (r c) -> p r c", c=64)
                        K.act(o3, cen, AF.Identity, [zp, drv], [zrl], scale=drv[:, DV_C0 + ct:DV_C0 + ct + 1])
                        for (vn, ki) in kinds:
                            K.stt(o3, views[vn], drv[:, DV_CS + 14 * ki + ct:DV_CS + 14 * ki + ct + 1], o3, ALU.mult, ALU.add, [zp, drv, zrl], [zrl])
                else:
                    for g in range(4):
                        if lo == 0:
                            K.memset("pool", stg_[g][:, :, 63:64], 0.0, [stg_[g]])
                        if hi == 0:
                            K.memset("pool", stg_[g][:, :, 192:193], 0.0, [stg_[g]])
                    for ct in range(14):
                        g, q_ = ct // 4, ct % 4
                        lo_, hi_ = ct * 128, ct * 128 + 128
                        kinds = []
                        if lo_ < 880: kinds.append((stg_[g][:, q_, 63:191], 4))
                        if hi_ > 880: kinds.append((stg_[g][:, q_, 65:193], 5))
                        K.act(zrl[:, ct, :], stg_[g][:, q_, 64:192], AF.Identity, [stg_[g], drv], [zrl], scale=drv[:, DV_C0 + ct:DV_C0 + ct + 1])
                        for (vw, ki) in kinds:
                            K.stt(zrl[:, ct, :], vw, drv[:, DV_CS + 14 * ki + ct:DV_CS + 14 * ki + ct + 1], zrl[:, ct, :], ALU.mult, ALU.add,
                                  [stg_[g], drv, zrl], [zrl])
                for g in range(2):
                    dst = zr_d.ap[7 * g * 128:(7 * g + 7) * 128, ec0:ec0 + 128].rearrange("(h p) t -> p h t", p=128)
                    K.dma(dst, zrl[:, 7 * g:7 * g + 7, :], [zrl], [zr_d])
        K.run_streams([lambda s=s: lstream(s) for s in range(NS)])
        K.barrier()
        K.stack.close()

    def run_pass(dirn):
        final = dirn == 1
        K.stack = ExitStack()
        wlb = K.sb([128, 4, 512], BF16, "wlb")
        g2b = K.sb([96, 512], BF16, "g2b")
        outer = K.stack
        K.stack = ExitStack()
        tmpw = K.sb([128, 4, 512], F32, "tmpw")
        K.dma(tmpw[:], wl_d[:, :, :], [wl_d], [tmpw])
        K.cp("dve", wlb[:], tmpw[:], [tmpw], [wlb])
        tmpg = K.sb([96, 512], F32, "tmpg")
        K.dma(tmpg[:], g2_d[:, :], [g2_d], [tmpg])
        K.cp("dve", g2b[:], tmpg[:], [tmpg], [g2b])
        K.barrier()
        K.stack.close()
        K.stack = outer
        S = K.sb([128, 4, 128], F32, "S")
        Sbf = [K.sb([128, 4, 128], BF16, f"Sbf{i}") for i in range(4)]
        H = [K.sb([128, 64], F32, f"H{j}") for j in range(4)]
        Hbd = [[K.sb([128, 128], BF16, f"Hbd{j}_{c}") for c in range(2)] for j in range(4)]
        MTbd = [[K.sb([128, 128], F32, f"MT{j}_{c}") for c in range(2)] for j in range(4)]
        K.memset("pool", S[:], 0.0, [S])
        for j in range(4):
            K.memset("pool", H[j][:], 0.0, [H[j]])
            for c in range(2):
                K.memset("pool", Hbd[j][c][:], 0.0, [Hbd[j][c]])
                K.memset("pool", MTbd[j][c][:], 0.0, [MTbd[j][c]])
        for i in range(4):
            K.memset("pool", Sbf[i][:], 0.0, [Sbf[i]])
        ld_q = K.sb([128, 4, 128], F32, "ldq")
        ld_f = K.sb([128, 4, 128], F32, "ldf")
        ld_i = K.sb([128, 4, 128], F32, "ldi")
        zrl = K.sb([128, 14, 128], F32, "zrl")
        zrl_g = [zrl, zrl, zrl, zrl]
        mshg = cstb[:, C_M32B if dirn else C_M32F, :]
        msi = cstb[:, C_MSB:C_MSB + 2, :] if dirn else cstb[:, C_MSF:C_MSF + 2, :]
        ms32 = cst[:, C_MSB, :] if dirn else cst[:, C_MSF, :]
        mnt32 = cst[:, C_MSF, :] if dirn else cst[:, C_MSB, :]
        id32r = K.sb([128, 2, 128], F32, "id32r")
        msir = K.sb([128, 2, 2, 128], BF16, "msir")
        mshgr = K.sb([128, 4, 128], BF16, "mshgr")
        for e_ in range(2):
            K.cp("pool", id32r[:, e_, :], ident, [cst], [id32r])
            K.cp("pool", msir[:, e_, :, :], msi, [cstb], [msir])
        for h_ in range(4):
            K.cp("pool", mshgr[:, h_, :], mshg, [cstb], [mshgr])
        rst32 = cst[:, C_RST32, :]
        rst64 = cst[:, C_RST64, :]
        blk64 = cst[:, C_BLK64, :]

        if dirn == 0:
            order = [("ctx", 0), ("ctx", 1)] + [("x", j) for j in range(NTX)]
        else:
            order = [("ctx", 1), ("ctx", 0)] + [("x", j) for j in range(NTX - 1, -1, -1)]

        def T32(name, shape=(128, 4, 128)):
            return K.sb(list(shape), F32, name)

        def T16(name, shape=(128, 4, 128)):
            return K.sb(list(shape), BF16, name)
        P32 = [T32(f"w32_{i}") for i in range(11)]
        H32 = [T32(f"h32_{i}") for i in range(8)]
        sgq = qh = H32[0]
        sgf = H32[1]
        ff = lg = H32[2]
        kdh = H32[3]
        bcum = H32[4]
        tmp1 = H32[5]
        e1 = H32[6]
        e2 = H32[7]
        sw = sq = P32[0]
        asg = P32[1]
        cs = kdr = P32[2]
        csm = rn = P32[3]
        E1 = P32[4]
        E2 = P32[5]
        tmpa = P32[6]
        bvec = P32[7]
        E3 = P32[8]
        kk = P32[9]
        kkn = P32[10]
        qb, kb, ib16, ATm = [T16(n) for n in ("qb", "kb", "ib16", "ATm")]
        kbTm = [T16(f"kbTm{c}") for c in range(4)]
        iT = T16("iT")
        stmp = T32("stmp")
        Lb = K.sb([128, 128], BF16, "Lb")
        vb16 = T16("vb16")
        if final:
            sgd = K.sb([96, 128], BF16, "sgd")
            asg2, tmpa2, bonp = T32("asg2"), T32("tmpa2"), T32("bonp")
        osb = [K.sb([128, 8, 128], F32, f"osb{i}") for i in range(2)]
        exb = [K.sb([128, 8, 128], F32, f"exb{i}") for i in range(2)] if final else None
        AR = [K.sb([128, 4, 2, 128], BF16, f"AR{i}") for i in range(2)]
        bt = [T16(f"bt{i}") for i in range(2)]
        kt = [T16(f"kt{i}") for i in range(2)]
        aT = [T16(f"aT{i}") for i in range(2)]
        vT = [T16(f"vT{i}") for i in range(2)]
        btTm = [[T16(f"btTm{i}_{c}") for c in range(2)] for i in range(2)]
        ktTm = [[T16(f"ktTm{i}_{c}") for c in range(2)] for i in range(2)]
        gam = [K.sb([128, 4, 2], F32, f"gam{i}") for i in range(2)]
        Amat = [K.sb([128, 2, 2, 2, 128], BF16, f"Amat{i}") for i in range(4)]
        XP = [K.sb([128, 2, 2, 128], F32, f"XP{i}") for i in range(4)]
        XT = [K.sb([128, 2, 128], F32, f"XT{i}") for i in range(4)]
        Pbf = [K.sb([128, 2, 128], BF16, f"Pbf{i}") for i in range(4)]
        AW = [K.sb([128, 2, 128], BF16, f"AW{i}") for i in range(4)]
        AU = [K.sb([128, 2, 128], BF16, f"AU{i}") for i in range(4)]
        QT = [K.sb([128, 128], BF16, f"QT{i}") for i in range(4)]
        AUXH = [Buf(PS[6 + i // 2].ap[:, (i % 2) * 256:(i % 2) * 256 + 256], f"aux{i}", root=PS[6 + i // 2]) for i in range(4)]

        def front_hg(n):
            kind, j = order[n]
            isx = kind == "x"
            fb = n % 2
            ec0 = (0 if kind == "ctx" else CTX) + j * 128

            def hv(ct0):
                return zT_d.ap[ct0 * 128:(ct0 + 4) * 128, ec0:ec0 + 128].rearrange("(h p) t -> p h t", p=128)
            K.dma(ld_q[:], hv(0), [zT_d], [ld_q])
            K.dma(ld_f[:], hv(4 + 4 * dirn), [zT_d], [ld_f])
            K.dma(ld_i[:], hv(12), [zT_d], [ld_i])
            zq, zf, zi_ = ld_q, ld_f, ld_i
            K.act(sgq[:], zq[:], AF.Sigmoid, [zq], [sgq])
            K.act(sgf[:], zf[:], AF.Sigmoid, [zf], [sgf])
            K.tt("pool", qh[:], zq[:], sgq[:], ALU.mult, [zq, sgq], [qh])
            for h in range(4):
                c_ = dirn * 4 + h
                K.ts("dve", ff[:, h, :], sgf[:, h, :], drv[:, DV_OML + c_:DV_OML + c_ + 1], drv[:, DV_LB + c_:DV_LB + c_ + 1],
                     ALU.mult, ALU.add, [sgf, drv], [ff])
                K.act(kdh[:, h, :], sgf[:, h, :], AF.Identity, [sgf, drv], [kdh],
                      scale=drv[:, DV_NOML + c_:DV_NOML + c_ + 1], bias=drv[:, DV_OML + c_:DV_OML + c_ + 1])
            K.act(lg[:], ff[:], AF.Ln, [ff], [lg])
            for h in range(4):
                K.op("dve", lambda h=h: nc.vector.tensor_tensor_scan(bcum[:, h, :], rst32, lg[:, h, :], 0.0, ALU.mult, ALU.add),
                     [lg, cst], [bcum])
            if dirn:
                bv4 = bcum[:].rearrange("p h (c t) -> p (h c) t", t=32)
                K.tt("pool", tmp1[:], lg[:], bcum[:], ALU.subtract, [lg, bcum], [tmp1])
                K.tt("dve", e2[:].rearrange("p h (c t) -> p (h c) t", t=32), tmp1[:].rearrange("p h (c t) -> p (h c) t", t=32),
                     _bc(bv4[:, :, 31:32], [128, 16, 32]), ALU.add, [tmp1, bcum], [e2])
                K.cp("pool", bcum[:], e2[:], [e2], [bcum])
            K.act(e1[:], bcum[:], AF.Exp, [bcum], [e1])
            K.act(e2[:], bcum[:], AF.Exp, [bcum], [e2], scale=-1.0)
            K.tt("dve", qb[:], qh[:], e1[:], ALU.mult, [qh, e1], [qb])
            K.tt("pool", kb[:], kdh[:], e2[:], ALU.mult, [kdh, e2], [kb])
            K.cp("pool", ib16[:], zi_[:], [zi_], [ib16])
            tb = psb(0)
            for h in range(4):
                K.tr(tb[:, h * 128:(h + 1) * 128], kb[:, h, :], identb, [kb, cstb], [PS[0]])
            for h in range(4):
                K.tr(tb[:, 512 + h * 128:512 + (h + 1) * 128], ib16[:, h, :], identb, [ib16, cstb], [PS[0]])
            for c in range(4):
                K.act(kbTm[c][:].rearrange("p h t -> p (h t)"), tb[:, 0:512], AF.Identity, [PS[0], cst], [kbTm[c]],
                      scale=cst[:, C_ROWM, c:c + 1])
            K.cp("act", iT[:].rearrange("p h t -> p (h t)"), tb[:, 512:1024], [PS[0]], [iT])
            if isx:
                for h in range(4):
                    K.mm(PS[0][:, h * 128:(h + 1) * 128], kb[:, h, :], qb[:, h, :], [kb, qb], [PS[0]])
                K.tt("dve", ATm[:], PS[0][:, :].rearrange("p (h t) -> p h t", h=4), mshgr[:], ALU.mult, [PS[0], mshgr], [ATm])
            corder = [0, 1, 2, 3] if dirn == 0 else [3, 2, 1, 0]
            for ci, c in enumerate(corder):
                K.cp("act", Sbf[c][:], S[:], [S], [Sbf[c]])
                kvb = PS[0]
                for h in range(4):
                    K.mm(kvb[:, h * 128:(h + 1) * 128], kbTm[c][:, h, :], iT[:, h, :], [kbTm[c], iT], [kvb])
                dcol = c * 32 + (0 if dirn else 31)
                K.tt("dve", stmp[:], kvb[:, :].rearrange("p (h v) -> p h v", h=4), S[:], ALU.add, [kvb, S], [stmp])
                K.tt("pool", S[:], stmp[:], _bc(e1[:, :, dcol:dcol + 1], [128, 4, 128]), ALU.mult, [stmp, e1], [S])
            if isx:
                ob = PS[0]
                for h in range(4):
                    with K.atomic():
                        K.mm(ob[:, h * 128:(h + 1) * 128], iT[:, h, :], ATm[:, h, :], [iT, ATm], [ob], start=True, stop=False)
                        for c in range(4):
                            K.mm(ob[:, h * 128 + c * 32:h * 128 + (c + 1) * 32], Sbf[c][:, h, :], qb[:, h, c * 32:(c + 1) * 32],
                                 [Sbf[c], qb], [ob], start=False, stop=(c == 3))
                K.cp("act", osb[fb][:, 0:4, :], ob[:, :].rearrange("p (h t) -> p h t", h=4), [ob], [osb[fb]])

        def front_rw(n):
            kind, j = order[n]
            isx = kind == "x"
            fb = n % 2
            ec0 = (0 if kind == "ctx" else CTX) + j * 128
            for g in (3, 1, 0, 2):
                nt_ = 4 if g < 3 else 2
                src_ = zr_d.ap[4 * g * 128:(4 * g + nt_) * 128, ec0:ec0 + 128].rearrange("(h p) t -> p h t", p=128)
                K.dma(zrl[:, 4 * g:4 * g + nt_, :], src_, [zr_d], [zrl_g[g]])
            r_ = zrl[:, 0:4, :]
            k_ = zrl[:, 4:8, :]
            v_ = zrl[:, 8:12, :]
            K.act(Lb[0:64, :], zrl[0:64, 12, :], AF.Tanh, [zrl], [Lb])
            K.cp("pool", Lb[64:128, :], zrl[64:128, 12, :], [zrl], [Lb])
            pw = pa = PS[1]
            for jj in range(4):
                K.mm(pw[:, jj * 128:(jj + 1) * 128], wlb[:, dirn, jj * 128:(jj + 1) * 128], Lb[:], [wlb, Lb], [pw])
            for jj in range(4):
                K.act(sw[:, jj, :], pw[:, jj * 128:(jj + 1) * 128], AF.Sigmoid, [pw, ptab], [sw],
                      bias=ptab[:, PT_W0 + dirn * 4 + jj:PT_W0 + dirn * 4 + jj + 1])
            for jj in range(4):
                K.mm(pa[:, jj * 128:(jj + 1) * 128], wlb[:, 2 + dirn, jj * 128:(jj + 1) * 128], Lb[:], [wlb, Lb], [pa])
            for jj in range(4):
                K.act(asg[:, jj, :], pa[:, jj * 128:(jj + 1) * 128], AF.Sigmoid, [pa, ptab], [asg],
                      bias=ptab[:, PT_A0 + dirn * 4 + jj:PT_A0 + dirn * 4 + jj + 1])
            if final and isx:
                for jj in range(4):
                    K.mm(pa[:, jj * 128:(jj + 1) * 128], wlb[:, 2, jj * 128:(jj + 1) * 128], Lb[:], [wlb, Lb], [pa])
                for jj in range(4):
                    K.act(asg2[:, jj, :], pa[:, jj * 128:(jj + 1) * 128], AF.Sigmoid, [pa, ptab], [asg2],
                          bias=ptab[:, PT_A0 + jj:PT_A0 + jj + 1])
            for jj in range(4):
                K.op("dve", lambda jj=jj: nc.vector.tensor_tensor_scan(cs[:, jj, :], rst64, sw[:, jj, :], 0.0, ALU.mult, ALU.add),
                     [sw, cst], [cs])
            if dirn == 0:
                K.tt("pool", csm[:], cs[:], sw[:], ALU.subtract, [cs, sw], [csm])
            else:
                c8 = cs[:].rearrange("p j (c t) -> p (j c) t", t=64)
                K.tt("dve", csm[:].rearrange("p j (c t) -> p (j c) t", t=64), _bc(c8[:, :, 63:64], [128, 8, 64]), c8, ALU.subtract,
                     [cs], [csm])
                K.tt("pool", cs[:], csm[:], sw[:], ALU.add, [csm, sw], [cs])
            K.act(E1[:], csm[:], AF.Exp, [csm], [E1], scale=-LWS)
            K.act(E2[:], cs[:], AF.Exp, [cs], [E2], scale=-LWS)
            K.act(E3[:], cs[:], AF.Exp, [cs], [E3], scale=LWS)
            goff = 0 if dirn else 63
            K.cp("pool", gam[fb][:], E2[:].rearrange("p j (c t) -> p j c t", t=64)[:, :, :, goff], [E2], [gam[fb]])
            for jj in range(4):
                K.act(kk[:, jj, :], k_[:, jj, :], AF.Identity, [zrl, ptab], [kk], scale=ptab[:, PT_KK + jj:PT_KK + jj + 1])
            K.act(sq[:], kk[:], AF.Square, [kk], [sq])
            K.mm(PS[1][:, :], blk64, sq[:].rearrange("p j t -> p (j t)"), [cst, sq], [PS[1]])
            K.ts("dve", rn[:].rearrange("p j t -> p (j t)"), PS[1][:, :], epsc[:, 3:4], None, ALU.max, None, [PS[1], epsc], [rn])
            K.act(rn[:], rn[:], AF.Ln, [rn], [rn])
            K.act(rn[:], rn[:], AF.Exp, [rn], [rn], scale=-0.5)
            K.tt("dve", kkn[:], kk[:], rn[:], ALU.mult, [kk, rn], [kkn])
            for jj in range(4):
                K.act(tmpa[:, jj, :], asg[:, jj, :], AF.Identity, [asg, ptab, drv], [tmpa],
                      scale=ptab[:, PT_KA + jj:PT_KA + jj + 1], bias=drv[:, DV_OMKA + jj:DV_OMKA + jj + 1])
            K.tt("pool", kdr[:], k_, tmpa[:], ALU.mult, [zrl, tmpa], [kdr])
            K.tt("pool", bvec[:], kkn[:], asg[:], ALU.mult, [kkn, asg], [bvec])
            K.stt(AR[fb][:, :, 0, :], kkn[:], -1.0, E1[:], ALU.mult, ALU.mult, [kkn, E1], [AR[fb]])
            K.tt("dve", AR[fb][:, :, 1, :], r_, E2[:], ALU.mult, [zrl, E2], [AR[fb]])
            K.tt("dve", bt[fb][:], bvec[:], E3[:], ALU.mult, [bvec, E3], [bt[fb]])
            K.tt("pool", kt[fb][:], kdr[:], E3[:], ALU.mult, [kdr, E3], [kt[fb]])
            K.cp("pool", vb16[:], v_, [zrl], [vb16])
            tb0 = psb(1)
            for jj in range(4):
                K.tr(tb0[:, jj * 128:(jj + 1) * 128], AR[fb][:, jj, 0, :], identb, [AR[fb], cstb], [PS[1]])
                K.tr(tb0[:, 512 + jj * 128:512 + (jj + 1) * 128], vb16[:, jj, :], identb, [vb16, cstb], [PS[1]])
            K.cp("act", aT[fb][:].rearrange("p j t -> p (j t)"), tb0[:, 0:512], [PS[1]], [aT[fb]])
            K.cp("act", vT[fb][:].rearrange("p j t -> p (j t)"), tb0[:, 512:1024], [PS[1]], [vT[fb]])
            for jj in range(4):
                K.tr(tb0[:, jj * 128:(jj + 1) * 128], bt[fb][:, jj, :], identb, [bt[fb], cstb], [PS[1]])
                K.tr(tb0[:, 512 + jj * 128:512 + (jj + 1) * 128], kt[fb][:, jj, :], identb, [kt[fb], cstb], [PS[1]])
            for c in range(2):
                K.act(btTm[fb][c][:].rearrange("p j t -> p (j t)"), tb0[:, 0:512], AF.Identity, [PS[1], cst], [btTm[fb][c]],
                      scale=cst[:, C_ROWM, 4 + c:5 + c])
                K.act(ktTm[fb][c][:].rearrange("p j t -> p (j t)"), tb0[:, 512:1024], AF.Identity, [PS[1], cst], [ktTm[fb][c]],
                      scale=cst[:, C_ROWM, 4 + c:5 + c])
            if final and isx:
                K.act(sgd[:], zrl[0:96, 13, :], AF.Sigmoid, [zrl], [sgd])
                for jj in range(4):
                    K.act(tmpa2[:, jj, :], asg2[:, jj, :], AF.Identity, [asg2, ptab, drv], [tmpa2],
                          scale=ptab[:, PT_KA + jj:PT_KA + jj + 1], bias=drv[:, DV_OMKA + jj:DV_OMKA + jj + 1])
                K.tt("pool", tmpa2[:], tmpa2[:], tmpa[:], ALU.add, [tmpa2, tmpa], [tmpa2])
                K.tt("pool", tmpa2[:], tmpa2[:], k_, ALU.mult, [tmpa2, zrl], [tmpa2])
                for jj in range(4):
                    K.stt(bonp[:, jj, :], r_[:, jj, :], ptab[:, PT_RK + jj:PT_RK + jj + 1], tmpa2[:, jj, :], ALU.mult, ALU.mult,
                          [zrl, ptab, tmpa2], [bonp])
                K.mm(PS[1][:, :], blk64, bonp[:].rearrange("p j t -> p (j t)"), [cst, bonp], [PS[1]])
                K.tt("dve", exb[fb][:, 0:4, :], PS[1][:, :].rearrange("p (j t) -> p j t", j=4), v_, ALU.mult, [PS[1], zrl], [exb[fb]])
                pg = PS[1]
                for jj in range(4):
                    K.mm(pg[:, jj * 128:(jj + 1) * 128], g2b[:, jj * 128:(jj + 1) * 128], sgd[:], [g2b, sgd], [pg])
                K.cp("act", exb[fb][:, 4:8, :], pg[:, :].rearrange("p (j t) -> p j t", j=4), [pg], [exb[fb]])
                K.dma(ex_d.ap[j], exb[fb][:], [exb[fb]], [ex_d])

        def pairs(n, k):
            kind, j = order[n]
            isx = kind == "x"
            fb = n % 2
            jj = k
            ba = PS[2 + k]
            bb = AUXH[k]
            am, aw, au, qt, pbf = Amat[k], AW[k], AU[k], QT[k], Pbf[k]
            xp, xt = XP[k], XT[k]
            ar, bt_, kt_, aT_, vT_ = AR[fb], bt[fb], kt[fb], aT[fb], vT[fb]
            for e in range(2):
                ep = slice(e * 64, (e + 1) * 64)
                arf = ar[ep, jj, :, :].rearrange("p a t -> p (a t)")
                K.mm(ba[:, 0:256], bt_[ep, jj, :], arf, [bt_, ar], [ba])
                K.mm(ba[:, 256:512], kt_[ep, jj, :], arf, [kt_, ar], [ba])
                bv_ = ba[:, :].rearrange("p (w a t) -> p w a t", w=2, a=2)
                K.tt("dve", xp[:, e, 0, :], ba[:, 0:128], ms32, ALU.mult, [ba, cst], [xp])
                K.tt("dve", am[:, e, :, :, :], bv_, msir[:], ALU.mult, [ba, msir], [am])
            for e in range(2):
                ep = slice(e * 64, (e + 1) * 64)
                bk = ba if e == 0 else bb
                K.mm(bk[:, 0:128], ar[ep, jj, 0, :], bt_[ep, jj, :], [ar, bt_], [bk])
                K.tt("dve", xt[:, e, :], bk[:, 0:128], mnt32, ALU.mult, [bk, cst], [xt])
            K.cp("pool", xp[:, :, 1, :], id32r[:], [id32r], [xp])
            pav = ba[:, :].rearrange("p (e a t) -> p e a t", e=2, a=2)
            for lvl in range(6):
                last = lvl == 5
                for e in range(2):
                    if not last:
                        K.mm(ba[:, e * 256:(e + 1) * 256], xt[:, e, :], xp[:, e, :, :].rearrange("p a t -> p (a t)"), [xt, xp], [ba])
                    else:
                        K.mm(ba[:, e * 256 + 128:(e + 1) * 256], xt[:, e, :], xp[:, e, 1, :], [xt, xp], [ba])
                if not last:
                    for e in range(2):
                        K.mm(bb[:, e * 128:(e + 1) * 128], xp[:, e, 0, :], xt[:, e, :], [xp, xt], [bb])
                K.tt("dve", xp[:, :, 1, :], pav[:, :, 1, :], xp[:, :, 1, :], ALU.add, [ba, xp], [xp])
                if not last:
                    K.cp("act", xp[:, :, 0, :], pav[:, :, 0, :], [ba], [xp])
                    K.cp("act", xt[:], bb[:, 0:256].rearrange("p (e t) -> p e t", e=2), [bb], [xt])
            K.cp("pool", pbf[:], xp[:, :, 1, :], [xp], [pbf])
            for e in range(2):
                K.mm(ba[:, e * 64:(e + 1) * 64], am[:, e, 1, 0, :], vT_[:, jj, e * 64:(e + 1) * 64], [am, vT_], [ba])
            K.cp("pool", aw[:, :, 0:64], aT_[:, jj, :].rearrange("p (e k) -> p e k", e=2), [aT_], [aw])
            K.cp("act", aw[:, :, 64:128], ba[:, 0:128].rearrange("p (e v) -> p e v", e=2), [ba], [aw])
            for e in range(2):
                K.mm(bb[:, e * 128:(e + 1) * 128], pbf[:, e, :], aw[:, e, :], [pbf, aw], [bb])
            K.cp("dve", au[:], bb[:, 0:256].rearrange("p (e c) -> p e c", e=2), [bb], [au])
            if isx:
                for e in range(2):
                    K.mm(ba[e * 64:(e + 1) * 64, 128:256], au[:, e, 0:64], am[:, e, 0, 1, :], [au, am], [ba])
                K.tt("dve", qt[:], ba[:, 128:256], ar[:, jj, 1, :], ALU.add, [ba, ar], [qt])
            corder2 = [0, 1] if dirn == 0 else [1, 0]
            Hj = H[jj]
            for ci, c in enumerate(corder2):
                hb = Hbd[jj][c]
                mt = MTbd[jj][c]
                for e in range(2):
                    K.cp("act", hb[e * 64:(e + 1) * 64, e * 64:(e + 1) * 64], Hj[e * 64:(e + 1) * 64, :], [Hj], [hb])
                for e in range(2):
                    K.mm(bb[e * 64:(e + 1) * 64, 0:64], au[:, e, 0:64], btTm[fb][c][:, jj, e * 64:(e + 1) * 64], [au, btTm[fb][c]], [bb])
                for e in range(2):
                    K.tt("dve", mt[e * 64:(e + 1) * 64, e * 64:(e + 1) * 64], bb[e * 64:(e + 1) * 64, 0:64],
                         ident[e * 64:(e + 1) * 64, e * 64:(e + 1) * 64], ALU.add, [bb, cst], [mt])
                with K.atomic():
                    K.mm(ba[:, 384:448], mt[:], Hj[:], [mt, Hj], [ba], start=True, stop=False)
                    for e in range(2):
                        ep = slice(e * 64, (e + 1) * 64)
                        K.mm(ba[ep, 384:448], btTm[fb][c][:, jj, ep], au[:, e, 64:128], [btTm[fb][c], au], [ba], start=False, stop=False)
                        K.mm(ba[ep, 384:448], ktTm[fb][c][:, jj, ep], vT_[:, jj, ep], [ktTm[fb][c], vT_], [ba], start=False, stop=True)
                K.ts("dve", Hj[:], ba[:, 384:448], gam[fb][:, jj, c:c + 1], None, ALU.mult, None, [ba, gam[fb]], [Hj])
            if isx:
                with K.atomic():
                    for e in range(2):
                        ep = slice(e * 64, (e + 1) * 64)
                        K.mm(ba[ep, 256:384], au[:, e, 64:128], am[:, e, 0, 1, :], [au, am], [ba], start=True, stop=False)
                        K.mm(ba[ep, 256:384], vT_[:, jj, ep], am[:, e, 1, 1, :], [vT_, am], [ba], start=False, stop=False)
                    for c in range(2):
                        K.mm(ba[:, 256 + c * 64:256 + (c + 1) * 64], Hbd[jj][c][:], qt[:, c * 64:(c + 1) * 64], [Hbd[jj][c], qt], [ba],
                             start=False, stop=(c == 1))
                K.cp("act", osb[fb][:, 4 + jj, :], ba[:, 256:384], [ba], [osb[fb]])

        def tail(n):
            kind, j = order[n]
            if kind != "x":
                return
            fb = n % 2
            K.dma((ob_d if final else of_d).ap[j], osb[fb][:], [osb[fb]], [ob_d if final else of_d])

        import os
        NOIL = os.environ.get("NOIL", "0") == "1"
        NT_ = len(order)
        if NOIL:
            for n in range(NT_):
                front_hg(n); front_rw(n); pairs(n, 0); pairs(n, 1); pairs(n, 2); pairs(n, 3); tail(n)
        else:
            K.run_streams([lambda: front_hg(0), lambda: front_rw(0)])
            for n in range(NT_):
                fns = [lambda n=n, k=k: pairs(n, k) for k in range(4)]
                if n + 1 < NT_:
                    fns.append(lambda n=n: front_hg(n + 1))
                    fns.append(lambda n=n: front_rw(n + 1))
                K.run_streams(fns)
                tail(n)
        K.barrier()
        K.stack.close()

    def run_pc():
        K.stack = ExitStack()
        woutb = K.sb([128, 8, D], BF16, "woutb")
        lnb_ = K.sb([128, 2, D], F32, "ln1bc")
        K.dma(lnb_[:, 0, :], lnrow_d.ap[0:1, :].partition_broadcast(128), [lnrow_d], [lnb_])
        K.dma(lnb_[:, 1, :], lnrow_d.ap[1:2, :].partition_broadcast(128), [lnrow_d], [lnb_])
        wo_v = wout_d.ap.rearrange("(k p) c -> p k c", p=128)
        wos = [K.sb([128, D], F32, f"wos{i}") for i in range(2)]
        for dk in range(8):
            K.dma(wos[dk % 2][:], wo_v[:, dk, :], [wout_d], [wos[dk % 2]])
            K.cp("act" if dk % 2 else "dve", woutb[:, dk, :], wos[dk % 2][:], [wos[dk % 2]], [woutb])
        blk64 = cst[:, C_BLK64, :]
        onesf = cst[:, C_ONES, :]
        NS = 3
        bufs = []
        for s in range(NS):
            d_ = {}
            d_["of"] = K.sb([128, 8, 128], F32, f"pc_of{s}")
            d_["ob"] = K.sb([128, 8, 128], F32, f"pc_ob{s}")
            d_["ex"] = K.sb([128, 8, 128], F32, f"pc_ex{s}")
            d_["og"] = K.sb([128, 4, 128], F32, f"pc_og{s}")
            d_["x"] = K.sb([128, D], F32, f"pc_x{s}")
            for nm in ("ohg", "sqh", "rsth", "sog", "ysb", "ycen", "yn2"):
                d_[nm] = K.sb([128, 4, 128], F32, f"pc_{nm}{s}")
            d_["sq2"], d_["rstd2"] = d_["sqh"], d_["rsth"]
            d_["yT"] = K.sb([128, 8, 128], BF16, f"pc_yT{s}")
            d_["h1"] = K.sb([128, D], F32, f"pc_h1{s}")
            d_["x1t"] = K.sb([128, D], F32, f"pc_x1t{s}")
            d_["st1"] = K.sb([128, 16], F32, f"pc_st{s}")
            d_["dbg"] = K.sb([128, 8, 128], F32, f"pc_dbg{s}") if debug else None
            bufs.append(d_)

        def pc_stream(s):
            B_ = bufs[s]
            pA = pB = PS[2 * s]
            pC = pD = PS[2 * s + 1]
            for j in range(s, NTX, NS):
                ec0 = CTX + j * 128
                K.dma(B_["of"][:], of_d.ap[j], [of_d], [B_["of"]])
                K.dma(B_["ob"][:], ob_d.ap[j], [ob_d], [B_["ob"]])
                K.dma(B_["ex"][:], ex_d.ap[j], [ex_d], [B_["ex"]])
                K.dma(B_["og"][:], zT_d.ap[16 * 128:20 * 128, ec0:ec0 + 128].rearrange("(h p) t -> p h t", p=128), [zT_d], [B_["og"]])
                K.dma(B_["x"][:], x_d[j * 128:(j + 1) * 128, :], [x_d], [B_["x"]])
                ohg, sqh, rsth, sog, ysb, ycen, sq2, rstd2, yn2 = [B_[k_] for k_ in ("ohg", "sqh", "rsth", "sog", "ysb", "ycen", "sq2", "rstd2", "yn2")]
                yT, h1, x1t, st1, xt = B_["yT"], B_["h1"], B_["x1t"], B_["st1"], B_["x"]
                K.tt("pool", ohg[:], B_["of"][:, 0:4, :], B_["ob"][:, 0:4, :], ALU.add, [B_["of"], B_["ob"]], [ohg])
                K.tt("pool", sqh[:], ohg[:], ohg[:], ALU.mult, [ohg], [sqh])
                K.mm(pA[:, :], onesf, sqh[:].rearrange("p h t -> p (h t)"), [cst, sqh], [pA])
                K.act(rsth[:].rearrange("p h t -> p (h t)"), pA[:, :], AF.Ln, [pA, epsc], [rsth], bias=epsc[:, 1:2], scale=1.0 / 128.0)
                K.act(rsth[:], rsth[:], AF.Exp, [rsth], [rsth], scale=-0.5)
                K.act(sog[:], B_["og"][:], AF.Sigmoid, [B_["og"]], [sog])
                K.tt("pool", sog[:], sog[:], B_["og"][:], ALU.mult, [sog, B_["og"]], [sog])
                K.tt("dve", ohg[:], ohg[:], rsth[:], ALU.mult, [ohg, rsth], [ohg])
                K.stt(yT[:, 0:4, :], ohg[:], ptab[:, PT_NW:PT_NW + 1], sog[:], ALU.mult, ALU.mult, [ohg, ptab, sog], [yT])
                K.tt("pool", ysb[:], B_["of"][:, 4:8, :], B_["ob"][:, 4:8, :], ALU.add, [B_["of"], B_["ob"]], [ysb])
                K.mm(pB[:, :], blk64, ysb[:].rearrange("p j t -> p (j t)"), [cst, ysb], [pB])
                K.stt(ycen[:], pB[:, :].rearrange("p (j t) -> p j t", j=4), -1.0 / 64.0, ysb[:], ALU.mult, ALU.add, [pB, ysb], [ycen])
                K.tt("pool", sq2[:], ycen[:], ycen[:], ALU.mult, [ycen], [sq2])
                K.mm(pB[:, :], blk64, sq2[:].rearrange("p j t -> p (j t)"), [cst, sq2], [pB])
                K.act(rstd2[:].rearrange("p j t -> p (j t)"), pB[:, :], AF.Ln, [pB, epsc], [rstd2], bias=epsc[:, 2:3], scale=1.0 / 64.0)
                K.act(rstd2[:], rstd2[:], AF.Exp, [rstd2], [rstd2], scale=-0.5)
                K.tt("dve", ycen[:], ycen[:], rstd2[:], ALU.mult, [ycen, rstd2], [ycen])
                for jj in range(4):
                    K.ts("pool", yn2[:, jj, :], ycen[:, jj, :], ptab[:, PT_LNW + jj:PT_LNW + jj + 1], ptab[:, PT_LNB + jj:PT_LNB + jj + 1],
                         ALU.mult, ALU.add, [ycen, ptab], [yn2])
                K.tt("pool", yn2[:], yn2[:], B_["ex"][:, 0:4, :], ALU.add, [yn2, B_["ex"]], [yn2])
                K.tt("dve", yT[:, 4:8, :], yn2[:], B_["ex"][:, 4:8, :], ALU.mult, [yn2, B_["ex"]], [yT])
                if debug:
                    K.cp("pool", B_["dbg"][:], yT[:], [yT], [B_["dbg"]])
                    K.dma(dbg["yT"].ap[j], B_["dbg"][:], [B_["dbg"]], [dbg["yT"]])
                for dh in range(2):
                    bank = pC if dh == 0 else pD
                    with K.atomic():
                        for m in range(8):
                            K.mm(bank[:, :], yT[:, m, :], woutb[:, m, dh * 512:(dh + 1) * 512], [yT, woutb], [bank], start=(m == 0), stop=(m == 7))
                    K.tt("dve", h1[:, dh * 512:(dh + 1) * 512], bank[:, :], gb[:, 0, dh * 512:(dh + 1) * 512], ALU.mult, [bank, gb], [h1])
                K.stt(h1[:], xt[:], ALPHA, h1[:], ALU.mult, ALU.add, [xt, h1], [h1])
                ln_stats2(K, nc, h1, h1.ap, st1, epsc, 1)
                K.ts("dve", x1t[:], h1[:], st1[:, 12:13], st1[:, 15:16], ALU.subtract, ALU.mult, [h1, st1], [x1t])
                K.tt("pool", x1t[:], x1t[:], lnb_[:, 0, :], ALU.mult, [x1t, lnb_], [x1t])
                K.tt("pool", x1t[:], x1t[:], lnb_[:, 1, :], ALU.add, [x1t, lnb_], [x1t])
                K.dma(x1_d[j * 128:(j + 1) * 128, :], x1t[:], [x1t], [x1_d])
                if debug:
                    K.dma(dbg["x1"][j * 128:(j + 1) * 128, :], x1t[:], [x1t], [dbg["x1"]])
        K.run_streams([lambda s=s: pc_stream(s) for s in range(NS)])
        K.barrier()
        K.stack.close()

    if upto < 1:
        return nc, K
    run_lerp()
    run_pass(0)
    if upto < 2:
        return nc, K
    run_pass(1)
    if upto < 2.5:
        return nc, K
    run_pc()
    if upto < 3:
        return nc, K

    GT = 2
    GC = GT * 128
    K.stack = ExitStack()
    wgb = K.sb([128, 8, DFF], BF16, "wgb")
    wub = K.sb([128, 8, DFF], BF16, "wub")
    wdb = K.sb([128, NFT, D], BF16, "wdb")
    outer4 = K.stack
    K.stack = ExitStack()
    stg = [K.sb([128, DFF], F32, f"stg{i}") for i in range(2)]
    si = 0
    for (wd_, wb_, nk, ncol) in ((wg_d, wgb, 8, DFF), (wu_d, wub, 8, DFF), (wd_d, wdb, NFT, D)):
        v = wd_.ap.rearrange("(k p) c -> p k c", p=128)
        for kk_ in range(nk):
            s = stg[si % 2]
            K.dma(s[:, 0:ncol], v[:, kk_, :], [wd_], [s])
            K.cp(("act", "dve", "pool")[si % 3], wb_[:, kk_, :], s[:, 0:ncol], [s], [wb_])
            si += 1
    K.barrier()
    K.stack.close()
    K.stack = outer4
    ln2bc = K.sb([128, 2, D], F32, "ln2bc")
    K.dma(ln2bc[:, 0, :], lnrow_d.ap[2:3, :].partition_broadcast(128), [lnrow_d], [ln2bc])
    K.dma(ln2bc[:, 1, :], lnrow_d.ap[3:4, :].partition_broadcast(128), [lnrow_d], [ln2bc])
    x1g = [K.sb([128, GT, D], F32, f"x1g{i}") for i in range(2)]
    u2T = [K.sb([128, 8, GC], BF16, f"u2T{i}") for i in range(2)]
    hT = K.sb([128, NFT, GC], BF16, "hT")
    xnb2 = [K.sb([128, D], BF16, "xnb2_0")] * 2
    st2 = [K.sb([128, 16], F32, f"st2_{i}") for i in range(2)]
    sgl = [K.sb([128, GC], F32, f"sgl{i}") for i in range(2)]
    h2 = [K.sb([128, D], F32, f"h2_{i}") for i in range(2)]
    st3 = [K.sb([128, 16], F32, f"st3_{i}") for i in range(2)]
    ngrp = (NTX + GT - 1) // GT
    GUB = [PS[i] for i in (0, 1, 2, 3, 6)]

    def ffn_pro(g):
        nt = min(GT, NTX - g * GT)
        xg = x1g[g % 2]
        ut = u2T[g % 2]
        for i in range(nt):
            t = g * GT + i
            K.dma(xg[:, i, :], x1_d[t * 128:(t + 1) * 128, :], [x1_d], [xg])
        for i in range(nt):
            st = st2[i % 2]
            xb = xnb2[i % 2]
            ln_stats2(K, nc, xg, xg[:, i, :], st, epsc, 0)
            K.ts("dve", xb[:], xg[:, i, :], st[:, 12:13], st[:, 15:16], ALU.subtract, ALU.mult, [xg, st], [xb])
            tb = psb(7)
            with K.atomic():
                for dk in range(8):
                    K.tr(tb[:, dk * 128:(dk + 1) * 128], xb[:, dk * 128:(dk + 1) * 128], identb, [xb, cstb], [PS[7]])
            for dk in range(8):
                K.act(ut[:, dk, i * 128:(i + 1) * 128], tb[:, dk * 128:(dk + 1) * 128], AF.Identity, [PS[7], opsc, modT], [ut],
                      bias=modT[:, 24 + dk, 0:1], scale=opsc[:, 2, dk:dk + 1])

    def ffn_main(g):
        nt = min(GT, NTX - g * GT)
        ncol = nt * 128
        xg = x1g[g % 2]
        ut = u2T[g % 2]
        for ft in range(NFT):
            bg, bu = GUB[(2 * ft) % 5], GUB[(2 * ft + 1) % 5]
            with K.atomic():
                for dk in range(8):
                    K.mm(bg[:, 0:ncol], wgb[:, dk, ft * 128:(ft + 1) * 128], ut[:, dk, 0:ncol], [wgb, ut], [bg], start=(dk == 0), stop=(dk == 7))
            with K.atomic():
                for dk in range(8):
                    K.mm(bu[:, 0:ncol], wub[:, dk, ft * 128:(ft + 1) * 128], ut[:, dk, 0:ncol], [wub, ut], [bu], start=(dk == 0), stop=(dk == 7))
            sg_ = sgl[ft % 2]
            K.act(sg_[:, 0:ncol], bg[:, 0:ncol], AF.Sigmoid, [bg], [sg_])
            K.tt("dve", sg_[:, 0:ncol], sg_[:, 0:ncol], bg[:, 0:ncol], ALU.mult, [sg_, bg], [sg_])
            K.tt("dve", hT[:, ft, 0:ncol], sg_[:, 0:ncol], bu[:, 0:ncol], ALU.mult, [sg_, bu], [hT])
        for i in range(nt):
            t = g * GT + i
            hh = h2[t % 2]
            st = st3[t % 2]
            for dh in range(2):
                bank = PS[4 + dh]
                with K.atomic():
                    for ft in range(NFT):
                        K.mm(bank[:, :], hT[:, ft, i * 128:(i + 1) * 128], wdb[:, ft, dh * 512:(dh + 1) * 512], [hT, wdb], [bank],
                             start=(ft == 0), stop=(ft == NFT - 1))
                K.tt("dve", hh[:, dh * 512:(dh + 1) * 512], bank[:, :], gb[:, 1, dh * 512:(dh + 1) * 512], ALU.mult, [bank, gb], [hh])
            K.stt(hh[:], xg[:, i, :], ALPHA, hh[:], ALU.mult, ALU.add, [xg, hh], [hh])
            ln_stats2(K, nc, hh, hh.ap, st, epsc, 1)
            K.ts("dve", hh[:], hh[:], st[:, 12:13], st[:, 15:16], ALU.subtract, ALU.mult, [hh, st], [hh])
            K.tt("pool", hh[:], hh[:], ln2bc[:, 0, :], ALU.mult, [hh, ln2bc], [hh])
            K.tt("pool", hh[:], hh[:], ln2bc[:, 1, :], ALU.add, [hh, ln2bc], [hh])
            K.dma(out_d[t * 128:(t + 1) * 128, :], hh[:], [hh], [out_d])

    ffn_pro(0)
    for g in range(ngrp):
        fns = [lambda g=g: ffn_main(g)]
        if g + 1 < ngrp:
            fns.append(lambda g=g: ffn_pro(g + 1))
        K.run_streams(fns)
    K.barrier()
    K.stack.close()
    return nc, K


def ln_stats2(K, nc, src, ap, st, epsc, eps_col):
    K.op("dve", lambda: nc.vector.bn_stats(st[:, 0:6], ap[:, 0:512]), [src], [st])
    K.op("dve", lambda: nc.vector.bn_stats(st[:, 6:12], ap[:, 512:1024]), [src], [st])
    K.op("dve", lambda: nc.vector.bn_aggr(st[:, 12:14], st[:, 0:12]), [st], [st])
    K.act(st[:, 14:15], st[:, 13:14], AF.Ln, [st, epsc], [st], bias=epsc[:, eps_col:eps_col + 1])
    K.act(st[:, 15:16], st[:, 14:15], AF.Exp, [st], [st], scale=-0.5)


def _consts():
    c = np.zeros((128, NCONST, 128), np.float32)
    s = np.arange(128)[:, None]
    t = np.arange(128)[None, :]
    c[:, C_ID] = (s == t)
    c[:, C_M32F] = (s // 32 == t // 32) & (s <= t)
    c[:, C_M32B] = (s // 32 == t // 32) & (s >= t)
    c[:, C_MSF] = (s // 64 == t // 64) & (s < t)
    c[:, C_MIF] = (s // 64 == t // 64) & (s <= t)
    c[:, C_MSB] = (s // 64 == t // 64) & (s > t)
    c[:, C_MIB] = (s // 64 == t // 64) & (s >= t)
    c[:, C_BLK64] = (s // 64 == t // 64)
    c[:, C_ONES] = 1.0
    c[:, C_RST32] = np.broadcast_to((t % 32 != 0), (128, 128))
    c[:, C_RST64] = np.broadcast_to((t % 64 != 0), (128, 128))
    rm = np.zeros((128, 128), np.float32)
    for k in range(4):
        rm[:, k] = (np.arange(128) // 32 == k)
    for k in range(2):
        rm[:, 4 + k] = (np.arange(128) // 64 == k)
    c[:, C_ROWM] = rm
    return c


def _fm(v, nt):
    return np.ascontiguousarray(np.asarray(v, np.float32).reshape(nt, 128).T)


def _ptab(inp):
    pt = np.zeros((128, NPT), np.float32)
    lbl = np.asarray(inp["hgrn_lb_logits"], np.float32)
    for d in range(2):
        pt[:, PT_L0 + 4 * d:PT_L0 + 4 * d + 4] = _fm(lbl[0, d], 4)
        pt[:, PT_L1 + 4 * d:PT_L1 + 4 * d + 4] = _fm(lbl[1, d], 4)
    pt[:, PT_NW] = np.asarray(inp["hgrn_norm_w"], np.float32)[0]
    mu = np.zeros(14 * 128, np.float32)
    mu[:1760] = np.asarray(inp["rwkv_mu"], np.float32)[0]
    pt[:, PT_MU:PT_MU + 14] = _fm(mu, 14)
    ch = np.arange(14 * 128)
    valid = ch < 1760
    masks = [ch < 440, (ch >= 440) & (ch < 880), (ch >= 880) & (ch < 1320), (ch >= 1320) & valid, ch < 880, (ch >= 880) & valid]
    for i, m in enumerate(masks):
        pt[:, PT_ML + 14 * i:PT_ML + 14 * (i + 1)] = _fm(m.astype(np.float32), 14)
    for d in range(2):
        pt[:, PT_W0 + 4 * d:PT_W0 + 4 * d + 4] = _fm(inp["rwkv_w0"][0, d], 4)
        pt[:, PT_A0 + 4 * d:PT_A0 + 4 * d + 4] = _fm(inp["rwkv_a0"][0, d], 4)
    pt[:, PT_KK:PT_KK + 4] = _fm(inp["rwkv_k_k"][0], 4)
    pt[:, PT_KA:PT_KA + 4] = _fm(inp["rwkv_k_a"][0], 4)
    pt[:, PT_RK:PT_RK + 4] = _fm(np.asarray(inp["rwkv_r_k"])[0].reshape(512), 4)
    pt[:, PT_LNW:PT_LNW + 4] = _fm(inp["rwkv_lnx_w"][0], 4)
    pt[:, PT_LNB:PT_LNB + 4] = _fm(inp["rwkv_lnx_b"][0], 4)
    pt[:, PT_BADA:PT_BADA + 48] = _fm(inp["b_ada"][0], 48)
    return pt


def _shared_maps(inp):
    f = lambda a: np.ascontiguousarray(np.asarray(a, np.float32))
    wl4 = np.zeros((128, 4, 512), np.float32)
    wl4[0:32, 0] = inp["rwkv_w2"][0, 0]
    wl4[32:64, 1] = inp["rwkv_w2"][0, 1]
    wl4[64:96, 2] = inp["rwkv_a2"][0, 0]
    wl4[96:128, 3] = inp["rwkv_a2"][0, 1]
    lnrows = np.stack([f(inp["ln1_g"])[0], f(inp["ln1_b"])[0], f(inp["ln2_g"])[0], f(inp["ln2_b"])[0]], 0)
    return {
        "w_ada": f(inp["w_ada"])[0], "b_ada_row": f(inp["b_ada"]), "w_in": f(inp["w_in"])[0], "ptab": _ptab(inp),
        "consts": _consts(), "wl4": wl4, "g2": f(inp["rwkv_g2"])[0], "w_out": f(inp["w_out"])[0],
        "lnrows": np.ascontiguousarray(lnrows), "w_gate": f(inp["w_ffn_gate"])[0], "w_up": f(inp["w_ffn_up"])[0],
        "w_down": f(inp["w_ffn_down"])[0],
    }


def _core_map(inp, shared, b):
    m = dict(shared)
    m["x"] = np.ascontiguousarray(np.asarray(inp["x"][b], np.float32))
    m["ctx"] = np.ascontiguousarray(np.asarray(inp["ctx"][b], np.float32))
    cv = np.zeros((128, 16), np.float32)
    cv[:, 0::2] = np.asarray(inp["c"][b], np.float32).reshape(8, 128).T
    cv[:, 1::2] = np.asarray(inp["c_ctx"], np.float32).reshape(8, 128).T
    m["cv"] = cv
    return m


_NC_CACHE = {}


def kernel(**inputs):
    x = np.asarray(inputs["x"])
    B, T, _ = x.shape
    if T not in _NC_CACHE:
        _NC_CACHE[T] = build(T)[0]
    nc = _NC_CACHE[T]
    shared = _shared_maps(inputs)
    in_maps = [_core_map(inputs, shared, b) for b in range(B)]
    res = run_bass_kernel_spmd(nc, in_maps, core_ids=list(range(B)))
    return np.stack([np.asarray(r["out"], np.float32) for r in res.results], 0)
```
